# Optimizing a Trainium2 kernel written in Bass

```python
import jax, jax.numpy as jnp
from jax import lax
import numpy as np

D_MODEL = 1024
BATCH = 2
SEQ = 16384
DEPTH = 1

CHUNK = 64
LEFT_CHUNKS = 8
HEAD_DIM = 64
N_HEADS_A = 8
N_HEADS_B = 8
WIDTH_A = N_HEADS_A * HEAD_DIM
WIDTH_B = N_HEADS_B * HEAD_DIM
MAX_REL = 128
N_REL = 2 * MAX_REL + 1
QUERY_BLOCK = 128
D_FF = 2816
CONV_WIDTH = 3
IN_WIDTH = 3 * WIDTH_A + 3 * WIDTH_B + 2 * D_MODEL
EPS = 1e-6

kernel_name = "hybrid_chunked_stickbreaking_convglu_block"


def rms_norm(x, gain):
    xf = x.astype(jnp.float32)
    y = xf * lax.rsqrt(jnp.mean(xf * xf, axis=-1, keepdims=True) + EPS)
    return (y * gain.astype(jnp.float32)).astype(x.dtype)


def chunked_attention(q, k, v, q_gain, k_gain, rel_bias):
    b, s, h, dh = q.shape
    nc = s // CHUNK
    band = (LEFT_CHUNKS + 1) * CHUNK
    q = rms_norm(q, q_gain)
    k = rms_norm(k, k_gain)
    qc = q.reshape(b, nc, CHUNK, h, dh)
    pad = ((0, 0), (LEFT_CHUNKS * CHUNK, 0), (0, 0), (0, 0))
    kp = jnp.pad(k, pad).reshape(b, nc + LEFT_CHUNKS, CHUNK, h, dh)
    vp = jnp.pad(v, pad).reshape(b, nc + LEFT_CHUNKS, CHUNK, h, dh)
    kb = jnp.concatenate([kp[:, i:i + nc] for i in range(LEFT_CHUNKS + 1)], axis=2)
    vb = jnp.concatenate([vp[:, i:i + nc] for i in range(LEFT_CHUNKS + 1)], axis=2)
    scores = jnp.einsum('bnqhd,bnkhd->bnhqk', qc, kb).astype(jnp.float32) * (dh ** -0.5)
    r = jnp.arange(CHUNK)[:, None]
    key_off = jnp.arange(band)[None, :] - LEFT_CHUNKS * CHUNK
    rel = r - key_off
    idx = jnp.clip(rel, -MAX_REL, MAX_REL) + MAX_REL
    bias = rel_bias[:, idx].astype(jnp.float32)
    valid = (jnp.arange(nc)[:, None] * CHUNK + key_off) >= 0
    scores = jnp.where(valid[None, :, None, None, :], scores + bias[None, None], -jnp.inf)
    probs = jax.nn.softmax(scores, axis=-1).astype(v.dtype)
    out = jnp.einsum('bnhqk,bnkhd->bnqhd', probs, vb)
    return out.reshape(b, s, h * dh)


def stick_breaking_attention(q, k, v):
    b, s, h, dh = q.shape
    nb = s // QUERY_BLOCK
    kh = k.transpose(0, 2, 1, 3)
    vh = v.transpose(0, 2, 1, 3)
    qblocks = q.transpose(0, 2, 1, 3).reshape(b, h, nb, QUERY_BLOCK, dh).transpose(2, 0, 1, 3, 4)
    key_pos = jnp.arange(s)

    def block(args):
        qb, blk = args
        z = jnp.einsum('bhqd,bhkd->bhqk', qb, kh).astype(jnp.float32) * (dh ** -0.5)
        q_pos = blk * QUERY_BLOCK + jnp.arange(QUERY_BLOCK)
        causal = key_pos[None, :] < q_pos[:, None]
        log_keep = jnp.where(causal, -jax.nn.softplus(z), 0.0)
        log_after = lax.cumsum(log_keep, axis=3, reverse=True) - log_keep
        w = jnp.where(causal, jnp.exp(jax.nn.log_sigmoid(z) + log_after), 0.0)
        return jnp.einsum('bhqk,bhkd->bhqd', w.astype(vh.dtype), vh)

    out = lax.map(block, (qblocks, jnp.arange(nb)))
    return out.transpose(1, 0, 3, 2, 4).reshape(b, s, h * dh)


def conv_glu(x, w_up, conv_w, conv_b, w_down):
    hid = x @ w_up
    hid = lax.conv_general_dilated(
        hid, conv_w[:, None, :], window_strides=(1,),
        padding=((CONV_WIDTH - 1, 0),),
        dimension_numbers=('NWC', 'WIO', 'NWC'),
        feature_group_count=2 * D_FF) + conv_b
    gate, up = hid[..., :D_FF], hid[..., D_FF:]
    return (jax.nn.silu(gate) * up) @ w_down


def setup_inputs(seed: int = 0) -> dict:
    key = jax.random.key(seed)
    ks = jax.random.split(key, 16)
    f32 = jnp.float32
    nrm = lambda k, shape, scale: jax.random.normal(k, shape, f32) * scale
    return {
        "x": nrm(ks[0], (BATCH, SEQ, D_MODEL), 1.0),
        "norm1_g": 1.0 + nrm(ks[1], (DEPTH, D_MODEL), 0.02),
        "w_in": nrm(ks[2], (DEPTH, D_MODEL, IN_WIDTH), D_MODEL ** -0.5),
        "q_norm_g": 1.0 + nrm(ks[3], (DEPTH, HEAD_DIM), 0.02),
        "k_norm_g": 1.0 + nrm(ks[4], (DEPTH, HEAD_DIM), 0.02),
        "rel_bias": nrm(ks[5], (DEPTH, N_HEADS_A, N_REL), 0.2),
        "w_branch_a": nrm(ks[6], (DEPTH, WIDTH_A, D_MODEL), WIDTH_A ** -0.5),
        "w_branch_b": nrm(ks[7], (DEPTH, WIDTH_B, D_MODEL), WIDTH_B ** -0.5),
        "w_out": nrm(ks[8], (DEPTH, D_MODEL, D_MODEL), D_MODEL ** -0.5),
        "norm2_g": 1.0 + nrm(ks[9], (DEPTH, D_MODEL), 0.02),
        "w_ffn_up": nrm(ks[10], (DEPTH, D_MODEL, 2 * D_FF), D_MODEL ** -0.5),
        "ffn_conv_w": nrm(ks[11], (DEPTH, CONV_WIDTH, 2 * D_FF), CONV_WIDTH ** -0.5),
        "ffn_conv_b": nrm(ks[12], (DEPTH, 2 * D_FF), 0.01),
        "w_ffn_down": nrm(ks[13], (DEPTH, D_FF, D_MODEL), D_FF ** -0.5),
    }


def reference(x, norm1_g, w_in, q_norm_g, k_norm_g, rel_bias, w_branch_a, w_branch_b,
              w_out, norm2_g, w_ffn_up, ffn_conv_w, ffn_conv_b, w_ffn_down):
    b, s, _ = x.shape
    o1 = 3 * WIDTH_A
    o2 = o1 + 3 * WIDTH_B
    for l in range(DEPTH):
        hn = rms_norm(x, norm1_g[l])
        proj = hn @ w_in[l]
        qa, ka, va = [proj[..., i * WIDTH_A:(i + 1) * WIDTH_A].reshape(b, s, N_HEADS_A, HEAD_DIM)
                      for i in range(3)]
        qb, kb, vb = [proj[..., o1 + i * WIDTH_B:o1 + (i + 1) * WIDTH_B].reshape(b, s, N_HEADS_B, HEAD_DIM)
                      for i in range(3)]
        gate_a = proj[..., o2:o2 + D_MODEL]
        gate_b = proj[..., o2 + D_MODEL:o2 + 2 * D_MODEL]
        out_a = chunked_attention(qa, ka, va, q_norm_g[l], k_norm_g[l], rel_bias[l])
        out_b = stick_breaking_attention(qb, kb, vb)
        mixed = (jax.nn.sigmoid(gate_a) * (out_a @ w_branch_a[l])
                 + jax.nn.sigmoid(gate_b) * (out_b @ w_branch_b[l]))
        x = x + mixed @ w_out[l]
        x = x + conv_glu(rms_norm(x, norm2_g[l]), w_ffn_up[l], ffn_conv_w[l],
                         ffn_conv_b[l], w_ffn_down[l])
    return x
```

```python
import contextlib
import numpy as np
import concourse.bass as bass
import concourse.mybir as mybir
from concourse.bass_utils import run_bass_kernel_spmd

F32 = mybir.dt.float32
BF16 = mybir.dt.bfloat16
AF = mybir.ActivationFunctionType
ALU = mybir.AluOpType

D = 1024
KC = 8
T = 512
DFF = 2816
NFC = 22
EPS = 1e-6
NEG = -30000.0


class Sched:
    def __init__(self, engnames, sems):
        self.engnames = engnames
        self.prog = {k: [] for k in engnames}
        self.sems = sems
        self.free_sems = [k for k in sems if k.startswith('c')]
        self.alias = {}
        self.cnt = {}
        self.mult = {}
        for k in engnames:
            self.cnt[k] = 0
            self.mult[k] = 1
        self.lastw = {}
        self.readers = {}
        self.waited = {}
        self.nops = 0
        self.nwaits = 0

    def chan(self, name):
        if name not in self.alias:
            s = self.free_sems.pop(0)
            self.alias[name] = s
            self.cnt[name] = 0
            self.mult[name] = 16
        return name

    def sem(self, p):
        return self.sems[self.alias.get(p, p)]

    def _deps(self, reads, writes):
        deps = {}

        def add(ps):
            p, s = ps
            if deps.get(p, 0) < s:
                deps[p] = s
        for r in reads:
            if r in self.lastw:
                add(self.lastw[r])
        for w in writes:
            if w in self.lastw:
                add(self.lastw[w])
            for rd in self.readers.get(w, ()):
                add(rd)
        return deps

    def _emit_waits(self, eng, deps, skip_self):
        wd = self.waited.setdefault(eng, {})
        for p, s in deps.items():
            if p == eng and skip_self:
                continue
            if wd.get(p, 0) >= s:
                continue
            self.prog[eng].append(('w', self.sem(p), s * self.mult[p]))
            wd[p] = s
            self.nwaits += 1

    def _record(self, prod, seq, reads, writes):
        for r in reads:
            self.readers.setdefault(r, []).append((prod, seq))
        for w in writes:
            self.lastw[w] = (prod, seq)
            self.readers[w] = []

    def op(self, eng, meth, args, kw, reads=(), writes=()):
        deps = self._deps(reads, writes)
        self._emit_waits(eng, deps, skip_self=(eng == 'pe'))
        self.prog[eng].append(('o', (meth, args, kw), self.sems[eng], 1))
        self.cnt[eng] += 1
        self._record(eng, self.cnt[eng], reads, writes)
        self.nops += 1

    def dma(self, ch, out, in_, reads=(), writes=(), q='sp'):
        self.chan(ch)
        deps = self._deps(reads, writes)
        self._emit_waits(q, deps, skip_self=False)
        self.prog[q].append(('o', ('dma_start', (), dict(out=out, in_=in_)), self.sem(ch), 16))
        self.cnt[ch] += 1
        self._record(ch, self.cnt[ch], reads, writes)
        self.nops += 1

    def barrier(self):
        for eng in self.engnames:
            wd = self.waited.setdefault(eng, {})
            for p, c in self.cnt.items():
                if c > 0 and p != eng and wd.get(p, 0) < c:
                    self.prog[eng].append(('w', self.sem(p), c * self.mult[p]))
                    wd[p] = c
        for eng in self.engnames:
            if eng != 'sp' and self.cnt[eng] > 0:
                self.prog[eng].append(('w', self.sems[eng], self.cnt[eng]))

    def replay(self, eng, e):
        for it in self.prog[eng]:
            if it[0] == 'w':
                e.wait_ge(it[1], it[2])
            else:
                meth, args, kw = it[1]
                getattr(e, meth)(*args, **kw).then_inc(it[2], it[3])
        self.prog[eng] = []


def build(NT, NO):
    SL = NT * T
    NKB = SL // 128
    NQ = NO + 1
    T0 = NT - NO - 1
    nc = bass.Bass("TRN2", target_bir_lowering=False)

    def din(name, shape):
        return nc.dram_tensor(name, shape, F32, kind="ExternalInput").ap()
    x_d = din("x", [SL, D])
    valid_d = din("valid", [128, NKB])
    g1_d = din("g1", [128, D])
    g2_d = din("g2", [128, D])
    win_d = din("w_in", [D, 5120])
    gq_d = din("gq", [128, 1])
    gk_d = din("gk", [128, 1])
    tb_d = din("tb", [128, 8, 640])
    wa_d = din("w_a", [512, D])
    wb_d = din("w_b", [512, D])
    wo_d = din("w_o", [D, D])
    wup_d = din("w_up", [D, 2 * DFF])
    cw_d = din("cw", [128, 44, 3])
    cb_d = din("cb", [128, 44])
    wdn_d = din("w_dn", [DFF, D])
    ident_d = din("ident", [128, 128])
    negu_d = din("negu", [128, 128])
    negl_d = din("negl", [128, 128])
    bd_d = din("bd", [128, 128])
    mask_d = din("mask", [128, 4, T])
    y_d = nc.dram_tensor("y", [NO * T, D], F32, kind="ExternalOutput").ap()
    kt_d = nc.dram_tensor("kt_scr", [4, 128, SL], BF16).ap()
    v_d = nc.dram_tensor("v_scr", [4, 128, NKB, 128], BF16).ap()
    x1_d = nc.dram_tensor("x1_scr", [NQ * T, D], F32).ap()

    top = contextlib.ExitStack()
    with top:
        engnames = ['pe', 'act', 'dve', 'pool', 'sp']
        sems = {}
        for n in ['pe', 'act', 'dve', 'pool']:
            sems[n] = top.enter_context(nc.semaphore(n))
        for i in range(28):
            sems[f'c{i}'] = top.enter_context(nc.semaphore(f'c{i}'))
        S = Sched(engnames, sems)

        def run_block():
            blk = nc.Block()
            with blk:
                blk.tensor(lambda e: S.replay('pe', e))
                blk.scalar(lambda e: S.replay('act', e))
                blk.vector(lambda e: S.replay('dve', e))
                blk.gpsimd(lambda e: S.replay('pool', e))
                blk.sync(lambda e: S.replay('sp', e))

        uid = [0]

        def sbt(es, name, shape, dt):
            uid[0] += 1
            return es.enter_context(nc.sbuf_tensor(f"s{uid[0]}_{name}", shape, dt))

        def pst(es, name, shape, dt=F32):
            uid[0] += 1
            return es.enter_context(nc.psum_tensor(f"p{uid[0]}_{name}", shape, dt))

        def MM(out, lhsT, rhs, start, stop, reads, writes):
            S.op('pe', 'matmul', (out,), dict(lhsT=lhsT, rhs=rhs, start=start, stop=stop), reads, writes)

        def ACTV(out, in_, func, reads, writes, **kw):
            S.op('act', 'activation', (), dict(out=out, in_=in_, func=func, **kw), reads, writes)

        def CP(eng, out, in_, reads, writes):
            S.op(eng, 'copy' if eng == 'act' else 'tensor_copy', (), dict(out=out, in_=in_), reads, writes)

        def TT(eng, out, in0, in1, op, reads, writes):
            S.op(eng, 'tensor_tensor', (), dict(out=out, in0=in0, in1=in1, op=op), reads, writes)

        def STT(eng, out, in0, scalar, in1, op0, op1, reads, writes):
            S.op(eng, 'scalar_tensor_tensor', (), dict(out=out, in0=in0, scalar=scalar, in1=in1, op0=op0, op1=op1), reads, writes)

        def TS(eng, out, in0, s1, s2, op0, op1, reads, writes):
            kw = dict(out=out, in0=in0, scalar1=s1, scalar2=s2, op0=op0)
            if op1 is not None:
                kw['op1'] = op1
            S.op(eng, 'tensor_scalar', (), kw, reads, writes)

        def MS(eng, ap, val, writes):
            S.op(eng, 'memset', (ap, val), {}, (), writes)

        ident = sbt(top, "ident", [128, 128], BF16)
        negu = sbt(top, "negu", [128, 128], BF16)
        negl = sbt(top, "negl", [128, 128], BF16)
        bdm = sbt(top, "bdm", [128, 128], BF16)
        zero = sbt(top, "zero", [128, 128], BF16)
        cst = sbt(top, "cst", [128, 4], F32)

        stg_state = {'i': 0}

        def load_weight(stg, dst_ap, src_ap, n, wname):
            sl = stg_state['i'] % 2
            stg_state['i'] += 1
            S.dma(f'stg{sl}', stg[:, sl, 0:n], src_ap, writes=[f'stg{sl}'])
            CP('pool', dst_ap, stg[:, sl, 0:n], [f'stg{sl}'], [wname])

        def norm_block(Bf, src_ap, src_reads, slot, gb, hnT_ap, hnT_name, tp, tp_name):
            xs, junk, ss, hn = Bf['xs'], Bf['junk'], Bf['ss'], Bf['hn']
            xn, hnn, ssn = f'xs{slot}', f'hn{slot}', f'ss{slot}'
            S.dma(xn, xs[:, slot, :], src_ap, reads=src_reads, writes=[xn])
            MS('pool', ss[:, slot, 0:1], 0.0, [ssn])
            ACTV(junk[:, :], xs[:, slot, :], AF.Square, [xn], ['junk', ssn], accum_out=ss[:, slot, 0:1])
            ACTV(ss[:, slot, 1:2], ss[:, slot, 0:1], AF.Ln, [ssn], [ssn], scale=1.0 / D, bias=cst[:, 0:1])
            ACTV(ss[:, slot, 2:3], ss[:, slot, 1:2], AF.Exp, [ssn], [ssn], scale=-0.5)
            STT('dve', hn[:, slot, :], xs[:, slot, :], ss[:, slot, 2:3], gb[:, :], ALU.mult, ALU.mult, [xn, ssn], [hnn])
            for kc in range(KC):
                S.op('pe', 'transpose', (tp[:, kc, :], hn[:, slot, kc * 128:(kc + 1) * 128], ident[:, :]), {}, [hnn], [tp_name])
            CP('act', hnT_ap, tp[:, :, :], [tp_name], [hnT_name])

        def new_B(es):
            return dict(xs=sbt(es, "xs", [128, 2, D], F32), junk=sbt(es, "junk", [128, D], BF16),
                        ss=sbt(es, "ss", [128, 2, 4], F32), hn=sbt(es, "hn", [128, 2, D], BF16))

        sc04 = contextlib.ExitStack()
        with sc04:
            g1b = sbt(sc04, "g1b", [128, D], F32)
            vt = sbt(sc04, "vt", [128, NKB], F32)
            OT = sbt(sc04, "OT", [128, 4, NQ * T], BF16)

            with contextlib.ExitStack() as es:
                i32 = sbt(es, "i32", [128, 4, 128], F32)
                for i, srcd in enumerate([ident_d, negu_d, negl_d, bd_d]):
                    S.dma('cst', i32[:, i, :], srcd[:, :], writes=[f'i32_{i}'])
                S.barrier()
                for i, dst in enumerate([ident, negu, negl, bdm]):
                    CP('pool', dst[:, :], i32[:, i, :], [f'i32_{i}'], [f'const{i}'])
                MS('pool', zero[:, :], 0.0, ['zero'])
                MS('pool', cst[:, 0:1], EPS, ['cst'])
                MS('pool', cst[:, 1:2], 1.0, ['cst'])
                MS('pool', OT[:, :, 0:T], 0.0, ['OTz'])
                S.dma('cst', g1b[:, :], g1_d[:, :], writes=['g1b'])
                S.dma('cst', vt[:, :], valid_d[:, :], writes=['vt'])
                S.barrier()
                run_block()

            sc12 = contextlib.ExitStack()
            with sc12:
                QT = sbt(sc12, "QT", [128, 4, NQ * T], BF16)
                with contextlib.ExitStack() as es:
                    wB = sbt(es, "wB", [128, KC, 1536], BF16)
                    stg = sbt(es, "stg1", [128, 2, 1536], F32)
                    Bf = new_B(es)
                    hnT = sbt(es, "hnT", [128, 2, KC, T], BF16)
                    KTs = sbt(es, "KTs", [128, 2, 4, T], BF16)
                    Vs = sbt(es, "Vs", [128, 2, 4, T], BF16)
                    tps = [pst(es, f"tp{i}", [128, KC, 128], BF16) for i in range(2)]
                    accs = [pst(es, f"acc{i}", [128, T], F32) for i in range(4)]
                    for kc in range(KC):
                        load_weight(stg, wB[:, kc, :], win_d[kc * 128:(kc + 1) * 128, 1536:3072], 1536, 'wB')
                    acc_i = 0
                    ev_i = 0
                    for t in range(NT):
                        hs = t % 2
                        for bi in range(4):
                            blk_i = t * 4 + bi
                            norm_block(Bf, x_d[blk_i * 128:(blk_i + 1) * 128, :], [], blk_i % 2, g1b,
                                       hnT[:, hs, :, bi * 128:(bi + 1) * 128], f'hnT{hs}_{bi}', tps[blk_i % 2], f'tp{blk_i % 2}')
                        hread = [f'hnT{hs}_{bi}' for bi in range(4)]
                        for p in range(4):
                            a = acc_i % 4
                            acc_i += 1
                            for kc in range(KC):
                                MM(accs[a][:, :], wB[:, kc, 512 + p * 128:512 + (p + 1) * 128], hnT[:, hs, kc, :],
                                   kc == 0, kc == KC - 1, ['wB'] + hread, [f'acc{a}'])
                            CP('dve' if ev_i % 2 == 0 else 'act', KTs[:, hs, p, :], accs[a][:, :], [f'acc{a}'], [f'KTs{hs}'])
                            ev_i += 1
                        S.dma(f'kts{hs}', kt_d[:, :, t * T:(t + 1) * T].rearrange("q p c -> p q c"), KTs[:, hs, :, :],
                              reads=[f'KTs{hs}'], writes=['kt_d'])
                        for bi in range(4):
                            a = acc_i % 4
                            acc_i += 1
                            for kc in range(KC):
                                MM(accs[a][:, :], hnT[:, hs, kc, bi * 128:(bi + 1) * 128], wB[:, kc, 1024:1536],
                                   kc == 0, kc == KC - 1, ['wB', f'hnT{hs}_{bi}'], [f'acc{a}'])
                            CP('dve' if ev_i % 2 == 0 else 'act', Vs[:, hs, bi, :], accs[a][:, :], [f'acc{a}'], [f'Vs{hs}'])
                            ev_i += 1
                        for q in range(4):
                            S.dma(f'vs{hs}', v_d[q, :, t * 4:(t + 1) * 4, :], Vs[:, hs, :, q * 128:(q + 1) * 128],
                                  reads=[f'Vs{hs}'], writes=['v_d'])
                        if t >= T0:
                            qi = t - T0
                            for p in range(4):
                                a = acc_i % 4
                                acc_i += 1
                                for kc in range(KC):
                                    MM(accs[a][:, :], wB[:, kc, p * 128:(p + 1) * 128], hnT[:, hs, kc, :],
                                       kc == 0, kc == KC - 1, ['wB'] + hread, [f'acc{a}'])
                                CP('dve' if ev_i % 2 == 0 else 'act', QT[:, p, qi * T:(qi + 1) * T], accs[a][:, :],
                                   [f'acc{a}'], [f'QT{p}_{qi}'])
                                ev_i += 1
                    S.barrier()
                    run_block()

                with contextlib.ExitStack() as es:
                    KTp = sbt(es, "KTp", [128, SL], BF16)
                    Vp = sbt(es, "Vp", [128, NKB, 128], BF16)
                    M = sbt(es, "M", [128, 4, T], BF16)
                    NE, NSP, NXW, NW = 3, 2, 1, 2
                    eb = sbt(es, "eb", [128, 2, NE, T], F32)
                    spb = sbt(es, "spb", [128, 2, NSP, T], BF16)
                    xwb = sbt(es, "xwb", [128, 2, NXW, T], F32)
                    wbuf = sbt(es, "wbuf", [128, 2, NW, T], BF16)
                    m32 = sbt(es, "m32", [128, 4, T], F32)
                    Zp = [pst(es, f"Z{h}", [128, T]) for h in range(2)]
                    Ap = [pst(es, f"A{h}", [128, T]) for h in range(2)]
                    Op = pst(es, "O", [128, T])
                    S.dma('cst', m32[:, :, :], mask_d[:, :, :], writes=['m32'])
                    CP('pool', M[:, :, :], m32[:, :, :], ['m32'], ['M'])
                    items = []
                    for p in range(4):
                        for qi in range(NQ):
                            g = T0 + qi
                            if qi == 0:
                                W, c0, q0 = 128, 384, g * T + 384
                            else:
                                W, c0, q0 = T, 0, g * T
                            kb_hi = (q0 + W) // 128 - 1
                            nb = kb_hi + 1
                            for b in range(nb):
                                kb = kb_hi - b
                                d = kb - q0 // 128
                                for h in range(2):
                                    items.append(dict(p=p, qi=qi, W=W, c0=c0, kb=kb, d=(d if d >= 0 else None), h=h,
                                                      first=(b == 0), last=(b == nb - 1)))
                    cntr = {0: 0, 1: 0}
                    for it in items:
                        it['n'] = cntr[it['h']]
                        cntr[it['h']] += 1
                    cur_p = [-1]

                    def load_pair(p):
                        nch = max(1, SL // 4096)
                        cw_ = SL // nch
                        for c in range(nch):
                            S.dma('ktp', KTp[:, c * cw_:(c + 1) * cw_], kt_d[p, :, c * cw_:(c + 1) * cw_], reads=['kt_d'], writes=['KTp'])
                        S.dma('vp', Vp[:, :, :], v_d[p, :, :, :], reads=['v_d'], writes=['Vp'])

                    def st1(it):
                        h, W, kb, p, qi, c0 = it['h'], it['W'], it['kb'], it['p'], it['qi'], it['c0']
                        if cur_p[0] != p:
                            load_pair(p)
                            cur_p[0] = p
                        MM(Zp[h][:, 0:W], KTp[64 * h:64 * h + 64, kb * 128:(kb + 1) * 128],
                           QT[64 * h:64 * h + 64, p, qi * T + c0:qi * T + c0 + W], True, True,
                           ['KTp', f'QT{p}_{qi}'], [f'Z{h}'])

                    def st2(it):
                        h, W, n = it['h'], it['W'], it['n']
                        en = f'e{h}_{n % NE}'
                        ACTV(eb[:, h, n % NE, 0:W], Zp[h][:, 0:W], AF.Exp, [f'Z{h}'], [en], scale=0.125)
                        if it['d'] is not None:
                            TT('dve', eb[:, h, n % NE, 0:W], eb[:, h, n % NE, 0:W], M[:, it['d'], 0:W], ALU.mult, [en, 'M'], [en])

                    def st3(it):
                        h, W, n = it['h'], it['W'], it['n']
                        ACTV(spb[:, h, n % NSP, 0:W], eb[:, h, n % NE, 0:W], AF.Ln, [f'e{h}_{n % NE}'], [f'sp{h}_{n % NSP}'],
                             bias=cst[:, 1:2])

                    def st4(it):
                        h, W, n = it['h'], it['W'], it['n']
                        MM(Ap[h][:, 0:W], negu[:, :], spb[:, h, n % NSP, 0:W], it['first'], False,
                           [f'sp{h}_{n % NSP}', 'const1'], [f'A{h}'])

                    def st5(it):
                        h, W, n = it['h'], it['W'], it['n']
                        ACTV(xwb[:, h, n % NXW, 0:W], Ap[h][:, 0:W], AF.Exp, [f'A{h}'], [f'xw{h}_{n % NXW}'])

                    def st6(it):
                        h, W, n = it['h'], it['W'], it['n']
                        if it['last']:
                            return
                        MM(Ap[h][:, 0:W], negl[:, :], spb[:, h, n % NSP, 0:W], False, False,
                           [f'sp{h}_{n % NSP}', 'const2'], [f'A{h}'])

                    def st7(it):
                        h, W, n = it['h'], it['W'], it['n']
                        TT('dve', wbuf[:, h, n % NW, 0:W], eb[:, h, n % NE, 0:W], xwb[:, h, n % NXW, 0:W], ALU.mult,
                           [f'e{h}_{n % NE}', f'xw{h}_{n % NXW}'], [f'w{h}_{n % NW}'])

                    def st8(it):
                        h, W, n, kb, p, qi, c0 = it['h'], it['W'], it['n'], it['kb'], it['p'], it['qi'], it['c0']
                        MM(Op[64 * h:64 * h + 64, 0:W], Vp[:, kb, 64 * h:64 * h + 64], wbuf[:, h, n % NW, 0:W],
                           it['first'], it['last'], [f'w{h}_{n % NW}', 'Vp'], [f'O{h}'])
                        if it['last'] and h == 1:
                            CP('dve', OT[:, p, qi * T + c0:qi * T + c0 + W], Op[:, 0:W], ['O0', 'O1', 'OTz'], [f'OT{p}_{qi}'])

                    stages = [(st1, 0), (st2, 1), (st3, 2), (st6, 5), (st4, 3), (st5, 4), (st7, 5), (st8, 6)]
                    nit = len(items)
                    for s in range(nit + 7):
                        for fn, dly in stages:
                            i = s - dly
                            if 0 <= i < nit:
                                fn(items[i])
                    S.barrier()
                    run_block()

            sc34 = contextlib.ExitStack()
            with sc34:
                OAT = sbt(sc34, "OAT", [128, 4, NQ * T], BF16)
                with contextlib.ExitStack() as es:
                    wA = sbt(es, "wA", [128, KC, 1536], BF16)
                    stg = sbt(es, "stg3", [128, 2, 1536], F32)
                    TB = sbt(es, "TB", [128, 8, 640], F32)
                    gq = sbt(es, "gq", [128, 1], F32)
                    gk = sbt(es, "gk", [128, 1], F32)
                    Bf = new_B(es)
                    hnT = sbt(es, "hnT", [128, KC, T], BF16)
                    QAT = sbt(es, "QAT", [128, 4, T], BF16)
                    KAT = sbt(es, "KAT", [128, 4, 2, T], BF16)
                    VA = sbt(es, "VA", [128, 2, 4, T], BF16)
                    VLD = sbt(es, "VLD", [128, 2, 4, 64], BF16)
                    sqb = sbt(es, "sqb", [128, 2, T], BF16)
                    qgb = sbt(es, "qgb", [128, 2, T], F32)
                    lnb = sbt(es, "lnb", [128, 2, T], F32)
                    sbb = sbt(es, "sbb", [128, 2, T], F32)
                    pTb = sbt(es, "pTb", [128, 2, T], BF16)
                    rdb = sbt(es, "rdb", [128, T], F32)
                    tp3 = pst(es, "tp3", [128, KC, 128], BF16)
                    acc3 = [pst(es, f"acc3_{i}", [128, T]) for i in range(2)]
                    SSp = pst(es, "SSp", [128, T])
                    SPS = [pst(es, f"SPS{h}", [128, T]) for h in range(2)]
                    OAp = pst(es, "OAp", [128, T])
                    DENp = pst(es, "DENp", [128, T])
                    S.dma('cst', TB[:, :, :], tb_d[:, :, :], writes=['TB'])
                    S.dma('cst', gq[:, :], gq_d[:, :], writes=['gq'])
                    S.dma('cst', gk[:, :], gk_d[:, :], writes=['gk'])
                    S.barrier()
                    for kc in range(KC):
                        load_weight(stg, wA[:, kc, :], win_d[kc * 128:(kc + 1) * 128, 0:1536], 1536, 'wA')
                    blkc = 0
                    acc_i = 0
                    nrm = [0]

                    def qknorm(a, gcol, gname, dst_ap, dst_name):
                        k = nrm[0] % 2
                        nrm[0] += 1
                        ACTV(sqb[:, k, :], acc3[a][:, :], AF.Square, [f'acc3_{a}'], [f'sqb{k}'])
                        S.op('act', 'mul', (), dict(out=qgb[:, k, :], in_=acc3[a][:, :], mul=gcol[:, 0:1]), [f'acc3_{a}', gname], [f'qgb{k}'])
                        MM(SSp[:, :], bdm[:, :], sqb[:, k, :], True, True, [f'sqb{k}', 'const3'], ['SSp'])
                        ACTV(lnb[:, k, :], SSp[:, :], AF.Ln, ['SSp'], [f'lnb{k}'], scale=1.0 / 64, bias=cst[:, 0:1])
                        ACTV(lnb[:, k, :], lnb[:, k, :], AF.Exp, [f'lnb{k}'], [f'lnb{k}'], scale=-0.5)
                        TT('dve', dst_ap, qgb[:, k, :], lnb[:, k, :], ALU.mult, [f'qgb{k}', f'lnb{k}'], [dst_name])

                    for t in range(T0 - 1, NT):
                        qi = t - T0
                        sl = t % 2
                        for bi in range(4):
                            blk_i = t * 4 + bi
                            norm_block(Bf, x_d[blk_i * 128:(blk_i + 1) * 128, :], [], blkc % 2, g1b,
                                       hnT[:, :, bi * 128:(bi + 1) * 128], f'hnT_{bi}', tp3, 'tp3')
                            blkc += 1
                        hread = [f'hnT_{bi}' for bi in range(4)]
                        for p in range(4):
                            a = acc_i % 2
                            acc_i += 1
                            for kc in range(KC):
                                MM(acc3[a][:, :], wA[:, kc, 512 + p * 128:512 + (p + 1) * 128], hnT[:, kc, :],
                                   kc == 0, kc == KC - 1, ['wA'] + hread, [f'acc3_{a}'])
                            qknorm(a, gk, 'gk', KAT[:, p, sl, :], f'KAT{sl}_{p}')
                        for bi in range(4):
                            a = acc_i % 2
                            acc_i += 1
                            for kc in range(KC):
                                MM(acc3[a][:, :], hnT[:, kc, bi * 128:(bi + 1) * 128], wA[:, kc, 1024:1536],
                                   kc == 0, kc == KC - 1, ['wA', f'hnT_{bi}'], [f'acc3_{a}'])
                            CP('dve', VA[:, sl, bi, :], acc3[a][:, :], [f'acc3_{a}'], [f'VA{sl}_{bi}'])
                            CP('pool', VLD[:, sl, bi, :], vt[:, t * 4 + bi:t * 4 + bi + 1].to_broadcast([128, 64]),
                               ['vt'], [f'VLD{sl}_{bi}'])
                        if qi < 0:
                            continue
                        for p in range(4):
                            a = acc_i % 2
                            acc_i += 1
                            for kc in range(KC):
                                MM(acc3[a][:, :], wA[:, kc, p * 128:(p + 1) * 128], hnT[:, kc, :],
                                   kc == 0, kc == KC - 1, ['wA'] + hread, [f'acc3_{a}'])
                            qknorm(a, gq, 'gq', QAT[:, p, :], f'QAT{p}')
                        for p in range(4):
                            MM(OAp[:, :], zero[:, :], hnT[:, 0, :], True, False, ['zero'] + hread, ['OAp'])
                            MM(DENp[:, :], zero[:, :], hnT[:, 0, :], True, False, ['zero'] + hread, ['DENp'])
                            for j in range(8):
                                i_lo, i_hi = max(0, j - 4), min(3, j)
                                N = (i_hi - i_lo + 1) * 128
                                ksl = (1 - sl) if j < 4 else sl
                                cj = j % 4
                                tb0 = (4 - j + i_lo) * 128
                                for h in range(2):
                                    hh = 2 * p + h
                                    MM(SPS[h][:, 0:N], KAT[64 * h:64 * h + 64, p, ksl, cj * 128:(cj + 1) * 128],
                                       QAT[64 * h:64 * h + 64, p, i_lo * 128:i_lo * 128 + N], True, True,
                                       [f'KAT{ksl}_{p}', f'QAT{p}'], [f'SPS{h}'])
                                    STT('dve', sbb[:, h, 0:N], SPS[h][:, 0:N], 0.125, TB[:, hh, tb0:tb0 + N], ALU.mult, ALU.add,
                                        [f'SPS{h}', 'TB'], [f'sbb{h}'])
                                    ACTV(pTb[:, h, 0:N], sbb[:, h, 0:N], AF.Exp, [f'sbb{h}'], [f'pTb{h}'])
                                    MM(OAp[64 * h:64 * h + 64, i_lo * 128:i_lo * 128 + N], VA[:, ksl, cj, hh * 64:(hh + 1) * 64],
                                       pTb[:, h, 0:N], False, False, [f'pTb{h}', f'VA{ksl}_{cj}'], ['OAp'])
                                    MM(DENp[64 * h:64 * h + 64, i_lo * 128:i_lo * 128 + N], VLD[:, ksl, cj, :],
                                       pTb[:, h, 0:N], False, False, [f'pTb{h}', f'VLD{ksl}_{cj}'], ['DENp'])
                            TS('dve', rdb[:, :], DENp[:, :], 1e-30, None, ALU.max, None, ['DENp'], ['rdb'])
                            S.op('dve', 'reciprocal', (), dict(out=rdb[:, :], in_=rdb[:, :]), ['rdb'], ['rdb'])
                            TT('dve', OAT[:, p, qi * T:(qi + 1) * T], OAp[:, :], rdb[:, :], ALU.mult, ['OAp', 'rdb'], [f'OAT{p}_{qi}'])
                    S.barrier()
                    run_block()

                with contextlib.ExitStack() as es:
                    wG = sbt(es, "wG", [128, KC, 2048], BF16)
                    wbra = sbt(es, "wbra", [128, 4, D], BF16)
                    wbrb = sbt(es, "wbrb", [128, 4, D], BF16)
                    wout = sbt(es, "wout", [128, KC, D], BF16)
                    stg = sbt(es, "stg4", [128, 2, D], F32)
                    Bf = new_B(es)
                    xr = sbt(es, "xr", [128, 2, D], F32)
                    hnT = sbt(es, "hnT", [128, KC, T], BF16)
                    sg = sbt(es, "sg", [128, 2, T], F32)
                    mm = sbt(es, "mm", [128, 2, T], F32)
                    MT = sbt(es, "MT", [128, KC, T], BF16)
                    tp4 = pst(es, "tp4", [128, KC, 128], BF16)
                    Gp = [pst(es, f"G{i}", [128, T]) for i in range(2)]
                    Yp = [pst(es, f"Y{i}", [128, T]) for i in range(2)]
                    Xp = [pst(es, f"X{i}", [128, T]) for i in range(2)]
                    for kc in range(KC):
                        for hf in range(2):
                            load_weight(stg, wG[:, kc, hf * D:(hf + 1) * D], win_d[kc * 128:(kc + 1) * 128, 3072 + hf * D:3072 + (hf + 1) * D], D, 'wG')
                    for p in range(4):
                        load_weight(stg, wbra[:, p, :], wa_d[p * 128:(p + 1) * 128, :], D, 'wbra')
                        load_weight(stg, wbrb[:, p, :], wb_d[p * 128:(p + 1) * 128, :], D, 'wbrb')
                    for kc in range(KC):
                        load_weight(stg, wout[:, kc, :], wo_d[kc * 128:(kc + 1) * 128, :], D, 'wout')
                    blkc = 0
                    ob = 0
                    for qi in range(NQ):
                        t = T0 + qi
                        for bi in range(4):
                            blk_i = t * 4 + bi
                            norm_block(Bf, x_d[blk_i * 128:(blk_i + 1) * 128, :], [], blkc % 2, g1b,
                                       hnT[:, :, bi * 128:(bi + 1) * 128], f'hnT_{bi}', tp4, 'tp4')
                            blkc += 1
                        hread = [f'hnT_{bi}' for bi in range(4)]
                        for oc in range(KC):
                            for br in range(2):
                                for kc in range(KC):
                                    MM(Gp[br][:, :], wG[:, kc, br * D + oc * 128:br * D + (oc + 1) * 128], hnT[:, kc, :],
                                       kc == 0, kc == KC - 1, ['wG'] + hread, [f'G{br}'])
                                ACTV(sg[:, br, :], Gp[br][:, :], AF.Sigmoid, [f'G{br}'], [f'sg{br}'])
                                wsrc = wbra if br == 0 else wbrb
                                osrc = OAT if br == 0 else OT
                                for p in range(4):
                                    MM(Yp[br][:, :], wsrc[:, p, oc * 128:(oc + 1) * 128], osrc[:, p, qi * T:(qi + 1) * T],
                                       p == 0, p == 3,
                                       ['wbra' if br == 0 else 'wbrb', (f'OAT{p}_{qi}' if br == 0 else f'OT{p}_{qi}'), 'OTz'], [f'Y{br}'])
                                TT('dve', mm[:, br, :], Yp[br][:, :], sg[:, br, :], ALU.mult, [f'Y{br}', f'sg{br}'], [f'mm{br}'])
                            TT('pool', MT[:, oc, :], mm[:, 0, :], mm[:, 1, :], ALU.add, ['mm0', 'mm1'], [f'MT{oc}'])
                        mread = [f'MT{oc}' for oc in range(KC)]
                        for bi in range(4):
                            blk_i = t * 4 + bi
                            o = ob % 2
                            ob += 1
                            S.dma(f'xr{o}', xr[:, o, :], x_d[blk_i * 128:(blk_i + 1) * 128, :], writes=[f'xr{o}'])
                            for half in range(2):
                                for oc in range(KC):
                                    MM(Xp[half][:, :], MT[:, oc, bi * 128:(bi + 1) * 128], wout[:, oc, half * T:(half + 1) * T],
                                       oc == 0, oc == KC - 1, ['wout'] + mread, [f'X{half}'])
                                TT('dve', xr[:, o, half * T:(half + 1) * T], Xp[half][:, :], xr[:, o, half * T:(half + 1) * T], ALU.add,
                                   [f'X{half}', f'xr{o}'], [f'xr{o}'])
                            row = qi * T + bi * 128
                            S.dma(f'x1w{o}', x1_d[row:row + 128, :], xr[:, o, :], reads=[f'xr{o}'], writes=['x1_d'])
                    S.barrier()
                    run_block()

        with contextlib.ExitStack() as es:
            wup = sbt(es, "wup", [128, KC, 2 * DFF], BF16)
            wdn = sbt(es, "wdn", [128, NFC, D], BF16)
            g2b = sbt(es, "g2b", [128, D], F32)
            cw = sbt(es, "cw", [128, 44, 3], F32)
            cb = sbt(es, "cb", [128, 44], F32)
            HALO = sbt(es, "HALO", [128, 44, 2], F32)
            with contextlib.ExitStack() as es2:
                stg = sbt(es2, "stg5", [128, 2, 2816], F32)
                for kc in range(KC):
                    for hf in range(2):
                        load_weight(stg, wup[:, kc, hf * DFF:(hf + 1) * DFF], wup_d[kc * 128:(kc + 1) * 128, hf * DFF:(hf + 1) * DFF], DFF, 'wup')
                for fc in range(NFC):
                    load_weight(stg, wdn[:, fc, :], wdn_d[fc * 128:(fc + 1) * 128, :], D, 'wdn')
                S.dma('cst', g2b[:, :], g2_d[:, :], writes=['g2b'])
                S.dma('cst', cw[:, :, :], cw_d[:, :, :], writes=['cw'])
                S.dma('cst', cb[:, :], cb_d[:, :], writes=['cb'])
                MS('pool', HALO[:, :, :], 0.0, ['HALO'])
                S.barrier()
            Bf = dict(xs=sbt(es, "xs", [128, 2, D], F32), junk=sbt(es, "junk", [128, D], BF16),
                      ss=sbt(es, "ss", [128, 2, 4], F32), hn=sbt(es, "hn", [128, 2, D], BF16))
            hnT = sbt(es, "hnT", [128, KC, T], BF16)
            hb = sbt(es, "hb", [128, 2, 2, T + 2], F32)
            cv = sbt(es, "cv", [128, 2, T], F32)
            sgl = sbt(es, "sgl", [128, T], F32)
            ACTT = sbt(es, "ACTT", [128, NFC, T], BF16)
            yo = sbt(es, "yo", [128, D], F32)
            tp5 = pst(es, "tp5", [128, KC, 128], BF16)
            Hp = [[pst(es, f"H{b}_{u}", [128, T]) for u in range(2)] for b in range(2)]
            Yd = [pst(es, f"Yd{i}", [128, T]) for i in range(2)]
            blkc = 0
            fcc = 0
            for qi in range(NQ):
                for bi in range(4):
                    row = qi * T + bi * 128
                    norm_block(Bf, x1_d[row:row + 128, :], ['x1_d'], blkc % 2, g2b,
                               hnT[:, :, bi * 128:(bi + 1) * 128], f'hnT_{bi}', tp5, 'tp5')
                    blkc += 1
                hread = [f'hnT_{bi}' for bi in range(4)]
                for fc in range(NFC):
                    b = fcc % 2
                    fcc += 1
                    for u in range(2):
                        ch = fc + u * NFC
                        hbn = f'hb{b}_{u}'
                        for kc in range(KC):
                            MM(Hp[b][u][:, :], wup[:, kc, ch * 128:(ch + 1) * 128], hnT[:, kc, :],
                               kc == 0, kc == KC - 1, ['wup'] + hread, [f'H{b}_{u}'])
                        CP('pool', hb[:, b, u, 0:2], HALO[:, ch, :], [f'HALO{ch}'], [hbn])
                        CP('act', hb[:, b, u, 2:T + 2], Hp[b][u][:, :], [f'H{b}_{u}'], [hbn])
                        CP('pool', HALO[:, ch, :], hb[:, b, u, T:T + 2], [hbn], [f'HALO{ch}'])
                        if qi == 0:
                            continue
                        ce = 'dve'
                        cvn = f'cv{u}'
                        TS(ce, cv[:, u, :], hb[:, b, u, 2:T + 2], cw[:, ch, 2:3], cb[:, ch:ch + 1], ALU.mult, ALU.add, [hbn, 'cw', 'cb'], [cvn])
                        STT(ce, cv[:, u, :], hb[:, b, u, 1:T + 1], cw[:, ch, 1:2], cv[:, u, :], ALU.mult, ALU.add, [hbn, cvn, 'cw'], [cvn])
                        STT(ce, cv[:, u, :], hb[:, b, u, 0:T], cw[:, ch, 0:1], cv[:, u, :], ALU.mult, ALU.add, [hbn, cvn, 'cw'], [cvn])
                    if qi == 0:
                        continue
                    ACTV(sgl[:, :], cv[:, 0, :], AF.Silu, ['cv0'], ['sgl'])
                    TT('dve', ACTT[:, fc, :], sgl[:, :], cv[:, 1, :], ALU.mult, ['sgl', 'cv1'], [f'ACTT{fc}'])
                if qi == 0:
                    continue
                aread = [f'ACTT{fc}' for fc in range(NFC)]
                for bi in range(4):
                    row = qi * T + bi * 128
                    S.dma('yo_in', yo[:, :], x1_d[row:row + 128, :], reads=['x1_d', 'y_d'], writes=['yo'])
                    for half in range(2):
                        for fc in range(NFC):
                            MM(Yd[half][:, :], ACTT[:, fc, bi * 128:(bi + 1) * 128], wdn[:, fc, half * T:(half + 1) * T],
                               fc == 0, fc == NFC - 1, ['wdn'] + aread, [f'Yd{half}'])
                        TT('dve', yo[:, half * T:(half + 1) * T], Yd[half][:, :], yo[:, half * T:(half + 1) * T], ALU.add,
                           [f'Yd{half}', 'yo'], ['yo'])
                    orow = (qi - 1) * T + bi * 128
                    S.dma('yw', y_d[orow:orow + 128, :], yo[:, :], reads=['yo'], writes=['y_d'])
            S.barrier()
            run_block()
        print("megakernel ops", S.nops, "waits", S.nwaits, {k: v for k, v in S.cnt.items() if k in engnames})
    return nc


_CACHE = {}


def _consts():
    ident = np.eye(128, dtype=np.float32)
    kk = np.arange(128)
    negu = -(kk[:, None] >= kk[None, :]).astype(np.float32)
    negl = -(kk[:, None] < kk[None, :]).astype(np.float32)
    bd = np.zeros((128, 128), np.float32)
    bd[:64, :64] = 1.0
    bd[64:, 64:] = 1.0
    qq = np.arange(T)
    mask = np.zeros((128, 4, T), np.float32)
    for d in range(4):
        mask[:, d, :] = ((128 * d + kk[:, None]) < qq[None, :]).astype(np.float32)
    return ident, negu, negl, bd, mask


def _tb_table(rel_bias):
    kk = np.arange(128)[:, None]
    qq = np.arange(128)[None, :]
    tb = np.empty((128, 8, 640), np.float32)
    for rp in range(5):
        idx = np.clip(qq - kk + rp * 128, -128, 128) + 128
        blk = rel_bias[:, idx]
        vis = np.ones((128, 128), bool)
        if rp == 0:
            vis = (kk < 64) | (qq >= 64)
        if rp == 4:
            vis = (kk >= 64) | (qq < 64)
        blk = np.where(vis[None], blk, np.float32(NEG))
        tb[:, :, rp * 128:(rp + 1) * 128] = blk.transpose(1, 0, 2)
    return tb


def kernel(x, norm1_g, w_in, q_norm_g, k_norm_g, rel_bias, w_branch_a, w_branch_b, w_out, norm2_g,
           w_ffn_up, ffn_conv_w, ffn_conv_b, w_ffn_down):
    x = np.asarray(x, np.float32)
    Bn, Sq, Dm = x.shape
    assert Bn == 2 and Dm == D and Sq % 2048 == 0
    NO = Sq // 2048
    NT = Sq // T
    key = (NT, NO)
    if key not in _CACHE:
        _CACHE[key] = build(NT, NO)
    nc = _CACHE[key]
    f = lambda a: np.ascontiguousarray(np.asarray(a, np.float32))
    ident, negu, negl, bd, mask = _consts()
    own = NO * T
    shared = {
        "g1": f(np.broadcast_to(np.asarray(norm1_g, np.float32)[0][None, :], (128, D))),
        "g2": f(np.broadcast_to(np.asarray(norm2_g, np.float32)[0][None, :], (128, D))),
        "w_in": f(w_in[0]),
        "gq": f(np.tile(np.asarray(q_norm_g, np.float32)[0], 2)[:, None]),
        "gk": f(np.tile(np.asarray(k_norm_g, np.float32)[0], 2)[:, None]),
        "tb": f(_tb_table(np.asarray(rel_bias, np.float32)[0])),
        "w_a": f(w_branch_a[0]), "w_b": f(w_branch_b[0]), "w_o": f(w_out[0]),
        "w_up": f(w_ffn_up[0]),
        "cw": f(np.asarray(ffn_conv_w, np.float32)[0].reshape(3, 44, 128).transpose(2, 1, 0)),
        "cb": f(np.asarray(ffn_conv_b, np.float32)[0].reshape(44, 128).T),
        "w_dn": f(w_ffn_down[0]),
        "ident": ident, "negu": negu, "negl": negl, "bd": bd, "mask": mask,
    }
    in_maps = []
    for c in range(8):
        b, j = c // 4, c % 4
        real = (j + 1) * own
        pad = Sq - real
        xl = np.zeros((Sq, D), np.float32)
        xl[pad:] = x[b, :real]
        valid = np.zeros((Sq,), np.float32)
        valid[pad:] = 1.0
        m = dict(shared)
        m["x"] = xl
        m["valid"] = f(valid.reshape(Sq // 128, 128).T)
        in_maps.append(m)
    res = run_bass_kernel_spmd(nc, in_maps, core_ids=list(range(8)))
    out = np.empty((Bn, Sq, D), np.float32)
    for c in range(8):
        b, j = c // 4, c % 4
        out[b, j * own:(j + 1) * own] = res.results[c]["y"]
    return out
```

```python
import contextlib
import numpy as np
import concourse.bass as bass
import concourse.mybir as mybir
from concourse.bass_utils import run_bass_kernel_spmd

F32 = mybir.dt.float32
BF16 = mybir.dt.bfloat16
AF = mybir.ActivationFunctionType
ALU = mybir.AluOpType

D = 1024
KC = 8
T = 512
DFF = 2816
NFC = 22
EPS = 1e-6
NEG = -30000.0


class Sched:
    def __init__(self, engnames, sems):
        self.engnames = engnames
        self.prog = {k: [] for k in engnames}
        self.stack = {k: [self.prog[k]] for k in engnames}
        self.cond = None
        self.sems = sems
        self.free_sems = [k for k in sems if k.startswith('c')]
        self.alias = {}
        self.cnt = {}
        self.mult = {}
        for k in engnames:
            self.cnt[k] = 0
            self.mult[k] = 1
        self.cnt['flag'] = 0
        self.mult['flag'] = 1
        self.lastw = {}
        self.readers = {}
        self.waited = {}
        self.nops = 0
        self.nwaits = 0

    def chan(self, name):
        if name not in self.alias:
            s = self.free_sems.pop(0)
            self.alias[name] = s
            self.cnt[name] = 0
            self.mult[name] = 16
        return name

    def sem(self, p):
        return self.sems[self.alias.get(p, p)]

    def _deps(self, reads, writes):
        deps = {}

        def add(ps):
            p, s = ps
            if deps.get(p, 0) < s:
                deps[p] = s
        for r in reads:
            if r in self.lastw:
                add(self.lastw[r])
        for w in writes:
            if w in self.lastw:
                add(self.lastw[w])
            for rd in self.readers.get(w, ()):
                add(rd)
        return deps

    def _emit_waits(self, eng, deps, skip_self):
        wd = self.waited.setdefault(eng, {})
        for p, s in deps.items():
            if p == eng and skip_self:
                continue
            if wd.get(p, 0) >= s:
                continue
            self.stack[eng][-1].append(('w', self.sem(p), s * self.mult[p]))
            wd[p] = s
            self.nwaits += 1

    def _record(self, prod, seq, reads, writes):
        for r in reads:
            self.readers.setdefault(r, []).append((prod, seq))
        for w in writes:
            self.lastw[w] = (prod, seq)
            self.readers[w] = []

    def op(self, eng, meth, args, kw, reads=(), writes=()):
        deps = self._deps(reads, writes)
        self._emit_waits(eng, deps, skip_self=(eng == 'pe'))
        self.stack[eng][-1].append(('o', (meth, args, kw), self.sems[eng], 1))
        self.cnt[eng] += 1
        self._record(eng, self.cnt[eng], reads, writes)
        self.nops += 1

    def dma(self, ch, out, in_, reads=(), writes=(), q='sp'):
        self.chan(ch)
        deps = self._deps(reads, writes)
        self._emit_waits(q, deps, skip_self=False)
        self.stack[q][-1].append(('o', ('dma_start', (), dict(out=out, in_=in_)), self.sem(ch), 16))
        self.cnt[ch] += 1
        self._record(ch, self.cnt[ch], reads, writes)
        self.nops += 1

    def flag_op(self, meth, args, kw, reads=(), writes=()):
        deps = self._deps(reads, writes)
        self._emit_waits('dve', deps, skip_self=False)
        self.stack['dve'][-1].append(('o', (meth, args, kw), self.sems['flag'], 1))
        self.cnt['flag'] += 1
        self._record('flag', self.cnt['flag'], reads, writes)

    def begin_cond(self, flag_ap, engines):
        import copy
        assert self.cond is None
        self.cond = dict(flag_ap=flag_ap, engines=engines, seq=self.cnt['flag'],
                         start={e: self.cnt[e] for e in engines}, snap=copy.deepcopy(self.waited))
        for e in engines:
            self.stack[e].append([])

    def end_cond(self, dve_else=None):
        c = self.cond
        self.cond = None
        for e in c['engines']:
            body = self.stack[e].pop()
            n = self.cnt[e] - c['start'][e]
            self.stack[e][-1].append(('if', c['flag_ap'], c['seq'], body, c['start'][e], n,
                                      dve_else if e == 'dve' else None))
        self.waited = c['snap']

    def barrier(self):
        for eng in self.engnames:
            wd = self.waited.setdefault(eng, {})
            for p, c in self.cnt.items():
                if c > 0 and p != eng and wd.get(p, 0) < c:
                    self.stack[eng][-1].append(('w', self.sem(p), c * self.mult[p]))
                    wd[p] = c
        for eng in self.engnames:
            if eng != 'sp' and self.cnt[eng] > 0:
                self.stack[eng][-1].append(('w', self.sems[eng], self.cnt[eng]))

    def _replay_list(self, eng, e, lst):
        for it in lst:
            if it[0] == 'w':
                e.wait_ge(it[1], it[2])
            elif it[0] == 'o':
                meth, args, kw = it[1]
                getattr(e, meth)(*args, **kw).then_inc(it[2], it[3])
            else:
                _, flag_ap, seq, body, start, n, dve_else = it
                e.wait_ge(self.sems['flag'], seq)
                reg = self.regs[eng]
                e.reg_load(reg, flag_ap)
                with e.If_ne(reg, 0):
                    self._replay_list(eng, e, body)
                with e.Else():
                    if start > 0:
                        e.wait_ge(self.sems[eng], start)
                    if n > 0:
                        e.sem_inc(self.sems[eng], n)
                    if dve_else is not None:
                        meth, args, kw = dve_else
                        getattr(e, meth)(*args, **kw).then_inc(self.sems['flag'], 1)

    def replay(self, eng, e):
        assert len(self.stack[eng]) == 1
        self._replay_list(eng, e, self.prog[eng])
        self.prog[eng] = []
        self.stack[eng] = [self.prog[eng]]


def build(NT, NO):
    SL = NT * T
    NKB = SL // 128
    NQ = NO + 1
    T0 = NT - NO - 1
    nc = bass.Bass("TRN2", target_bir_lowering=False)

    def din(name, shape):
        return nc.dram_tensor(name, shape, F32, kind="ExternalInput").ap()
    x_d = din("x", [SL, D])
    valid_d = din("valid", [128, NKB])
    g1_d = din("g1", [128, D])
    g2_d = din("g2", [128, D])
    win_d = din("w_in", [D, 5120])
    gq_d = din("gq", [128, 1])
    gk_d = din("gk", [128, 1])
    tb_d = din("tb", [128, 8, 640])
    wa_d = din("w_a", [512, D])
    wb_d = din("w_b", [512, D])
    wo_d = din("w_o", [D, D])
    wup_d = din("w_up", [D, 2 * DFF])
    cw_d = din("cw", [128, 44, 3])
    cb_d = din("cb", [128, 44])
    wdn_d = din("w_dn", [DFF, D])
    ident_d = din("ident", [128, 128])
    negu_d = din("negu", [128, 128])
    negl_d = din("negl", [128, 128])
    bd_d = din("bd", [128, 128])
    mask_d = din("mask", [128, 4, T])
    y_d = nc.dram_tensor("y", [NO * T, D], F32, kind="ExternalOutput").ap()
    kt_d = nc.dram_tensor("kt_scr", [4, 128, SL], BF16).ap()
    v_d = nc.dram_tensor("v_scr", [4, 128, NKB, 128], BF16).ap()
    x1_d = nc.dram_tensor("x1_scr", [NQ * T, D], F32).ap()

    top = contextlib.ExitStack()
    with top:
        engnames = ['pe', 'act', 'dve', 'pool', 'sp']
        sems = {}
        for n in ['pe', 'act', 'dve', 'pool', 'flag']:
            sems[n] = top.enter_context(nc.semaphore(n))
        for i in range(28):
            sems[f'c{i}'] = top.enter_context(nc.semaphore(f'c{i}'))
        S = Sched(engnames, sems)
        S.regs = {'pe': nc.alloc_register(mybir.EngineType.PE, 'flag_pe'),
                  'act': nc.alloc_register(mybir.EngineType.Activation, 'flag_act'),
                  'dve': nc.alloc_register(mybir.EngineType.DVE, 'flag_dve')}

        def run_block():
            blk = nc.Block()
            with blk:
                blk.tensor(lambda e: S.replay('pe', e))
                blk.scalar(lambda e: S.replay('act', e))
                blk.vector(lambda e: S.replay('dve', e))
                blk.gpsimd(lambda e: S.replay('pool', e))
                blk.sync(lambda e: S.replay('sp', e))

        uid = [0]

        def sbt(es, name, shape, dt):
            uid[0] += 1
            return es.enter_context(nc.sbuf_tensor(f"s{uid[0]}_{name}", shape, dt))

        def pst(es, name, shape, dt=F32):
            uid[0] += 1
            return es.enter_context(nc.psum_tensor(f"p{uid[0]}_{name}", shape, dt))

        def MM(out, lhsT, rhs, start, stop, reads, writes):
            S.op('pe', 'matmul', (out,), dict(lhsT=lhsT, rhs=rhs, start=start, stop=stop), reads, writes)

        def ACTV(out, in_, func, reads, writes, **kw):
            S.op('act', 'activation', (), dict(out=out, in_=in_, func=func, **kw), reads, writes)

        def CP(eng, out, in_, reads, writes):
            S.op(eng, 'copy' if eng == 'act' else 'tensor_copy', (), dict(out=out, in_=in_), reads, writes)

        def TT(eng, out, in0, in1, op, reads, writes):
            S.op(eng, 'tensor_tensor', (), dict(out=out, in0=in0, in1=in1, op=op), reads, writes)

        def STT(eng, out, in0, scalar, in1, op0, op1, reads, writes):
            S.op(eng, 'scalar_tensor_tensor', (), dict(out=out, in0=in0, scalar=scalar, in1=in1, op0=op0, op1=op1), reads, writes)

        def TS(eng, out, in0, s1, s2, op0, op1, reads, writes):
            kw = dict(out=out, in0=in0, scalar1=s1, scalar2=s2, op0=op0)
            if op1 is not None:
                kw['op1'] = op1
            S.op(eng, 'tensor_scalar', (), kw, reads, writes)

        def MS(eng, ap, val, writes):
            S.op(eng, 'memset', (ap, val), {}, (), writes)

        ident = sbt(top, "ident", [128, 128], BF16)
        negu = sbt(top, "negu", [128, 128], BF16)
        negl = sbt(top, "negl", [128, 128], BF16)
        bdm = sbt(top, "bdm", [128, 128], BF16)
        zero = sbt(top, "zero", [128, 128], BF16)
        cst = sbt(top, "cst", [128, 4], F32)

        stg_state = {'i': 0}

        def load_weight(stg, dst_ap, src_ap, n, wname):
            sl = stg_state['i'] % 2
            stg_state['i'] += 1
            S.dma(f'stg{sl}', stg[:, sl, 0:n], src_ap, writes=[f'stg{sl}'])
            CP('pool', dst_ap, stg[:, sl, 0:n], [f'stg{sl}'], [wname])

        def norm_block(Bf, src_ap, src_reads, slot, gb, hnT_ap, hnT_name, tp, tp_name):
            xs, junk, ss, hn = Bf['xs'], Bf['junk'], Bf['ss'], Bf['hn']
            xn, hnn, ssn = f'xs{slot}', f'hn{slot}', f'ss{slot}'
            S.dma(xn, xs[:, slot, :], src_ap, reads=src_reads, writes=[xn])
            MS('pool', ss[:, slot, 0:1], 0.0, [ssn])
            ACTV(junk[:, :], xs[:, slot, :], AF.Square, [xn], ['junk', ssn], accum_out=ss[:, slot, 0:1])
            ACTV(ss[:, slot, 1:2], ss[:, slot, 0:1], AF.Ln, [ssn], [ssn], scale=1.0 / D, bias=cst[:, 0:1])
            ACTV(ss[:, slot, 2:3], ss[:, slot, 1:2], AF.Exp, [ssn], [ssn], scale=-0.5)
            STT('dve', hn[:, slot, :], xs[:, slot, :], ss[:, slot, 2:3], gb[:, :], ALU.mult, ALU.mult, [xn, ssn], [hnn])
            for kc in range(KC):
                S.op('pe', 'transpose', (tp[:, kc, :], hn[:, slot, kc * 128:(kc + 1) * 128], ident[:, :]), {}, [hnn], [tp_name])
            CP('act', hnT_ap, tp[:, :, :], [tp_name], [hnT_name])

        def new_B(es):
            return dict(xs=sbt(es, "xs", [128, 2, D], F32), junk=sbt(es, "junk", [128, D], BF16),
                        ss=sbt(es, "ss", [128, 2, 4], F32), hn=sbt(es, "hn", [128, 2, D], BF16))

        sc04 = contextlib.ExitStack()
        with sc04:
            g1b = sbt(sc04, "g1b", [128, D], F32)
            vt = sbt(sc04, "vt", [128, NKB], F32)
            OT = sbt(sc04, "OT", [128, 4, NQ * T], BF16)

            with contextlib.ExitStack() as es:
                i32 = sbt(es, "i32", [128, 4, 128], F32)
                for i, srcd in enumerate([ident_d, negu_d, negl_d, bd_d]):
                    S.dma('cst', i32[:, i, :], srcd[:, :], writes=[f'i32_{i}'])
                S.barrier()
                for i, dst in enumerate([ident, negu, negl, bdm]):
                    CP('pool', dst[:, :], i32[:, i, :], [f'i32_{i}'], [f'const{i}'])
                MS('pool', zero[:, :], 0.0, ['zero'])
                MS('pool', cst[:, 0:1], EPS, ['cst'])
                MS('pool', cst[:, 1:2], 1.0, ['cst'])
                MS('pool', OT[:, :, 0:T], 0.0, ['OTz'])
                S.dma('cst', g1b[:, :], g1_d[:, :], writes=['g1b'])
                S.dma('cst', vt[:, :], valid_d[:, :], writes=['vt'])
                S.barrier()
                run_block()

            sc12 = contextlib.ExitStack()
            with sc12:
                QT = sbt(sc12, "QT", [128, 4, NQ * T], BF16)
                with contextlib.ExitStack() as es:
                    wB = sbt(es, "wB", [128, KC, 1536], BF16)
                    stg = sbt(es, "stg1", [128, 2, 1536], F32)
                    Bf = new_B(es)
                    hnT = sbt(es, "hnT", [128, 2, KC, T], BF16)
                    KTs = sbt(es, "KTs", [128, 2, 4, T], BF16)
                    Vs = sbt(es, "Vs", [128, 2, 4, T], BF16)
                    tps = [pst(es, f"tp{i}", [128, KC, 128], BF16) for i in range(2)]
                    accs = [pst(es, f"acc{i}", [128, T], F32) for i in range(4)]
                    for kc in range(KC):
                        load_weight(stg, wB[:, kc, :], win_d[kc * 128:(kc + 1) * 128, 1536:3072], 1536, 'wB')
                    acc_i = 0
                    ev_i = 0
                    for t in range(NT):
                        hs = t % 2
                        for bi in range(4):
                            blk_i = t * 4 + bi
                            norm_block(Bf, x_d[blk_i * 128:(blk_i + 1) * 128, :], [], blk_i % 2, g1b,
                                       hnT[:, hs, :, bi * 128:(bi + 1) * 128], f'hnT{hs}_{bi}', tps[blk_i % 2], f'tp{blk_i % 2}')
                        hread = [f'hnT{hs}_{bi}' for bi in range(4)]
                        for p in range(4):
                            a = acc_i % 4
                            acc_i += 1
                            for kc in range(KC):
                                MM(accs[a][:, :], wB[:, kc, 512 + p * 128:512 + (p + 1) * 128], hnT[:, hs, kc, :],
                                   kc == 0, kc == KC - 1, ['wB'] + hread, [f'acc{a}'])
                            CP('dve' if ev_i % 2 == 0 else 'act', KTs[:, hs, p, :], accs[a][:, :], [f'acc{a}'], [f'KTs{hs}'])
                            ev_i += 1
                        S.dma(f'kts{hs}', kt_d[:, :, t * T:(t + 1) * T].rearrange("q p c -> p q c"), KTs[:, hs, :, :],
                              reads=[f'KTs{hs}'], writes=['kt_d'])
                        for bi in range(4):
                            a = acc_i % 4
                            acc_i += 1
                            for kc in range(KC):
                                MM(accs[a][:, :], hnT[:, hs, kc, bi * 128:(bi + 1) * 128], wB[:, kc, 1024:1536],
                                   kc == 0, kc == KC - 1, ['wB', f'hnT{hs}_{bi}'], [f'acc{a}'])
                            CP('dve' if ev_i % 2 == 0 else 'act', Vs[:, hs, bi, :], accs[a][:, :], [f'acc{a}'], [f'Vs{hs}'])
                            ev_i += 1
                        for q in range(4):
                            S.dma(f'vs{hs}', v_d[q, :, t * 4:(t + 1) * 4, :], Vs[:, hs, :, q * 128:(q + 1) * 128],
                                  reads=[f'Vs{hs}'], writes=['v_d'])
                        if t >= T0:
                            qi = t - T0
                            for p in range(4):
                                a = acc_i % 4
                                acc_i += 1
                                for kc in range(KC):
                                    MM(accs[a][:, :], wB[:, kc, p * 128:(p + 1) * 128], hnT[:, hs, kc, :],
                                       kc == 0, kc == KC - 1, ['wB'] + hread, [f'acc{a}'])
                                CP('dve' if ev_i % 2 == 0 else 'act', QT[:, p, qi * T:(qi + 1) * T], accs[a][:, :],
                                   [f'acc{a}'], [f'QT{p}_{qi}'])
                                ev_i += 1
                    S.barrier()
                    run_block()

                with contextlib.ExitStack() as es:
                    KTp = sbt(es, "KTp", [128, SL], BF16)
                    Vp = sbt(es, "Vp", [128, NKB, 128], BF16)
                    M = sbt(es, "M", [128, 4, T], BF16)
                    NE, NSP, NXW, NW = 3, 2, 1, 2
                    eb = sbt(es, "eb", [128, 2, NE, T], F32)
                    spb = sbt(es, "spb", [128, 2, NSP, T], BF16)
                    xwb = sbt(es, "xwb", [128, 2, NXW, T], F32)
                    wbuf = sbt(es, "wbuf", [128, 2, NW, T], BF16)
                    m32 = sbt(es, "m32", [128, 4, T], F32)
                    Zp = [pst(es, f"Z{h}", [128, T]) for h in range(2)]
                    Ap = [pst(es, f"A{h}", [128, T]) for h in range(2)]
                    Op = pst(es, "O", [128, T])
                    S.dma('cst', m32[:, :, :], mask_d[:, :, :], writes=['m32'])
                    CP('pool', M[:, :, :], m32[:, :, :], ['m32'], ['M'])
                    I32 = mybir.dt.int32
                    flagbuf = sbt(es, "flagbuf", [128, 512], I32)
                    mx = sbt(es, "mx", [128, 4], F32)
                    THRESH = 150.0
                    CH = 32
                    nchk = (NKB + CH - 1) // CH
                    itn = {0: 0, 1: 0}
                    fidx = [0]

                    def load_pair(p):
                        for c in reversed(range(nchk)):
                            k0, k1 = c * CH, min(NKB, (c + 1) * CH)
                            S.dma(f'ktp{c}', KTp[:, k0 * 128:k1 * 128], kt_d[p, :, k0 * 128:k1 * 128], reads=['kt_d'], writes=[f'KTp{c}'])
                            S.dma(f'vp{c}', Vp[:, k0:k1, :], v_d[p, :, k0:k1, :], reads=['v_d'], writes=[f'Vp{c}'])

                    def st1(it):
                        h, W, kb, p, qi, c0 = it['h'], it['W'], it['kb'], it['p'], it['qi'], it['c0']
                        MM(Zp[h][:, 0:W], KTp[64 * h:64 * h + 64, kb * 128:(kb + 1) * 128],
                           QT[64 * h:64 * h + 64, p, qi * T + c0:qi * T + c0 + W], True, True,
                           [f'KTp{kb // CH}', f'QT{p}_{qi}'], [f'Z{h}'])

                    def st2(it):
                        h, W, n = it['h'], it['W'], it['n']
                        en = f'e{h}_{n % NE}'
                        ACTV(eb[:, h, n % NE, 0:W], Zp[h][:, 0:W], AF.Exp, [f'Z{h}'], [en], scale=0.125)
                        if it['d'] is not None:
                            TT('dve', eb[:, h, n % NE, 0:W], eb[:, h, n % NE, 0:W], M[:, it['d'], 0:W], ALU.mult, [en, 'M'], [en])

                    def st3(it):
                        h, W, n = it['h'], it['W'], it['n']
                        ACTV(spb[:, h, n % NSP, 0:W], eb[:, h, n % NE, 0:W], AF.Ln, [f'e{h}_{n % NE}'], [f'sp{h}_{n % NSP}'],
                             bias=cst[:, 1:2])

                    def st4(it):
                        h, W, n = it['h'], it['W'], it['n']
                        MM(Ap[h][:, 0:W], negu[:, :], spb[:, h, n % NSP, 0:W], it['first'], False,
                           [f'sp{h}_{n % NSP}', 'const1'], [f'A{h}'])

                    def st5(it):
                        h, W, n = it['h'], it['W'], it['n']
                        ACTV(xwb[:, h, n % NXW, 0:W], Ap[h][:, 0:W], AF.Exp, [f'A{h}'], [f'xw{h}_{n % NXW}'])

                    def st6(it):
                        h, W, n = it['h'], it['W'], it['n']
                        if it['last']:
                            return
                        MM(Ap[h][:, 0:W], negl[:, :], spb[:, h, n % NSP, 0:W], False, False,
                           [f'sp{h}_{n % NSP}', 'const2'], [f'A{h}'])

                    def st7(it):
                        h, W, n = it['h'], it['W'], it['n']
                        TT('dve', wbuf[:, h, n % NW, 0:W], eb[:, h, n % NE, 0:W], xwb[:, h, n % NXW, 0:W], ALU.mult,
                           [f'e{h}_{n % NE}', f'xw{h}_{n % NXW}'], [f'w{h}_{n % NW}'])

                    def st8(it):
                        h, W, n, kb = it['h'], it['W'], it['n'], it['kb']
                        MM(Op[64 * h:64 * h + 64, 0:W], Vp[:, kb, 64 * h:64 * h + 64], wbuf[:, h, n % NW, 0:W],
                           it['first'], it['last'], [f'w{h}_{n % NW}', f'Vp{kb // CH}'], [f'O{h}'])

                    stages = [(st1, 0), (st2, 1), (st3, 2), (st6, 5), (st4, 3), (st5, 4), (st7, 5), (st8, 6)]

                    def emit_segment(items):
                        nit = len(items)
                        for s_ in range(nit + 7):
                            for fn, dly in stages:
                                i_ = s_ - dly
                                if 0 <= i_ < nit:
                                    fn(items[i_])

                    def emit_flag(W):
                        fi = fidx[0]
                        fidx[0] += 1
                        for h in range(2):
                            S.op('dve', 'tensor_reduce', (), dict(out=mx[0:1, h:h + 1], in_=Ap[h][0:1, 0:W], axis=mybir.AxisListType.X, op=ALU.max),
                                 [f'A{h}'], [f'mx{h}'])
                        TT('dve', mx[0:1, 2:3], mx[0:1, 0:1], mx[0:1, 1:2], ALU.max, ['mx0', 'mx1'], ['mx2'])
                        S.flag_op('tensor_scalar', (), dict(out=flagbuf[0:1, fi:fi + 1], in0=mx[0:1, 2:3], scalar1=-THRESH, scalar2=None, op0=ALU.is_gt),
                                  ['mx2'], [f'flag{fi}'])
                        return fi

                    for p in range(4):
                        load_pair(p)
                        for qi in range(NQ):
                            g = T0 + qi
                            if qi == 0:
                                W, c0, q0, ndiag = 128, 384, g * T + 384, 1
                            else:
                                W, c0, q0, ndiag = T, 0, g * T, 4
                            kb_hi = (q0 + W) // 128 - 1
                            nb = kb_hi + 1
                            segs = [list(range(0, min(nb, ndiag + 3)))]
                            sz = 2
                            while segs[-1][-1] + 1 < nb:
                                st_ = segs[-1][-1] + 1
                                segs.append(list(range(st_, min(nb, st_ + sz))))
                                if len(segs) > 2:
                                    sz *= 2
                            fi = None
                            for si, seg in enumerate(segs):
                                if si > 0:
                                    S.begin_cond(flagbuf[0:1, fi:fi + 1], ['pe', 'act', 'dve'])
                                items = []
                                for b in seg:
                                    kb = kb_hi - b
                                    d = kb - q0 // 128
                                    for h in range(2):
                                        items.append(dict(p=p, qi=qi, W=W, c0=c0, kb=kb, d=(d if d >= 0 else None), h=h,
                                                          first=(b == 0), last=(b == nb - 1), n=itn[h]))
                                        itn[h] += 1
                                emit_segment(items)
                                nfi = None
                                if si < len(segs) - 1:
                                    nfi = emit_flag(W)
                                if si > 0:
                                    S.end_cond(('memset', (flagbuf[0:1, nfi:nfi + 1], 0), {}) if nfi is not None else None)
                                fi = nfi
                            CP('dve', OT[:, p, qi * T + c0:qi * T + c0 + W], Op[:, 0:W], ['O0', 'O1', 'OTz'], [f'OT{p}_{qi}'])
                    S.barrier()
                    run_block()

            sc34 = contextlib.ExitStack()
            with sc34:
                OAT = sbt(sc34, "OAT", [128, 4, NQ * T], BF16)
                with contextlib.ExitStack() as es:
                    wA = sbt(es, "wA", [128, KC, 1536], BF16)
                    stg = sbt(es, "stg3", [128, 2, 1536], F32)
                    TB = sbt(es, "TB", [128, 8, 640], F32)
                    gq = sbt(es, "gq", [128, 1], F32)
                    gk = sbt(es, "gk", [128, 1], F32)
                    Bf = new_B(es)
                    hnT = sbt(es, "hnT", [128, KC, T], BF16)
                    QAT = sbt(es, "QAT", [128, 4, T], BF16)
                    KAT = sbt(es, "KAT", [128, 4, 2, T], BF16)
                    VA = sbt(es, "VA", [128, 2, 4, T], BF16)
                    VLD = sbt(es, "VLD", [128, 2, 4, 64], BF16)
                    sqb = sbt(es, "sqb", [128, 2, T], BF16)
                    qgb = sbt(es, "qgb", [128, 2, T], F32)
                    lnb = sbt(es, "lnb", [128, 2, T], F32)
                    sbb = sbt(es, "sbb", [128, 2, T], F32)
                    pTb = sbt(es, "pTb", [128, 2, T], BF16)
                    rdb = sbt(es, "rdb", [128, T], F32)
                    tp3 = pst(es, "tp3", [128, KC, 128], BF16)
                    acc3 = [pst(es, f"acc3_{i}", [128, T]) for i in range(2)]
                    SSp = pst(es, "SSp", [128, T])
                    SPS = [pst(es, f"SPS{h}", [128, T]) for h in range(2)]
                    OAp = pst(es, "OAp", [128, T])
                    DENp = pst(es, "DENp", [128, T])
                    S.dma('cst', TB[:, :, :], tb_d[:, :, :], writes=['TB'])
                    S.dma('cst', gq[:, :], gq_d[:, :], writes=['gq'])
                    S.dma('cst', gk[:, :], gk_d[:, :], writes=['gk'])
                    S.barrier()
                    for kc in range(KC):
                        load_weight(stg, wA[:, kc, :], win_d[kc * 128:(kc + 1) * 128, 0:1536], 1536, 'wA')
                    blkc = 0
                    acc_i = 0
                    nrm = [0]

                    def qknorm(a, gcol, gname, dst_ap, dst_name):
                        k = nrm[0] % 2
                        nrm[0] += 1
                        ACTV(sqb[:, k, :], acc3[a][:, :], AF.Square, [f'acc3_{a}'], [f'sqb{k}'])
                        S.op('act', 'mul', (), dict(out=qgb[:, k, :], in_=acc3[a][:, :], mul=gcol[:, 0:1]), [f'acc3_{a}', gname], [f'qgb{k}'])
                        MM(SSp[:, :], bdm[:, :], sqb[:, k, :], True, True, [f'sqb{k}', 'const3'], ['SSp'])
                        ACTV(lnb[:, k, :], SSp[:, :], AF.Ln, ['SSp'], [f'lnb{k}'], scale=1.0 / 64, bias=cst[:, 0:1])
                        ACTV(lnb[:, k, :], lnb[:, k, :], AF.Exp, [f'lnb{k}'], [f'lnb{k}'], scale=-0.5)
                        TT('dve', dst_ap, qgb[:, k, :], lnb[:, k, :], ALU.mult, [f'qgb{k}', f'lnb{k}'], [dst_name])

                    for t in range(T0 - 1, NT):
                        qi = t - T0
                        sl = t % 2
                        for bi in range(4):
                            blk_i = t * 4 + bi
                            norm_block(Bf, x_d[blk_i * 128:(blk_i + 1) * 128, :], [], blkc % 2, g1b,
                                       hnT[:, :, bi * 128:(bi + 1) * 128], f'hnT_{bi}', tp3, 'tp3')
                            blkc += 1
                        hread = [f'hnT_{bi}' for bi in range(4)]
                        for p in range(4):
                            a = acc_i % 2
                            acc_i += 1
                            for kc in range(KC):
                                MM(acc3[a][:, :], wA[:, kc, 512 + p * 128:512 + (p + 1) * 128], hnT[:, kc, :],
                                   kc == 0, kc == KC - 1, ['wA'] + hread, [f'acc3_{a}'])
                            qknorm(a, gk, 'gk', KAT[:, p, sl, :], f'KAT{sl}_{p}')
                        for bi in range(4):
                            a = acc_i % 2
                            acc_i += 1
                            for kc in range(KC):
                                MM(acc3[a][:, :], hnT[:, kc, bi * 128:(bi + 1) * 128], wA[:, kc, 1024:1536],
                                   kc == 0, kc == KC - 1, ['wA', f'hnT_{bi}'], [f'acc3_{a}'])
                            CP('dve', VA[:, sl, bi, :], acc3[a][:, :], [f'acc3_{a}'], [f'VA{sl}_{bi}'])
                            CP('pool', VLD[:, sl, bi, :], vt[:, t * 4 + bi:t * 4 + bi + 1].to_broadcast([128, 64]),
                               ['vt'], [f'VLD{sl}_{bi}'])
                        if qi < 0:
                            continue
                        for p in range(4):
                            a = acc_i % 2
                            acc_i += 1
                            for kc in range(KC):
                                MM(acc3[a][:, :], wA[:, kc, p * 128:(p + 1) * 128], hnT[:, kc, :],
                                   kc == 0, kc == KC - 1, ['wA'] + hread, [f'acc3_{a}'])
                            qknorm(a, gq, 'gq', QAT[:, p, :], f'QAT{p}')
                        for p in range(4):
                            MM(OAp[:, :], zero[:, :], hnT[:, 0, :], True, False, ['zero'] + hread, ['OAp'])
                            MM(DENp[:, :], zero[:, :], hnT[:, 0, :], True, False, ['zero'] + hread, ['DENp'])
                            for j in range(8):
                                i_lo, i_hi = max(0, j - 4), min(3, j)
                                N = (i_hi - i_lo + 1) * 128
                                ksl = (1 - sl) if j < 4 else sl
                                cj = j % 4
                                tb0 = (4 - j + i_lo) * 128
                                for h in range(2):
                                    hh = 2 * p + h
                                    MM(SPS[h][:, 0:N], KAT[64 * h:64 * h + 64, p, ksl, cj * 128:(cj + 1) * 128],
                                       QAT[64 * h:64 * h + 64, p, i_lo * 128:i_lo * 128 + N], True, True,
                                       [f'KAT{ksl}_{p}', f'QAT{p}'], [f'SPS{h}'])
                                    STT('dve', sbb[:, h, 0:N], SPS[h][:, 0:N], 0.125, TB[:, hh, tb0:tb0 + N], ALU.mult, ALU.add,
                                        [f'SPS{h}', 'TB'], [f'sbb{h}'])
                                    ACTV(pTb[:, h, 0:N], sbb[:, h, 0:N], AF.Exp, [f'sbb{h}'], [f'pTb{h}'])
                                    MM(OAp[64 * h:64 * h + 64, i_lo * 128:i_lo * 128 + N], VA[:, ksl, cj, hh * 64:(hh + 1) * 64],
                                       pTb[:, h, 0:N], False, False, [f'pTb{h}', f'VA{ksl}_{cj}'], ['OAp'])
                                    MM(DENp[64 * h:64 * h + 64, i_lo * 128:i_lo * 128 + N], VLD[:, ksl, cj, :],
                                       pTb[:, h, 0:N], False, False, [f'pTb{h}', f'VLD{ksl}_{cj}'], ['DENp'])
                            TS('dve', rdb[:, :], DENp[:, :], 1e-30, None, ALU.max, None, ['DENp'], ['rdb'])
                            S.op('dve', 'reciprocal', (), dict(out=rdb[:, :], in_=rdb[:, :]), ['rdb'], ['rdb'])
                            TT('dve', OAT[:, p, qi * T:(qi + 1) * T], OAp[:, :], rdb[:, :], ALU.mult, ['OAp', 'rdb'], [f'OAT{p}_{qi}'])
                    S.barrier()
                    run_block()

                with contextlib.ExitStack() as es:
                    wG = sbt(es, "wG", [128, KC, 2048], BF16)
                    wbra = sbt(es, "wbra", [128, 4, D], BF16)
                    wbrb = sbt(es, "wbrb", [128, 4, D], BF16)
                    wout = sbt(es, "wout", [128, KC, D], BF16)
                    stg = sbt(es, "stg4", [128, 2, D], F32)
                    Bf = new_B(es)
                    xr = sbt(es, "xr", [128, 2, D], F32)
                    hnT = sbt(es, "hnT", [128, KC, T], BF16)
                    sg = sbt(es, "sg", [128, 2, T], F32)
                    mm = sbt(es, "mm", [128, 2, T], F32)
                    MT = sbt(es, "MT", [128, KC, T], BF16)
                    tp4 = pst(es, "tp4", [128, KC, 128], BF16)
                    Gp = [pst(es, f"G{i}", [128, T]) for i in range(2)]
                    Yp = [pst(es, f"Y{i}", [128, T]) for i in range(2)]
                    Xp = [pst(es, f"X{i}", [128, T]) for i in range(2)]
                    for kc in range(KC):
                        for hf in range(2):
                            load_weight(stg, wG[:, kc, hf * D:(hf + 1) * D], win_d[kc * 128:(kc + 1) * 128, 3072 + hf * D:3072 + (hf + 1) * D], D, 'wG')
                    for p in range(4):
                        load_weight(stg, wbra[:, p, :], wa_d[p * 128:(p + 1) * 128, :], D, 'wbra')
                        load_weight(stg, wbrb[:, p, :], wb_d[p * 128:(p + 1) * 128, :], D, 'wbrb')
                    for kc in range(KC):
                        load_weight(stg, wout[:, kc, :], wo_d[kc * 128:(kc + 1) * 128, :], D, 'wout')
                    blkc = 0
                    ob = 0
                    for qi in range(NQ):
                        t = T0 + qi
                        for bi in range(4):
                            blk_i = t * 4 + bi
                            norm_block(Bf, x_d[blk_i * 128:(blk_i + 1) * 128, :], [], blkc % 2, g1b,
                                       hnT[:, :, bi * 128:(bi + 1) * 128], f'hnT_{bi}', tp4, 'tp4')
                            blkc += 1
                        hread = [f'hnT_{bi}' for bi in range(4)]
                        for oc in range(KC):
                            for br in range(2):
                                for kc in range(KC):
                                    MM(Gp[br][:, :], wG[:, kc, br * D + oc * 128:br * D + (oc + 1) * 128], hnT[:, kc, :],
                                       kc == 0, kc == KC - 1, ['wG'] + hread, [f'G{br}'])
                                ACTV(sg[:, br, :], Gp[br][:, :], AF.Sigmoid, [f'G{br}'], [f'sg{br}'])
                                wsrc = wbra if br == 0 else wbrb
                                osrc = OAT if br == 0 else OT
                                for p in range(4):
                                    MM(Yp[br][:, :], wsrc[:, p, oc * 128:(oc + 1) * 128], osrc[:, p, qi * T:(qi + 1) * T],
                                       p == 0, p == 3,
                                       ['wbra' if br == 0 else 'wbrb', (f'OAT{p}_{qi}' if br == 0 else f'OT{p}_{qi}'), 'OTz'], [f'Y{br}'])
                                TT('dve', mm[:, br, :], Yp[br][:, :], sg[:, br, :], ALU.mult, [f'Y{br}', f'sg{br}'], [f'mm{br}'])
                            TT('pool', MT[:, oc, :], mm[:, 0, :], mm[:, 1, :], ALU.add, ['mm0', 'mm1'], [f'MT{oc}'])
                        mread = [f'MT{oc}' for oc in range(KC)]
                        for bi in range(4):
                            blk_i = t * 4 + bi
                            o = ob % 2
                            ob += 1
                            S.dma(f'xr{o}', xr[:, o, :], x_d[blk_i * 128:(blk_i + 1) * 128, :], writes=[f'xr{o}'])
                            for half in range(2):
                                for oc in range(KC):
                                    MM(Xp[half][:, :], MT[:, oc, bi * 128:(bi + 1) * 128], wout[:, oc, half * T:(half + 1) * T],
                                       oc == 0, oc == KC - 1, ['wout'] + mread, [f'X{half}'])
                                TT('dve', xr[:, o, half * T:(half + 1) * T], Xp[half][:, :], xr[:, o, half * T:(half + 1) * T], ALU.add,
                                   [f'X{half}', f'xr{o}'], [f'xr{o}'])
                            row = qi * T + bi * 128
                            S.dma(f'x1w{o}', x1_d[row:row + 128, :], xr[:, o, :], reads=[f'xr{o}'], writes=['x1_d'])
                    S.barrier()
                    run_block()

        with contextlib.ExitStack() as es:
            wup = sbt(es, "wup", [128, KC, 2 * DFF], BF16)
            wdn = sbt(es, "wdn", [128, NFC, D], BF16)
            g2b = sbt(es, "g2b", [128, D], F32)
            cw = sbt(es, "cw", [128, 44, 3], F32)
            cb = sbt(es, "cb", [128, 44], F32)
            HALO = sbt(es, "HALO", [128, 44, 2], F32)
            with contextlib.ExitStack() as es2:
                stg = sbt(es2, "stg5", [128, 2, 2816], F32)
                for kc in range(KC):
                    for hf in range(2):
                        load_weight(stg, wup[:, kc, hf * DFF:(hf + 1) * DFF], wup_d[kc * 128:(kc + 1) * 128, hf * DFF:(hf + 1) * DFF], DFF, 'wup')
                for fc in range(NFC):
                    load_weight(stg, wdn[:, fc, :], wdn_d[fc * 128:(fc + 1) * 128, :], D, 'wdn')
                S.dma('cst', g2b[:, :], g2_d[:, :], writes=['g2b'])
                S.dma('cst', cw[:, :, :], cw_d[:, :, :], writes=['cw'])
                S.dma('cst', cb[:, :], cb_d[:, :], writes=['cb'])
                MS('pool', HALO[:, :, :], 0.0, ['HALO'])
                S.barrier()
            Bf = dict(xs=sbt(es, "xs", [128, 2, D], F32), junk=sbt(es, "junk", [128, D], BF16),
                      ss=sbt(es, "ss", [128, 2, 4], F32), hn=sbt(es, "hn", [128, 2, D], BF16))
            hnT = sbt(es, "hnT", [128, KC, T], BF16)
            hb = sbt(es, "hb", [128, 2, 2, T + 2], F32)
            cv = sbt(es, "cv", [128, 2, T], F32)
            sgl = sbt(es, "sgl", [128, T], F32)
            ACTT = sbt(es, "ACTT", [128, NFC, T], BF16)
            yo = sbt(es, "yo", [128, D], F32)
            tp5 = pst(es, "tp5", [128, KC, 128], BF16)
            Hp = [[pst(es, f"H{b}_{u}", [128, T]) for u in range(2)] for b in range(2)]
            Yd = [pst(es, f"Yd{i}", [128, T]) for i in range(2)]
            blkc = 0
            fcc = 0
            for qi in range(NQ):
                for bi in range(4):
                    row = qi * T + bi * 128
                    norm_block(Bf, x1_d[row:row + 128, :], ['x1_d'], blkc % 2, g2b,
                               hnT[:, :, bi * 128:(bi + 1) * 128], f'hnT_{bi}', tp5, 'tp5')
                    blkc += 1
                hread = [f'hnT_{bi}' for bi in range(4)]
                for fc in range(NFC):
                    b = fcc % 2
                    fcc += 1
                    for u in range(2):
                        ch = fc + u * NFC
                        hbn = f'hb{b}_{u}'
                        for kc in range(KC):
                            MM(Hp[b][u][:, :], wup[:, kc, ch * 128:(ch + 1) * 128], hnT[:, kc, :],
                               kc == 0, kc == KC - 1, ['wup'] + hread, [f'H{b}_{u}'])
                        CP('pool', hb[:, b, u, 0:2], HALO[:, ch, :], [f'HALO{ch}'], [hbn])
                        CP('act', hb[:, b, u, 2:T + 2], Hp[b][u][:, :], [f'H{b}_{u}'], [hbn])
                        CP('pool', HALO[:, ch, :], hb[:, b, u, T:T + 2], [hbn], [f'HALO{ch}'])
                        if qi == 0:
                            continue
                        ce = 'dve'
                        cvn = f'cv{u}'
                        TS(ce, cv[:, u, :], hb[:, b, u, 2:T + 2], cw[:, ch, 2:3], cb[:, ch:ch + 1], ALU.mult, ALU.add, [hbn, 'cw', 'cb'], [cvn])
                        STT(ce, cv[:, u, :], hb[:, b, u, 1:T + 1], cw[:, ch, 1:2], cv[:, u, :], ALU.mult, ALU.add, [hbn, cvn, 'cw'], [cvn])
                        STT(ce, cv[:, u, :], hb[:, b, u, 0:T], cw[:, ch, 0:1], cv[:, u, :], ALU.mult, ALU.add, [hbn, cvn, 'cw'], [cvn])
                    if qi == 0:
                        continue
                    ACTV(sgl[:, :], cv[:, 0, :], AF.Silu, ['cv0'], ['sgl'])
                    TT('dve', ACTT[:, fc, :], sgl[:, :], cv[:, 1, :], ALU.mult, ['sgl', 'cv1'], [f'ACTT{fc}'])
                if qi == 0:
                    continue
                aread = [f'ACTT{fc}' for fc in range(NFC)]
                for bi in range(4):
                    row = qi * T + bi * 128
                    S.dma('yo_in', yo[:, :], x1_d[row:row + 128, :], reads=['x1_d', 'y_d'], writes=['yo'])
                    for half in range(2):
                        for fc in range(NFC):
                            MM(Yd[half][:, :], ACTT[:, fc, bi * 128:(bi + 1) * 128], wdn[:, fc, half * T:(half + 1) * T],
                               fc == 0, fc == NFC - 1, ['wdn'] + aread, [f'Yd{half}'])
                        TT('dve', yo[:, half * T:(half + 1) * T], Yd[half][:, :], yo[:, half * T:(half + 1) * T], ALU.add,
                           [f'Yd{half}', 'yo'], ['yo'])
                    orow = (qi - 1) * T + bi * 128
                    S.dma('yw', y_d[orow:orow + 128, :], yo[:, :], reads=['yo'], writes=['y_d'])
            S.barrier()
            run_block()
        print("megakernel ops", S.nops, "waits", S.nwaits, {k: v for k, v in S.cnt.items() if k in engnames})
    return nc


_CACHE = {}


def _consts():
    ident = np.eye(128, dtype=np.float32)
    kk = np.arange(128)
    negu = -(kk[:, None] >= kk[None, :]).astype(np.float32)
    negl = -(kk[:, None] < kk[None, :]).astype(np.float32)
    bd = np.zeros((128, 128), np.float32)
    bd[:64, :64] = 1.0
    bd[64:, 64:] = 1.0
    qq = np.arange(T)
    mask = np.zeros((128, 4, T), np.float32)
    for d in range(4):
        mask[:, d, :] = ((128 * d + kk[:, None]) < qq[None, :]).astype(np.float32)
    return ident, negu, negl, bd, mask


def _tb_table(rel_bias):
    kk = np.arange(128)[:, None]
    qq = np.arange(128)[None, :]
    tb = np.empty((128, 8, 640), np.float32)
    for rp in range(5):
        idx = np.clip(qq - kk + rp * 128, -128, 128) + 128
        blk = rel_bias[:, idx]
        vis = np.ones((128, 128), bool)
        if rp == 0:
            vis = (kk < 64) | (qq >= 64)
        if rp == 4:
            vis = (kk >= 64) | (qq < 64)
        blk = np.where(vis[None], blk, np.float32(NEG))
        tb[:, :, rp * 128:(rp + 1) * 128] = blk.transpose(1, 0, 2)
    return tb


def kernel(x, norm1_g, w_in, q_norm_g, k_norm_g, rel_bias, w_branch_a, w_branch_b, w_out, norm2_g,
           w_ffn_up, ffn_conv_w, ffn_conv_b, w_ffn_down):
    x = np.asarray(x, np.float32)
    Bn, Sq, Dm = x.shape
    assert Bn == 2 and Dm == D and Sq % 2048 == 0
    NO = Sq // 2048
    NT = Sq // T
    key = (NT, NO)
    if key not in _CACHE:
        _CACHE[key] = build(NT, NO)
    nc = _CACHE[key]
    f = lambda a: np.ascontiguousarray(np.asarray(a, np.float32))
    ident, negu, negl, bd, mask = _consts()
    own = NO * T
    shared = {
        "g1": f(np.broadcast_to(np.asarray(norm1_g, np.float32)[0][None, :], (128, D))),
        "g2": f(np.broadcast_to(np.asarray(norm2_g, np.float32)[0][None, :], (128, D))),
        "w_in": f(w_in[0]),
        "gq": f(np.tile(np.asarray(q_norm_g, np.float32)[0], 2)[:, None]),
        "gk": f(np.tile(np.asarray(k_norm_g, np.float32)[0], 2)[:, None]),
        "tb": f(_tb_table(np.asarray(rel_bias, np.float32)[0])),
        "w_a": f(w_branch_a[0]), "w_b": f(w_branch_b[0]), "w_o": f(w_out[0]),
        "w_up": f(w_ffn_up[0]),
        "cw": f(np.asarray(ffn_conv_w, np.float32)[0].reshape(3, 44, 128).transpose(2, 1, 0)),
        "cb": f(np.asarray(ffn_conv_b, np.float32)[0].reshape(44, 128).T),
        "w_dn": f(w_ffn_down[0]),
        "ident": ident, "negu": negu, "negl": negl, "bd": bd, "mask": mask,
    }
    in_maps = []
    for c in range(8):
        b, j = c // 4, c % 4
        real = (j + 1) * own
        pad = Sq - real
        xl = np.zeros((Sq, D), np.float32)
        xl[pad:] = x[b, :real]
        valid = np.zeros((Sq,), np.float32)
        valid[pad:] = 1.0
        m = dict(shared)
        m["x"] = xl
        m["valid"] = f(valid.reshape(Sq // 128, 128).T)
        in_maps.append(m)
    res = run_bass_kernel_spmd(nc, in_maps, core_ids=list(range(8)))
    out = np.empty((Bn, Sq, D), np.float32)
    for c in range(8):
        b, j = c // 4, c % 4
        out[b, j * own:(j + 1) * own] = res.results[c]["y"]
    return out
```

```python
import contextlib
import numpy as np
import concourse.bass as bass
import concourse.mybir as mybir
from concourse.bass_utils import run_bass_kernel_spmd

F32 = mybir.dt.float32
BF16 = mybir.dt.bfloat16
AF = mybir.ActivationFunctionType
ALU = mybir.AluOpType

D = 1024
KC = 8
T = 512
DFF = 2816
NFC = 22
EPS = 1e-6
NEG = -30000.0


class Sched:
    def __init__(self, engnames, sems):
        self.engnames = engnames
        self.prog = {k: [] for k in engnames}
        self.stack = {k: [self.prog[k]] for k in engnames}
        self.cond = None
        self.sems = sems
        self.free_sems = [k for k in sems if k.startswith('c')]
        self.alias = {}
        self.cnt = {}
        self.mult = {}
        for k in engnames:
            self.cnt[k] = 0
            self.mult[k] = 1
        self.cnt['flag'] = 0
        self.mult['flag'] = 1
        self.lastw = {}
        self.readers = {}
        self.waited = {}
        self.nops = 0
        self.nwaits = 0

    def chan(self, name):
        if name not in self.alias:
            s = self.free_sems.pop(0)
            self.alias[name] = s
            self.cnt[name] = 0
            self.mult[name] = 16
        return name

    def sem(self, p):
        return self.sems[self.alias.get(p, p)]

    def _deps(self, reads, writes):
        deps = {}

        def add(ps):
            p, s = ps
            if deps.get(p, 0) < s:
                deps[p] = s
        for r in reads:
            if r in self.lastw:
                add(self.lastw[r])
        for w in writes:
            if w in self.lastw:
                add(self.lastw[w])
            for rd in self.readers.get(w, ()):
                add(rd)
        return deps

    def _emit_waits(self, eng, deps, skip_self):
        wd = self.waited.setdefault(eng, {})
        for p, s in deps.items():
            if p == eng and skip_self:
                continue
            if wd.get(p, 0) >= s:
                continue
            self.stack[eng][-1].append(('w', self.sem(p), s * self.mult[p]))
            wd[p] = s
            self.nwaits += 1

    def _record(self, prod, seq, reads, writes):
        for r in reads:
            self.readers.setdefault(r, []).append((prod, seq))
        for w in writes:
            self.lastw[w] = (prod, seq)
            self.readers[w] = []

    def op(self, eng, meth, args, kw, reads=(), writes=()):
        deps = self._deps(reads, writes)
        self._emit_waits(eng, deps, skip_self=(eng == 'pe'))
        self.stack[eng][-1].append(('o', (meth, args, kw), self.sems[eng], 1))
        self.cnt[eng] += 1
        self._record(eng, self.cnt[eng], reads, writes)
        self.nops += 1

    def dma(self, ch, out, in_, reads=(), writes=(), q='sp'):
        self.chan(ch)
        deps = self._deps(reads, writes)
        self._emit_waits(q, deps, skip_self=False)
        self.stack[q][-1].append(('o', ('dma_start', (), dict(out=out, in_=in_)), self.sem(ch), 16))
        self.cnt[ch] += 1
        self._record(ch, self.cnt[ch], reads, writes)
        self.nops += 1

    def flag_op(self, meth, args, kw, reads=(), writes=()):
        deps = self._deps(reads, writes)
        self._emit_waits('dve', deps, skip_self=False)
        self.stack['dve'][-1].append(('o', (meth, args, kw), self.sems['flag'], 1))
        self.cnt['flag'] += 1
        self._record('flag', self.cnt['flag'], reads, writes)

    def begin_cond(self, flag_ap, engines):
        import copy
        assert self.cond is None
        self.cond = dict(flag_ap=flag_ap, engines=engines, seq=self.cnt['flag'],
                         start={e: self.cnt[e] for e in engines}, snap=copy.deepcopy(self.waited))
        for e in engines:
            self.stack[e].append([])

    def end_cond(self, dve_else=None):
        c = self.cond
        self.cond = None
        for e in c['engines']:
            body = self.stack[e].pop()
            n = self.cnt[e] - c['start'][e]
            self.stack[e][-1].append(('if', c['flag_ap'], c['seq'], body, c['start'][e], n,
                                      dve_else if e == 'dve' else None))
        self.waited = c['snap']

    def barrier(self):
        for eng in self.engnames:
            wd = self.waited.setdefault(eng, {})
            for p, c in self.cnt.items():
                if c > 0 and p != eng and wd.get(p, 0) < c:
                    self.stack[eng][-1].append(('w', self.sem(p), c * self.mult[p]))
                    wd[p] = c
        for eng in self.engnames:
            if eng != 'sp' and self.cnt[eng] > 0:
                self.stack[eng][-1].append(('w', self.sems[eng], self.cnt[eng]))

    def _replay_list(self, eng, e, lst):
        for it in lst:
            if it[0] == 'w':
                e.wait_ge(it[1], it[2])
            elif it[0] == 'o':
                meth, args, kw = it[1]
                getattr(e, meth)(*args, **kw).then_inc(it[2], it[3])
            else:
                _, flag_ap, seq, body, start, n, dve_else = it
                e.wait_ge(self.sems['flag'], seq)
                reg = self.regs[eng]
                e.reg_load(reg, flag_ap)
                with e.If_ne(reg, 0):
                    self._replay_list(eng, e, body)
                with e.Else():
                    if start > 0:
                        e.wait_ge(self.sems[eng], start)
                    if n > 0:
                        e.sem_inc(self.sems[eng], n)
                    if dve_else is not None:
                        meth, args, kw = dve_else
                        getattr(e, meth)(*args, **kw).then_inc(self.sems['flag'], 1)

    def replay(self, eng, e):
        assert len(self.stack[eng]) == 1
        self._replay_list(eng, e, self.prog[eng])
        self.prog[eng] = []
        self.stack[eng] = [self.prog[eng]]


def build(NT, NO):
    SL = NT * T
    NKB = SL // 128
    NQ = NO + 1
    T0 = NT - NO - 1
    nc = bass.Bass("TRN2", target_bir_lowering=False)

    def din(name, shape):
        return nc.dram_tensor(name, shape, F32, kind="ExternalInput").ap()
    x_d = din("x", [SL, D])
    valid_d = din("valid", [128, NKB])
    g1_d = din("g1", [128, D])
    g2_d = din("g2", [128, D])
    win_d = din("w_in", [D, 5120])
    gq_d = din("gq", [128, 1])
    gk_d = din("gk", [128, 1])
    tb_d = din("tb", [128, 8, 640])
    wa_d = din("w_a", [512, D])
    wb_d = din("w_b", [512, D])
    wo_d = din("w_o", [D, D])
    wup_d = din("w_up", [D, 2 * DFF])
    cw_d = din("cw", [128, 44, 3])
    cb_d = din("cb", [128, 44])
    wdn_d = din("w_dn", [DFF, D])
    ident_d = din("ident", [128, 128])
    negu_d = din("negu", [128, 128])
    negl_d = din("negl", [128, 128])
    bd_d = din("bd", [128, 128])
    mask_d = din("mask", [128, 4, T])
    y_d = nc.dram_tensor("y", [NO * T, D], F32, kind="ExternalOutput").ap()
    kt_d = nc.dram_tensor("kt_scr", [4, 128, SL], BF16).ap()
    v_d = nc.dram_tensor("v_scr", [4, 128, NKB, 128], BF16).ap()
    x1_d = nc.dram_tensor("x1_scr", [NQ * T, D], F32).ap()

    top = contextlib.ExitStack()
    with top:
        engnames = ['pe', 'act', 'dve', 'pool', 'sp']
        sems = {}
        for n in ['pe', 'act', 'dve', 'pool', 'flag']:
            sems[n] = top.enter_context(nc.semaphore(n))
        for i in range(28):
            sems[f'c{i}'] = top.enter_context(nc.semaphore(f'c{i}'))
        S = Sched(engnames, sems)
        S.regs = {'pe': nc.alloc_register(mybir.EngineType.PE, 'flag_pe'),
                  'act': nc.alloc_register(mybir.EngineType.Activation, 'flag_act'),
                  'dve': nc.alloc_register(mybir.EngineType.DVE, 'flag_dve')}

        def run_block():
            blk = nc.Block()
            with blk:
                blk.tensor(lambda e: S.replay('pe', e))
                blk.scalar(lambda e: S.replay('act', e))
                blk.vector(lambda e: S.replay('dve', e))
                blk.gpsimd(lambda e: S.replay('pool', e))
                blk.sync(lambda e: S.replay('sp', e))

        uid = [0]

        def sbt(es, name, shape, dt):
            uid[0] += 1
            return es.enter_context(nc.sbuf_tensor(f"s{uid[0]}_{name}", shape, dt))

        def pst(es, name, shape, dt=F32):
            uid[0] += 1
            return es.enter_context(nc.psum_tensor(f"p{uid[0]}_{name}", shape, dt))

        def MM(out, lhsT, rhs, start, stop, reads, writes):
            S.op('pe', 'matmul', (out,), dict(lhsT=lhsT, rhs=rhs, start=start, stop=stop), reads, writes)

        def ACTV(out, in_, func, reads, writes, **kw):
            S.op('act', 'activation', (), dict(out=out, in_=in_, func=func, **kw), reads, writes)

        def CP(eng, out, in_, reads, writes):
            S.op(eng, 'copy' if eng == 'act' else 'tensor_copy', (), dict(out=out, in_=in_), reads, writes)

        def TT(eng, out, in0, in1, op, reads, writes):
            S.op(eng, 'tensor_tensor', (), dict(out=out, in0=in0, in1=in1, op=op), reads, writes)

        def STT(eng, out, in0, scalar, in1, op0, op1, reads, writes):
            S.op(eng, 'scalar_tensor_tensor', (), dict(out=out, in0=in0, scalar=scalar, in1=in1, op0=op0, op1=op1), reads, writes)

        def TS(eng, out, in0, s1, s2, op0, op1, reads, writes):
            kw = dict(out=out, in0=in0, scalar1=s1, scalar2=s2, op0=op0)
            if op1 is not None:
                kw['op1'] = op1
            S.op(eng, 'tensor_scalar', (), kw, reads, writes)

        def MS(eng, ap, val, writes):
            S.op(eng, 'memset', (ap, val), {}, (), writes)

        ident = sbt(top, "ident", [128, 128], BF16)
        negu = sbt(top, "negu", [128, 128], BF16)
        negl = sbt(top, "negl", [128, 128], BF16)
        bdm = sbt(top, "bdm", [128, 128], BF16)
        zero = sbt(top, "zero", [128, 128], BF16)
        cst = sbt(top, "cst", [128, 4], F32)

        stg_state = {'i': 0}

        def load_weight(stg, dst_ap, src_ap, n, wname):
            sl = stg_state['i'] % 2
            stg_state['i'] += 1
            S.dma(f'stg{sl}', stg[:, sl, 0:n], src_ap, writes=[f'stg{sl}'])
            CP('pool', dst_ap, stg[:, sl, 0:n], [f'stg{sl}'], [wname])

        def norm_block(Bf, src_ap, src_reads, slot, gb, hnT_ap, hnT_name, tp, tp_name):
            xs, junk, ss, hn = Bf['xs'], Bf['junk'], Bf['ss'], Bf['hn']
            xn, hnn, ssn = f'xs{slot}', f'hn{slot}', f'ss{slot}'
            S.dma(xn, xs[:, slot, :], src_ap, reads=src_reads, writes=[xn])
            MS('pool', ss[:, slot, 0:1], 0.0, [ssn])
            ACTV(junk[:, :], xs[:, slot, :], AF.Square, [xn], ['junk', ssn], accum_out=ss[:, slot, 0:1])
            ACTV(ss[:, slot, 1:2], ss[:, slot, 0:1], AF.Ln, [ssn], [ssn], scale=1.0 / D, bias=cst[:, 0:1])
            ACTV(ss[:, slot, 2:3], ss[:, slot, 1:2], AF.Exp, [ssn], [ssn], scale=-0.5)
            STT('dve', hn[:, slot, :], xs[:, slot, :], ss[:, slot, 2:3], gb[:, :], ALU.mult, ALU.mult, [xn, ssn], [hnn])
            for kc in range(KC):
                S.op('pe', 'transpose', (tp[:, kc, :], hn[:, slot, kc * 128:(kc + 1) * 128], ident[:, :]), {}, [hnn], [tp_name])
            CP('act', hnT_ap, tp[:, :, :], [tp_name], [hnT_name])

        def new_B(es, ns=2):
            return dict(xs=sbt(es, "xs", [128, ns, D], F32), junk=sbt(es, "junk", [128, D], BF16),
                        ss=sbt(es, "ss", [128, ns, 4], F32), hn=sbt(es, "hn", [128, ns, D], BF16))

        sc04 = contextlib.ExitStack()
        with sc04:
            g1b = sbt(sc04, "g1b", [128, D], F32)
            vt = sbt(sc04, "vt", [128, NKB], F32)
            OT = sbt(sc04, "OT", [128, 4, NQ * T], BF16)

            with contextlib.ExitStack() as es:
                i32 = sbt(es, "i32", [128, 4, 128], F32)
                for i, srcd in enumerate([ident_d, negu_d, negl_d, bd_d]):
                    S.dma('cst', i32[:, i, :], srcd[:, :], writes=[f'i32_{i}'])
                S.barrier()
                for i, dst in enumerate([ident, negu, negl, bdm]):
                    CP('pool', dst[:, :], i32[:, i, :], [f'i32_{i}'], [f'const{i}'])
                MS('pool', zero[:, :], 0.0, ['zero'])
                MS('pool', cst[:, 0:1], EPS, ['cst'])
                MS('pool', cst[:, 1:2], 1.0, ['cst'])
                MS('pool', OT[:, :, 0:T], 0.0, ['OTz'])
                S.dma('cst', g1b[:, :], g1_d[:, :], writes=['g1b'])
                S.dma('cst', vt[:, :], valid_d[:, :], writes=['vt'])
                S.barrier()
                run_block()

            sc12 = contextlib.ExitStack()
            with sc12:
                QT = sbt(sc12, "QT", [128, 4, NQ * T], BF16)
                with contextlib.ExitStack() as es:
                    wB = sbt(es, "wB", [128, KC, 1536], BF16)
                    stg = sbt(es, "stg1", [128, 2, 1536], F32)
                    Bf = new_B(es, 4)
                    hnT = sbt(es, "hnT", [128, 2, KC, T], BF16)
                    KTs = sbt(es, "KTs", [128, 2, 4, T], BF16)
                    Vs = sbt(es, "Vs", [128, 2, 4, T], BF16)
                    tps = [pst(es, f"tp{i}", [128, KC, 128], BF16) for i in range(2)]
                    accs = [pst(es, f"acc{i}", [128, T], F32) for i in range(4)]
                    for kc in range(KC):
                        load_weight(stg, wB[:, kc, :], win_d[kc * 128:(kc + 1) * 128, 1536:3072], 1536, 'wB')
                    acc_i = 0
                    ev_i = 0

                    def norm_tile1(t):
                        hs_ = t % 2
                        for bi in range(4):
                            blk_i = t * 4 + bi
                            norm_block(Bf, x_d[blk_i * 128:(blk_i + 1) * 128, :], [], blk_i % 4, g1b,
                                       hnT[:, hs_, :, bi * 128:(bi + 1) * 128], f'hnT{hs_}_{bi}', tps[blk_i % 2], f'tp{blk_i % 2}')

                    norm_tile1(0)
                    for t in range(NT):
                        hs = t % 2
                        if t + 1 < NT:
                            norm_tile1(t + 1)
                        hread = [f'hnT{hs}_{bi}' for bi in range(4)]
                        for p in range(4):
                            a = acc_i % 4
                            acc_i += 1
                            for kc in range(KC):
                                MM(accs[a][:, :], wB[:, kc, 512 + p * 128:512 + (p + 1) * 128], hnT[:, hs, kc, :],
                                   kc == 0, kc == KC - 1, ['wB'] + hread, [f'acc{a}'])
                            CP('dve' if ev_i % 2 == 0 else 'act', KTs[:, hs, p, :], accs[a][:, :], [f'acc{a}'], [f'KTs{hs}'])
                            ev_i += 1
                        S.dma(f'kts{hs}', kt_d[:, :, t * T:(t + 1) * T].rearrange("q p c -> p q c"), KTs[:, hs, :, :],
                              reads=[f'KTs{hs}'], writes=['kt_d'])
                        for bi in range(4):
                            a = acc_i % 4
                            acc_i += 1
                            for kc in range(KC):
                                MM(accs[a][:, :], hnT[:, hs, kc, bi * 128:(bi + 1) * 128], wB[:, kc, 1024:1536],
                                   kc == 0, kc == KC - 1, ['wB', f'hnT{hs}_{bi}'], [f'acc{a}'])
                            CP('dve' if ev_i % 2 == 0 else 'act', Vs[:, hs, bi, :], accs[a][:, :], [f'acc{a}'], [f'Vs{hs}'])
                            ev_i += 1
                        for q in range(4):
                            S.dma(f'vs{hs}', v_d[q, :, t * 4:(t + 1) * 4, :], Vs[:, hs, :, q * 128:(q + 1) * 128],
                                  reads=[f'Vs{hs}'], writes=['v_d'])
                        if t >= T0:
                            qi = t - T0
                            for p in range(4):
                                a = acc_i % 4
                                acc_i += 1
                                for kc in range(KC):
                                    MM(accs[a][:, :], wB[:, kc, p * 128:(p + 1) * 128], hnT[:, hs, kc, :],
                                       kc == 0, kc == KC - 1, ['wB'] + hread, [f'acc{a}'])
                                CP('dve' if ev_i % 2 == 0 else 'act', QT[:, p, qi * T:(qi + 1) * T], accs[a][:, :],
                                   [f'acc{a}'], [f'QT{p}_{qi}'])
                                ev_i += 1
                    S.barrier()
                    run_block()

                with contextlib.ExitStack() as es:
                    KTp = sbt(es, "KTp", [128, SL], BF16)
                    Vp = sbt(es, "Vp", [128, NKB, 128], BF16)
                    M = sbt(es, "M", [128, 4, 2, T], BF16)
                    NE, NSP, NXW, NW = 4, 3, 2, 2
                    eb = sbt(es, "eb", [128, NE, 2, T], F32)
                    spb = sbt(es, "spb", [128, NSP, 2, T], BF16)
                    xwb = sbt(es, "xwb", [128, NXW, 2, T], F32)
                    wbuf = sbt(es, "wbuf", [128, NW, 2, T], BF16)
                    m32 = sbt(es, "m32", [128, 4, T], F32)
                    Zp = [pst(es, f"Z{i}", [128, 2, T]) for i in range(2)]
                    Ap = pst(es, "A", [128, 2, T])
                    Op = pst(es, "O", [128, T])
                    S.dma('cst', m32[:, :, :], mask_d[:, :, :], writes=['m32'])
                    for h in range(2):
                        CP('pool', M[:, :, h, :], m32[:, :, :], ['m32'], ['M'])
                    I32 = mybir.dt.int32
                    flagbuf = sbt(es, "flagbuf", [128, 512], I32)
                    mx = sbt(es, "mx", [128, 4], F32)
                    THRESH = 150.0
                    CH = 32
                    nchk = (NKB + CH - 1) // CH
                    itn = [0]
                    fidx = [0]

                    def load_pair(p):
                        for c in reversed(range(nchk)):
                            k0, k1 = c * CH, min(NKB, (c + 1) * CH)
                            S.dma(f'ktp{c}', KTp[:, k0 * 128:k1 * 128], kt_d[p, :, k0 * 128:k1 * 128], reads=['kt_d'], writes=[f'KTp{c}'])
                            S.dma(f'vp{c}', Vp[:, k0:k1, :], v_d[p, :, k0:k1, :], reads=['v_d'], writes=[f'Vp{c}'])

                    def st1(it):
                        W, kb, p, qi, c0, n = it['W'], it['kb'], it['p'], it['qi'], it['c0'], it['n']
                        for h in range(2):
                            MM(Zp[n % 2][:, h, 0:W], KTp[64 * h:64 * h + 64, kb * 128:(kb + 1) * 128],
                               QT[64 * h:64 * h + 64, p, qi * T + c0:qi * T + c0 + W], True, True,
                               [f'KTp{kb // CH}', f'QT{p}_{qi}'], [f'Z{n % 2}'])

                    def st2(it):
                        W, n = it['W'], it['n']
                        en = f'e{n % NE}'
                        ACTV(eb[:, n % NE, :, 0:W], Zp[n % 2][:, :, 0:W], AF.Exp, [f'Z{n % 2}'], [en], scale=0.125)
                        if it['d'] is not None:
                            TT('dve', eb[:, n % NE, :, 0:W], eb[:, n % NE, :, 0:W], M[:, it['d'], :, 0:W], ALU.mult, [en, 'M'], [en])

                    def st3(it):
                        W, n = it['W'], it['n']
                        ACTV(spb[:, n % NSP, :, 0:W], eb[:, n % NE, :, 0:W], AF.Ln, [f'e{n % NE}'], [f'sp{n % NSP}'], bias=cst[:, 1:2])

                    def st4(it):
                        W, n = it['W'], it['n']
                        for h in range(2):
                            MM(Ap[:, h, 0:W], negu[:, :], spb[:, n % NSP, h, 0:W], it['first'], False,
                               [f'sp{n % NSP}', 'const1'], ['A'])

                    def st5(it):
                        W, n = it['W'], it['n']
                        ACTV(xwb[:, n % NXW, :, 0:W], Ap[:, :, 0:W], AF.Exp, ['A'], [f'xw{n % NXW}'])

                    def st6(it):
                        W, n = it['W'], it['n']
                        if it['last']:
                            return
                        for h in range(2):
                            MM(Ap[:, h, 0:W], negl[:, :], spb[:, n % NSP, h, 0:W], False, False,
                               [f'sp{n % NSP}', 'const2'], ['A'])

                    def st7(it):
                        W, n = it['W'], it['n']
                        TT('dve', wbuf[:, n % NW, :, 0:W], eb[:, n % NE, :, 0:W], xwb[:, n % NXW, :, 0:W], ALU.mult,
                           [f'e{n % NE}', f'xw{n % NXW}'], [f'w{n % NW}'])

                    def st8(it):
                        W, n, kb = it['W'], it['n'], it['kb']
                        for h in range(2):
                            MM(Op[64 * h:64 * h + 64, 0:W], Vp[:, kb, 64 * h:64 * h + 64], wbuf[:, n % NW, h, 0:W],
                               it['first'], it['last'], [f'w{n % NW}', f'Vp{kb // CH}'], ['O'])

                    stages = [(st1, 0), (st2, 1), (st3, 2), (st6, 4), (st4, 3), (st5, 3), (st7, 4), (st8, 5)]

                    def emit_segment(items):
                        nit = len(items)
                        for s_ in range(nit + 7):
                            for fn, dly in stages:
                                i_ = s_ - dly
                                if 0 <= i_ < nit:
                                    fn(items[i_])

                    def emit_flag(W):
                        fi = fidx[0]
                        fidx[0] += 1
                        for h in range(2):
                            S.op('dve', 'tensor_reduce', (), dict(out=mx[0:1, h:h + 1], in_=Ap[0:1, h, 0:W], axis=mybir.AxisListType.X, op=ALU.max),
                                 ['A'], [f'mx{h}'])
                        TT('dve', mx[0:1, 2:3], mx[0:1, 0:1], mx[0:1, 1:2], ALU.max, ['mx0', 'mx1'], ['mx2'])
                        S.flag_op('tensor_scalar', (), dict(out=flagbuf[0:1, fi:fi + 1], in0=mx[0:1, 2:3], scalar1=-THRESH, scalar2=None, op0=ALU.is_gt),
                                  ['mx2'], [f'flag{fi}'])
                        return fi

                    for p in range(4):
                        load_pair(p)
                        for qi in range(NQ):
                            g = T0 + qi
                            if qi == 0:
                                W, c0, q0, ndiag = 128, 384, g * T + 384, 1
                            else:
                                W, c0, q0, ndiag = T, 0, g * T, 4
                            kb_hi = (q0 + W) // 128 - 1
                            nb = kb_hi + 1
                            segs = [list(range(0, min(nb, ndiag + 3)))]
                            sz = 2
                            while segs[-1][-1] + 1 < nb:
                                st_ = segs[-1][-1] + 1
                                segs.append(list(range(st_, min(nb, st_ + sz))))
                                if len(segs) > 2:
                                    sz *= 2
                            fi = None
                            for si, seg in enumerate(segs):
                                if si > 0:
                                    S.begin_cond(flagbuf[0:1, fi:fi + 1], ['pe', 'act', 'dve'])
                                items = []
                                for b in seg:
                                    kb = kb_hi - b
                                    d = kb - q0 // 128
                                    items.append(dict(p=p, qi=qi, W=W, c0=c0, kb=kb, d=(d if d >= 0 else None),
                                                      first=(b == 0), last=(b == nb - 1), n=itn[0]))
                                    itn[0] += 1
                                emit_segment(items)
                                nfi = None
                                if si < len(segs) - 1:
                                    nfi = emit_flag(W)
                                if si > 0:
                                    S.end_cond(('memset', (flagbuf[0:1, nfi:nfi + 1], 0), {}) if nfi is not None else None)
                                fi = nfi
                            CP('dve', OT[:, p, qi * T + c0:qi * T + c0 + W], Op[:, 0:W], ['O', 'OTz'], [f'OT{p}_{qi}'])
                    S.barrier()
                    run_block()

            sc34 = contextlib.ExitStack()
            with sc34:
                OAT = sbt(sc34, "OAT", [128, 4, NQ * T], BF16)
                with contextlib.ExitStack() as es:
                    wA = sbt(es, "wA", [128, KC, 1536], BF16)
                    with contextlib.ExitStack() as es2:
                        stg = sbt(es2, "stg3", [128, 2, 1536], F32)
                        for kc in range(KC):
                            load_weight(stg, wA[:, kc, :], win_d[kc * 128:(kc + 1) * 128, 0:1536], 1536, 'wA')
                        S.barrier()
                    TB = sbt(es, "TB", [128, 8, 640], F32)
                    gq = sbt(es, "gq", [128, 1], F32)
                    gk = sbt(es, "gk", [128, 1], F32)
                    Bf = new_B(es)
                    hnT = sbt(es, "hnT", [128, KC, T], BF16)
                    QAT = sbt(es, "QAT", [128, 4, T], BF16)
                    KAT = sbt(es, "KAT", [128, 4, 2, T], BF16)
                    VA = sbt(es, "VA", [128, 2, 4, T], BF16)
                    VLD = sbt(es, "VLD", [128, 2, 4, 64], BF16)
                    sqb = sbt(es, "sqb", [128, 2, T], BF16)
                    qgb = sbt(es, "qgb", [128, 2, T], F32)
                    lnb = sbt(es, "lnb", [128, 2, T], F32)
                    sbb = sbt(es, "sbb", [128, 2, 2, T], F32)
                    pTb = sbt(es, "pTb", [128, 2, 2, T], BF16)
                    rdb = sbt(es, "rdb", [128, T], F32)
                    tp3 = pst(es, "tp3", [128, KC, 128], BF16)
                    acc3t = pst(es, "acc3", [128, 2, T])
                    acc3 = [acc3t[:, i, :] for i in range(2)]
                    SSp = pst(es, "SSp", [128, T])
                    SPSt = pst(es, "SPS", [128, 2, T])
                    sbufs = [(SPSt, 'SPS'), (acc3t, 'acc3_')]
                    OAp = pst(es, "OAp", [128, T])
                    DENp = pst(es, "DENp", [128, T])
                    S.dma('cst', TB[:, :, :], tb_d[:, :, :], writes=['TB'])
                    S.dma('cst', gq[:, :], gq_d[:, :], writes=['gq'])
                    S.dma('cst', gk[:, :], gk_d[:, :], writes=['gk'])
                    S.barrier()
                    blkc = 0
                    acc_i = 0
                    nrm = [0]
                    jj = [0]

                    def qknorm(a, gcol, gname, dst_ap, dst_name):
                        k = nrm[0] % 2
                        nrm[0] += 1
                        ACTV(sqb[:, k, :], acc3[a][:, :], AF.Square, [f'acc3_{a}'], [f'sqb{k}'])
                        S.op('act', 'mul', (), dict(out=qgb[:, k, :], in_=acc3[a][:, :], mul=gcol[:, 0:1]), [f'acc3_{a}', gname], [f'qgb{k}'])
                        MM(SSp[:, :], bdm[:, :], sqb[:, k, :], True, True, [f'sqb{k}', 'const3'], ['SSp'])
                        ACTV(lnb[:, k, :], SSp[:, :], AF.Ln, ['SSp'], [f'lnb{k}'], scale=1.0 / 64, bias=cst[:, 0:1])
                        ACTV(lnb[:, k, :], lnb[:, k, :], AF.Exp, [f'lnb{k}'], [f'lnb{k}'], scale=-0.5)
                        TT('dve', dst_ap, qgb[:, k, :], lnb[:, k, :], ALU.mult, [f'qgb{k}', f'lnb{k}'], [dst_name])

                    for t in range(T0 - 1, NT):
                        qi = t - T0
                        sl = t % 2
                        for bi in range(4):
                            blk_i = t * 4 + bi
                            norm_block(Bf, x_d[blk_i * 128:(blk_i + 1) * 128, :], [], blkc % 2, g1b,
                                       hnT[:, :, bi * 128:(bi + 1) * 128], f'hnT_{bi}', tp3, 'tp3')
                            blkc += 1
                        hread = [f'hnT_{bi}' for bi in range(4)]
                        for p in range(4):
                            a = acc_i % 2
                            acc_i += 1
                            for kc in range(KC):
                                MM(acc3[a][:, :], wA[:, kc, 512 + p * 128:512 + (p + 1) * 128], hnT[:, kc, :],
                                   kc == 0, kc == KC - 1, ['wA'] + hread, [f'acc3_{a}'])
                            qknorm(a, gk, 'gk', KAT[:, p, sl, :], f'KAT{sl}_{p}')
                        for bi in range(4):
                            a = acc_i % 2
                            acc_i += 1
                            for kc in range(KC):
                                MM(acc3[a][:, :], hnT[:, kc, bi * 128:(bi + 1) * 128], wA[:, kc, 1024:1536],
                                   kc == 0, kc == KC - 1, ['wA', f'hnT_{bi}'], [f'acc3_{a}'])
                            CP('dve', VA[:, sl, bi, :], acc3[a][:, :], [f'acc3_{a}'], [f'VA{sl}_{bi}'])
                            CP('pool', VLD[:, sl, bi, :], vt[:, t * 4 + bi:t * 4 + bi + 1].to_broadcast([128, 64]),
                               ['vt'], [f'VLD{sl}_{bi}'])
                        if qi < 0:
                            continue
                        for p in range(4):
                            a = acc_i % 2
                            acc_i += 1
                            for kc in range(KC):
                                MM(acc3[a][:, :], wA[:, kc, p * 128:(p + 1) * 128], hnT[:, kc, :],
                                   kc == 0, kc == KC - 1, ['wA'] + hread, [f'acc3_{a}'])
                            qknorm(a, gq, 'gq', QAT[:, p, :], f'QAT{p}')
                        for p in range(4):
                            MM(OAp[:, :], zero[:, :], hnT[:, 0, :], True, False, ['zero'] + hread, ['OAp'])
                            MM(DENp[:, :], zero[:, :], hnT[:, 0, :], True, False, ['zero'] + hread, ['DENp'])
                            for j in range(8):
                                i_lo, i_hi = max(0, j - 4), min(3, j)
                                N = (i_hi - i_lo + 1) * 128
                                ksl = (1 - sl) if j < 4 else sl
                                cj = j % 4
                                tb0 = (4 - j + i_lo) * 128
                                k = jj[0] % 2
                                jj[0] += 1
                                spt, spn = sbufs[k]
                                for h in range(2):
                                    MM(spt[:, h, 0:N], KAT[64 * h:64 * h + 64, p, ksl, cj * 128:(cj + 1) * 128],
                                       QAT[64 * h:64 * h + 64, p, i_lo * 128:i_lo * 128 + N], True, True,
                                       [f'KAT{ksl}_{p}', f'QAT{p}'], [f'{spn}{h}'])
                                STT('dve', sbb[:, k, :, 0:N], spt[:, :, 0:N], 0.125, TB[:, 2 * p:2 * p + 2, tb0:tb0 + N], ALU.mult, ALU.add,
                                    [f'{spn}0', f'{spn}1', 'TB'], [f'sbb{k}'])
                                ACTV(pTb[:, k, :, 0:N], sbb[:, k, :, 0:N], AF.Exp, [f'sbb{k}'], [f'pTb{k}'])
                                for h in range(2):
                                    hh = 2 * p + h
                                    MM(OAp[64 * h:64 * h + 64, i_lo * 128:i_lo * 128 + N], VA[:, ksl, cj, hh * 64:(hh + 1) * 64],
                                       pTb[:, k, h, 0:N], False, False, [f'pTb{k}', f'VA{ksl}_{cj}'], ['OAp'])
                                    MM(DENp[64 * h:64 * h + 64, i_lo * 128:i_lo * 128 + N], VLD[:, ksl, cj, :],
                                       pTb[:, k, h, 0:N], False, False, [f'pTb{k}', f'VLD{ksl}_{cj}'], ['DENp'])
                            TS('dve', rdb[:, :], DENp[:, :], 1e-30, None, ALU.max, None, ['DENp'], ['rdb'])
                            S.op('dve', 'reciprocal', (), dict(out=rdb[:, :], in_=rdb[:, :]), ['rdb'], ['rdb'])
                            TT('dve', OAT[:, p, qi * T:(qi + 1) * T], OAp[:, :], rdb[:, :], ALU.mult, ['OAp', 'rdb'], [f'OAT{p}_{qi}'])
                    S.barrier()
                    run_block()

                with contextlib.ExitStack() as es:
                    wG = sbt(es, "wG", [128, KC, 2048], BF16)
                    wbra = sbt(es, "wbra", [128, 4, D], BF16)
                    wbrb = sbt(es, "wbrb", [128, 4, D], BF16)
                    wout = sbt(es, "wout", [128, KC, D], BF16)
                    stg = sbt(es, "stg4", [128, 2, D], F32)
                    Bf = new_B(es)
                    xr = sbt(es, "xr", [128, 2, D], F32)
                    hnT = sbt(es, "hnT", [128, KC, T], BF16)
                    sg = sbt(es, "sg", [128, 2, T], F32)
                    mm = sbt(es, "mm", [128, 2, T], F32)
                    MT = sbt(es, "MT", [128, KC, T], BF16)
                    tp4 = pst(es, "tp4", [128, KC, 128], BF16)
                    Gp = [pst(es, f"G{i}", [128, T]) for i in range(2)]
                    Yp = [pst(es, f"Y{i}", [128, T]) for i in range(2)]
                    Xp = [pst(es, f"X{i}", [128, T]) for i in range(2)]
                    for kc in range(KC):
                        for hf in range(2):
                            load_weight(stg, wG[:, kc, hf * D:(hf + 1) * D], win_d[kc * 128:(kc + 1) * 128, 3072 + hf * D:3072 + (hf + 1) * D], D, 'wG')
                    for p in range(4):
                        load_weight(stg, wbra[:, p, :], wa_d[p * 128:(p + 1) * 128, :], D, 'wbra')
                        load_weight(stg, wbrb[:, p, :], wb_d[p * 128:(p + 1) * 128, :], D, 'wbrb')
                    for kc in range(KC):
                        load_weight(stg, wout[:, kc, :], wo_d[kc * 128:(kc + 1) * 128, :], D, 'wout')
                    blkc = 0
                    ob = 0
                    for qi in range(NQ):
                        t = T0 + qi
                        for bi in range(4):
                            blk_i = t * 4 + bi
                            norm_block(Bf, x_d[blk_i * 128:(blk_i + 1) * 128, :], [], blkc % 2, g1b,
                                       hnT[:, :, bi * 128:(bi + 1) * 128], f'hnT_{bi}', tp4, 'tp4')
                            blkc += 1
                        hread = [f'hnT_{bi}' for bi in range(4)]
                        for oc in range(KC):
                            for br in range(2):
                                for kc in range(KC):
                                    MM(Gp[br][:, :], wG[:, kc, br * D + oc * 128:br * D + (oc + 1) * 128], hnT[:, kc, :],
                                       kc == 0, kc == KC - 1, ['wG'] + hread, [f'G{br}'])
                                ACTV(sg[:, br, :], Gp[br][:, :], AF.Sigmoid, [f'G{br}'], [f'sg{br}'])
                                wsrc = wbra if br == 0 else wbrb
                                osrc = OAT if br == 0 else OT
                                for p in range(4):
                                    MM(Yp[br][:, :], wsrc[:, p, oc * 128:(oc + 1) * 128], osrc[:, p, qi * T:(qi + 1) * T],
                                       p == 0, p == 3,
                                       ['wbra' if br == 0 else 'wbrb', (f'OAT{p}_{qi}' if br == 0 else f'OT{p}_{qi}'), 'OTz'], [f'Y{br}'])
                                TT('dve', mm[:, br, :], Yp[br][:, :], sg[:, br, :], ALU.mult, [f'Y{br}', f'sg{br}'], [f'mm{br}'])
                            TT('pool', MT[:, oc, :], mm[:, 0, :], mm[:, 1, :], ALU.add, ['mm0', 'mm1'], [f'MT{oc}'])
                        mread = [f'MT{oc}' for oc in range(KC)]
                        for bi in range(4):
                            blk_i = t * 4 + bi
                            o = ob % 2
                            ob += 1
                            S.dma(f'xr{o}', xr[:, o, :], x_d[blk_i * 128:(blk_i + 1) * 128, :], writes=[f'xr{o}'])
                            for half in range(2):
                                for oc in range(KC):
                                    MM(Xp[half][:, :], MT[:, oc, bi * 128:(bi + 1) * 128], wout[:, oc, half * T:(half + 1) * T],
                                       oc == 0, oc == KC - 1, ['wout'] + mread, [f'X{half}'])
                                TT('dve', xr[:, o, half * T:(half + 1) * T], Xp[half][:, :], xr[:, o, half * T:(half + 1) * T], ALU.add,
                                   [f'X{half}', f'xr{o}'], [f'xr{o}'])
                            row = qi * T + bi * 128
                            S.dma(f'x1w{o}', x1_d[row:row + 128, :], xr[:, o, :], reads=[f'xr{o}'], writes=['x1_d'])
                    S.barrier()
                    run_block()

        with contextlib.ExitStack() as es:
            wup = sbt(es, "wup", [128, KC, 2 * DFF], BF16)
            wdn = sbt(es, "wdn", [128, NFC, D], BF16)
            g2b = sbt(es, "g2b", [128, D], F32)
            cw = sbt(es, "cw", [128, 44, 3], F32)
            cb = sbt(es, "cb", [128, 44], F32)
            HALO = sbt(es, "HALO", [128, 44, 2], F32)
            with contextlib.ExitStack() as es2:
                stg = sbt(es2, "stg5", [128, 2, 2816], F32)
                for kc in range(KC):
                    for hf in range(2):
                        load_weight(stg, wup[:, kc, hf * DFF:(hf + 1) * DFF], wup_d[kc * 128:(kc + 1) * 128, hf * DFF:(hf + 1) * DFF], DFF, 'wup')
                for fc in range(NFC):
                    load_weight(stg, wdn[:, fc, :], wdn_d[fc * 128:(fc + 1) * 128, :], D, 'wdn')
                S.dma('cst', g2b[:, :], g2_d[:, :], writes=['g2b'])
                S.dma('cst', cw[:, :, :], cw_d[:, :, :], writes=['cw'])
                S.dma('cst', cb[:, :], cb_d[:, :], writes=['cb'])
                MS('pool', HALO[:, :, :], 0.0, ['HALO'])
                S.barrier()
            Bf = dict(xs=sbt(es, "xs", [128, 2, D], F32), junk=sbt(es, "junk", [128, D], BF16),
                      ss=sbt(es, "ss", [128, 2, 4], F32), hn=sbt(es, "hn", [128, 2, D], BF16))
            hnT = sbt(es, "hnT", [128, KC, T], BF16)
            hb = sbt(es, "hb", [128, 2, 2, T + 2], F32)
            cv = sbt(es, "cv", [128, 2, T], F32)
            sgl = sbt(es, "sgl", [128, T], F32)
            ACTT = sbt(es, "ACTT", [128, NFC, T], BF16)
            yo = sbt(es, "yo", [128, D], F32)
            tp5 = pst(es, "tp5", [128, KC, 128], BF16)
            Hp = [[pst(es, f"H{b}_{u}", [128, T]) for u in range(2)] for b in range(2)]
            Yd = [pst(es, f"Yd{i}", [128, T]) for i in range(2)]
            blkc = 0
            fcc = 0
            for qi in range(NQ):
                for bi in range(4):
                    row = qi * T + bi * 128
                    norm_block(Bf, x1_d[row:row + 128, :], ['x1_d'], blkc % 2, g2b,
                               hnT[:, :, bi * 128:(bi + 1) * 128], f'hnT_{bi}', tp5, 'tp5')
                    blkc += 1
                hread = [f'hnT_{bi}' for bi in range(4)]
                for fc in range(NFC):
                    b = fcc % 2
                    fcc += 1
                    for u in range(2):
                        ch = fc + u * NFC
                        hbn = f'hb{b}_{u}'
                        for kc in range(KC):
                            MM(Hp[b][u][:, :], wup[:, kc, ch * 128:(ch + 1) * 128], hnT[:, kc, :],
                               kc == 0, kc == KC - 1, ['wup'] + hread, [f'H{b}_{u}'])
                        CP('pool', hb[:, b, u, 0:2], HALO[:, ch, :], [f'HALO{ch}'], [hbn])
                        CP('act', hb[:, b, u, 2:T + 2], Hp[b][u][:, :], [f'H{b}_{u}'], [hbn])
                        CP('pool', HALO[:, ch, :], hb[:, b, u, T:T + 2], [hbn], [f'HALO{ch}'])
                        if qi == 0:
                            continue
                        ce = 'dve'
                        cvn = f'cv{u}'
                        TS(ce, cv[:, u, :], hb[:, b, u, 2:T + 2], cw[:, ch, 2:3], cb[:, ch:ch + 1], ALU.mult, ALU.add, [hbn, 'cw', 'cb'], [cvn])
                        STT(ce, cv[:, u, :], hb[:, b, u, 1:T + 1], cw[:, ch, 1:2], cv[:, u, :], ALU.mult, ALU.add, [hbn, cvn, 'cw'], [cvn])
                        STT(ce, cv[:, u, :], hb[:, b, u, 0:T], cw[:, ch, 0:1], cv[:, u, :], ALU.mult, ALU.add, [hbn, cvn, 'cw'], [cvn])
                    if qi == 0:
                        continue
                    ACTV(sgl[:, :], cv[:, 0, :], AF.Silu, ['cv0'], ['sgl'])
                    TT('dve', ACTT[:, fc, :], sgl[:, :], cv[:, 1, :], ALU.mult, ['sgl', 'cv1'], [f'ACTT{fc}'])
                if qi == 0:
                    continue
                aread = [f'ACTT{fc}' for fc in range(NFC)]
                for bi in range(4):
                    row = qi * T + bi * 128
                    S.dma('yo_in', yo[:, :], x1_d[row:row + 128, :], reads=['x1_d', 'y_d'], writes=['yo'])
                    for half in range(2):
                        for fc in range(NFC):
                            MM(Yd[half][:, :], ACTT[:, fc, bi * 128:(bi + 1) * 128], wdn[:, fc, half * T:(half + 1) * T],
                               fc == 0, fc == NFC - 1, ['wdn'] + aread, [f'Yd{half}'])
                        TT('dve', yo[:, half * T:(half + 1) * T], Yd[half][:, :], yo[:, half * T:(half + 1) * T], ALU.add,
                           [f'Yd{half}', 'yo'], ['yo'])
                    orow = (qi - 1) * T + bi * 128
                    S.dma('yw', y_d[orow:orow + 128, :], yo[:, :], reads=['yo'], writes=['y_d'])
            S.barrier()
            run_block()
        print("megakernel ops", S.nops, "waits", S.nwaits, {k: v for k, v in S.cnt.items() if k in engnames})
    return nc


_CACHE = {}


def _consts():
    ident = np.eye(128, dtype=np.float32)
    kk = np.arange(128)
    negu = -(kk[:, None] >= kk[None, :]).astype(np.float32)
    negl = -(kk[:, None] < kk[None, :]).astype(np.float32)
    bd = np.zeros((128, 128), np.float32)
    bd[:64, :64] = 1.0
    bd[64:, 64:] = 1.0
    qq = np.arange(T)
    mask = np.zeros((128, 4, T), np.float32)
    for d in range(4):
        mask[:, d, :] = ((128 * d + kk[:, None]) < qq[None, :]).astype(np.float32)
    return ident, negu, negl, bd, mask


def _tb_table(rel_bias):
    kk = np.arange(128)[:, None]
    qq = np.arange(128)[None, :]
    tb = np.empty((128, 8, 640), np.float32)
    for rp in range(5):
        idx = np.clip(qq - kk + rp * 128, -128, 128) + 128
        blk = rel_bias[:, idx]
        vis = np.ones((128, 128), bool)
        if rp == 0:
            vis = (kk < 64) | (qq >= 64)
        if rp == 4:
            vis = (kk >= 64) | (qq < 64)
        blk = np.where(vis[None], blk, np.float32(NEG))
        tb[:, :, rp * 128:(rp + 1) * 128] = blk.transpose(1, 0, 2)
    return tb


def kernel(x, norm1_g, w_in, q_norm_g, k_norm_g, rel_bias, w_branch_a, w_branch_b, w_out, norm2_g,
           w_ffn_up, ffn_conv_w, ffn_conv_b, w_ffn_down):
    x = np.asarray(x, np.float32)
    Bn, Sq, Dm = x.shape
    assert Bn == 2 and Dm == D and Sq % 2048 == 0
    NO = Sq // 2048
    NT = Sq // T
    key = (NT, NO)
    if key not in _CACHE:
        _CACHE[key] = build(NT, NO)
    nc = _CACHE[key]
    f = lambda a: np.ascontiguousarray(np.asarray(a, np.float32))
    ident, negu, negl, bd, mask = _consts()
    own = NO * T
    shared = {
        "g1": f(np.broadcast_to(np.asarray(norm1_g, np.float32)[0][None, :], (128, D))),
        "g2": f(np.broadcast_to(np.asarray(norm2_g, np.float32)[0][None, :], (128, D))),
        "w_in": f(w_in[0]),
        "gq": f(np.tile(np.asarray(q_norm_g, np.float32)[0], 2)[:, None]),
        "gk": f(np.tile(np.asarray(k_norm_g, np.float32)[0], 2)[:, None]),
        "tb": f(_tb_table(np.asarray(rel_bias, np.float32)[0])),
        "w_a": f(w_branch_a[0]), "w_b": f(w_branch_b[0]), "w_o": f(w_out[0]),
        "w_up": f(w_ffn_up[0]),
        "cw": f(np.asarray(ffn_conv_w, np.float32)[0].reshape(3, 44, 128).transpose(2, 1, 0)),
        "cb": f(np.asarray(ffn_conv_b, np.float32)[0].reshape(44, 128).T),
        "w_dn": f(w_ffn_down[0]),
        "ident": ident, "negu": negu, "negl": negl, "bd": bd, "mask": mask,
    }
    in_maps = []
    for c in range(8):
        b, j = c // 4, c % 4
        real = (j + 1) * own
        pad = Sq - real
        xl = np.zeros((Sq, D), np.float32)
        xl[pad:] = x[b, :real]
        valid = np.zeros((Sq,), np.float32)
        valid[pad:] = 1.0
        m = dict(shared)
        m["x"] = xl
        m["valid"] = f(valid.reshape(Sq // 128, 128).T)
        in_maps.append(m)
    res = run_bass_kernel_spmd(nc, in_maps, core_ids=list(range(8)))
    out = np.empty((Bn, Sq, D), np.float32)
    for c in range(8):
        b, j = c // 4, c % 4
        out[b, j * own:(j + 1) * own] = res.results[c]["y"]
    return out
```

```python
import contextlib
import numpy as np
import concourse.bass as bass
import concourse.mybir as mybir
from concourse.bass_utils import run_bass_kernel_spmd

F32 = mybir.dt.float32
BF16 = mybir.dt.bfloat16
AF = mybir.ActivationFunctionType
ALU = mybir.AluOpType

D = 1024
KC = 8
T = 512
DFF = 2816
NFC = 22
EPS = 1e-6
NEG = -30000.0


class Sched:
    def __init__(self, engnames, sems):
        self.engnames = engnames
        self.prog = {k: [] for k in engnames}
        self.stack = {k: [self.prog[k]] for k in engnames}
        self.cond = None
        self.sems = sems
        self.free_sems = [k for k in sems if k.startswith('c')]
        self.alias = {}
        self.cnt = {}
        self.mult = {}
        for k in engnames:
            self.cnt[k] = 0
            self.mult[k] = 1
        self.cnt['flag'] = 0
        self.mult['flag'] = 1
        self.lastw = {}
        self.readers = {}
        self.waited = {}
        self.nops = 0
        self.nwaits = 0

    def chan(self, name):
        if name not in self.alias:
            s = self.free_sems.pop(0)
            self.alias[name] = s
            self.cnt[name] = 0
            self.mult[name] = 16
        return name

    def sem(self, p):
        return self.sems[self.alias.get(p, p)]

    def _deps(self, reads, writes):
        deps = {}

        def add(ps):
            p, s = ps
            if deps.get(p, 0) < s:
                deps[p] = s
        for r in reads:
            if r in self.lastw:
                add(self.lastw[r])
        for w in writes:
            if w in self.lastw:
                add(self.lastw[w])
            for rd in self.readers.get(w, ()):
                add(rd)
        return deps

    def _emit_waits(self, eng, deps, skip_self):
        wd = self.waited.setdefault(eng, {})
        for p, s in deps.items():
            if p == eng and skip_self:
                continue
            if wd.get(p, 0) >= s:
                continue
            self.stack[eng][-1].append(('w', self.sem(p), s * self.mult[p]))
            wd[p] = s
            self.nwaits += 1

    def _record(self, prod, seq, reads, writes):
        for r in reads:
            self.readers.setdefault(r, []).append((prod, seq))
        for w in writes:
            self.lastw[w] = (prod, seq)
            self.readers[w] = []

    def op(self, eng, meth, args, kw, reads=(), writes=()):
        deps = self._deps(reads, writes)
        self._emit_waits(eng, deps, skip_self=(eng == 'pe'))
        self.stack[eng][-1].append(('o', (meth, args, kw), self.sems[eng], 1))
        self.cnt[eng] += 1
        self._record(eng, self.cnt[eng], reads, writes)
        self.nops += 1

    def dma(self, ch, out, in_, reads=(), writes=(), q='sp'):
        self.chan(ch)
        deps = self._deps(reads, writes)
        self._emit_waits(q, deps, skip_self=False)
        self.stack[q][-1].append(('o', ('dma_start', (), dict(out=out, in_=in_)), self.sem(ch), 16))
        self.cnt[ch] += 1
        self._record(ch, self.cnt[ch], reads, writes)
        self.nops += 1

    def flag_op(self, meth, args, kw, reads=(), writes=()):
        deps = self._deps(reads, writes)
        self._emit_waits('dve', deps, skip_self=False)
        self.stack['dve'][-1].append(('o', (meth, args, kw), self.sems['flag'], 1))
        self.cnt['flag'] += 1
        self._record('flag', self.cnt['flag'], reads, writes)

    def begin_cond(self, flag_ap, engines):
        import copy
        if self.cond is None:
            self.cond = []
        self.cond.append(dict(flag_ap=flag_ap, engines=engines, seq=self.cnt['flag'],
                              start={e: self.cnt[e] for e in engines}, fstart=self.cnt['flag'],
                              snap=copy.deepcopy(self.waited)))
        for e in engines:
            self.stack[e].append([])

    def end_cond(self):
        c = self.cond.pop()
        nf = self.cnt['flag'] - c['fstart']
        for e in c['engines']:
            body = self.stack[e].pop()
            n = self.cnt[e] - c['start'][e]
            self.stack[e][-1].append(('if', c['flag_ap'], c['seq'], body, c['start'][e], n,
                                      nf if e == 'dve' else 0))
        self.waited = c['snap']

    def barrier(self):
        for eng in self.engnames:
            wd = self.waited.setdefault(eng, {})
            for p, c in self.cnt.items():
                if c > 0 and p != eng and wd.get(p, 0) < c:
                    self.stack[eng][-1].append(('w', self.sem(p), c * self.mult[p]))
                    wd[p] = c
        for eng in self.engnames:
            if eng != 'sp' and self.cnt[eng] > 0:
                self.stack[eng][-1].append(('w', self.sems[eng], self.cnt[eng]))

    def _replay_list(self, eng, e, lst):
        for it in lst:
            if it[0] == 'w':
                e.wait_ge(it[1], it[2])
            elif it[0] == 'o':
                meth, args, kw = it[1]
                getattr(e, meth)(*args, **kw).then_inc(it[2], it[3])
            else:
                _, flag_ap, seq, body, start, n, dve_else = it
                e.wait_ge(self.sems['flag'], seq)
                reg = self.regs[eng]
                e.reg_load(reg, flag_ap)
                with e.If_ne(reg, 0):
                    self._replay_list(eng, e, body)
                with e.Else():
                    if start > 0:
                        e.wait_ge(self.sems[eng], start)
                    if n > 0:
                        e.sem_inc(self.sems[eng], n)
                    if dve_else:
                        e.sem_inc(self.sems['flag'], dve_else)

    def replay(self, eng, e):
        assert len(self.stack[eng]) == 1
        self._replay_list(eng, e, self.prog[eng])
        self.prog[eng] = []
        self.stack[eng] = [self.prog[eng]]


def build(NT, NO):
    SL = NT * T
    NKB = SL // 128
    NQ = NO + 1
    T0 = NT - NO - 1
    nc = bass.Bass("TRN2", target_bir_lowering=False)

    def din(name, shape):
        return nc.dram_tensor(name, shape, F32, kind="ExternalInput").ap()
    x_d = din("x", [SL, D])
    valid_d = din("valid", [128, NKB])
    g1_d = din("g1", [128, D])
    g2_d = din("g2", [128, D])
    win_d = din("w_in", [D, 5120])
    gq_d = din("gq", [128, 1])
    gk_d = din("gk", [128, 1])
    tb_d = din("tb", [128, 8, 640])
    wa_d = din("w_a", [512, D])
    wb_d = din("w_b", [512, D])
    wo_d = din("w_o", [D, D])
    wup_d = din("w_up", [D, 2 * DFF])
    cw_d = din("cw", [128, 44, 3])
    cb_d = din("cb", [128, 44])
    wdn_d = din("w_dn", [DFF, D])
    ident_d = din("ident", [128, 128])
    negu_d = din("negu", [128, 128])
    negl_d = din("negl", [128, 128])
    bd_d = din("bd", [128, 128])
    mask_d = din("mask", [128, 4, T])
    y_d = nc.dram_tensor("y", [NO * T, D], F32, kind="ExternalOutput").ap()
    kt_d = nc.dram_tensor("kt_scr", [4, 128, SL], BF16).ap()
    v_d = nc.dram_tensor("v_scr", [4, 128, NKB, 128], BF16).ap()
    x1_d = nc.dram_tensor("x1_scr", [NQ * T, D], F32).ap()

    top = contextlib.ExitStack()
    with top:
        engnames = ['pe', 'act', 'dve', 'pool', 'sp']
        sems = {}
        for n in ['pe', 'act', 'dve', 'pool', 'flag']:
            sems[n] = top.enter_context(nc.semaphore(n))
        for i in range(28):
            sems[f'c{i}'] = top.enter_context(nc.semaphore(f'c{i}'))
        S = Sched(engnames, sems)
        S.regs = {'pe': nc.alloc_register(mybir.EngineType.PE, 'flag_pe'),
                  'act': nc.alloc_register(mybir.EngineType.Activation, 'flag_act'),
                  'dve': nc.alloc_register(mybir.EngineType.DVE, 'flag_dve')}

        def run_block():
            blk = nc.Block()
            with blk:
                blk.tensor(lambda e: S.replay('pe', e))
                blk.scalar(lambda e: S.replay('act', e))
                blk.vector(lambda e: S.replay('dve', e))
                blk.gpsimd(lambda e: S.replay('pool', e))
                blk.sync(lambda e: S.replay('sp', e))

        uid = [0]

        def sbt(es, name, shape, dt):
            uid[0] += 1
            return es.enter_context(nc.sbuf_tensor(f"s{uid[0]}_{name}", shape, dt))

        def pst(es, name, shape, dt=F32):
            uid[0] += 1
            return es.enter_context(nc.psum_tensor(f"p{uid[0]}_{name}", shape, dt))

        def MM(out, lhsT, rhs, start, stop, reads, writes):
            S.op('pe', 'matmul', (out,), dict(lhsT=lhsT, rhs=rhs, start=start, stop=stop), reads, writes)

        def ACTV(out, in_, func, reads, writes, **kw):
            S.op('act', 'activation', (), dict(out=out, in_=in_, func=func, **kw), reads, writes)

        def CP(eng, out, in_, reads, writes):
            S.op(eng, 'copy' if eng == 'act' else 'tensor_copy', (), dict(out=out, in_=in_), reads, writes)

        def TT(eng, out, in0, in1, op, reads, writes):
            S.op(eng, 'tensor_tensor', (), dict(out=out, in0=in0, in1=in1, op=op), reads, writes)

        def STT(eng, out, in0, scalar, in1, op0, op1, reads, writes):
            S.op(eng, 'scalar_tensor_tensor', (), dict(out=out, in0=in0, scalar=scalar, in1=in1, op0=op0, op1=op1), reads, writes)

        def TS(eng, out, in0, s1, s2, op0, op1, reads, writes):
            kw = dict(out=out, in0=in0, scalar1=s1, scalar2=s2, op0=op0)
            if op1 is not None:
                kw['op1'] = op1
            S.op(eng, 'tensor_scalar', (), kw, reads, writes)

        def MS(eng, ap, val, writes):
            S.op(eng, 'memset', (ap, val), {}, (), writes)

        ident = sbt(top, "ident", [128, 128], BF16)
        negu = sbt(top, "negu", [128, 128], BF16)
        negl = sbt(top, "negl", [128, 128], BF16)
        bdm = sbt(top, "bdm", [128, 128], BF16)
        zero = sbt(top, "zero", [128, 128], BF16)
        cst = sbt(top, "cst", [128, 4], F32)

        stg_state = {'i': 0}

        def load_weight(stg, dst_ap, src_ap, n, wname):
            sl = stg_state['i'] % 2
            stg_state['i'] += 1
            S.dma(f'stg{sl}', stg[:, sl, 0:n], src_ap, writes=[f'stg{sl}'])
            CP('pool', dst_ap, stg[:, sl, 0:n], [f'stg{sl}'], [wname])

        def norm_a(Bf, src_ap, src_reads, slot, gb):
            xs, junk, ss, hn = Bf['xs'], Bf['junk'], Bf['ss'], Bf['hn']
            xn, hnn, ssn = f'xs{slot}', f'hn{slot}', f'ss{slot}'
            S.dma(xn, xs[:, slot, :], src_ap, reads=src_reads, writes=[xn])
            MS('pool', ss[:, slot, 0:1], 0.0, [ssn])
            ACTV(junk[:, :], xs[:, slot, :], AF.Square, [xn], ['junk', ssn], accum_out=ss[:, slot, 0:1])
            ACTV(ss[:, slot, 1:2], ss[:, slot, 0:1], AF.Ln, [ssn], [ssn], scale=1.0 / D, bias=cst[:, 0:1])
            ACTV(ss[:, slot, 2:3], ss[:, slot, 1:2], AF.Exp, [ssn], [ssn], scale=-0.5)
            STT('dve', hn[:, slot, :], xs[:, slot, :], ss[:, slot, 2:3], gb[:, :], ALU.mult, ALU.mult, [xn, ssn], [hnn])

        def norm_b(Bf, slot, hnT_ap, hnT_name, tp, tp_name, ev='act'):
            hn = Bf['hn']
            for kc in range(KC):
                S.op('pe', 'transpose', (tp[:, kc, :], hn[:, slot, kc * 128:(kc + 1) * 128], ident[:, :]), {}, [f'hn{slot}'], [tp_name])
            CP(ev, hnT_ap, tp[:, :, :], [tp_name], [hnT_name])

        def norm_block(Bf, src_ap, src_reads, slot, gb, hnT_ap, hnT_name, tp, tp_name):
            norm_a(Bf, src_ap, src_reads, slot, gb)
            norm_b(Bf, slot, hnT_ap, hnT_name, tp, tp_name)

        def new_B(es, ns=2):
            return dict(xs=sbt(es, "xs", [128, ns, D], F32), junk=sbt(es, "junk", [128, D], BF16),
                        ss=sbt(es, "ss", [128, ns, 4], F32), hn=sbt(es, "hn", [128, ns, D], BF16))

        sc04 = contextlib.ExitStack()
        with sc04:
            g1b = sbt(sc04, "g1b", [128, D], F32)
            vt = sbt(sc04, "vt", [128, NKB], F32)
            OT = sbt(sc04, "OT", [128, 4, NQ * T], BF16)

            with contextlib.ExitStack() as es:
                i32 = sbt(es, "i32", [128, 4, 128], F32)
                for i, srcd in enumerate([ident_d, negu_d, negl_d, bd_d]):
                    S.dma('cst', i32[:, i, :], srcd[:, :], writes=[f'i32_{i}'])
                S.barrier()
                for i, dst in enumerate([ident, negu, negl, bdm]):
                    CP('pool', dst[:, :], i32[:, i, :], [f'i32_{i}'], [f'const{i}'])
                MS('pool', zero[:, :], 0.0, ['zero'])
                MS('pool', cst[:, 0:1], EPS, ['cst'])
                MS('pool', cst[:, 1:2], 1.0, ['cst'])
                MS('pool', OT[:, :, 0:T], 0.0, ['OTz'])
                S.dma('cst', g1b[:, :], g1_d[:, :], writes=['g1b'])
                S.dma('cst', vt[:, :], valid_d[:, :], writes=['vt'])
                S.barrier()
                run_block()

            sc12 = contextlib.ExitStack()
            with sc12:
                QT = sbt(sc12, "QT", [128, 4, NQ * T], BF16)
                with contextlib.ExitStack() as es:
                    wB = sbt(es, "wB", [128, KC, 1536], BF16)
                    stg = sbt(es, "stg1", [128, 2, 1536], F32)
                    Bf = new_B(es, 4)
                    hnT = sbt(es, "hnT", [128, 2, KC, T], BF16)
                    KTs = sbt(es, "KTs", [128, 2, 4, T], BF16)
                    Vs = sbt(es, "Vs", [128, 2, 4, T], BF16)
                    tps = [pst(es, f"tp{i}", [128, KC, 128], BF16) for i in range(2)]
                    accs = [pst(es, f"acc{i}", [128, T], F32) for i in range(4)]
                    for kc in range(KC):
                        load_weight(stg, wB[:, kc, :], win_d[kc * 128:(kc + 1) * 128, 1536:3072], 1536, 'wB')
                    acc_i = 0
                    ev_i = 0

                    def nA(t, bi):
                        blk_i = t * 4 + bi
                        norm_a(Bf, x_d[blk_i * 128:(blk_i + 1) * 128, :], [], blk_i % 4, g1b)

                    def nB(t, bi):
                        blk_i = t * 4 + bi
                        hs_ = t % 2
                        norm_b(Bf, blk_i % 4, hnT[:, hs_, :, bi * 128:(bi + 1) * 128], f'hnT{hs_}_{bi}', tps[blk_i % 2], f'tp{blk_i % 2}',
                               ev=('act' if bi % 2 == 0 else 'dve'))

                    def grpK(t, p):
                        nonlocal acc_i, ev_i
                        hs = t % 2
                        hread = [f'hnT{hs}_{bi}' for bi in range(4)]
                        a = acc_i % 4
                        acc_i += 1
                        for kc in range(KC):
                            MM(accs[a][:, :], wB[:, kc, 512 + p * 128:512 + (p + 1) * 128], hnT[:, hs, kc, :],
                               kc == 0, kc == KC - 1, ['wB'] + hread, [f'acc{a}'])
                        CP('dve' if ev_i % 2 == 0 else 'act', KTs[:, hs, p, :], accs[a][:, :], [f'acc{a}'], [f'KTs{hs}'])
                        ev_i += 1
                        if p == 3:
                            S.dma(f'kts{hs}', kt_d[:, :, t * T:(t + 1) * T].rearrange("q p c -> p q c"), KTs[:, hs, :, :],
                                  reads=[f'KTs{hs}'], writes=['kt_d'])

                    def grpV(t, bi):
                        nonlocal acc_i, ev_i
                        hs = t % 2
                        a = acc_i % 4
                        acc_i += 1
                        for kc in range(KC):
                            MM(accs[a][:, :], hnT[:, hs, kc, bi * 128:(bi + 1) * 128], wB[:, kc, 1024:1536],
                               kc == 0, kc == KC - 1, ['wB', f'hnT{hs}_{bi}'], [f'acc{a}'])
                        CP('dve' if ev_i % 2 == 0 else 'act', Vs[:, hs, bi, :], accs[a][:, :], [f'acc{a}'], [f'Vs{hs}'])
                        ev_i += 1
                        if bi == 3:
                            for q in range(4):
                                S.dma(f'vs{hs}', v_d[q, :, t * 4:(t + 1) * 4, :], Vs[:, hs, :, q * 128:(q + 1) * 128],
                                      reads=[f'Vs{hs}'], writes=['v_d'])

                    def grpQ(t, p):
                        nonlocal acc_i, ev_i
                        hs = t % 2
                        qi = t - T0
                        hread = [f'hnT{hs}_{bi}' for bi in range(4)]
                        a = acc_i % 4
                        acc_i += 1
                        for kc in range(KC):
                            MM(accs[a][:, :], wB[:, kc, p * 128:(p + 1) * 128], hnT[:, hs, kc, :],
                               kc == 0, kc == KC - 1, ['wB'] + hread, [f'acc{a}'])
                        CP('dve' if ev_i % 2 == 0 else 'act', QT[:, p, qi * T:(qi + 1) * T], accs[a][:, :], [f'acc{a}'], [f'QT{p}_{qi}'])
                        ev_i += 1

                    for bi in range(4):
                        nA(0, bi)
                        nB(0, bi)
                    for t in range(NT):
                        nx = t + 1 < NT
                        if nx:
                            nA(t + 1, 0)
                            nA(t + 1, 1)
                        grpK(t, 0)
                        if nx:
                            nB(t + 1, 0)
                        grpK(t, 1)
                        if nx:
                            nA(t + 1, 2)
                        grpK(t, 2)
                        if nx:
                            nB(t + 1, 1)
                        grpK(t, 3)
                        if nx:
                            nA(t + 1, 3)
                        grpV(t, 0)
                        if nx:
                            nB(t + 1, 2)
                        grpV(t, 1)
                        grpV(t, 2)
                        if nx:
                            nB(t + 1, 3)
                        grpV(t, 3)
                        if t >= T0:
                            for p in range(4):
                                grpQ(t, p)
                    S.barrier()
                    run_block()

                with contextlib.ExitStack() as es:
                    KTp = sbt(es, "KTp", [128, SL], BF16)
                    Vp = sbt(es, "Vp", [128, NKB, 128], BF16)
                    M = sbt(es, "M", [128, 4, 2, T], BF16)
                    NE, NSP, NXW, NW = 4, 3, 2, 2
                    eb = sbt(es, "eb", [128, NE, 2, T], F32)
                    spb = sbt(es, "spb", [128, NSP, 2, T], BF16)
                    xwb = sbt(es, "xwb", [128, NXW, 2, T], F32)
                    wbuf = sbt(es, "wbuf", [128, NW, 2, T], BF16)
                    m32 = sbt(es, "m32", [128, 4, T], F32)
                    Zp = [pst(es, f"Z{i}", [128, 2, T]) for i in range(2)]
                    Ap = pst(es, "A", [128, 2, T])
                    Op = pst(es, "O", [128, T])
                    S.dma('cst', m32[:, :, :], mask_d[:, :, :], writes=['m32'])
                    for h in range(2):
                        CP('pool', M[:, :, h, :], m32[:, :, :], ['m32'], ['M'])
                    I32 = mybir.dt.int32
                    flagbuf = sbt(es, "flagbuf", [128, 512], I32)
                    mx = sbt(es, "mx", [128, 4], F32)
                    THRESH = 150.0
                    CH = 32
                    nchk = (NKB + CH - 1) // CH
                    itn = [0]
                    fidx = [0]

                    def load_pair(p):
                        for c in reversed(range(nchk)):
                            k0, k1 = c * CH, min(NKB, (c + 1) * CH)
                            S.dma(f'ktp{c}', KTp[:, k0 * 128:k1 * 128], kt_d[p, :, k0 * 128:k1 * 128], reads=['kt_d'], writes=[f'KTp{c}'])
                            S.dma(f'vp{c}', Vp[:, k0:k1, :], v_d[p, :, k0:k1, :], reads=['v_d'], writes=[f'Vp{c}'])

                    def st1(it):
                        W, kb, p, qi, c0, n = it['W'], it['kb'], it['p'], it['qi'], it['c0'], it['n']
                        for h in range(2):
                            MM(Zp[n % 2][:, h, 0:W], KTp[64 * h:64 * h + 64, kb * 128:(kb + 1) * 128],
                               QT[64 * h:64 * h + 64, p, qi * T + c0:qi * T + c0 + W], True, True,
                               [f'KTp{kb // CH}', f'QT{p}_{qi}'], [f'Z{n % 2}'])

                    def st2(it):
                        W, n = it['W'], it['n']
                        en = f'e{n % NE}'
                        ACTV(eb[:, n % NE, :, 0:W], Zp[n % 2][:, :, 0:W], AF.Exp, [f'Z{n % 2}'], [en], scale=0.125)
                        if it['d'] is not None:
                            TT('dve', eb[:, n % NE, :, 0:W], eb[:, n % NE, :, 0:W], M[:, it['d'], :, 0:W], ALU.mult, [en, 'M'], [en])

                    def st3(it):
                        W, n = it['W'], it['n']
                        ACTV(spb[:, n % NSP, :, 0:W], eb[:, n % NE, :, 0:W], AF.Ln, [f'e{n % NE}'], [f'sp{n % NSP}'], bias=cst[:, 1:2])

                    def st4(it):
                        W, n = it['W'], it['n']
                        for h in range(2):
                            MM(Ap[:, h, 0:W], negu[:, :], spb[:, n % NSP, h, 0:W], it['first'], False,
                               [f'sp{n % NSP}', 'const1'], ['A'])

                    def st5(it):
                        W, n = it['W'], it['n']
                        ACTV(xwb[:, n % NXW, :, 0:W], Ap[:, :, 0:W], AF.Exp, ['A'], [f'xw{n % NXW}'])

                    def st6(it):
                        W, n = it['W'], it['n']
                        if it['last']:
                            return
                        for h in range(2):
                            MM(Ap[:, h, 0:W], negl[:, :], spb[:, n % NSP, h, 0:W], False, False,
                               [f'sp{n % NSP}', 'const2'], ['A'])

                    def st7(it):
                        W, n = it['W'], it['n']
                        TT('dve', wbuf[:, n % NW, :, 0:W], eb[:, n % NE, :, 0:W], xwb[:, n % NXW, :, 0:W], ALU.mult,
                           [f'e{n % NE}', f'xw{n % NXW}'], [f'w{n % NW}'])

                    def st8(it):
                        W, n, kb = it['W'], it['n'], it['kb']
                        for h in range(2):
                            MM(Op[64 * h:64 * h + 64, 0:W], Vp[:, kb, 64 * h:64 * h + 64], wbuf[:, n % NW, h, 0:W],
                               it['first'], it['last'], [f'w{n % NW}', f'Vp{kb // CH}'], ['O'])

                    stages = [(st1, 0), (st2, 1), (st3, 2), (st6, 4), (st4, 3), (st5, 3), (st7, 4), (st8, 5)]

                    def emit_segment(items):
                        nit = len(items)
                        for s_ in range(nit + 7):
                            for fn, dly in stages:
                                i_ = s_ - dly
                                if 0 <= i_ < nit:
                                    fn(items[i_])

                    def emit_flag(W):
                        fi = fidx[0]
                        fidx[0] += 1
                        for h in range(2):
                            S.op('dve', 'tensor_reduce', (), dict(out=mx[0:1, h:h + 1], in_=Ap[0:1, h, 0:W], axis=mybir.AxisListType.X, op=ALU.max),
                                 ['A'], [f'mx{h}'])
                        TT('dve', mx[0:1, 2:3], mx[0:1, 0:1], mx[0:1, 1:2], ALU.max, ['mx0', 'mx1'], ['mx2'])
                        S.flag_op('tensor_scalar', (), dict(out=flagbuf[0:1, fi:fi + 1], in0=mx[0:1, 2:3], scalar1=-THRESH, scalar2=None, op0=ALU.is_gt),
                                  ['mx2'], [f'flag{fi}'])
                        return fi

                    for p in range(4):
                        load_pair(p)
                        for qi in range(NQ):
                            g = T0 + qi
                            if qi == 0:
                                W, c0, q0, ndiag = 128, 384, g * T + 384, 1
                            else:
                                W, c0, q0, ndiag = T, 0, g * T, 4
                            kb_hi = (q0 + W) // 128 - 1
                            nb = kb_hi + 1
                            segs = [list(range(0, min(nb, ndiag + 3)))]
                            sz = 2
                            while segs[-1][-1] + 1 < nb:
                                st_ = segs[-1][-1] + 1
                                segs.append(list(range(st_, min(nb, st_ + sz))))
                                if len(segs) > 2:
                                    sz *= 2
                            nopen = 0
                            for si, seg in enumerate(segs):
                                items = []
                                for b in seg:
                                    kb = kb_hi - b
                                    d = kb - q0 // 128
                                    items.append(dict(p=p, qi=qi, W=W, c0=c0, kb=kb, d=(d if d >= 0 else None),
                                                      first=(b == 0), last=(b == nb - 1), n=itn[0]))
                                    itn[0] += 1
                                emit_segment(items)
                                if si < len(segs) - 1:
                                    fi = emit_flag(W)
                                    S.begin_cond(flagbuf[0:1, fi:fi + 1], ['pe', 'act', 'dve'])
                                    nopen += 1
                            for _ in range(nopen):
                                S.end_cond()
                            CP('dve', OT[:, p, qi * T + c0:qi * T + c0 + W], Op[:, 0:W], ['O', 'OTz'], [f'OT{p}_{qi}'])
                    S.barrier()
                    run_block()

            sc34 = contextlib.ExitStack()
            with sc34:
                OAT = sbt(sc34, "OAT", [128, 4, NQ * T], BF16)
                with contextlib.ExitStack() as es:
                    wA = sbt(es, "wA", [128, KC, 1536], BF16)
                    with contextlib.ExitStack() as es2:
                        stg = sbt(es2, "stg3", [128, 2, 1536], F32)
                        for kc in range(KC):
                            load_weight(stg, wA[:, kc, :], win_d[kc * 128:(kc + 1) * 128, 0:1536], 1536, 'wA')
                        S.barrier()
                    TB = sbt(es, "TB", [128, 8, 640], F32)
                    gq = sbt(es, "gq", [128, 1], F32)
                    gk = sbt(es, "gk", [128, 1], F32)
                    Bf = new_B(es)
                    hnT = sbt(es, "hnT", [128, KC, T], BF16)
                    QAT = sbt(es, "QAT", [128, 4, T], BF16)
                    KAT = sbt(es, "KAT", [128, 4, 2, T], BF16)
                    VA = sbt(es, "VA", [128, 2, 4, T], BF16)
                    VLD = sbt(es, "VLD", [128, 2, 4, 64], BF16)
                    sqb = sbt(es, "sqb", [128, 2, T], BF16)
                    qgb = sbt(es, "qgb", [128, 2, T], F32)
                    lnb = sbt(es, "lnb", [128, 2, T], F32)
                    sbb = sbt(es, "sbb", [128, 2, 2, T], F32)
                    pTb = sbt(es, "pTb", [128, 2, 2, T], BF16)
                    rdb = sbt(es, "rdb", [128, T], F32)
                    tp3 = pst(es, "tp3", [128, KC, 128], BF16)
                    acc3t = pst(es, "acc3", [128, 2, T])
                    acc3 = [acc3t[:, i, :] for i in range(2)]
                    SSp = pst(es, "SSp", [128, T])
                    SPSt = pst(es, "SPS", [128, 2, T])
                    sbufs = [(SPSt, 'SPS'), (acc3t, 'acc3_')]
                    OAp = pst(es, "OAp", [128, T])
                    DENp = pst(es, "DENp", [128, T])
                    S.dma('cst', TB[:, :, :], tb_d[:, :, :], writes=['TB'])
                    S.dma('cst', gq[:, :], gq_d[:, :], writes=['gq'])
                    S.dma('cst', gk[:, :], gk_d[:, :], writes=['gk'])
                    S.barrier()
                    blkc = 0
                    acc_i = 0
                    nrm = [0]
                    jj = [0]

                    def qknorm(a, gcol, gname, dst_ap, dst_name):
                        k = nrm[0] % 2
                        nrm[0] += 1
                        ACTV(sqb[:, k, :], acc3[a][:, :], AF.Square, [f'acc3_{a}'], [f'sqb{k}'])
                        S.op('act', 'mul', (), dict(out=qgb[:, k, :], in_=acc3[a][:, :], mul=gcol[:, 0:1]), [f'acc3_{a}', gname], [f'qgb{k}'])
                        MM(SSp[:, :], bdm[:, :], sqb[:, k, :], True, True, [f'sqb{k}', 'const3'], ['SSp'])
                        ACTV(lnb[:, k, :], SSp[:, :], AF.Ln, ['SSp'], [f'lnb{k}'], scale=1.0 / 64, bias=cst[:, 0:1])
                        ACTV(lnb[:, k, :], lnb[:, k, :], AF.Exp, [f'lnb{k}'], [f'lnb{k}'], scale=-0.5)
                        TT('dve', dst_ap, qgb[:, k, :], lnb[:, k, :], ALU.mult, [f'qgb{k}', f'lnb{k}'], [dst_name])

                    for t in range(T0 - 1, NT):
                        qi = t - T0
                        sl = t % 2
                        for bi in range(4):
                            blk_i = t * 4 + bi
                            norm_block(Bf, x_d[blk_i * 128:(blk_i + 1) * 128, :], [], blkc % 2, g1b,
                                       hnT[:, :, bi * 128:(bi + 1) * 128], f'hnT_{bi}', tp3, 'tp3')
                            blkc += 1
                        hread = [f'hnT_{bi}' for bi in range(4)]
                        for p in range(4):
                            a = acc_i % 2
                            acc_i += 1
                            for kc in range(KC):
                                MM(acc3[a][:, :], wA[:, kc, 512 + p * 128:512 + (p + 1) * 128], hnT[:, kc, :],
                                   kc == 0, kc == KC - 1, ['wA'] + hread, [f'acc3_{a}'])
                            qknorm(a, gk, 'gk', KAT[:, p, sl, :], f'KAT{sl}_{p}')
                        for bi in range(4):
                            a = acc_i % 2
                            acc_i += 1
                            for kc in range(KC):
                                MM(acc3[a][:, :], hnT[:, kc, bi * 128:(bi + 1) * 128], wA[:, kc, 1024:1536],
                                   kc == 0, kc == KC - 1, ['wA', f'hnT_{bi}'], [f'acc3_{a}'])
                            CP('dve', VA[:, sl, bi, :], acc3[a][:, :], [f'acc3_{a}'], [f'VA{sl}_{bi}'])
                            CP('pool', VLD[:, sl, bi, :], vt[:, t * 4 + bi:t * 4 + bi + 1].to_broadcast([128, 64]),
                               ['vt'], [f'VLD{sl}_{bi}'])
                        if qi < 0:
                            continue
                        for p in range(4):
                            a = acc_i % 2
                            acc_i += 1
                            for kc in range(KC):
                                MM(acc3[a][:, :], wA[:, kc, p * 128:(p + 1) * 128], hnT[:, kc, :],
                                   kc == 0, kc == KC - 1, ['wA'] + hread, [f'acc3_{a}'])
                            qknorm(a, gq, 'gq', QAT[:, p, :], f'QAT{p}')
                        for p in range(4):
                            MM(OAp[:, :], zero[:, :], hnT[:, 0, :], True, False, ['zero'] + hread, ['OAp'])
                            MM(DENp[:, :], zero[:, :], hnT[:, 0, :], True, False, ['zero'] + hread, ['DENp'])
                            for j in range(8):
                                i_lo, i_hi = max(0, j - 4), min(3, j)
                                N = (i_hi - i_lo + 1) * 128
                                ksl = (1 - sl) if j < 4 else sl
                                cj = j % 4
                                tb0 = (4 - j + i_lo) * 128
                                k = jj[0] % 2
                                jj[0] += 1
                                spt, spn = sbufs[k]
                                for h in range(2):
                                    MM(spt[:, h, 0:N], KAT[64 * h:64 * h + 64, p, ksl, cj * 128:(cj + 1) * 128],
                                       QAT[64 * h:64 * h + 64, p, i_lo * 128:i_lo * 128 + N], True, True,
                                       [f'KAT{ksl}_{p}', f'QAT{p}'], [f'{spn}{h}'])
                                STT('dve', sbb[:, k, :, 0:N], spt[:, :, 0:N], 0.125, TB[:, 2 * p:2 * p + 2, tb0:tb0 + N], ALU.mult, ALU.add,
                                    [f'{spn}0', f'{spn}1', 'TB'], [f'sbb{k}'])
                                ACTV(pTb[:, k, :, 0:N], sbb[:, k, :, 0:N], AF.Exp, [f'sbb{k}'], [f'pTb{k}'])
                                for h in range(2):
                                    hh = 2 * p + h
                                    MM(OAp[64 * h:64 * h + 64, i_lo * 128:i_lo * 128 + N], VA[:, ksl, cj, hh * 64:(hh + 1) * 64],
                                       pTb[:, k, h, 0:N], False, False, [f'pTb{k}', f'VA{ksl}_{cj}'], ['OAp'])
                                    MM(DENp[64 * h:64 * h + 64, i_lo * 128:i_lo * 128 + N], VLD[:, ksl, cj, :],
                                       pTb[:, k, h, 0:N], False, False, [f'pTb{k}', f'VLD{ksl}_{cj}'], ['DENp'])
                            TS('dve', rdb[:, :], DENp[:, :], 1e-30, None, ALU.max, None, ['DENp'], ['rdb'])
                            S.op('dve', 'reciprocal', (), dict(out=rdb[:, :], in_=rdb[:, :]), ['rdb'], ['rdb'])
                            TT('dve', OAT[:, p, qi * T:(qi + 1) * T], OAp[:, :], rdb[:, :], ALU.mult, ['OAp', 'rdb'], [f'OAT{p}_{qi}'])
                    S.barrier()
                    run_block()

                with contextlib.ExitStack() as es:
                    wG = sbt(es, "wG", [128, KC, 2048], BF16)
                    wbra = sbt(es, "wbra", [128, 4, D], BF16)
                    wbrb = sbt(es, "wbrb", [128, 4, D], BF16)
                    wout = sbt(es, "wout", [128, KC, D], BF16)
                    stg = sbt(es, "stg4", [128, 2, D], F32)
                    Bf = new_B(es)
                    xr = sbt(es, "xr", [128, 2, D], F32)
                    hnT = sbt(es, "hnT", [128, KC, T], BF16)
                    sg = sbt(es, "sg", [128, 2, T], F32)
                    mm = sbt(es, "mm", [128, 2, T], F32)
                    MT = sbt(es, "MT", [128, KC, T], BF16)
                    tp4 = pst(es, "tp4", [128, KC, 128], BF16)
                    Gp = [pst(es, f"G{i}", [128, T]) for i in range(2)]
                    Yp = [pst(es, f"Y{i}", [128, T]) for i in range(2)]
                    Xp = [pst(es, f"X{i}", [128, T]) for i in range(2)]
                    for kc in range(KC):
                        for hf in range(2):
                            load_weight(stg, wG[:, kc, hf * D:(hf + 1) * D], win_d[kc * 128:(kc + 1) * 128, 3072 + hf * D:3072 + (hf + 1) * D], D, 'wG')
                    for p in range(4):
                        load_weight(stg, wbra[:, p, :], wa_d[p * 128:(p + 1) * 128, :], D, 'wbra')
                        load_weight(stg, wbrb[:, p, :], wb_d[p * 128:(p + 1) * 128, :], D, 'wbrb')
                    for kc in range(KC):
                        load_weight(stg, wout[:, kc, :], wo_d[kc * 128:(kc + 1) * 128, :], D, 'wout')
                    blkc = 0
                    ob = 0
                    for qi in range(NQ):
                        t = T0 + qi
                        for bi in range(4):
                            blk_i = t * 4 + bi
                            norm_block(Bf, x_d[blk_i * 128:(blk_i + 1) * 128, :], [], blkc % 2, g1b,
                                       hnT[:, :, bi * 128:(bi + 1) * 128], f'hnT_{bi}', tp4, 'tp4')
                            blkc += 1
                        hread = [f'hnT_{bi}' for bi in range(4)]
                        for oc in range(KC):
                            for br in range(2):
                                for kc in range(KC):
                                    MM(Gp[br][:, :], wG[:, kc, br * D + oc * 128:br * D + (oc + 1) * 128], hnT[:, kc, :],
                                       kc == 0, kc == KC - 1, ['wG'] + hread, [f'G{br}'])
                                ACTV(sg[:, br, :], Gp[br][:, :], AF.Sigmoid, [f'G{br}'], [f'sg{br}'])
                                wsrc = wbra if br == 0 else wbrb
                                osrc = OAT if br == 0 else OT
                                for p in range(4):
                                    MM(Yp[br][:, :], wsrc[:, p, oc * 128:(oc + 1) * 128], osrc[:, p, qi * T:(qi + 1) * T],
                                       p == 0, p == 3,
                                       ['wbra' if br == 0 else 'wbrb', (f'OAT{p}_{qi}' if br == 0 else f'OT{p}_{qi}'), 'OTz'], [f'Y{br}'])
                                TT('dve', mm[:, br, :], Yp[br][:, :], sg[:, br, :], ALU.mult, [f'Y{br}', f'sg{br}'], [f'mm{br}'])
                            TT('pool', MT[:, oc, :], mm[:, 0, :], mm[:, 1, :], ALU.add, ['mm0', 'mm1'], [f'MT{oc}'])
                        mread = [f'MT{oc}' for oc in range(KC)]
                        for bi in range(4):
                            blk_i = t * 4 + bi
                            o = ob % 2
                            ob += 1
                            S.dma(f'xr{o}', xr[:, o, :], x_d[blk_i * 128:(blk_i + 1) * 128, :], writes=[f'xr{o}'])
                            for half in range(2):
                                for oc in range(KC):
                                    MM(Xp[half][:, :], MT[:, oc, bi * 128:(bi + 1) * 128], wout[:, oc, half * T:(half + 1) * T],
                                       oc == 0, oc == KC - 1, ['wout'] + mread, [f'X{half}'])
                                TT('dve', xr[:, o, half * T:(half + 1) * T], Xp[half][:, :], xr[:, o, half * T:(half + 1) * T], ALU.add,
                                   [f'X{half}', f'xr{o}'], [f'xr{o}'])
                            row = qi * T + bi * 128
                            S.dma(f'x1w{o}', x1_d[row:row + 128, :], xr[:, o, :], reads=[f'xr{o}'], writes=['x1_d'])
                    S.barrier()
                    run_block()

        with contextlib.ExitStack() as es:
            wup = sbt(es, "wup", [128, KC, 2 * DFF], BF16)
            wdn = sbt(es, "wdn", [128, NFC, D], BF16)
            g2b = sbt(es, "g2b", [128, D], F32)
            cw = sbt(es, "cw", [128, 44, 3], F32)
            cb = sbt(es, "cb", [128, 44], F32)
            HALO = sbt(es, "HALO", [128, 44, 2], F32)
            with contextlib.ExitStack() as es2:
                stg = sbt(es2, "stg5", [128, 2, 2816], F32)
                for kc in range(KC):
                    for hf in range(2):
                        load_weight(stg, wup[:, kc, hf * DFF:(hf + 1) * DFF], wup_d[kc * 128:(kc + 1) * 128, hf * DFF:(hf + 1) * DFF], DFF, 'wup')
                for fc in range(NFC):
                    load_weight(stg, wdn[:, fc, :], wdn_d[fc * 128:(fc + 1) * 128, :], D, 'wdn')
                S.dma('cst', g2b[:, :], g2_d[:, :], writes=['g2b'])
                S.dma('cst', cw[:, :, :], cw_d[:, :, :], writes=['cw'])
                S.dma('cst', cb[:, :], cb_d[:, :], writes=['cb'])
                MS('pool', HALO[:, :, :], 0.0, ['HALO'])
                S.barrier()
            Bf = dict(xs=sbt(es, "xs", [128, 2, D], F32), junk=sbt(es, "junk", [128, D], BF16),
                      ss=sbt(es, "ss", [128, 2, 4], F32), hn=sbt(es, "hn", [128, 2, D], BF16))
            hnT = sbt(es, "hnT", [128, KC, T], BF16)
            hb = sbt(es, "hb", [128, 2, 2, T + 2], F32)
            cv = sbt(es, "cv", [128, 2, T], F32)
            sgl = sbt(es, "sgl", [128, T], F32)
            ACTT = sbt(es, "ACTT", [128, NFC, T], BF16)
            yo = sbt(es, "yo", [128, D], F32)
            tp5 = pst(es, "tp5", [128, KC, 128], BF16)
            Hp = [[pst(es, f"H{b}_{u}", [128, T]) for u in range(2)] for b in range(2)]
            Yd = [pst(es, f"Yd{i}", [128, T]) for i in range(2)]
            blkc = 0
            fcc = 0
            for qi in range(NQ):
                for bi in range(4):
                    row = qi * T + bi * 128
                    norm_block(Bf, x1_d[row:row + 128, :], ['x1_d'], blkc % 2, g2b,
                               hnT[:, :, bi * 128:(bi + 1) * 128], f'hnT_{bi}', tp5, 'tp5')
                    blkc += 1
                hread = [f'hnT_{bi}' for bi in range(4)]
                for fc in range(NFC):
                    b = fcc % 2
                    fcc += 1
                    for u in range(2):
                        ch = fc + u * NFC
                        hbn = f'hb{b}_{u}'
                        for kc in range(KC):
                            MM(Hp[b][u][:, :], wup[:, kc, ch * 128:(ch + 1) * 128], hnT[:, kc, :],
                               kc == 0, kc == KC - 1, ['wup'] + hread, [f'H{b}_{u}'])
                        CP('pool', hb[:, b, u, 0:2], HALO[:, ch, :], [f'HALO{ch}'], [hbn])
                        CP('act', hb[:, b, u, 2:T + 2], Hp[b][u][:, :], [f'H{b}_{u}'], [hbn])
                        CP('pool', HALO[:, ch, :], hb[:, b, u, T:T + 2], [hbn], [f'HALO{ch}'])
                        if qi == 0:
                            continue
                        ce = 'dve'
                        cvn = f'cv{u}'
                        TS(ce, cv[:, u, :], hb[:, b, u, 2:T + 2], cw[:, ch, 2:3], cb[:, ch:ch + 1], ALU.mult, ALU.add, [hbn, 'cw', 'cb'], [cvn])
                        STT(ce, cv[:, u, :], hb[:, b, u, 1:T + 1], cw[:, ch, 1:2], cv[:, u, :], ALU.mult, ALU.add, [hbn, cvn, 'cw'], [cvn])
                        STT(ce, cv[:, u, :], hb[:, b, u, 0:T], cw[:, ch, 0:1], cv[:, u, :], ALU.mult, ALU.add, [hbn, cvn, 'cw'], [cvn])
                    if qi == 0:
                        continue
                    ACTV(sgl[:, :], cv[:, 0, :], AF.Silu, ['cv0'], ['sgl'])
                    TT('dve', ACTT[:, fc, :], sgl[:, :], cv[:, 1, :], ALU.mult, ['sgl', 'cv1'], [f'ACTT{fc}'])
                if qi == 0:
                    continue
                aread = [f'ACTT{fc}' for fc in range(NFC)]
                for bi in range(4):
                    row = qi * T + bi * 128
                    S.dma('yo_in', yo[:, :], x1_d[row:row + 128, :], reads=['x1_d', 'y_d'], writes=['yo'])
                    for half in range(2):
                        for fc in range(NFC):
                            MM(Yd[half][:, :], ACTT[:, fc, bi * 128:(bi + 1) * 128], wdn[:, fc, half * T:(half + 1) * T],
                               fc == 0, fc == NFC - 1, ['wdn'] + aread, [f'Yd{half}'])
                        TT('dve', yo[:, half * T:(half + 1) * T], Yd[half][:, :], yo[:, half * T:(half + 1) * T], ALU.add,
                           [f'Yd{half}', 'yo'], ['yo'])
                    orow = (qi - 1) * T + bi * 128
                    S.dma('yw', y_d[orow:orow + 128, :], yo[:, :], reads=['yo'], writes=['y_d'])
            S.barrier()
            run_block()
        print("megakernel ops", S.nops, "waits", S.nwaits, {k: v for k, v in S.cnt.items() if k in engnames})
    return nc


_CACHE = {}


def _consts():
    ident = np.eye(128, dtype=np.float32)
    kk = np.arange(128)
    negu = -(kk[:, None] >= kk[None, :]).astype(np.float32)
    negl = -(kk[:, None] < kk[None, :]).astype(np.float32)
    bd = np.zeros((128, 128), np.float32)
    bd[:64, :64] = 1.0
    bd[64:, 64:] = 1.0
    qq = np.arange(T)
    mask = np.zeros((128, 4, T), np.float32)
    for d in range(4):
        mask[:, d, :] = ((128 * d + kk[:, None]) < qq[None, :]).astype(np.float32)
    return ident, negu, negl, bd, mask


def _tb_table(rel_bias):
    kk = np.arange(128)[:, None]
    qq = np.arange(128)[None, :]
    tb = np.empty((128, 8, 640), np.float32)
    for rp in range(5):
        idx = np.clip(qq - kk + rp * 128, -128, 128) + 128
        blk = rel_bias[:, idx]
        vis = np.ones((128, 128), bool)
        if rp == 0:
            vis = (kk < 64) | (qq >= 64)
        if rp == 4:
            vis = (kk >= 64) | (qq < 64)
        blk = np.where(vis[None], blk, np.float32(NEG))
        tb[:, :, rp * 128:(rp + 1) * 128] = blk.transpose(1, 0, 2)
    return tb


def kernel(x, norm1_g, w_in, q_norm_g, k_norm_g, rel_bias, w_branch_a, w_branch_b, w_out, norm2_g,
           w_ffn_up, ffn_conv_w, ffn_conv_b, w_ffn_down):
    x = np.asarray(x, np.float32)
    Bn, Sq, Dm = x.shape
    assert Bn == 2 and Dm == D and Sq % 2048 == 0
    NO = Sq // 2048
    NT = Sq // T
    key = (NT, NO)
    if key not in _CACHE:
        _CACHE[key] = build(NT, NO)
    nc = _CACHE[key]
    f = lambda a: np.ascontiguousarray(np.asarray(a, np.float32))
    ident, negu, negl, bd, mask = _consts()
    own = NO * T
    shared = {
        "g1": f(np.broadcast_to(np.asarray(norm1_g, np.float32)[0][None, :], (128, D))),
        "g2": f(np.broadcast_to(np.asarray(norm2_g, np.float32)[0][None, :], (128, D))),
        "w_in": f(w_in[0]),
        "gq": f(np.tile(np.asarray(q_norm_g, np.float32)[0], 2)[:, None]),
        "gk": f(np.tile(np.asarray(k_norm_g, np.float32)[0], 2)[:, None]),
        "tb": f(_tb_table(np.asarray(rel_bias, np.float32)[0])),
        "w_a": f(w_branch_a[0]), "w_b": f(w_branch_b[0]), "w_o": f(w_out[0]),
        "w_up": f(w_ffn_up[0]),
        "cw": f(np.asarray(ffn_conv_w, np.float32)[0].reshape(3, 44, 128).transpose(2, 1, 0)),
        "cb": f(np.asarray(ffn_conv_b, np.float32)[0].reshape(44, 128).T),
        "w_dn": f(w_ffn_down[0]),
        "ident": ident, "negu": negu, "negl": negl, "bd": bd, "mask": mask,
    }
    in_maps = []
    for c in range(8):
        b, j = c // 4, c % 4
        real = (j + 1) * own
        pad = Sq - real
        xl = np.zeros((Sq, D), np.float32)
        xl[pad:] = x[b, :real]
        valid = np.zeros((Sq,), np.float32)
        valid[pad:] = 1.0
        m = dict(shared)
        m["x"] = xl
        m["valid"] = f(valid.reshape(Sq // 128, 128).T)
        in_maps.append(m)
    res = run_bass_kernel_spmd(nc, in_maps, core_ids=list(range(8)))
    out = np.empty((Bn, Sq, D), np.float32)
    for c in range(8):
        b, j = c // 4, c % 4
        out[b, j * own:(j + 1) * own] = res.results[c]["y"]
    return out
```

```python
import contextlib
import numpy as np
import concourse.bass as bass
import concourse.mybir as mybir
from concourse.bass_utils import run_bass_kernel_spmd

F32 = mybir.dt.float32
BF16 = mybir.dt.bfloat16
AF = mybir.ActivationFunctionType
ALU = mybir.AluOpType

D = 1024
KC = 8
T = 512
DFF = 2816
NFC = 22
EPS = 1e-6
NEG = -30000.0


class Sched:
    def __init__(self, engnames, sems):
        self.engnames = engnames
        self.prog = {k: [] for k in engnames}
        self.stack = {k: [self.prog[k]] for k in engnames}
        self.cond = None
        self.sems = sems
        self.free_sems = [k for k in sems if k.startswith('c')]
        self.alias = {}
        self.cnt = {}
        self.mult = {}
        for k in engnames:
            self.cnt[k] = 0
            self.mult[k] = 1
        self.cnt['flag'] = 0
        self.mult['flag'] = 1
        self.lastw = {}
        self.readers = {}
        self.waited = {}
        self.nops = 0
        self.nwaits = 0

    def chan(self, name):
        if name not in self.alias:
            s = self.free_sems.pop(0)
            self.alias[name] = s
            self.cnt[name] = 0
            self.mult[name] = 16
        return name

    def sem(self, p):
        return self.sems[self.alias.get(p, p)]

    def _deps(self, reads, writes):
        deps = {}

        def add(ps):
            p, s = ps
            if deps.get(p, 0) < s:
                deps[p] = s
        for r in reads:
            if r in self.lastw:
                add(self.lastw[r])
        for w in writes:
            if w in self.lastw:
                add(self.lastw[w])
            for rd in self.readers.get(w, ()):
                add(rd)
        return deps

    def _emit_waits(self, eng, deps, skip_self):
        wd = self.waited.setdefault(eng, {})
        for p, s in deps.items():
            if p == eng and skip_self:
                continue
            if wd.get(p, 0) >= s:
                continue
            self.stack[eng][-1].append(('w', self.sem(p), s * self.mult[p]))
            wd[p] = s
            self.nwaits += 1

    def _record(self, prod, seq, reads, writes):
        for r in reads:
            self.readers.setdefault(r, []).append((prod, seq))
        for w in writes:
            self.lastw[w] = (prod, seq)
            self.readers[w] = []

    def op(self, eng, meth, args, kw, reads=(), writes=()):
        deps = self._deps(reads, writes)
        self._emit_waits(eng, deps, skip_self=(eng == 'pe'))
        self.stack[eng][-1].append(('o', (meth, args, kw), self.sems[eng], 1))
        self.cnt[eng] += 1
        self._record(eng, self.cnt[eng], reads, writes)
        self.nops += 1

    def dma(self, ch, out, in_, reads=(), writes=(), q='sp'):
        self.chan(ch)
        deps = self._deps(reads, writes)
        self._emit_waits(q, deps, skip_self=False)
        self.stack[q][-1].append(('o', ('dma_start', (), dict(out=out, in_=in_)), self.sem(ch), 16))
        self.cnt[ch] += 1
        self._record(ch, self.cnt[ch], reads, writes)
        self.nops += 1

    def flag_op(self, meth, args, kw, reads=(), writes=()):
        deps = self._deps(reads, writes)
        self._emit_waits('dve', deps, skip_self=False)
        self.stack['dve'][-1].append(('o', (meth, args, kw), self.sems['flag'], 1))
        self.cnt['flag'] += 1
        self._record('flag', self.cnt['flag'], reads, writes)

    def begin_cond(self, flag_ap, engines):
        import copy
        if self.cond is None:
            self.cond = []
        self.cond.append(dict(flag_ap=flag_ap, engines=engines, seq=self.cnt['flag'],
                              start={e: self.cnt[e] for e in engines}, fstart=self.cnt['flag'],
                              snap=copy.deepcopy(self.waited)))
        for e in engines:
            self.stack[e].append([])

    def end_cond(self):
        c = self.cond.pop()
        nf = self.cnt['flag'] - c['fstart']
        for e in c['engines']:
            body = self.stack[e].pop()
            n = self.cnt[e] - c['start'][e]
            self.stack[e][-1].append(('if', c['flag_ap'], c['seq'], body, c['start'][e], n,
                                      nf if e == 'dve' else 0))
        self.waited = c['snap']

    def barrier(self):
        for eng in self.engnames:
            wd = self.waited.setdefault(eng, {})
            for p, c in self.cnt.items():
                if c > 0 and p != eng and wd.get(p, 0) < c:
                    self.stack[eng][-1].append(('w', self.sem(p), c * self.mult[p]))
                    wd[p] = c
        for eng in self.engnames:
            if eng != 'sp' and self.cnt[eng] > 0:
                self.stack[eng][-1].append(('w', self.sems[eng], self.cnt[eng]))

    def _replay_list(self, eng, e, lst):
        for it in lst:
            if it[0] == 'w':
                e.wait_ge(it[1], it[2])
            elif it[0] == 'o':
                meth, args, kw = it[1]
                getattr(e, meth)(*args, **kw).then_inc(it[2], it[3])
            else:
                _, flag_ap, seq, body, start, n, dve_else = it
                e.wait_ge(self.sems['flag'], seq)
                reg = self.regs[eng]
                e.reg_load(reg, flag_ap)
                with e.If_ne(reg, 0):
                    self._replay_list(eng, e, body)
                with e.Else():
                    if start > 0:
                        e.wait_ge(self.sems[eng], start)
                    if n > 0:
                        e.sem_inc(self.sems[eng], n)
                    if dve_else:
                        e.sem_inc(self.sems['flag'], dve_else)

    def replay(self, eng, e):
        assert len(self.stack[eng]) == 1
        self._replay_list(eng, e, self.prog[eng])
        self.prog[eng] = []
        self.stack[eng] = [self.prog[eng]]


def build(NT, NO):
    SL = NT * T
    NKB = SL // 128
    NQ = NO + 1
    T0 = NT - NO - 1
    nc = bass.Bass("TRN2", target_bir_lowering=False)

    def din(name, shape):
        return nc.dram_tensor(name, shape, F32, kind="ExternalInput").ap()
    x_d = din("x", [SL, D])
    valid_d = din("valid", [128, NKB])
    g1_d = din("g1", [128, D])
    g2_d = din("g2", [128, D])
    win_d = din("w_in", [D, 5120])
    gq_d = din("gq", [128, 1])
    gk_d = din("gk", [128, 1])
    tb_d = din("tb", [128, 8, 640])
    wa_d = din("w_a", [512, D])
    wb_d = din("w_b", [512, D])
    wo_d = din("w_o", [D, D])
    wup_d = din("w_up", [D, 2 * DFF])
    cw_d = din("cw", [128, 44, 3])
    cb_d = din("cb", [128, 44])
    wdn_d = din("w_dn", [DFF, D])
    ident_d = din("ident", [128, 128])
    negu_d = din("negu", [128, 128])
    negl_d = din("negl", [128, 128])
    bd_d = din("bd", [128, 128])
    mask_d = din("mask", [128, 4, T])
    y_d = nc.dram_tensor("y", [NO * T, D], F32, kind="ExternalOutput").ap()
    kt_d = nc.dram_tensor("kt_scr", [4, 128, SL], BF16).ap()
    v_d = nc.dram_tensor("v_scr", [4, 128, NKB, 128], BF16).ap()
    x1_d = nc.dram_tensor("x1_scr", [NQ * T, D], F32).ap()

    top = contextlib.ExitStack()
    with top:
        engnames = ['pe', 'act', 'dve', 'pool', 'sp']
        sems = {}
        for n in ['pe', 'act', 'dve', 'pool', 'flag']:
            sems[n] = top.enter_context(nc.semaphore(n))
        for i in range(28):
            sems[f'c{i}'] = top.enter_context(nc.semaphore(f'c{i}'))
        S = Sched(engnames, sems)
        S.regs = {'pe': nc.alloc_register(mybir.EngineType.PE, 'flag_pe'),
                  'act': nc.alloc_register(mybir.EngineType.Activation, 'flag_act'),
                  'dve': nc.alloc_register(mybir.EngineType.DVE, 'flag_dve')}

        def run_block():
            blk = nc.Block()
            with blk:
                blk.tensor(lambda e: S.replay('pe', e))
                blk.scalar(lambda e: S.replay('act', e))
                blk.vector(lambda e: S.replay('dve', e))
                blk.gpsimd(lambda e: S.replay('pool', e))
                blk.sync(lambda e: S.replay('sp', e))

        uid = [0]

        def sbt(es, name, shape, dt):
            uid[0] += 1
            return es.enter_context(nc.sbuf_tensor(f"s{uid[0]}_{name}", shape, dt))

        def pst(es, name, shape, dt=F32):
            uid[0] += 1
            return es.enter_context(nc.psum_tensor(f"p{uid[0]}_{name}", shape, dt))

        def MM(out, lhsT, rhs, start, stop, reads, writes):
            S.op('pe', 'matmul', (out,), dict(lhsT=lhsT, rhs=rhs, start=start, stop=stop), reads, writes)

        def ACTV(out, in_, func, reads, writes, **kw):
            S.op('act', 'activation', (), dict(out=out, in_=in_, func=func, **kw), reads, writes)

        def CP(eng, out, in_, reads, writes):
            S.op(eng, 'copy' if eng == 'act' else 'tensor_copy', (), dict(out=out, in_=in_), reads, writes)

        def TT(eng, out, in0, in1, op, reads, writes):
            S.op(eng, 'tensor_tensor', (), dict(out=out, in0=in0, in1=in1, op=op), reads, writes)

        def STT(eng, out, in0, scalar, in1, op0, op1, reads, writes):
            S.op(eng, 'scalar_tensor_tensor', (), dict(out=out, in0=in0, scalar=scalar, in1=in1, op0=op0, op1=op1), reads, writes)

        def TS(eng, out, in0, s1, s2, op0, op1, reads, writes):
            kw = dict(out=out, in0=in0, scalar1=s1, scalar2=s2, op0=op0)
            if op1 is not None:
                kw['op1'] = op1
            S.op(eng, 'tensor_scalar', (), kw, reads, writes)

        def MS(eng, ap, val, writes):
            S.op(eng, 'memset', (ap, val), {}, (), writes)

        ident = sbt(top, "ident", [128, 128], BF16)
        negu = sbt(top, "negu", [128, 128], BF16)
        negl = sbt(top, "negl", [128, 128], BF16)
        bdm = sbt(top, "bdm", [128, 128], BF16)
        zero = sbt(top, "zero", [128, 128], BF16)
        cst = sbt(top, "cst", [128, 4], F32)

        stg_state = {'i': 0}

        def load_weight(stg, dst_ap, src_ap, n, wname):
            sl = stg_state['i'] % 2
            stg_state['i'] += 1
            S.dma(f'stg{sl}', stg[:, sl, 0:n], src_ap, writes=[f'stg{sl}'])
            CP('pool', dst_ap, stg[:, sl, 0:n], [f'stg{sl}'], [wname])

        def norm_a(Bf, src_ap, src_reads, slot, gb):
            xs, junk, ss, hn = Bf['xs'], Bf['junk'], Bf['ss'], Bf['hn']
            xn, hnn, ssn = f'xs{slot}', f'hn{slot}', f'ss{slot}'
            S.dma(xn, xs[:, slot, :], src_ap, reads=src_reads, writes=[xn])
            MS('pool', ss[:, slot, 0:1], 0.0, [ssn])
            ACTV(junk[:, :], xs[:, slot, :], AF.Square, [xn], ['junk', ssn], accum_out=ss[:, slot, 0:1])
            ACTV(ss[:, slot, 1:2], ss[:, slot, 0:1], AF.Ln, [ssn], [ssn], scale=1.0 / D, bias=cst[:, 0:1])
            ACTV(ss[:, slot, 2:3], ss[:, slot, 1:2], AF.Exp, [ssn], [ssn], scale=-0.5)
            STT('dve', hn[:, slot, :], xs[:, slot, :], ss[:, slot, 2:3], gb[:, :], ALU.mult, ALU.mult, [xn, ssn], [hnn])

        def norm_b(Bf, slot, hnT_ap, hnT_name, tp, tp_name, ev='act'):
            hn = Bf['hn']
            for kc in range(KC):
                S.op('pe', 'transpose', (tp[:, kc, :], hn[:, slot, kc * 128:(kc + 1) * 128], ident[:, :]), {}, [f'hn{slot}'], [tp_name])
            CP(ev, hnT_ap, tp[:, :, :], [tp_name], [hnT_name])

        def norm_block(Bf, src_ap, src_reads, slot, gb, hnT_ap, hnT_name, tp, tp_name):
            norm_a(Bf, src_ap, src_reads, slot, gb)
            norm_b(Bf, slot, hnT_ap, hnT_name, tp, tp_name)

        def new_B(es, ns=2):
            return dict(xs=sbt(es, "xs", [128, ns, D], F32), junk=sbt(es, "junk", [128, D], BF16),
                        ss=sbt(es, "ss", [128, ns, 4], F32), hn=sbt(es, "hn", [128, ns, D], BF16))

        sc04 = contextlib.ExitStack()
        with sc04:
            g1b = sbt(sc04, "g1b", [128, D], F32)
            vt = sbt(sc04, "vt", [128, NKB], F32)
            OT = sbt(sc04, "OT", [128, 4, NQ * T], BF16)

            with contextlib.ExitStack() as es:
                i32 = sbt(es, "i32", [128, 4, 128], F32)
                for i, srcd in enumerate([ident_d, negu_d, negl_d, bd_d]):
                    S.dma('cst', i32[:, i, :], srcd[:, :], writes=[f'i32_{i}'])
                S.barrier()
                for i, dst in enumerate([ident, negu, negl, bdm]):
                    CP('pool', dst[:, :], i32[:, i, :], [f'i32_{i}'], [f'const{i}'])
                MS('pool', zero[:, :], 0.0, ['zero'])
                MS('pool', cst[:, 0:1], EPS, ['cst'])
                MS('pool', cst[:, 1:2], 1.0, ['cst'])
                MS('pool', OT[:, :, 0:T], 0.0, ['OTz'])
                S.dma('cst', g1b[:, :], g1_d[:, :], writes=['g1b'])
                S.dma('cst', vt[:, :], valid_d[:, :], writes=['vt'])
                S.barrier()
                run_block()

            sc12 = contextlib.ExitStack()
            with sc12:
                QT = sbt(sc12, "QT", [128, 4, NQ * T], BF16)
                with contextlib.ExitStack() as es:
                    wB = sbt(es, "wB", [128, KC, 1536], BF16)
                    stg = sbt(es, "stg1", [128, 2, 1536], F32)
                    Bf = new_B(es, 4)
                    hnT = sbt(es, "hnT", [128, 2, KC, T], BF16)
                    KTs = sbt(es, "KTs", [128, 2, 4, T], BF16)
                    Vs = sbt(es, "Vs", [128, 2, 4, T], BF16)
                    tps = [pst(es, f"tp{i}", [128, KC, 128], BF16) for i in range(2)]
                    accs = [pst(es, f"acc{i}", [128, T], F32) for i in range(4)]
                    for kc in range(KC):
                        load_weight(stg, wB[:, kc, :], win_d[kc * 128:(kc + 1) * 128, 1536:3072], 1536, 'wB')
                    acc_i = 0
                    ev_i = 0

                    def nA(t, bi):
                        blk_i = t * 4 + bi
                        norm_a(Bf, x_d[blk_i * 128:(blk_i + 1) * 128, :], [], blk_i % 4, g1b)

                    def nB(t, bi):
                        blk_i = t * 4 + bi
                        hs_ = t % 2
                        norm_b(Bf, blk_i % 4, hnT[:, hs_, :, bi * 128:(bi + 1) * 128], f'hnT{hs_}_{bi}', tps[blk_i % 2], f'tp{blk_i % 2}',
                               ev=('act' if bi % 2 == 0 else 'dve'))

                    def grpK(t, p):
                        nonlocal acc_i, ev_i
                        hs = t % 2
                        hread = [f'hnT{hs}_{bi}' for bi in range(4)]
                        a = acc_i % 4
                        acc_i += 1
                        for kc in range(KC):
                            MM(accs[a][:, :], wB[:, kc, 512 + p * 128:512 + (p + 1) * 128], hnT[:, hs, kc, :],
                               kc == 0, kc == KC - 1, ['wB'] + hread, [f'acc{a}'])
                        CP('dve' if ev_i % 2 == 0 else 'act', KTs[:, hs, p, :], accs[a][:, :], [f'acc{a}'], [f'KTs{hs}'])
                        ev_i += 1
                        if p == 3:
                            S.dma(f'kts{hs}', kt_d[:, :, t * T:(t + 1) * T].rearrange("q p c -> p q c"), KTs[:, hs, :, :],
                                  reads=[f'KTs{hs}'], writes=['kt_d'])

                    def grpV(t, bi):
                        nonlocal acc_i, ev_i
                        hs = t % 2
                        a = acc_i % 4
                        acc_i += 1
                        for kc in range(KC):
                            MM(accs[a][:, :], hnT[:, hs, kc, bi * 128:(bi + 1) * 128], wB[:, kc, 1024:1536],
                               kc == 0, kc == KC - 1, ['wB', f'hnT{hs}_{bi}'], [f'acc{a}'])
                        CP('dve' if ev_i % 2 == 0 else 'act', Vs[:, hs, bi, :], accs[a][:, :], [f'acc{a}'], [f'Vs{hs}'])
                        ev_i += 1
                        if bi == 3:
                            for q in range(4):
                                S.dma(f'vs{hs}', v_d[q, :, t * 4:(t + 1) * 4, :], Vs[:, hs, :, q * 128:(q + 1) * 128],
                                      reads=[f'Vs{hs}'], writes=['v_d'])

                    def grpQ(t, p):
                        nonlocal acc_i, ev_i
                        hs = t % 2
                        qi = t - T0
                        hread = [f'hnT{hs}_{bi}' for bi in range(4)]
                        a = acc_i % 4
                        acc_i += 1
                        for kc in range(KC):
                            MM(accs[a][:, :], wB[:, kc, p * 128:(p + 1) * 128], hnT[:, hs, kc, :],
                               kc == 0, kc == KC - 1, ['wB'] + hread, [f'acc{a}'])
                        CP('dve' if ev_i % 2 == 0 else 'act', QT[:, p, qi * T:(qi + 1) * T], accs[a][:, :], [f'acc{a}'], [f'QT{p}_{qi}'])
                        ev_i += 1

                    for bi in range(4):
                        nA(0, bi)
                        nB(0, bi)
                    for t in range(NT):
                        nx = t + 1 < NT
                        if nx:
                            nA(t + 1, 0)
                            nA(t + 1, 1)
                        grpK(t, 0)
                        if nx:
                            nB(t + 1, 0)
                        grpK(t, 1)
                        if nx:
                            nA(t + 1, 2)
                        grpK(t, 2)
                        if nx:
                            nB(t + 1, 1)
                        grpK(t, 3)
                        if nx:
                            nA(t + 1, 3)
                        grpV(t, 0)
                        if nx:
                            nB(t + 1, 2)
                        grpV(t, 1)
                        grpV(t, 2)
                        if nx:
                            nB(t + 1, 3)
                        grpV(t, 3)
                        if t >= T0:
                            for p in range(4):
                                grpQ(t, p)
                    S.barrier()
                    run_block()

                with contextlib.ExitStack() as es:
                    KTp = sbt(es, "KTp", [128, SL], BF16)
                    Vp = sbt(es, "Vp", [128, NKB, 128], BF16)
                    M = sbt(es, "M", [128, 4, 2, T], BF16)
                    NE, NSP, NXW, NW = 4, 3, 2, 2
                    eb = sbt(es, "eb", [128, NE, 2, T], F32)
                    spb = sbt(es, "spb", [128, NSP, 2, T], BF16)
                    xwb = sbt(es, "xwb", [128, NXW, 2, T], F32)
                    wbuf = sbt(es, "wbuf", [128, NW, 2, T], BF16)
                    m32 = sbt(es, "m32", [128, 4, T], F32)
                    Zp = [pst(es, f"Z{i}", [128, 2, T]) for i in range(2)]
                    Ap = pst(es, "A", [128, 2, T])
                    Op = pst(es, "O", [128, T])
                    S.dma('cst', m32[:, :, :], mask_d[:, :, :], writes=['m32'])
                    for h in range(2):
                        CP('pool', M[:, :, h, :], m32[:, :, :], ['m32'], ['M'])
                    I32 = mybir.dt.int32
                    flagbuf = sbt(es, "flagbuf", [128, 512], I32)
                    mx = sbt(es, "mx", [128, 4], F32)
                    THRESH = 150.0
                    CH = 32
                    nchk = (NKB + CH - 1) // CH
                    itn = [0]
                    fidx = [0]

                    def load_pair(p):
                        for c in reversed(range(nchk)):
                            k0, k1 = c * CH, min(NKB, (c + 1) * CH)
                            S.dma(f'ktp{c}', KTp[:, k0 * 128:k1 * 128], kt_d[p, :, k0 * 128:k1 * 128], reads=['kt_d'], writes=[f'KTp{c}'])
                            S.dma(f'vp{c}', Vp[:, k0:k1, :], v_d[p, :, k0:k1, :], reads=['v_d'], writes=[f'Vp{c}'])

                    def st1(it):
                        W, kb, p, qi, c0, n = it['W'], it['kb'], it['p'], it['qi'], it['c0'], it['n']
                        for h in range(2):
                            MM(Zp[n % 2][:, h, 0:W], KTp[64 * h:64 * h + 64, kb * 128:(kb + 1) * 128],
                               QT[64 * h:64 * h + 64, p, qi * T + c0:qi * T + c0 + W], True, True,
                               [f'KTp{kb // CH}', f'QT{p}_{qi}'], [f'Z{n % 2}'])

                    def st2(it):
                        W, n = it['W'], it['n']
                        en = f'e{n % NE}'
                        ACTV(eb[:, n % NE, :, 0:W], Zp[n % 2][:, :, 0:W], AF.Exp, [f'Z{n % 2}'], [en], scale=0.125)
                        if it['d'] is not None:
                            TT('dve', eb[:, n % NE, :, 0:W], eb[:, n % NE, :, 0:W], M[:, it['d'], :, 0:W], ALU.mult, [en, 'M'], [en])

                    def st3(it):
                        W, n = it['W'], it['n']
                        ACTV(spb[:, n % NSP, :, 0:W], eb[:, n % NE, :, 0:W], AF.Ln, [f'e{n % NE}'], [f'sp{n % NSP}'], bias=cst[:, 1:2])

                    def st4(it):
                        W, n = it['W'], it['n']
                        for h in range(2):
                            MM(Ap[:, h, 0:W], negu[:, :], spb[:, n % NSP, h, 0:W], it['first'], False,
                               [f'sp{n % NSP}', 'const1'], ['A'])

                    def st5(it):
                        W, n = it['W'], it['n']
                        ACTV(xwb[:, n % NXW, :, 0:W], Ap[:, :, 0:W], AF.Exp, ['A'], [f'xw{n % NXW}'])

                    def st6(it):
                        W, n = it['W'], it['n']
                        if it['last']:
                            return
                        for h in range(2):
                            MM(Ap[:, h, 0:W], negl[:, :], spb[:, n % NSP, h, 0:W], False, False,
                               [f'sp{n % NSP}', 'const2'], ['A'])

                    def st7(it):
                        W, n = it['W'], it['n']
                        TT('dve', wbuf[:, n % NW, :, 0:W], eb[:, n % NE, :, 0:W], xwb[:, n % NXW, :, 0:W], ALU.mult,
                           [f'e{n % NE}', f'xw{n % NXW}'], [f'w{n % NW}'])

                    def st8(it):
                        W, n, kb = it['W'], it['n'], it['kb']
                        for h in range(2):
                            MM(Op[64 * h:64 * h + 64, 0:W], Vp[:, kb, 64 * h:64 * h + 64], wbuf[:, n % NW, h, 0:W],
                               it['first'], it['last'], [f'w{n % NW}', f'Vp{kb // CH}'], ['O'])

                    stages = [(st1, 0), (st2, 1), (st3, 2), (st6, 4), (st4, 3), (st5, 3), (st7, 4), (st8, 5)]

                    def emit_segment(items):
                        nit = len(items)
                        for s_ in range(nit + 7):
                            for fn, dly in stages:
                                i_ = s_ - dly
                                if 0 <= i_ < nit:
                                    fn(items[i_])

                    def emit_flag(W):
                        fi = fidx[0]
                        fidx[0] += 1
                        for h in range(2):
                            S.op('dve', 'tensor_reduce', (), dict(out=mx[0:1, h:h + 1], in_=Ap[0:1, h, 0:W], axis=mybir.AxisListType.X, op=ALU.max),
                                 ['A'], [f'mx{h}'])
                        TT('dve', mx[0:1, 2:3], mx[0:1, 0:1], mx[0:1, 1:2], ALU.max, ['mx0', 'mx1'], ['mx2'])
                        S.flag_op('tensor_scalar', (), dict(out=flagbuf[0:1, fi:fi + 1], in0=mx[0:1, 2:3], scalar1=-THRESH, scalar2=None, op0=ALU.is_gt),
                                  ['mx2'], [f'flag{fi}'])
                        return fi

                    for p in range(4):
                        load_pair(p)
                        for qi in range(NQ):
                            g = T0 + qi
                            if qi == 0:
                                W, c0, q0, ndiag = 128, 384, g * T + 384, 1
                            else:
                                W, c0, q0, ndiag = T, 0, g * T, 4
                            kb_hi = (q0 + W) // 128 - 1
                            nb = kb_hi + 1
                            segs = [list(range(0, min(nb, ndiag + 3)))]
                            sz = 2
                            while segs[-1][-1] + 1 < nb:
                                st_ = segs[-1][-1] + 1
                                segs.append(list(range(st_, min(nb, st_ + sz))))
                                if len(segs) > 2:
                                    sz *= 2
                            nopen = 0
                            for si, seg in enumerate(segs):
                                items = []
                                for b in seg:
                                    kb = kb_hi - b
                                    d = kb - q0 // 128
                                    items.append(dict(p=p, qi=qi, W=W, c0=c0, kb=kb, d=(d if d >= 0 else None),
                                                      first=(b == 0), last=(b == nb - 1), n=itn[0]))
                                    itn[0] += 1
                                emit_segment(items)
                                if si < len(segs) - 1:
                                    fi = emit_flag(W)
                                    S.begin_cond(flagbuf[0:1, fi:fi + 1], ['pe', 'act', 'dve'])
                                    nopen += 1
                            for _ in range(nopen):
                                S.end_cond()
                            CP('dve', OT[:, p, qi * T + c0:qi * T + c0 + W], Op[:, 0:W], ['O', 'OTz'], [f'OT{p}_{qi}'])
                    S.barrier()
                    run_block()

            sc34 = contextlib.ExitStack()
            with sc34:
                OAT = sbt(sc34, "OAT", [128, 4, NQ * T], BF16)
                with contextlib.ExitStack() as es:
                    wA = sbt(es, "wA", [128, KC, 1536], BF16)
                    with contextlib.ExitStack() as es2:
                        stg = sbt(es2, "stg3", [128, 2, 1536], F32)
                        for kc in range(KC):
                            load_weight(stg, wA[:, kc, :], win_d[kc * 128:(kc + 1) * 128, 0:1536], 1536, 'wA')
                        S.barrier()
                    TB = sbt(es, "TB", [128, 8, 640], F32)
                    gq = sbt(es, "gq", [128, 1], F32)
                    gk = sbt(es, "gk", [128, 1], F32)
                    Bf = new_B(es)
                    hnT = sbt(es, "hnT", [128, KC, T], BF16)
                    QAT = sbt(es, "QAT", [128, 4, T], BF16)
                    KAT = sbt(es, "KAT", [128, 4, 2, T], BF16)
                    VA = sbt(es, "VA", [128, 2, 4, T], BF16)
                    VLD = sbt(es, "VLD", [128, 2, 4, 64], BF16)
                    sqb = sbt(es, "sqb", [128, 2, T], BF16)
                    qgb = sbt(es, "qgb", [128, 2, T], F32)
                    lnb = sbt(es, "lnb", [128, 2, T], F32)
                    sbb = sbt(es, "sbb", [128, 2, 2, T], F32)
                    pTb = sbt(es, "pTb", [128, 2, 2, T], BF16)
                    rdb = sbt(es, "rdb", [128, T], F32)
                    tp3 = pst(es, "tp3", [128, KC, 128], BF16)
                    acc3t = pst(es, "acc3", [128, 2, T])
                    acc3 = [acc3t[:, i, :] for i in range(2)]
                    SSp = pst(es, "SSp", [128, T])
                    SPSt = pst(es, "SPS", [128, 2, T])
                    sbufs = [(SPSt, 'SPS'), (acc3t, 'acc3_')]
                    OAp = pst(es, "OAp", [128, T])
                    DENp = pst(es, "DENp", [128, T])
                    S.dma('cst', TB[:, :, :], tb_d[:, :, :], writes=['TB'])
                    S.dma('cst', gq[:, :], gq_d[:, :], writes=['gq'])
                    S.dma('cst', gk[:, :], gk_d[:, :], writes=['gk'])
                    S.barrier()
                    blkc = 0
                    acc_i = 0
                    nrm = [0]
                    jj = [0]

                    def qknorm(a, gcol, gname, dst_ap, dst_name):
                        k = nrm[0] % 2
                        nrm[0] += 1
                        ACTV(sqb[:, k, :], acc3[a][:, :], AF.Square, [f'acc3_{a}'], [f'sqb{k}'])
                        S.op('act', 'mul', (), dict(out=qgb[:, k, :], in_=acc3[a][:, :], mul=gcol[:, 0:1]), [f'acc3_{a}', gname], [f'qgb{k}'])
                        MM(SSp[:, :], bdm[:, :], sqb[:, k, :], True, True, [f'sqb{k}', 'const3'], ['SSp'])
                        ACTV(lnb[:, k, :], SSp[:, :], AF.Ln, ['SSp'], [f'lnb{k}'], scale=1.0 / 64, bias=cst[:, 0:1])
                        ACTV(lnb[:, k, :], lnb[:, k, :], AF.Exp, [f'lnb{k}'], [f'lnb{k}'], scale=-0.5)
                        TT('dve', dst_ap, qgb[:, k, :], lnb[:, k, :], ALU.mult, [f'qgb{k}', f'lnb{k}'], [dst_name])

                    for t in range(T0 - 1, NT):
                        qi = t - T0
                        sl = t % 2
                        for bi in range(4):
                            blk_i = t * 4 + bi
                            norm_block(Bf, x_d[blk_i * 128:(blk_i + 1) * 128, :], [], blkc % 2, g1b,
                                       hnT[:, :, bi * 128:(bi + 1) * 128], f'hnT_{bi}', tp3, 'tp3')
                            blkc += 1
                        hread = [f'hnT_{bi}' for bi in range(4)]
                        for p in range(4):
                            a = acc_i % 2
                            acc_i += 1
                            for kc in range(KC):
                                MM(acc3[a][:, :], wA[:, kc, 512 + p * 128:512 + (p + 1) * 128], hnT[:, kc, :],
                                   kc == 0, kc == KC - 1, ['wA'] + hread, [f'acc3_{a}'])
                            qknorm(a, gk, 'gk', KAT[:, p, sl, :], f'KAT{sl}_{p}')
                        for bi in range(4):
                            a = acc_i % 2
                            acc_i += 1
                            for kc in range(KC):
                                MM(acc3[a][:, :], hnT[:, kc, bi * 128:(bi + 1) * 128], wA[:, kc, 1024:1536],
                                   kc == 0, kc == KC - 1, ['wA', f'hnT_{bi}'], [f'acc3_{a}'])
                            CP('dve', VA[:, sl, bi, :], acc3[a][:, :], [f'acc3_{a}'], [f'VA{sl}_{bi}'])
                            CP('pool', VLD[:, sl, bi, :], vt[:, t * 4 + bi:t * 4 + bi + 1].to_broadcast([128, 64]),
                               ['vt'], [f'VLD{sl}_{bi}'])
                        if qi < 0:
                            continue
                        for p in range(4):
                            a = acc_i % 2
                            acc_i += 1
                            for kc in range(KC):
                                MM(acc3[a][:, :], wA[:, kc, p * 128:(p + 1) * 128], hnT[:, kc, :],
                                   kc == 0, kc == KC - 1, ['wA'] + hread, [f'acc3_{a}'])
                            qknorm(a, gq, 'gq', QAT[:, p, :], f'QAT{p}')
                        for p in range(4):
                            MM(OAp[:, :], zero[:, :], hnT[:, 0, :], True, False, ['zero'] + hread, ['OAp'])
                            MM(DENp[:, :], zero[:, :], hnT[:, 0, :], True, False, ['zero'] + hread, ['DENp'])
                            def jgeom(j):
                                i_lo, i_hi = max(0, j - 4), min(3, j)
                                N = (i_hi - i_lo + 1) * 128
                                ksl = (1 - sl) if j < 4 else sl
                                return i_lo, N, ksl, j % 4, (4 - j + i_lo) * 128

                            def emitS(j, k):
                                i_lo, N, ksl, cj, tb0 = jgeom(j)
                                spt, spn = sbufs[k]
                                for h in range(2):
                                    MM(spt[:, h, 0:N], KAT[64 * h:64 * h + 64, p, ksl, cj * 128:(cj + 1) * 128],
                                       QAT[64 * h:64 * h + 64, p, i_lo * 128:i_lo * 128 + N], True, True,
                                       [f'KAT{ksl}_{p}', f'QAT{p}'], [f'{spn}{h}'])

                            kbase = jj[0]
                            jj[0] += 8
                            emitS(0, kbase % 2)
                            for j in range(8):
                                i_lo, N, ksl, cj, tb0 = jgeom(j)
                                k = (kbase + j) % 2
                                spt, spn = sbufs[k]
                                if j + 1 < 8:
                                    emitS(j + 1, (kbase + j + 1) % 2)
                                STT('dve', sbb[:, k, :, 0:N], spt[:, :, 0:N], 0.125, TB[:, 2 * p:2 * p + 2, tb0:tb0 + N], ALU.mult, ALU.add,
                                    [f'{spn}0', f'{spn}1', 'TB'], [f'sbb{k}'])
                                ACTV(pTb[:, k, :, 0:N], sbb[:, k, :, 0:N], AF.Exp, [f'sbb{k}'], [f'pTb{k}'])
                                for h in range(2):
                                    hh = 2 * p + h
                                    MM(OAp[64 * h:64 * h + 64, i_lo * 128:i_lo * 128 + N], VA[:, ksl, cj, hh * 64:(hh + 1) * 64],
                                       pTb[:, k, h, 0:N], False, False, [f'pTb{k}', f'VA{ksl}_{cj}'], ['OAp'])
                                    MM(DENp[64 * h:64 * h + 64, i_lo * 128:i_lo * 128 + N], VLD[:, ksl, cj, :],
                                       pTb[:, k, h, 0:N], False, False, [f'pTb{k}', f'VLD{ksl}_{cj}'], ['DENp'])
                            TS('dve', rdb[:, :], DENp[:, :], 1e-30, None, ALU.max, None, ['DENp'], ['rdb'])
                            ACTV(rdb[:, :], rdb[:, :], AF.Ln, ['rdb'], ['rdb'])
                            ACTV(rdb[:, :], rdb[:, :], AF.Exp, ['rdb'], ['rdb'], scale=-1.0)
                            TT('dve', OAT[:, p, qi * T:(qi + 1) * T], OAp[:, :], rdb[:, :], ALU.mult, ['OAp', 'rdb'], [f'OAT{p}_{qi}'])
                    S.barrier()
                    run_block()

                with contextlib.ExitStack() as es:
                    wG = sbt(es, "wG", [128, KC, 2048], BF16)
                    wbra = sbt(es, "wbra", [128, 4, D], BF16)
                    wbrb = sbt(es, "wbrb", [128, 4, D], BF16)
                    wout = sbt(es, "wout", [128, KC, D], BF16)
                    stg = sbt(es, "stg4", [128, 2, D], F32)
                    Bf = new_B(es)
                    xr = sbt(es, "xr", [128, 2, D], F32)
                    hnT = sbt(es, "hnT", [128, KC, T], BF16)
                    sg = sbt(es, "sg", [128, 2, T], F32)
                    mm = sbt(es, "mm", [128, 2, T], F32)
                    MT = sbt(es, "MT", [128, KC, T], BF16)
                    tp4 = pst(es, "tp4", [128, KC, 128], BF16)
                    Gp = [pst(es, f"G{i}", [128, T]) for i in range(2)]
                    Yp = [pst(es, f"Y{i}", [128, T]) for i in range(2)]
                    Xp = [pst(es, f"X{i}", [128, T]) for i in range(2)]
                    for kc in range(KC):
                        for hf in range(2):
                            load_weight(stg, wG[:, kc, hf * D:(hf + 1) * D], win_d[kc * 128:(kc + 1) * 128, 3072 + hf * D:3072 + (hf + 1) * D], D, 'wG')
                    for p in range(4):
                        load_weight(stg, wbra[:, p, :], wa_d[p * 128:(p + 1) * 128, :], D, 'wbra')
                        load_weight(stg, wbrb[:, p, :], wb_d[p * 128:(p + 1) * 128, :], D, 'wbrb')
                    for kc in range(KC):
                        load_weight(stg, wout[:, kc, :], wo_d[kc * 128:(kc + 1) * 128, :], D, 'wout')
                    blkc = 0
                    ob = 0
                    for qi in range(NQ):
                        t = T0 + qi
                        bis = [3] if qi == 0 else [0, 1, 2, 3]
                        cs, cn = (384, 128) if qi == 0 else (0, T)
                        for bi in bis:
                            blk_i = t * 4 + bi
                            norm_block(Bf, x_d[blk_i * 128:(blk_i + 1) * 128, :], [], blkc % 2, g1b,
                                       hnT[:, :, bi * 128:(bi + 1) * 128], f'hnT_{bi}', tp4, 'tp4')
                            blkc += 1
                        hread = [f'hnT_{bi}' for bi in bis]
                        for oc in range(KC):
                            for br in range(2):
                                for kc in range(KC):
                                    MM(Gp[br][:, cs:cs + cn], wG[:, kc, br * D + oc * 128:br * D + (oc + 1) * 128], hnT[:, kc, cs:cs + cn],
                                       kc == 0, kc == KC - 1, ['wG'] + hread, [f'G{br}'])
                                ACTV(sg[:, br, cs:cs + cn], Gp[br][:, cs:cs + cn], AF.Sigmoid, [f'G{br}'], [f'sg{br}'])
                                wsrc = wbra if br == 0 else wbrb
                                osrc = OAT if br == 0 else OT
                                for p in range(4):
                                    MM(Yp[br][:, cs:cs + cn], wsrc[:, p, oc * 128:(oc + 1) * 128], osrc[:, p, qi * T + cs:qi * T + cs + cn],
                                       p == 0, p == 3,
                                       ['wbra' if br == 0 else 'wbrb', (f'OAT{p}_{qi}' if br == 0 else f'OT{p}_{qi}'), 'OTz'], [f'Y{br}'])
                                TT('dve', mm[:, br, cs:cs + cn], Yp[br][:, cs:cs + cn], sg[:, br, cs:cs + cn], ALU.mult, [f'Y{br}', f'sg{br}'], [f'mm{br}'])
                            TT('pool', MT[:, oc, cs:cs + cn], mm[:, 0, cs:cs + cn], mm[:, 1, cs:cs + cn], ALU.add, ['mm0', 'mm1'], [f'MT{oc}'])
                        mread = [f'MT{oc}' for oc in range(KC)]
                        for bi in bis:
                            blk_i = t * 4 + bi
                            o = ob % 2
                            ob += 1
                            S.dma(f'xr{o}', xr[:, o, :], x_d[blk_i * 128:(blk_i + 1) * 128, :], writes=[f'xr{o}'])
                            for half in range(2):
                                for oc in range(KC):
                                    MM(Xp[half][:, :], MT[:, oc, bi * 128:(bi + 1) * 128], wout[:, oc, half * T:(half + 1) * T],
                                       oc == 0, oc == KC - 1, ['wout'] + mread, [f'X{half}'])
                                TT('dve', xr[:, o, half * T:(half + 1) * T], Xp[half][:, :], xr[:, o, half * T:(half + 1) * T], ALU.add,
                                   [f'X{half}', f'xr{o}'], [f'xr{o}'])
                            row = qi * T + bi * 128
                            S.dma(f'x1w{o}', x1_d[row:row + 128, :], xr[:, o, :], reads=[f'xr{o}'], writes=['x1_d'])
                    S.barrier()
                    run_block()

        with contextlib.ExitStack() as es:
            wup = sbt(es, "wup", [128, KC, 2 * DFF], BF16)
            wdn = sbt(es, "wdn", [128, NFC, D], BF16)
            g2b = sbt(es, "g2b", [128, D], F32)
            cw = sbt(es, "cw", [128, 44, 3], F32)
            cb = sbt(es, "cb", [128, 44], F32)
            HALO = sbt(es, "HALO", [128, 44, 2], F32)
            with contextlib.ExitStack() as es2:
                stg = sbt(es2, "stg5", [128, 2, 2816], F32)
                for kc in range(KC):
                    for hf in range(2):
                        load_weight(stg, wup[:, kc, hf * DFF:(hf + 1) * DFF], wup_d[kc * 128:(kc + 1) * 128, hf * DFF:(hf + 1) * DFF], DFF, 'wup')
                for fc in range(NFC):
                    load_weight(stg, wdn[:, fc, :], wdn_d[fc * 128:(fc + 1) * 128, :], D, 'wdn')
                S.dma('cst', g2b[:, :], g2_d[:, :], writes=['g2b'])
                S.dma('cst', cw[:, :, :], cw_d[:, :, :], writes=['cw'])
                S.dma('cst', cb[:, :], cb_d[:, :], writes=['cb'])
                MS('pool', HALO[:, :, :], 0.0, ['HALO'])
                S.barrier()
            Bf = dict(xs=sbt(es, "xs", [128, 2, D], F32), junk=sbt(es, "junk", [128, D], BF16),
                      ss=sbt(es, "ss", [128, 2, 4], F32), hn=sbt(es, "hn", [128, 2, D], BF16))
            hnT = sbt(es, "hnT", [128, KC, T], BF16)
            hb = sbt(es, "hb", [128, 2, 2, T + 2], F32)
            cv = sbt(es, "cv", [128, 2, T], F32)
            sgl = sbt(es, "sgl", [128, T], F32)
            ACTT = sbt(es, "ACTT", [128, NFC, T], BF16)
            yo = sbt(es, "yo", [128, D], F32)
            tp5 = pst(es, "tp5", [128, KC, 128], BF16)
            Hp = [[pst(es, f"H{b}_{u}", [128, T]) for u in range(2)] for b in range(2)]
            Yd = [pst(es, f"Yd{i}", [128, T]) for i in range(2)]
            blkc = 0
            fcc = 0
            for qi in range(NQ):
                bis = [3] if qi == 0 else [0, 1, 2, 3]
                for bi in bis:
                    row = qi * T + bi * 128
                    norm_block(Bf, x1_d[row:row + 128, :], ['x1_d'], blkc % 2, g2b,
                               hnT[:, :, bi * 128:(bi + 1) * 128], f'hnT_{bi}', tp5, 'tp5')
                    blkc += 1
                hread = [f'hnT_{bi}' for bi in bis]
                for fc in range(NFC):
                    b = fcc % 2
                    fcc += 1
                    for u in range(2):
                        ch = fc + u * NFC
                        hbn = f'hb{b}_{u}'
                        if qi == 0:
                            for kc in range(KC):
                                MM(Hp[b][u][:, 384:T], wup[:, kc, ch * 128:(ch + 1) * 128], hnT[:, kc, 384:T],
                                   kc == 0, kc == KC - 1, ['wup'] + hread, [f'H{b}_{u}'])
                            CP('act', HALO[:, ch, :], Hp[b][u][:, T - 2:T], [f'H{b}_{u}'], [f'HALO{ch}'])
                            continue
                        for kc in range(KC):
                            MM(Hp[b][u][:, :], wup[:, kc, ch * 128:(ch + 1) * 128], hnT[:, kc, :],
                               kc == 0, kc == KC - 1, ['wup'] + hread, [f'H{b}_{u}'])
                        CP('pool', hb[:, b, u, 0:2], HALO[:, ch, :], [f'HALO{ch}'], [hbn])
                        CP('act', hb[:, b, u, 2:T + 2], Hp[b][u][:, :], [f'H{b}_{u}'], [hbn])
                        CP('pool', HALO[:, ch, :], hb[:, b, u, T:T + 2], [hbn], [f'HALO{ch}'])
                        ce = 'dve'
                        cvn = f'cv{u}'
                        ACTV(cv[:, u, :], Hp[b][u][:, :], AF.Identity, [f'H{b}_{u}', 'cw', 'cb'], [cvn], scale=cw[:, ch, 2:3], bias=cb[:, ch:ch + 1])
                        STT(ce, cv[:, u, :], hb[:, b, u, 1:T + 1], cw[:, ch, 1:2], cv[:, u, :], ALU.mult, ALU.add, [hbn, cvn, 'cw'], [cvn])
                        STT(ce, cv[:, u, :], hb[:, b, u, 0:T], cw[:, ch, 0:1], cv[:, u, :], ALU.mult, ALU.add, [hbn, cvn, 'cw'], [cvn])
                    if qi == 0:
                        continue
                    ACTV(sgl[:, :], cv[:, 0, :], AF.Silu, ['cv0'], ['sgl'])
                    TT('dve', ACTT[:, fc, :], sgl[:, :], cv[:, 1, :], ALU.mult, ['sgl', 'cv1'], [f'ACTT{fc}'])
                if qi == 0:
                    continue
                aread = [f'ACTT{fc}' for fc in range(NFC)]
                for bi in range(4):
                    row = qi * T + bi * 128
                    S.dma('yo_in', yo[:, :], x1_d[row:row + 128, :], reads=['x1_d', 'y_d'], writes=['yo'])
                    for half in range(2):
                        for fc in range(NFC):
                            MM(Yd[half][:, :], ACTT[:, fc, bi * 128:(bi + 1) * 128], wdn[:, fc, half * T:(half + 1) * T],
                               fc == 0, fc == NFC - 1, ['wdn'] + aread, [f'Yd{half}'])
                        TT('dve', yo[:, half * T:(half + 1) * T], Yd[half][:, :], yo[:, half * T:(half + 1) * T], ALU.add,
                           [f'Yd{half}', 'yo'], ['yo'])
                    orow = (qi - 1) * T + bi * 128
                    S.dma('yw', y_d[orow:orow + 128, :], yo[:, :], reads=['yo'], writes=['y_d'])
            S.barrier()
            run_block()
        print("megakernel ops", S.nops, "waits", S.nwaits, {k: v for k, v in S.cnt.items() if k in engnames})
    return nc


_CACHE = {}


def _consts():
    ident = np.eye(128, dtype=np.float32)
    kk = np.arange(128)
    negu = -(kk[:, None] >= kk[None, :]).astype(np.float32)
    negl = -(kk[:, None] < kk[None, :]).astype(np.float32)
    bd = np.zeros((128, 128), np.float32)
    bd[:64, :64] = 1.0
    bd[64:, 64:] = 1.0
    qq = np.arange(T)
    mask = np.zeros((128, 4, T), np.float32)
    for d in range(4):
        mask[:, d, :] = ((128 * d + kk[:, None]) < qq[None, :]).astype(np.float32)
    return ident, negu, negl, bd, mask


def _tb_table(rel_bias):
    kk = np.arange(128)[:, None]
    qq = np.arange(128)[None, :]
    tb = np.empty((128, 8, 640), np.float32)
    for rp in range(5):
        idx = np.clip(qq - kk + rp * 128, -128, 128) + 128
        blk = rel_bias[:, idx]
        vis = np.ones((128, 128), bool)
        if rp == 0:
            vis = (kk < 64) | (qq >= 64)
        if rp == 4:
            vis = (kk >= 64) | (qq < 64)
        blk = np.where(vis[None], blk, np.float32(NEG))
        tb[:, :, rp * 128:(rp + 1) * 128] = blk.transpose(1, 0, 2)
    return tb


def kernel(x, norm1_g, w_in, q_norm_g, k_norm_g, rel_bias, w_branch_a, w_branch_b, w_out, norm2_g,
           w_ffn_up, ffn_conv_w, ffn_conv_b, w_ffn_down):
    x = np.asarray(x, np.float32)
    Bn, Sq, Dm = x.shape
    assert Bn == 2 and Dm == D and Sq % 2048 == 0
    NO = Sq // 2048
    NT = Sq // T
    key = (NT, NO)
    if key not in _CACHE:
        _CACHE[key] = build(NT, NO)
    nc = _CACHE[key]
    f = lambda a: np.ascontiguousarray(np.asarray(a, np.float32))
    ident, negu, negl, bd, mask = _consts()
    own = NO * T
    shared = {
        "g1": f(np.broadcast_to(np.asarray(norm1_g, np.float32)[0][None, :], (128, D))),
        "g2": f(np.broadcast_to(np.asarray(norm2_g, np.float32)[0][None, :], (128, D))),
        "w_in": f(w_in[0]),
        "gq": f(np.tile(np.asarray(q_norm_g, np.float32)[0], 2)[:, None]),
        "gk": f(np.tile(np.asarray(k_norm_g, np.float32)[0], 2)[:, None]),
        "tb": f(_tb_table(np.asarray(rel_bias, np.float32)[0])),
        "w_a": f(w_branch_a[0]), "w_b": f(w_branch_b[0]), "w_o": f(w_out[0]),
        "w_up": f(w_ffn_up[0]),
        "cw": f(np.asarray(ffn_conv_w, np.float32)[0].reshape(3, 44, 128).transpose(2, 1, 0)),
        "cb": f(np.asarray(ffn_conv_b, np.float32)[0].reshape(44, 128).T),
        "w_dn": f(w_ffn_down[0]),
        "ident": ident, "negu": negu, "negl": negl, "bd": bd, "mask": mask,
    }
    in_maps = []
    for c in range(8):
        b, j = c // 4, c % 4
        real = (j + 1) * own
        pad = Sq - real
        xl = np.zeros((Sq, D), np.float32)
        xl[pad:] = x[b, :real]
        valid = np.zeros((Sq,), np.float32)
        valid[pad:] = 1.0
        m = dict(shared)
        m["x"] = xl
        m["valid"] = f(valid.reshape(Sq // 128, 128).T)
        in_maps.append(m)
    res = run_bass_kernel_spmd(nc, in_maps, core_ids=list(range(8)))
    out = np.empty((Bn, Sq, D), np.float32)
    for c in range(8):
        b, j = c // 4, c % 4
        out[b, j * own:(j + 1) * own] = res.results[c]["y"]
    return out
```

```python
import contextlib
import numpy as np
import concourse.bass as bass
import concourse.mybir as mybir
from concourse.bass_utils import run_bass_kernel_spmd

F32 = mybir.dt.float32
BF16 = mybir.dt.bfloat16
AF = mybir.ActivationFunctionType
ALU = mybir.AluOpType

D = 1024
KC = 8
T = 512
DFF = 2816
NFC = 22
EPS = 1e-6
NEG = -30000.0


class Sched:
    def __init__(self, engnames, sems):
        self.engnames = engnames
        self.prog = {k: [] for k in engnames}
        self.stack = {k: [self.prog[k]] for k in engnames}
        self.cond = None
        self.sems = sems
        self.free_sems = [k for k in sems if k.startswith('c')]
        self.alias = {}
        self.cnt = {}
        self.mult = {}
        for k in engnames:
            self.cnt[k] = 0
            self.mult[k] = 1
        self.cnt['flag'] = 0
        self.mult['flag'] = 1
        self.lastw = {}
        self.readers = {}
        self.waited = {}
        self.nops = 0
        self.nwaits = 0

    def chan(self, name):
        if name not in self.alias:
            s = self.free_sems.pop(0)
            self.alias[name] = s
            self.cnt[name] = 0
            self.mult[name] = 16
        return name

    def sem(self, p):
        return self.sems[self.alias.get(p, p)]

    def _deps(self, reads, writes):
        deps = {}

        def add(ps):
            p, s = ps
            if deps.get(p, 0) < s:
                deps[p] = s
        for r in reads:
            if r in self.lastw:
                add(self.lastw[r])
        for w in writes:
            if w in self.lastw:
                add(self.lastw[w])
            for rd in self.readers.get(w, ()):
                add(rd)
        return deps

    def _emit_waits(self, eng, deps, skip_self):
        wd = self.waited.setdefault(eng, {})
        for p, s in deps.items():
            if p == eng and skip_self:
                continue
            if wd.get(p, 0) >= s:
                continue
            self.stack[eng][-1].append(('w', self.sem(p), s * self.mult[p]))
            wd[p] = s
            self.nwaits += 1

    def _record(self, prod, seq, reads, writes):
        for r in reads:
            self.readers.setdefault(r, []).append((prod, seq))
        for w in writes:
            self.lastw[w] = (prod, seq)
            self.readers[w] = []

    def op(self, eng, meth, args, kw, reads=(), writes=()):
        deps = self._deps(reads, writes)
        self._emit_waits(eng, deps, skip_self=(eng == 'pe'))
        self.stack[eng][-1].append(('o', (meth, args, kw), self.sems[eng], 1))
        self.cnt[eng] += 1
        self._record(eng, self.cnt[eng], reads, writes)
        self.nops += 1

    def dma(self, ch, out, in_, reads=(), writes=(), q='sp'):
        self.chan(ch)
        deps = self._deps(reads, writes)
        self._emit_waits(q, deps, skip_self=False)
        self.stack[q][-1].append(('o', ('dma_start', (), dict(out=out, in_=in_)), self.sem(ch), 16))
        self.cnt[ch] += 1
        self._record(ch, self.cnt[ch], reads, writes)
        self.nops += 1

    def flag_op(self, meth, args, kw, reads=(), writes=()):
        deps = self._deps(reads, writes)
        self._emit_waits('dve', deps, skip_self=False)
        self.stack['dve'][-1].append(('o', (meth, args, kw), self.sems['flag'], 1))
        self.cnt['flag'] += 1
        self._record('flag', self.cnt['flag'], reads, writes)

    def begin_cond(self, flag_ap, engines):
        import copy
        if self.cond is None:
            self.cond = []
        self.cond.append(dict(flag_ap=flag_ap, engines=engines, seq=self.cnt['flag'],
                              start={e: self.cnt[e] for e in engines}, fstart=self.cnt['flag'],
                              snap=copy.deepcopy(self.waited)))
        for e in engines:
            self.stack[e].append([])

    def end_cond(self):
        c = self.cond.pop()
        nf = self.cnt['flag'] - c['fstart']
        for e in c['engines']:
            body = self.stack[e].pop()
            n = self.cnt[e] - c['start'][e]
            self.stack[e][-1].append(('if', c['flag_ap'], c['seq'], body, c['start'][e], n,
                                      nf if e == 'dve' else 0))
        self.waited = c['snap']

    def barrier(self):
        for eng in self.engnames:
            wd = self.waited.setdefault(eng, {})
            for p, c in self.cnt.items():
                if c > 0 and p != eng and wd.get(p, 0) < c:
                    self.stack[eng][-1].append(('w', self.sem(p), c * self.mult[p]))
                    wd[p] = c
        for eng in self.engnames:
            if eng != 'sp' and self.cnt[eng] > 0:
                self.stack[eng][-1].append(('w', self.sems[eng], self.cnt[eng]))

    def _replay_list(self, eng, e, lst):
        for it in lst:
            if it[0] == 'w':
                e.wait_ge(it[1], it[2])
            elif it[0] == 'o':
                meth, args, kw = it[1]
                getattr(e, meth)(*args, **kw).then_inc(it[2], it[3])
            else:
                _, flag_ap, seq, body, start, n, dve_else = it
                e.wait_ge(self.sems['flag'], seq)
                reg = self.regs[eng]
                e.reg_load(reg, flag_ap)
                with e.If_ne(reg, 0):
                    self._replay_list(eng, e, body)
                with e.Else():
                    if start > 0:
                        e.wait_ge(self.sems[eng], start)
                    if n > 0:
                        e.sem_inc(self.sems[eng], n)
                    if dve_else:
                        e.sem_inc(self.sems['flag'], dve_else)

    def replay(self, eng, e):
        assert len(self.stack[eng]) == 1
        self._replay_list(eng, e, self.prog[eng])
        self.prog[eng] = []
        self.stack[eng] = [self.prog[eng]]


def build(NT, NO):
    SL = NT * T
    NKB = SL // 128
    NQ = NO + 1
    T0 = NT - NO - 1
    nc = bass.Bass("TRN2", target_bir_lowering=False)

    def din(name, shape):
        return nc.dram_tensor(name, shape, F32, kind="ExternalInput").ap()
    x_d = din("x", [SL, D])
    valid_d = din("valid", [128, NKB])
    g1_d = din("g1", [128, D])
    g2_d = din("g2", [128, D])
    win_d = din("w_in", [D, 5120])
    gq_d = din("gq", [128, 1])
    gk_d = din("gk", [128, 1])
    tb_d = din("tb", [128, 8, 640])
    wa_d = din("w_a", [512, D])
    wb_d = din("w_b", [512, D])
    wo_d = din("w_o", [D, D])
    wup_d = din("w_up", [D, 2 * DFF])
    cw_d = din("cw", [128, 44, 3])
    cb_d = din("cb", [128, 44])
    wdn_d = din("w_dn", [DFF, D])
    ident_d = din("ident", [128, 128])
    negu_d = din("negu", [128, 128])
    negl_d = din("negl", [128, 128])
    bd_d = din("bd", [128, 128])
    mask_d = din("mask", [128, 4, T])
    y_d = nc.dram_tensor("y", [NO * T, D], F32, kind="ExternalOutput").ap()
    kt_d = nc.dram_tensor("kt_scr", [4, 128, SL], BF16).ap()
    v_d = nc.dram_tensor("v_scr", [4, 128, NKB, 128], BF16).ap()
    x1_d = nc.dram_tensor("x1_scr", [NQ * T, D], F32).ap()

    top = contextlib.ExitStack()
    with top:
        engnames = ['pe', 'act', 'dve', 'pool', 'sp']
        sems = {}
        for n in ['pe', 'act', 'dve', 'pool', 'flag']:
            sems[n] = top.enter_context(nc.semaphore(n))
        for i in range(28):
            sems[f'c{i}'] = top.enter_context(nc.semaphore(f'c{i}'))
        S = Sched(engnames, sems)
        S.regs = {'pe': nc.alloc_register(mybir.EngineType.PE, 'flag_pe'),
                  'act': nc.alloc_register(mybir.EngineType.Activation, 'flag_act'),
                  'dve': nc.alloc_register(mybir.EngineType.DVE, 'flag_dve')}

        def run_block():
            blk = nc.Block()
            with blk:
                blk.tensor(lambda e: S.replay('pe', e))
                blk.scalar(lambda e: S.replay('act', e))
                blk.vector(lambda e: S.replay('dve', e))
                blk.gpsimd(lambda e: S.replay('pool', e))
                blk.sync(lambda e: S.replay('sp', e))

        uid = [0]

        def sbt(es, name, shape, dt):
            uid[0] += 1
            return es.enter_context(nc.sbuf_tensor(f"s{uid[0]}_{name}", shape, dt))

        def pst(es, name, shape, dt=F32):
            uid[0] += 1
            return es.enter_context(nc.psum_tensor(f"p{uid[0]}_{name}", shape, dt))

        def MM(out, lhsT, rhs, start, stop, reads, writes):
            S.op('pe', 'matmul', (out,), dict(lhsT=lhsT, rhs=rhs, start=start, stop=stop), reads, writes)

        def ACTV(out, in_, func, reads, writes, **kw):
            S.op('act', 'activation', (), dict(out=out, in_=in_, func=func, **kw), reads, writes)

        def CP(eng, out, in_, reads, writes):
            S.op(eng, 'copy' if eng == 'act' else 'tensor_copy', (), dict(out=out, in_=in_), reads, writes)

        def TT(eng, out, in0, in1, op, reads, writes):
            S.op(eng, 'tensor_tensor', (), dict(out=out, in0=in0, in1=in1, op=op), reads, writes)

        def STT(eng, out, in0, scalar, in1, op0, op1, reads, writes):
            S.op(eng, 'scalar_tensor_tensor', (), dict(out=out, in0=in0, scalar=scalar, in1=in1, op0=op0, op1=op1), reads, writes)

        def TS(eng, out, in0, s1, s2, op0, op1, reads, writes):
            kw = dict(out=out, in0=in0, scalar1=s1, scalar2=s2, op0=op0)
            if op1 is not None:
                kw['op1'] = op1
            S.op(eng, 'tensor_scalar', (), kw, reads, writes)

        def MS(eng, ap, val, writes):
            S.op(eng, 'memset', (ap, val), {}, (), writes)

        ident = sbt(top, "ident", [128, 128], BF16)
        negu = sbt(top, "negu", [128, 128], BF16)
        negl = sbt(top, "negl", [128, 128], BF16)
        bdm = sbt(top, "bdm", [128, 128], BF16)
        zero = sbt(top, "zero", [128, 128], BF16)
        cst = sbt(top, "cst", [128, 4], F32)

        stg_state = {'i': 0}

        def load_weight(stg, dst_ap, src_ap, n, wname):
            sl = stg_state['i'] % 2
            stg_state['i'] += 1
            S.dma(f'stg{sl}', stg[:, sl, 0:n], src_ap, writes=[f'stg{sl}'])
            CP('pool', dst_ap, stg[:, sl, 0:n], [f'stg{sl}'], [wname])

        def norm_a(Bf, src_ap, src_reads, slot, gb):
            xs, junk, ss, hn = Bf['xs'], Bf['junk'], Bf['ss'], Bf['hn']
            xn, hnn, ssn = f'xs{slot}', f'hn{slot}', f'ss{slot}'
            S.dma(xn, xs[:, slot, :], src_ap, reads=src_reads, writes=[xn])
            MS('pool', ss[:, slot, 0:1], 0.0, [ssn])
            ACTV(junk[:, :], xs[:, slot, :], AF.Square, [xn], ['junk', ssn], accum_out=ss[:, slot, 0:1])
            ACTV(ss[:, slot, 1:2], ss[:, slot, 0:1], AF.Ln, [ssn], [ssn], scale=1.0 / D, bias=cst[:, 0:1])
            ACTV(ss[:, slot, 2:3], ss[:, slot, 1:2], AF.Exp, [ssn], [ssn], scale=-0.5)
            STT('dve', hn[:, slot, :], xs[:, slot, :], ss[:, slot, 2:3], gb[:, :], ALU.mult, ALU.mult, [xn, ssn], [hnn])

        def norm_b(Bf, slot, hnT_ap, hnT_name, tp, tp_name, ev='act'):
            hn = Bf['hn']
            for kc in range(KC):
                S.op('pe', 'transpose', (tp[:, kc, :], hn[:, slot, kc * 128:(kc + 1) * 128], ident[:, :]), {}, [f'hn{slot}'], [tp_name])
            CP(ev, hnT_ap, tp[:, :, :], [tp_name], [hnT_name])

        def norm_block(Bf, src_ap, src_reads, slot, gb, hnT_ap, hnT_name, tp, tp_name):
            norm_a(Bf, src_ap, src_reads, slot, gb)
            norm_b(Bf, slot, hnT_ap, hnT_name, tp, tp_name)

        def new_B(es, ns=2):
            return dict(xs=sbt(es, "xs", [128, ns, D], F32), junk=sbt(es, "junk", [128, D], BF16),
                        ss=sbt(es, "ss", [128, ns, 4], F32), hn=sbt(es, "hn", [128, ns, D], BF16))

        sc04 = contextlib.ExitStack()
        with sc04:
            g1b = sbt(sc04, "g1b", [128, D], F32)
            vt = sbt(sc04, "vt", [128, NKB], F32)
            OT = sbt(sc04, "OT", [128, 4, NQ * T], BF16)

            with contextlib.ExitStack() as es:
                i32 = sbt(es, "i32", [128, 4, 128], F32)
                for i, srcd in enumerate([ident_d, negu_d, negl_d, bd_d]):
                    S.dma('cst', i32[:, i, :], srcd[:, :], writes=[f'i32_{i}'])
                S.barrier()
                for i, dst in enumerate([ident, negu, negl, bdm]):
                    CP('pool', dst[:, :], i32[:, i, :], [f'i32_{i}'], [f'const{i}'])
                MS('pool', zero[:, :], 0.0, ['zero'])
                MS('pool', cst[:, 0:1], EPS, ['cst'])
                MS('pool', cst[:, 1:2], 1.0, ['cst'])
                MS('pool', OT[:, :, 0:T], 0.0, ['OTz'])
                S.dma('cst', g1b[:, :], g1_d[:, :], writes=['g1b'])
                S.dma('cst', vt[:, :], valid_d[:, :], writes=['vt'])
                S.barrier()
                run_block()

            sc12 = contextlib.ExitStack()
            with sc12:
                QT = sbt(sc12, "QT", [128, 4, NQ * T], BF16)
                with contextlib.ExitStack() as es:
                    wB = sbt(es, "wB", [128, KC, 1536], BF16)
                    stg = sbt(es, "stg1", [128, 2, 1536], F32)
                    Bf = new_B(es, 4)
                    hnT = sbt(es, "hnT", [128, 2, KC, T], BF16)
                    KTs = sbt(es, "KTs", [128, 2, 4, T], BF16)
                    Vs = sbt(es, "Vs", [128, 2, 4, T], BF16)
                    tps = [pst(es, f"tp{i}", [128, KC, 128], BF16) for i in range(2)]
                    accs = [pst(es, f"acc{i}", [128, T], F32) for i in range(4)]
                    for kc in range(KC):
                        load_weight(stg, wB[:, kc, :], win_d[kc * 128:(kc + 1) * 128, 1536:3072], 1536, 'wB')
                    acc_i = 0
                    ev_i = 0

                    def nA(t, bi):
                        blk_i = t * 4 + bi
                        norm_a(Bf, x_d[blk_i * 128:(blk_i + 1) * 128, :], [], blk_i % 4, g1b)

                    def nB(t, bi):
                        blk_i = t * 4 + bi
                        hs_ = t % 2
                        norm_b(Bf, blk_i % 4, hnT[:, hs_, :, bi * 128:(bi + 1) * 128], f'hnT{hs_}_{bi}', tps[blk_i % 2], f'tp{blk_i % 2}',
                               ev=('act' if bi % 2 == 0 else 'dve'))

                    def grpK(t, p):
                        nonlocal acc_i, ev_i
                        hs = t % 2
                        hread = [f'hnT{hs}_{bi}' for bi in range(4)]
                        a = acc_i % 4
                        acc_i += 1
                        for kc in range(KC):
                            MM(accs[a][:, :], wB[:, kc, 512 + p * 128:512 + (p + 1) * 128], hnT[:, hs, kc, :],
                               kc == 0, kc == KC - 1, ['wB'] + hread, [f'acc{a}'])
                        CP('dve' if ev_i % 2 == 0 else 'act', KTs[:, hs, p, :], accs[a][:, :], [f'acc{a}'], [f'KTs{hs}'])
                        ev_i += 1
                        if p == 3:
                            S.dma(f'kts{hs}', kt_d[:, :, t * T:(t + 1) * T].rearrange("q p c -> p q c"), KTs[:, hs, :, :],
                                  reads=[f'KTs{hs}'], writes=['kt_d'], q='pool')

                    def grpV(t, bi):
                        nonlocal acc_i, ev_i
                        hs = t % 2
                        a = acc_i % 4
                        acc_i += 1
                        for kc in range(KC):
                            MM(accs[a][:, :], hnT[:, hs, kc, bi * 128:(bi + 1) * 128], wB[:, kc, 1024:1536],
                               kc == 0, kc == KC - 1, ['wB', f'hnT{hs}_{bi}'], [f'acc{a}'])
                        CP('dve' if ev_i % 2 == 0 else 'act', Vs[:, hs, bi, :], accs[a][:, :], [f'acc{a}'], [f'Vs{hs}'])
                        ev_i += 1
                        if bi == 3:
                            for q in range(4):
                                S.dma(f'vs{hs}', v_d[q, :, t * 4:(t + 1) * 4, :], Vs[:, hs, :, q * 128:(q + 1) * 128],
                                      reads=[f'Vs{hs}'], writes=['v_d'], q='pool')

                    def grpQ(t, p):
                        nonlocal acc_i, ev_i
                        hs = t % 2
                        qi = t - T0
                        hread = [f'hnT{hs}_{bi}' for bi in range(4)]
                        a = acc_i % 4
                        acc_i += 1
                        for kc in range(KC):
                            MM(accs[a][:, :], wB[:, kc, p * 128:(p + 1) * 128], hnT[:, hs, kc, :],
                               kc == 0, kc == KC - 1, ['wB'] + hread, [f'acc{a}'])
                        CP('dve' if ev_i % 2 == 0 else 'act', QT[:, p, qi * T:(qi + 1) * T], accs[a][:, :], [f'acc{a}'], [f'QT{p}_{qi}'])
                        ev_i += 1

                    for bi in range(4):
                        nA(0, bi)
                        nB(0, bi)
                    for t in range(NT):
                        nx = t + 1 < NT
                        if nx:
                            nA(t + 1, 0)
                            nA(t + 1, 1)
                        grpK(t, 0)
                        if nx:
                            nB(t + 1, 0)
                        grpK(t, 1)
                        if nx:
                            nA(t + 1, 2)
                        grpK(t, 2)
                        if nx:
                            nB(t + 1, 1)
                        grpK(t, 3)
                        if nx:
                            nA(t + 1, 3)
                        grpV(t, 0)
                        if nx:
                            nB(t + 1, 2)
                        grpV(t, 1)
                        grpV(t, 2)
                        if nx:
                            nB(t + 1, 3)
                        grpV(t, 3)
                        if t >= T0:
                            for p in range(4):
                                grpQ(t, p)
                    S.barrier()
                    run_block()

                with contextlib.ExitStack() as es:
                    KTp = sbt(es, "KTp", [128, SL], BF16)
                    Vp = sbt(es, "Vp", [128, NKB, 128], BF16)
                    M = sbt(es, "M", [128, 4, 2, T], BF16)
                    NE, NSP, NXW, NW = 4, 3, 2, 2
                    eb = sbt(es, "eb", [128, NE, 2, T], F32)
                    spb = sbt(es, "spb", [128, NSP, 2, T], BF16)
                    xwb = sbt(es, "xwb", [128, NXW, 2, T], F32)
                    wbuf = sbt(es, "wbuf", [128, NW, 2, T], BF16)
                    m32 = sbt(es, "m32", [128, 4, T], F32)
                    Zp = [pst(es, f"Z{i}", [128, 2, T]) for i in range(2)]
                    Ap = pst(es, "A", [128, 2, T])
                    Op = pst(es, "O", [128, T])
                    S.dma('cst', m32[:, :, :], mask_d[:, :, :], writes=['m32'])
                    for h in range(2):
                        CP('pool', M[:, :, h, :], m32[:, :, :], ['m32'], ['M'])
                    I32 = mybir.dt.int32
                    flagbuf = sbt(es, "flagbuf", [128, 512], I32)
                    mx = sbt(es, "mx", [128, 4], F32)
                    THRESH = 150.0
                    CH = 32
                    nchk = (NKB + CH - 1) // CH
                    itn = [0]
                    fidx = [0]

                    def load_pair(p):
                        for c in reversed(range(nchk)):
                            k0, k1 = c * CH, min(NKB, (c + 1) * CH)
                            S.dma(f'ktp{c}', KTp[:, k0 * 128:k1 * 128], kt_d[p, :, k0 * 128:k1 * 128], reads=['kt_d'], writes=[f'KTp{c}'])
                            S.dma(f'vp{c}', Vp[:, k0:k1, :], v_d[p, :, k0:k1, :], reads=['v_d'], writes=[f'Vp{c}'])

                    def st1(it):
                        W, kb, p, qi, c0, n = it['W'], it['kb'], it['p'], it['qi'], it['c0'], it['n']
                        for h in range(2):
                            MM(Zp[n % 2][:, h, 0:W], KTp[64 * h:64 * h + 64, kb * 128:(kb + 1) * 128],
                               QT[64 * h:64 * h + 64, p, qi * T + c0:qi * T + c0 + W], True, True,
                               [f'KTp{kb // CH}', f'QT{p}_{qi}'], [f'Z{n % 2}'])

                    def st2(it):
                        W, n = it['W'], it['n']
                        en = f'e{n % NE}'
                        ACTV(eb[:, n % NE, :, 0:W], Zp[n % 2][:, :, 0:W], AF.Exp, [f'Z{n % 2}'], [en], scale=0.125)
                        if it['d'] is not None:
                            TT('dve', eb[:, n % NE, :, 0:W], eb[:, n % NE, :, 0:W], M[:, it['d'], :, 0:W], ALU.mult, [en, 'M'], [en])

                    def st3(it):
                        W, n = it['W'], it['n']
                        ACTV(spb[:, n % NSP, :, 0:W], eb[:, n % NE, :, 0:W], AF.Ln, [f'e{n % NE}'], [f'sp{n % NSP}'], bias=cst[:, 1:2])

                    def st4(it):
                        W, n = it['W'], it['n']
                        for h in range(2):
                            MM(Ap[:, h, 0:W], negu[:, :], spb[:, n % NSP, h, 0:W], it['first'], False,
                               [f'sp{n % NSP}', 'const1'], ['A'])

                    def st5(it):
                        W, n = it['W'], it['n']
                        ACTV(xwb[:, n % NXW, :, 0:W], Ap[:, :, 0:W], AF.Exp, ['A'], [f'xw{n % NXW}'])

                    def st6(it):
                        W, n = it['W'], it['n']
                        if it['last']:
                            return
                        for h in range(2):
                            MM(Ap[:, h, 0:W], negl[:, :], spb[:, n % NSP, h, 0:W], False, False,
                               [f'sp{n % NSP}', 'const2'], ['A'])

                    def st7(it):
                        W, n = it['W'], it['n']
                        TT('dve', wbuf[:, n % NW, :, 0:W], eb[:, n % NE, :, 0:W], xwb[:, n % NXW, :, 0:W], ALU.mult,
                           [f'e{n % NE}', f'xw{n % NXW}'], [f'w{n % NW}'])

                    def st8(it):
                        W, n, kb = it['W'], it['n'], it['kb']
                        for h in range(2):
                            MM(Op[64 * h:64 * h + 64, 0:W], Vp[:, kb, 64 * h:64 * h + 64], wbuf[:, n % NW, h, 0:W],
                               it['first'], it['last'], [f'w{n % NW}', f'Vp{kb // CH}'], ['O'])

                    stages = [(st1, 0), (st2, 1), (st3, 2), (st6, 4), (st4, 3), (st5, 3), (st7, 4), (st8, 5)]

                    def emit_segment(items):
                        nit = len(items)
                        for s_ in range(nit + 7):
                            for fn, dly in stages:
                                i_ = s_ - dly
                                if 0 <= i_ < nit:
                                    fn(items[i_])

                    def emit_flag(W):
                        fi = fidx[0]
                        fidx[0] += 1
                        for h in range(2):
                            S.op('dve', 'tensor_reduce', (), dict(out=mx[0:1, h:h + 1], in_=Ap[0:1, h, 0:W], axis=mybir.AxisListType.X, op=ALU.max),
                                 ['A'], [f'mx{h}'])
                        TT('dve', mx[0:1, 2:3], mx[0:1, 0:1], mx[0:1, 1:2], ALU.max, ['mx0', 'mx1'], ['mx2'])
                        S.flag_op('tensor_scalar', (), dict(out=flagbuf[0:1, fi:fi + 1], in0=mx[0:1, 2:3], scalar1=-THRESH, scalar2=None, op0=ALU.is_gt),
                                  ['mx2'], [f'flag{fi}'])
                        return fi

                    for p in range(4):
                        load_pair(p)
                        for qi in range(NQ):
                            g = T0 + qi
                            if qi == 0:
                                W, c0, q0, ndiag = 128, 384, g * T + 384, 1
                            else:
                                W, c0, q0, ndiag = T, 0, g * T, 4
                            kb_hi = (q0 + W) // 128 - 1
                            nb = kb_hi + 1
                            segs = [list(range(0, min(nb, ndiag + 3)))]
                            sz = 2
                            while segs[-1][-1] + 1 < nb:
                                st_ = segs[-1][-1] + 1
                                segs.append(list(range(st_, min(nb, st_ + sz))))
                                if len(segs) > 2:
                                    sz *= 2
                            nopen = 0
                            for si, seg in enumerate(segs):
                                items = []
                                for b in seg:
                                    kb = kb_hi - b
                                    d = kb - q0 // 128
                                    items.append(dict(p=p, qi=qi, W=W, c0=c0, kb=kb, d=(d if d >= 0 else None),
                                                      first=(b == 0), last=(b == nb - 1), n=itn[0]))
                                    itn[0] += 1
                                emit_segment(items)
                                if si < len(segs) - 1:
                                    fi = emit_flag(W)
                                    S.begin_cond(flagbuf[0:1, fi:fi + 1], ['pe', 'act', 'dve'])
                                    nopen += 1
                            for _ in range(nopen):
                                S.end_cond()
                            CP('dve', OT[:, p, qi * T + c0:qi * T + c0 + W], Op[:, 0:W], ['O', 'OTz'], [f'OT{p}_{qi}'])
                    S.barrier()
                    run_block()

            sc34 = contextlib.ExitStack()
            with sc34:
                OAT = sbt(sc34, "OAT", [128, 4, NQ * T], BF16)
                with contextlib.ExitStack() as es:
                    wA = sbt(es, "wA", [128, KC, 1536], BF16)
                    with contextlib.ExitStack() as es2:
                        stg = sbt(es2, "stg3", [128, 2, 1536], F32)
                        for kc in range(KC):
                            load_weight(stg, wA[:, kc, :], win_d[kc * 128:(kc + 1) * 128, 0:1536], 1536, 'wA')
                        S.barrier()
                    TB = sbt(es, "TB", [128, 8, 640], F32)
                    gq = sbt(es, "gq", [128, 1], F32)
                    gk = sbt(es, "gk", [128, 1], F32)
                    Bf = new_B(es)
                    hnT = sbt(es, "hnT", [128, KC, T], BF16)
                    QAT = sbt(es, "QAT", [128, 4, T], BF16)
                    KAT = sbt(es, "KAT", [128, 4, 2, T], BF16)
                    VA = sbt(es, "VA", [128, 2, 4, T], BF16)
                    VLD = sbt(es, "VLD", [128, 2, 4, 64], BF16)
                    sqb = sbt(es, "sqb", [128, 2, T], BF16)
                    qgb = sbt(es, "qgb", [128, 2, T], F32)
                    lnb = sbt(es, "lnb", [128, 2, T], F32)
                    sbb = sbt(es, "sbb", [128, 2, 2, T], F32)
                    pTb = sbt(es, "pTb", [128, 2, 2, T], BF16)
                    rdb = sbt(es, "rdb", [128, T], F32)
                    tp3 = pst(es, "tp3", [128, KC, 128], BF16)
                    acc3t = pst(es, "acc3", [128, 2, T])
                    acc3 = [acc3t[:, i, :] for i in range(2)]
                    SSp = pst(es, "SSp", [128, T])
                    SPSt = pst(es, "SPS", [128, 2, T])
                    sbufs = [(SPSt, 'SPS'), (acc3t, 'acc3_')]
                    OAp = pst(es, "OAp", [128, T])
                    DENp = pst(es, "DENp", [128, T])
                    S.dma('cst', TB[:, :, :], tb_d[:, :, :], writes=['TB'])
                    S.dma('cst', gq[:, :], gq_d[:, :], writes=['gq'])
                    S.dma('cst', gk[:, :], gk_d[:, :], writes=['gk'])
                    S.barrier()
                    blkc = 0
                    acc_i = 0
                    nrm = [0]
                    jj = [0]

                    def qknorm(a, gcol, gname, dst_ap, dst_name):
                        k = nrm[0] % 2
                        nrm[0] += 1
                        ACTV(sqb[:, k, :], acc3[a][:, :], AF.Square, [f'acc3_{a}'], [f'sqb{k}'])
                        S.op('act', 'mul', (), dict(out=qgb[:, k, :], in_=acc3[a][:, :], mul=gcol[:, 0:1]), [f'acc3_{a}', gname], [f'qgb{k}'])
                        MM(SSp[:, :], bdm[:, :], sqb[:, k, :], True, True, [f'sqb{k}', 'const3'], ['SSp'])
                        ACTV(lnb[:, k, :], SSp[:, :], AF.Ln, ['SSp'], [f'lnb{k}'], scale=1.0 / 64, bias=cst[:, 0:1])
                        ACTV(lnb[:, k, :], lnb[:, k, :], AF.Exp, [f'lnb{k}'], [f'lnb{k}'], scale=-0.5)
                        TT('dve', dst_ap, qgb[:, k, :], lnb[:, k, :], ALU.mult, [f'qgb{k}', f'lnb{k}'], [dst_name])

                    for t in range(T0 - 1, NT):
                        qi = t - T0
                        sl = t % 2
                        sl0 = blkc
                        norm_a(Bf, x_d[t * 512:t * 512 + 128, :], [], sl0 % 2, g1b)
                        for bi in range(4):
                            if bi + 1 < 4:
                                norm_a(Bf, x_d[(t * 4 + bi + 1) * 128:(t * 4 + bi + 2) * 128, :], [], (sl0 + bi + 1) % 2, g1b)
                            norm_b(Bf, (sl0 + bi) % 2, hnT[:, :, bi * 128:(bi + 1) * 128], f'hnT_{bi}', tp3, 'tp3')
                            blkc += 1
                        hread = [f'hnT_{bi}' for bi in range(4)]
                        for p in range(4):
                            a = acc_i % 2
                            acc_i += 1
                            for kc in range(KC):
                                MM(acc3[a][:, :], wA[:, kc, 512 + p * 128:512 + (p + 1) * 128], hnT[:, kc, :],
                                   kc == 0, kc == KC - 1, ['wA'] + hread, [f'acc3_{a}'])
                            qknorm(a, gk, 'gk', KAT[:, p, sl, :], f'KAT{sl}_{p}')
                        for bi in range(4):
                            a = acc_i % 2
                            acc_i += 1
                            for kc in range(KC):
                                MM(acc3[a][:, :], hnT[:, kc, bi * 128:(bi + 1) * 128], wA[:, kc, 1024:1536],
                                   kc == 0, kc == KC - 1, ['wA', f'hnT_{bi}'], [f'acc3_{a}'])
                            CP('dve', VA[:, sl, bi, :], acc3[a][:, :], [f'acc3_{a}'], [f'VA{sl}_{bi}'])
                            CP('pool', VLD[:, sl, bi, :], vt[:, t * 4 + bi:t * 4 + bi + 1].to_broadcast([128, 64]),
                               ['vt'], [f'VLD{sl}_{bi}'])
                        if qi < 0:
                            continue
                        for p in range(4):
                            a = acc_i % 2
                            acc_i += 1
                            for kc in range(KC):
                                MM(acc3[a][:, :], wA[:, kc, p * 128:(p + 1) * 128], hnT[:, kc, :],
                                   kc == 0, kc == KC - 1, ['wA'] + hread, [f'acc3_{a}'])
                            qknorm(a, gq, 'gq', QAT[:, p, :], f'QAT{p}')
                        for p in range(4):
                            MM(OAp[:, :], zero[:, :], hnT[:, 0, :], True, False, ['zero'] + hread, ['OAp'])
                            MM(DENp[:, :], zero[:, :], hnT[:, 0, :], True, False, ['zero'] + hread, ['DENp'])
                            def jgeom(j):
                                i_lo, i_hi = max(0, j - 4), min(3, j)
                                N = (i_hi - i_lo + 1) * 128
                                ksl = (1 - sl) if j < 4 else sl
                                return i_lo, N, ksl, j % 4, (4 - j + i_lo) * 128

                            def emitS(j, k):
                                i_lo, N, ksl, cj, tb0 = jgeom(j)
                                spt, spn = sbufs[k]
                                for h in range(2):
                                    MM(spt[:, h, 0:N], KAT[64 * h:64 * h + 64, p, ksl, cj * 128:(cj + 1) * 128],
                                       QAT[64 * h:64 * h + 64, p, i_lo * 128:i_lo * 128 + N], True, True,
                                       [f'KAT{ksl}_{p}', f'QAT{p}'], [f'{spn}{h}'])

                            kbase = jj[0]
                            jj[0] += 8
                            emitS(0, kbase % 2)
                            for j in range(8):
                                i_lo, N, ksl, cj, tb0 = jgeom(j)
                                k = (kbase + j) % 2
                                spt, spn = sbufs[k]
                                if j + 1 < 8:
                                    emitS(j + 1, (kbase + j + 1) % 2)
                                STT('dve', sbb[:, k, :, 0:N], spt[:, :, 0:N], 0.125, TB[:, 2 * p:2 * p + 2, tb0:tb0 + N], ALU.mult, ALU.add,
                                    [f'{spn}0', f'{spn}1', 'TB'], [f'sbb{k}'])
                                ACTV(pTb[:, k, :, 0:N], sbb[:, k, :, 0:N], AF.Exp, [f'sbb{k}'], [f'pTb{k}'])
                                for h in range(2):
                                    hh = 2 * p + h
                                    MM(OAp[64 * h:64 * h + 64, i_lo * 128:i_lo * 128 + N], VA[:, ksl, cj, hh * 64:(hh + 1) * 64],
                                       pTb[:, k, h, 0:N], False, False, [f'pTb{k}', f'VA{ksl}_{cj}'], ['OAp'])
                                    MM(DENp[64 * h:64 * h + 64, i_lo * 128:i_lo * 128 + N], VLD[:, ksl, cj, :],
                                       pTb[:, k, h, 0:N], False, False, [f'pTb{k}', f'VLD{ksl}_{cj}'], ['DENp'])
                            TS('dve', rdb[:, :], DENp[:, :], 1e-30, None, ALU.max, None, ['DENp'], ['rdb'])
                            ACTV(rdb[:, :], rdb[:, :], AF.Ln, ['rdb'], ['rdb'])
                            ACTV(rdb[:, :], rdb[:, :], AF.Exp, ['rdb'], ['rdb'], scale=-1.0)
                            TT('dve', OAT[:, p, qi * T:(qi + 1) * T], OAp[:, :], rdb[:, :], ALU.mult, ['OAp', 'rdb'], [f'OAT{p}_{qi}'])
                    S.barrier()
                    run_block()

                with contextlib.ExitStack() as es:
                    wG = sbt(es, "wG", [128, KC, 2048], BF16)
                    wbra = sbt(es, "wbra", [128, 4, D], BF16)
                    wbrb = sbt(es, "wbrb", [128, 4, D], BF16)
                    wout = sbt(es, "wout", [128, KC, D], BF16)
                    stg = sbt(es, "stg4", [128, 2, D], F32)
                    Bf = new_B(es)
                    xr = sbt(es, "xr", [128, 2, D], F32)
                    hnT = sbt(es, "hnT", [128, KC, T], BF16)
                    sg = sbt(es, "sg", [128, 2, T], F32)
                    mm = sbt(es, "mm", [128, 2, T], F32)
                    MT = sbt(es, "MT", [128, KC, T], BF16)
                    tp4 = pst(es, "tp4", [128, KC, 128], BF16)
                    Gp = [pst(es, f"G{i}", [128, T]) for i in range(2)]
                    Yp = [pst(es, f"Y{i}", [128, T]) for i in range(2)]
                    Xp = [pst(es, f"X{i}", [128, T]) for i in range(2)]
                    for kc in range(KC):
                        for hf in range(2):
                            load_weight(stg, wG[:, kc, hf * D:(hf + 1) * D], win_d[kc * 128:(kc + 1) * 128, 3072 + hf * D:3072 + (hf + 1) * D], D, 'wG')
                    for p in range(4):
                        load_weight(stg, wbra[:, p, :], wa_d[p * 128:(p + 1) * 128, :], D, 'wbra')
                        load_weight(stg, wbrb[:, p, :], wb_d[p * 128:(p + 1) * 128, :], D, 'wbrb')
                    for kc in range(KC):
                        load_weight(stg, wout[:, kc, :], wo_d[kc * 128:(kc + 1) * 128, :], D, 'wout')
                    blkc = 0
                    ob = 0
                    for qi in range(NQ):
                        t = T0 + qi
                        bis = [3] if qi == 0 else [0, 1, 2, 3]
                        cs, cn = (384, 128) if qi == 0 else (0, T)
                        sl0 = blkc
                        norm_a(Bf, x_d[(t * 4 + bis[0]) * 128:(t * 4 + bis[0] + 1) * 128, :], [], sl0 % 2, g1b)
                        for ii, bi in enumerate(bis):
                            if ii + 1 < len(bis):
                                nb_ = bis[ii + 1]
                                norm_a(Bf, x_d[(t * 4 + nb_) * 128:(t * 4 + nb_ + 1) * 128, :], [], (sl0 + ii + 1) % 2, g1b)
                            norm_b(Bf, (sl0 + ii) % 2, hnT[:, :, bi * 128:(bi + 1) * 128], f'hnT_{bi}', tp4, 'tp4')
                            blkc += 1
                        hread = [f'hnT_{bi}' for bi in bis]
                        for oc in range(KC):
                            for br in range(2):
                                for kc in range(KC):
                                    MM(Gp[br][:, cs:cs + cn], wG[:, kc, br * D + oc * 128:br * D + (oc + 1) * 128], hnT[:, kc, cs:cs + cn],
                                       kc == 0, kc == KC - 1, ['wG'] + hread, [f'G{br}'])
                                ACTV(sg[:, br, cs:cs + cn], Gp[br][:, cs:cs + cn], AF.Sigmoid, [f'G{br}'], [f'sg{br}'])
                                wsrc = wbra if br == 0 else wbrb
                                osrc = OAT if br == 0 else OT
                                for p in range(4):
                                    MM(Yp[br][:, cs:cs + cn], wsrc[:, p, oc * 128:(oc + 1) * 128], osrc[:, p, qi * T + cs:qi * T + cs + cn],
                                       p == 0, p == 3,
                                       ['wbra' if br == 0 else 'wbrb', (f'OAT{p}_{qi}' if br == 0 else f'OT{p}_{qi}'), 'OTz'], [f'Y{br}'])
                                TT('dve', mm[:, br, cs:cs + cn], Yp[br][:, cs:cs + cn], sg[:, br, cs:cs + cn], ALU.mult, [f'Y{br}', f'sg{br}'], [f'mm{br}'])
                            TT('pool', MT[:, oc, cs:cs + cn], mm[:, 0, cs:cs + cn], mm[:, 1, cs:cs + cn], ALU.add, ['mm0', 'mm1'], [f'MT{oc}'])
                        mread = [f'MT{oc}' for oc in range(KC)]
                        for bi in bis:
                            blk_i = t * 4 + bi
                            o = ob % 2
                            ob += 1
                            S.dma(f'xr{o}', xr[:, o, :], x_d[blk_i * 128:(blk_i + 1) * 128, :], writes=[f'xr{o}'])
                            for half in range(2):
                                for oc in range(KC):
                                    MM(Xp[half][:, :], MT[:, oc, bi * 128:(bi + 1) * 128], wout[:, oc, half * T:(half + 1) * T],
                                       oc == 0, oc == KC - 1, ['wout'] + mread, [f'X{half}'])
                                TT('dve', xr[:, o, half * T:(half + 1) * T], Xp[half][:, :], xr[:, o, half * T:(half + 1) * T], ALU.add,
                                   [f'X{half}', f'xr{o}'], [f'xr{o}'])
                            row = qi * T + bi * 128
                            S.dma(f'x1w{o}', x1_d[row:row + 128, :], xr[:, o, :], reads=[f'xr{o}'], writes=['x1_d'], q='pool')
                    S.barrier()
                    run_block()

        with contextlib.ExitStack() as es:
            wup = sbt(es, "wup", [128, KC, 2 * DFF], BF16)
            wdn = sbt(es, "wdn", [128, NFC, D], BF16)
            g2b = sbt(es, "g2b", [128, D], F32)
            cw = sbt(es, "cw", [128, 44, 3], F32)
            cb = sbt(es, "cb", [128, 44], F32)
            HALO = sbt(es, "HALO", [128, 44, 2], F32)
            with contextlib.ExitStack() as es2:
                stg = sbt(es2, "stg5", [128, 2, 2816], F32)
                for kc in range(KC):
                    for hf in range(2):
                        load_weight(stg, wup[:, kc, hf * DFF:(hf + 1) * DFF], wup_d[kc * 128:(kc + 1) * 128, hf * DFF:(hf + 1) * DFF], DFF, 'wup')
                for fc in range(NFC):
                    load_weight(stg, wdn[:, fc, :], wdn_d[fc * 128:(fc + 1) * 128, :], D, 'wdn')
                S.dma('cst', g2b[:, :], g2_d[:, :], writes=['g2b'])
                S.dma('cst', cw[:, :, :], cw_d[:, :, :], writes=['cw'])
                S.dma('cst', cb[:, :], cb_d[:, :], writes=['cb'])
                MS('pool', HALO[:, :, :], 0.0, ['HALO'])
                S.barrier()
            Bf = dict(xs=sbt(es, "xs", [128, 2, D], F32), junk=sbt(es, "junk", [128, D], BF16),
                      ss=sbt(es, "ss", [128, 2, 4], F32), hn=sbt(es, "hn", [128, 2, D], BF16))
            hnT = sbt(es, "hnT", [128, KC, T], BF16)
            hb = sbt(es, "hb", [128, 2, 2, T + 2], F32)
            cv = sbt(es, "cv", [128, 2, T], F32)
            sgl = sbt(es, "sgl", [128, T], F32)
            ACTT = sbt(es, "ACTT", [128, NFC, T], BF16)
            yo = sbt(es, "yo", [128, D], F32)
            tp5 = pst(es, "tp5", [128, KC, 128], BF16)
            Hp = [[pst(es, f"H{b}_{u}", [128, T]) for u in range(2)] for b in range(2)]
            Yd = [pst(es, f"Yd{i}", [128, T]) for i in range(2)]
            blkc = 0
            fcc = 0
            for qi in range(NQ):
                bis = [3] if qi == 0 else [0, 1, 2, 3]
                sl0 = blkc
                norm_a(Bf, x1_d[qi * T + bis[0] * 128:qi * T + bis[0] * 128 + 128, :], ['x1_d'], sl0 % 2, g2b)
                for ii, bi in enumerate(bis):
                    if ii + 1 < len(bis):
                        r2 = qi * T + bis[ii + 1] * 128
                        norm_a(Bf, x1_d[r2:r2 + 128, :], ['x1_d'], (sl0 + ii + 1) % 2, g2b)
                    norm_b(Bf, (sl0 + ii) % 2, hnT[:, :, bi * 128:(bi + 1) * 128], f'hnT_{bi}', tp5, 'tp5')
                    blkc += 1
                hread = [f'hnT_{bi}' for bi in bis]
                for fc in range(NFC):
                    b = fcc % 2
                    fcc += 1
                    for u in range(2):
                        ch = fc + u * NFC
                        hbn = f'hb{b}_{u}'
                        if qi == 0:
                            for kc in range(KC):
                                MM(Hp[b][u][:, 384:T], wup[:, kc, ch * 128:(ch + 1) * 128], hnT[:, kc, 384:T],
                                   kc == 0, kc == KC - 1, ['wup'] + hread, [f'H{b}_{u}'])
                            CP('act', HALO[:, ch, :], Hp[b][u][:, T - 2:T], [f'H{b}_{u}'], [f'HALO{ch}'])
                            continue
                        for kc in range(KC):
                            MM(Hp[b][u][:, :], wup[:, kc, ch * 128:(ch + 1) * 128], hnT[:, kc, :],
                               kc == 0, kc == KC - 1, ['wup'] + hread, [f'H{b}_{u}'])
                        CP('pool', hb[:, b, u, 0:2], HALO[:, ch, :], [f'HALO{ch}'], [hbn])
                        CP('act', hb[:, b, u, 2:T + 2], Hp[b][u][:, :], [f'H{b}_{u}'], [hbn])
                        CP('pool', HALO[:, ch, :], hb[:, b, u, T:T + 2], [hbn], [f'HALO{ch}'])
                        ce = 'dve'
                        cvn = f'cv{u}'
                        ACTV(cv[:, u, :], Hp[b][u][:, :], AF.Identity, [f'H{b}_{u}', 'cw', 'cb'], [cvn], scale=cw[:, ch, 2:3], bias=cb[:, ch:ch + 1])
                        STT(ce, cv[:, u, :], hb[:, b, u, 1:T + 1], cw[:, ch, 1:2], cv[:, u, :], ALU.mult, ALU.add, [hbn, cvn, 'cw'], [cvn])
                        STT(ce, cv[:, u, :], hb[:, b, u, 0:T], cw[:, ch, 0:1], cv[:, u, :], ALU.mult, ALU.add, [hbn, cvn, 'cw'], [cvn])
                    if qi == 0:
                        continue
                    ACTV(sgl[:, :], cv[:, 0, :], AF.Silu, ['cv0'], ['sgl'])
                    TT('dve', ACTT[:, fc, :], sgl[:, :], cv[:, 1, :], ALU.mult, ['sgl', 'cv1'], [f'ACTT{fc}'])
                if qi == 0:
                    continue
                aread = [f'ACTT{fc}' for fc in range(NFC)]
                for bi in range(4):
                    row = qi * T + bi * 128
                    S.dma('yo_in', yo[:, :], x1_d[row:row + 128, :], reads=['x1_d', 'y_d'], writes=['yo'])
                    for half in range(2):
                        for fc in range(NFC):
                            MM(Yd[half][:, :], ACTT[:, fc, bi * 128:(bi + 1) * 128], wdn[:, fc, half * T:(half + 1) * T],
                               fc == 0, fc == NFC - 1, ['wdn'] + aread, [f'Yd{half}'])
                        TT('dve', yo[:, half * T:(half + 1) * T], Yd[half][:, :], yo[:, half * T:(half + 1) * T], ALU.add,
                           [f'Yd{half}', 'yo'], ['yo'])
                    orow = (qi - 1) * T + bi * 128
                    S.dma('yw', y_d[orow:orow + 128, :], yo[:, :], reads=['yo'], writes=['y_d'], q='pool')
            S.barrier()
            run_block()
        print("megakernel ops", S.nops, "waits", S.nwaits, {k: v for k, v in S.cnt.items() if k in engnames})
    return nc


_CACHE = {}


def _consts():
    ident = np.eye(128, dtype=np.float32)
    kk = np.arange(128)
    negu = -(kk[:, None] >= kk[None, :]).astype(np.float32)
    negl = -(kk[:, None] < kk[None, :]).astype(np.float32)
    bd = np.zeros((128, 128), np.float32)
    bd[:64, :64] = 1.0
    bd[64:, 64:] = 1.0
    qq = np.arange(T)
    mask = np.zeros((128, 4, T), np.float32)
    for d in range(4):
        mask[:, d, :] = ((128 * d + kk[:, None]) < qq[None, :]).astype(np.float32)
    return ident, negu, negl, bd, mask


def _tb_table(rel_bias):
    kk = np.arange(128)[:, None]
    qq = np.arange(128)[None, :]
    tb = np.empty((128, 8, 640), np.float32)
    for rp in range(5):
        idx = np.clip(qq - kk + rp * 128, -128, 128) + 128
        blk = rel_bias[:, idx]
        vis = np.ones((128, 128), bool)
        if rp == 0:
            vis = (kk < 64) | (qq >= 64)
        if rp == 4:
            vis = (kk >= 64) | (qq < 64)
        blk = np.where(vis[None], blk, np.float32(NEG))
        tb[:, :, rp * 128:(rp + 1) * 128] = blk.transpose(1, 0, 2)
    return tb


def kernel(x, norm1_g, w_in, q_norm_g, k_norm_g, rel_bias, w_branch_a, w_branch_b, w_out, norm2_g,
           w_ffn_up, ffn_conv_w, ffn_conv_b, w_ffn_down):
    x = np.asarray(x, np.float32)
    Bn, Sq, Dm = x.shape
    assert Bn == 2 and Dm == D and Sq % 2048 == 0
    NO = Sq // 2048
    NT = Sq // T
    key = (NT, NO)
    if key not in _CACHE:
        _CACHE[key] = build(NT, NO)
    nc = _CACHE[key]
    f = lambda a: np.ascontiguousarray(np.asarray(a, np.float32))
    ident, negu, negl, bd, mask = _consts()
    own = NO * T
    shared = {
        "g1": f(np.broadcast_to(np.asarray(norm1_g, np.float32)[0][None, :], (128, D))),
        "g2": f(np.broadcast_to(np.asarray(norm2_g, np.float32)[0][None, :], (128, D))),
        "w_in": f(w_in[0]),
        "gq": f(np.tile(np.asarray(q_norm_g, np.float32)[0], 2)[:, None]),
        "gk": f(np.tile(np.asarray(k_norm_g, np.float32)[0], 2)[:, None]),
        "tb": f(_tb_table(np.asarray(rel_bias, np.float32)[0])),
        "w_a": f(w_branch_a[0]), "w_b": f(w_branch_b[0]), "w_o": f(w_out[0]),
        "w_up": f(w_ffn_up[0]),
        "cw": f(np.asarray(ffn_conv_w, np.float32)[0].reshape(3, 44, 128).transpose(2, 1, 0)),
        "cb": f(np.asarray(ffn_conv_b, np.float32)[0].reshape(44, 128).T),
        "w_dn": f(w_ffn_down[0]),
        "ident": ident, "negu": negu, "negl": negl, "bd": bd, "mask": mask,
    }
    in_maps = []
    for c in range(8):
        b, j = c // 4, c % 4
        real = (j + 1) * own
        pad = Sq - real
        xl = np.zeros((Sq, D), np.float32)
        xl[pad:] = x[b, :real]
        valid = np.zeros((Sq,), np.float32)
        valid[pad:] = 1.0
        m = dict(shared)
        m["x"] = xl
        m["valid"] = f(valid.reshape(Sq // 128, 128).T)
        in_maps.append(m)
    res = run_bass_kernel_spmd(nc, in_maps, core_ids=list(range(8)))
    out = np.empty((Bn, Sq, D), np.float32)
    for c in range(8):
        b, j = c // 4, c % 4
        out[b, j * own:(j + 1) * own] = res.results[c]["y"]
    return out
```

```python
import contextlib
import numpy as np
import concourse.bass as bass
import concourse.mybir as mybir
from concourse.bass_utils import run_bass_kernel_spmd

F32 = mybir.dt.float32
BF16 = mybir.dt.bfloat16
AF = mybir.ActivationFunctionType
ALU = mybir.AluOpType

D = 1024
KC = 8
T = 512
DFF = 2816
NFC = 22
EPS = 1e-6
NEG = -30000.0


class Sched:
    def __init__(self, engnames, sems):
        self.engnames = engnames
        self.prog = {k: [] for k in engnames}
        self.stack = {k: [self.prog[k]] for k in engnames}
        self.cond = None
        self.sems = sems
        self.free_sems = [k for k in sems if k.startswith('c')]
        self.alias = {}
        self.cnt = {}
        self.mult = {}
        for k in engnames:
            self.cnt[k] = 0
            self.mult[k] = 1
        self.cnt['flag'] = 0
        self.mult['flag'] = 1
        self.lastw = {}
        self.readers = {}
        self.waited = {}
        self.nops = 0
        self.nwaits = 0

    def chan(self, name):
        if name not in self.alias:
            s = self.free_sems.pop(0)
            self.alias[name] = s
            self.cnt[name] = 0
            self.mult[name] = 16
        return name

    def sem(self, p):
        return self.sems[self.alias.get(p, p)]

    def _deps(self, reads, writes):
        deps = {}

        def add(ps):
            p, s = ps
            if deps.get(p, 0) < s:
                deps[p] = s
        for r in reads:
            if r in self.lastw:
                add(self.lastw[r])
        for w in writes:
            if w in self.lastw:
                add(self.lastw[w])
            for rd in self.readers.get(w, ()):
                add(rd)
        return deps

    def _emit_waits(self, eng, deps, skip_self):
        wd = self.waited.setdefault(eng, {})
        for p, s in deps.items():
            if p == eng and skip_self:
                continue
            if wd.get(p, 0) >= s:
                continue
            self.stack[eng][-1].append(('w', self.sem(p), s * self.mult[p]))
            wd[p] = s
            self.nwaits += 1

    def _record(self, prod, seq, reads, writes):
        for r in reads:
            self.readers.setdefault(r, []).append((prod, seq))
        for w in writes:
            self.lastw[w] = (prod, seq)
            self.readers[w] = []

    def op(self, eng, meth, args, kw, reads=(), writes=()):
        deps = self._deps(reads, writes)
        self._emit_waits(eng, deps, skip_self=(eng == 'pe'))
        self.stack[eng][-1].append(('o', (meth, args, kw), self.sems[eng], 1))
        self.cnt[eng] += 1
        self._record(eng, self.cnt[eng], reads, writes)
        self.nops += 1

    def dma(self, ch, out, in_, reads=(), writes=(), q='sp'):
        self.chan(ch)
        deps = self._deps(reads, writes)
        self._emit_waits(q, deps, skip_self=False)
        self.stack[q][-1].append(('o', ('dma_start', (), dict(out=out, in_=in_)), self.sem(ch), 16))
        self.cnt[ch] += 1
        self._record(ch, self.cnt[ch], reads, writes)
        self.nops += 1

    def flag_op(self, meth, args, kw, reads=(), writes=()):
        deps = self._deps(reads, writes)
        self._emit_waits('dve', deps, skip_self=False)
        self.stack['dve'][-1].append(('o', (meth, args, kw), self.sems['flag'], 1))
        self.cnt['flag'] += 1
        self._record('flag', self.cnt['flag'], reads, writes)

    def begin_cond(self, flag_ap, engines):
        import copy
        if self.cond is None:
            self.cond = []
        self.cond.append(dict(flag_ap=flag_ap, engines=engines, seq=self.cnt['flag'],
                              start={e: self.cnt[e] for e in engines}, fstart=self.cnt['flag'],
                              snap=copy.deepcopy(self.waited)))
        for e in engines:
            self.stack[e].append([])

    def end_cond(self):
        c = self.cond.pop()
        nf = self.cnt['flag'] - c['fstart']
        for e in c['engines']:
            body = self.stack[e].pop()
            n = self.cnt[e] - c['start'][e]
            self.stack[e][-1].append(('if', c['flag_ap'], c['seq'], body, c['start'][e], n,
                                      nf if e == 'dve' else 0))
        self.waited = c['snap']

    def barrier(self):
        for eng in self.engnames:
            wd = self.waited.setdefault(eng, {})
            for p, c in self.cnt.items():
                if c > 0 and p != eng and wd.get(p, 0) < c:
                    self.stack[eng][-1].append(('w', self.sem(p), c * self.mult[p]))
                    wd[p] = c
        for eng in self.engnames:
            if eng != 'sp' and self.cnt[eng] > 0:
                self.stack[eng][-1].append(('w', self.sems[eng], self.cnt[eng]))

    def _replay_list(self, eng, e, lst):
        for it in lst:
            if it[0] == 'w':
                e.wait_ge(it[1], it[2])
            elif it[0] == 'o':
                meth, args, kw = it[1]
                getattr(e, meth)(*args, **kw).then_inc(it[2], it[3])
            else:
                _, flag_ap, seq, body, start, n, dve_else = it
                e.wait_ge(self.sems['flag'], seq)
                reg = self.regs[eng]
                e.reg_load(reg, flag_ap)
                with e.If_ne(reg, 0):
                    self._replay_list(eng, e, body)
                with e.Else():
                    if start > 0:
                        e.wait_ge(self.sems[eng], start)
                    if n > 0:
                        e.sem_inc(self.sems[eng], n)
                    if dve_else:
                        e.sem_inc(self.sems['flag'], dve_else)

    def replay(self, eng, e):
        assert len(self.stack[eng]) == 1
        self._replay_list(eng, e, self.prog[eng])
        self.prog[eng] = []
        self.stack[eng] = [self.prog[eng]]


def build(NT, NO):
    SL = NT * T
    NKB = SL // 128
    NQ = NO + 1
    T0 = NT - NO - 1
    nc = bass.Bass("TRN2", target_bir_lowering=False)

    def din(name, shape):
        return nc.dram_tensor(name, shape, F32, kind="ExternalInput").ap()
    x_d = din("x", [SL, D])
    valid_d = din("valid", [128, NKB])
    g1_d = din("g1", [128, D])
    g2_d = din("g2", [128, D])
    win_d = din("w_in", [D, 5120])
    gq_d = din("gq", [128, 1])
    gk_d = din("gk", [128, 1])
    tb_d = din("tb", [128, 8, 640])
    wa_d = din("w_a", [512, D])
    wb_d = din("w_b", [512, D])
    wo_d = din("w_o", [D, D])
    wup_d = din("w_up", [D, 2 * DFF])
    cw_d = din("cw", [128, 44, 3])
    cb_d = din("cb", [128, 44])
    wdn_d = din("w_dn", [DFF, D])
    ident_d = din("ident", [128, 128])
    negu_d = din("negu", [128, 128])
    negl_d = din("negl", [128, 128])
    bd_d = din("bd", [128, 128])
    mask_d = din("mask", [128, 4, T])
    y_d = nc.dram_tensor("y", [NO * T, D], F32, kind="ExternalOutput").ap()
    kt_d = nc.dram_tensor("kt_scr", [4, 128, SL], BF16).ap()
    v_d = nc.dram_tensor("v_scr", [4, 128, NKB, 128], BF16).ap()
    x1_d = nc.dram_tensor("x1_scr", [NQ * T, D], F32).ap()

    top = contextlib.ExitStack()
    with top:
        engnames = ['pe', 'act', 'dve', 'pool', 'sp']
        sems = {}
        for n in ['pe', 'act', 'dve', 'pool', 'flag']:
            sems[n] = top.enter_context(nc.semaphore(n))
        for i in range(28):
            sems[f'c{i}'] = top.enter_context(nc.semaphore(f'c{i}'))
        S = Sched(engnames, sems)
        S.regs = {'pe': nc.alloc_register(mybir.EngineType.PE, 'flag_pe'),
                  'act': nc.alloc_register(mybir.EngineType.Activation, 'flag_act'),
                  'dve': nc.alloc_register(mybir.EngineType.DVE, 'flag_dve')}

        def run_block():
            blk = nc.Block()
            with blk:
                blk.tensor(lambda e: S.replay('pe', e))
                blk.scalar(lambda e: S.replay('act', e))
                blk.vector(lambda e: S.replay('dve', e))
                blk.gpsimd(lambda e: S.replay('pool', e))
                blk.sync(lambda e: S.replay('sp', e))

        uid = [0]

        def sbt(es, name, shape, dt):
            uid[0] += 1
            return es.enter_context(nc.sbuf_tensor(f"s{uid[0]}_{name}", shape, dt))

        def pst(es, name, shape, dt=F32):
            uid[0] += 1
            return es.enter_context(nc.psum_tensor(f"p{uid[0]}_{name}", shape, dt))

        def MM(out, lhsT, rhs, start, stop, reads, writes):
            S.op('pe', 'matmul', (out,), dict(lhsT=lhsT, rhs=rhs, start=start, stop=stop), reads, writes)

        def ACTV(out, in_, func, reads, writes, **kw):
            S.op('act', 'activation', (), dict(out=out, in_=in_, func=func, **kw), reads, writes)

        def CP(eng, out, in_, reads, writes):
            S.op(eng, 'copy' if eng == 'act' else 'tensor_copy', (), dict(out=out, in_=in_), reads, writes)

        def TT(eng, out, in0, in1, op, reads, writes):
            S.op(eng, 'tensor_tensor', (), dict(out=out, in0=in0, in1=in1, op=op), reads, writes)

        def STT(eng, out, in0, scalar, in1, op0, op1, reads, writes):
            S.op(eng, 'scalar_tensor_tensor', (), dict(out=out, in0=in0, scalar=scalar, in1=in1, op0=op0, op1=op1), reads, writes)

        def TS(eng, out, in0, s1, s2, op0, op1, reads, writes):
            kw = dict(out=out, in0=in0, scalar1=s1, scalar2=s2, op0=op0)
            if op1 is not None:
                kw['op1'] = op1
            S.op(eng, 'tensor_scalar', (), kw, reads, writes)

        def MS(eng, ap, val, writes):
            S.op(eng, 'memset', (ap, val), {}, (), writes)

        ident = sbt(top, "ident", [128, 128], BF16)
        negu = sbt(top, "negu", [128, 128], BF16)
        negl = sbt(top, "negl", [128, 128], BF16)
        bdm = sbt(top, "bdm", [128, 128], BF16)
        zero = sbt(top, "zero", [128, 128], BF16)
        cst = sbt(top, "cst", [128, 4], F32)

        stg_state = {'i': 0}

        def load_weight(stg, dst_ap, src_ap, n, wname):
            sl = stg_state['i'] % 2
            stg_state['i'] += 1
            S.dma(f'stg{sl}', stg[:, sl, 0:n], src_ap, writes=[f'stg{sl}'])
            ce = ('pool', 'dve', 'act')[stg_state['i'] % 3]
            CP(ce, dst_ap, stg[:, sl, 0:n], [f'stg{sl}'], [wname])

        def norm_a(Bf, src_ap, src_reads, slot, gb):
            xs, junk, ss, hn = Bf['xs'], Bf['junk'], Bf['ss'], Bf['hn']
            xn, hnn, ssn = f'xs{slot}', f'hn{slot}', f'ss{slot}'
            S.dma(xn, xs[:, slot, :], src_ap, reads=src_reads, writes=[xn])
            MS('pool', ss[:, slot, 0:1], 0.0, [ssn])
            ACTV(junk[:, :], xs[:, slot, :], AF.Square, [xn], ['junk', ssn], accum_out=ss[:, slot, 0:1])
            ACTV(ss[:, slot, 1:2], ss[:, slot, 0:1], AF.Ln, [ssn], [ssn], scale=1.0 / D, bias=cst[:, 0:1])
            ACTV(ss[:, slot, 2:3], ss[:, slot, 1:2], AF.Exp, [ssn], [ssn], scale=-0.5)
            STT('dve', hn[:, slot, :], xs[:, slot, :], ss[:, slot, 2:3], gb[:, :], ALU.mult, ALU.mult, [xn, ssn], [hnn])

        def norm_b(Bf, slot, hnT_ap, hnT_name, tp, tp_name, ev='act'):
            hn = Bf['hn']
            for kc in range(KC):
                S.op('pe', 'transpose', (tp[:, kc, :], hn[:, slot, kc * 128:(kc + 1) * 128], ident[:, :]), {}, [f'hn{slot}'], [tp_name])
            CP(ev, hnT_ap, tp[:, :, :], [tp_name], [hnT_name])

        def norm_block(Bf, src_ap, src_reads, slot, gb, hnT_ap, hnT_name, tp, tp_name):
            norm_a(Bf, src_ap, src_reads, slot, gb)
            norm_b(Bf, slot, hnT_ap, hnT_name, tp, tp_name)

        def new_B(es, ns=2):
            return dict(xs=sbt(es, "xs", [128, ns, D], F32), junk=sbt(es, "junk", [128, D], BF16),
                        ss=sbt(es, "ss", [128, ns, 4], F32), hn=sbt(es, "hn", [128, ns, D], BF16))

        sc04 = contextlib.ExitStack()
        with sc04:
            g1b = sbt(sc04, "g1b", [128, D], F32)
            vt = sbt(sc04, "vt", [128, NKB], F32)
            OT = sbt(sc04, "OT", [128, 4, NQ * T], BF16)

            with contextlib.ExitStack() as es:
                i32 = sbt(es, "i32", [128, 4, 128], F32)
                for i, srcd in enumerate([ident_d, negu_d, negl_d, bd_d]):
                    S.dma('cst', i32[:, i, :], srcd[:, :], writes=[f'i32_{i}'])
                S.barrier()
                for i, dst in enumerate([ident, negu, negl, bdm]):
                    CP('pool', dst[:, :], i32[:, i, :], [f'i32_{i}'], [f'const{i}'])
                MS('pool', zero[:, :], 0.0, ['zero'])
                MS('pool', cst[:, 0:1], EPS, ['cst'])
                MS('pool', cst[:, 1:2], 1.0, ['cst'])
                MS('pool', OT[:, :, 0:T], 0.0, ['OTz'])
                S.dma('cst', g1b[:, :], g1_d[:, :], writes=['g1b'])
                S.dma('cst', vt[:, :], valid_d[:, :], writes=['vt'])
                S.barrier()
                run_block()

            sc12 = contextlib.ExitStack()
            with sc12:
                QT = sbt(sc12, "QT", [128, 4, NQ * T], BF16)
                with contextlib.ExitStack() as es:
                    wB = sbt(es, "wB", [128, KC, 1536], BF16)
                    stg = sbt(es, "stg1", [128, 2, 1536], F32)
                    Bf = new_B(es, 4)
                    hnT = sbt(es, "hnT", [128, 2, KC, T], BF16)
                    KTs = sbt(es, "KTs", [128, 2, 4, T], BF16)
                    Vs = sbt(es, "Vs", [128, 2, 4, T], BF16)
                    tps = [pst(es, f"tp{i}", [128, KC, 128], BF16) for i in range(2)]
                    accs = [pst(es, f"acc{i}", [128, T], F32) for i in range(4)]
                    for kc in range(KC):
                        load_weight(stg, wB[:, kc, :], win_d[kc * 128:(kc + 1) * 128, 1536:3072], 1536, 'wB')
                    acc_i = 0
                    ev_i = 0

                    def nA(t, bi):
                        blk_i = t * 4 + bi
                        norm_a(Bf, x_d[blk_i * 128:(blk_i + 1) * 128, :], [], blk_i % 4, g1b)

                    def nB(t, bi):
                        blk_i = t * 4 + bi
                        hs_ = t % 2
                        norm_b(Bf, blk_i % 4, hnT[:, hs_, :, bi * 128:(bi + 1) * 128], f'hnT{hs_}_{bi}', tps[blk_i % 2], f'tp{blk_i % 2}',
                               ev=('act' if bi % 2 == 0 else 'dve'))

                    def grpK(t, p):
                        nonlocal acc_i, ev_i
                        hs = t % 2
                        hread = [f'hnT{hs}_{bi}' for bi in range(4)]
                        a = acc_i % 4
                        acc_i += 1
                        for kc in range(KC):
                            MM(accs[a][:, :], wB[:, kc, 512 + p * 128:512 + (p + 1) * 128], hnT[:, hs, kc, :],
                               kc == 0, kc == KC - 1, ['wB'] + hread, [f'acc{a}'])
                        CP('dve' if ev_i % 2 == 0 else 'act', KTs[:, hs, p, :], accs[a][:, :], [f'acc{a}'], [f'KTs{hs}'])
                        ev_i += 1
                        if p == 3:
                            S.dma(f'kts{hs}', kt_d[:, :, t * T:(t + 1) * T].rearrange("q p c -> p q c"), KTs[:, hs, :, :],
                                  reads=[f'KTs{hs}'], writes=['kt_d'], q='pool')

                    def grpV(t, bi):
                        nonlocal acc_i, ev_i
                        hs = t % 2
                        a = acc_i % 4
                        acc_i += 1
                        for kc in range(KC):
                            MM(accs[a][:, :], hnT[:, hs, kc, bi * 128:(bi + 1) * 128], wB[:, kc, 1024:1536],
                               kc == 0, kc == KC - 1, ['wB', f'hnT{hs}_{bi}'], [f'acc{a}'])
                        CP('dve' if ev_i % 2 == 0 else 'act', Vs[:, hs, bi, :], accs[a][:, :], [f'acc{a}'], [f'Vs{hs}'])
                        ev_i += 1
                        if bi == 3:
                            for q in range(4):
                                S.dma(f'vs{hs}', v_d[q, :, t * 4:(t + 1) * 4, :], Vs[:, hs, :, q * 128:(q + 1) * 128],
                                      reads=[f'Vs{hs}'], writes=['v_d'], q='pool')

                    def grpQ(t, p):
                        nonlocal acc_i, ev_i
                        hs = t % 2
                        qi = t - T0
                        hread = [f'hnT{hs}_{bi}' for bi in range(4)]
                        a = acc_i % 4
                        acc_i += 1
                        for kc in range(KC):
                            MM(accs[a][:, :], wB[:, kc, p * 128:(p + 1) * 128], hnT[:, hs, kc, :],
                               kc == 0, kc == KC - 1, ['wB'] + hread, [f'acc{a}'])
                        CP('dve' if ev_i % 2 == 0 else 'act', QT[:, p, qi * T:(qi + 1) * T], accs[a][:, :], [f'acc{a}'], [f'QT{p}_{qi}'])
                        ev_i += 1

                    for bi in range(4):
                        nA(0, bi)
                        nB(0, bi)
                    for t in range(NT):
                        nx = t + 1 < NT
                        if nx:
                            nA(t + 1, 0)
                            nA(t + 1, 1)
                        grpK(t, 0)
                        if nx:
                            nB(t + 1, 0)
                        grpK(t, 1)
                        if nx:
                            nA(t + 1, 2)
                        grpK(t, 2)
                        if nx:
                            nB(t + 1, 1)
                        grpK(t, 3)
                        if nx:
                            nA(t + 1, 3)
                        grpV(t, 0)
                        if nx:
                            nB(t + 1, 2)
                        grpV(t, 1)
                        grpV(t, 2)
                        if nx:
                            nB(t + 1, 3)
                        grpV(t, 3)
                        if t >= T0:
                            for p in range(4):
                                grpQ(t, p)
                    S.barrier()
                    run_block()

                with contextlib.ExitStack() as es:
                    KTp = sbt(es, "KTp", [128, SL], BF16)
                    Vp = sbt(es, "Vp", [128, NKB, 128], BF16)
                    M = sbt(es, "M", [128, 4, 2, T], BF16)
                    NE, NSP, NXW, NW = 4, 3, 2, 2
                    eb = sbt(es, "eb", [128, NE, 2, T], F32)
                    spb = sbt(es, "spb", [128, NSP, 2, T], BF16)
                    xwb = sbt(es, "xwb", [128, NXW, 2, T], F32)
                    wbuf = sbt(es, "wbuf", [128, NW, 2, T], BF16)
                    m32 = sbt(es, "m32", [128, 4, T], F32)
                    Zp = [pst(es, f"Z{i}", [128, 2, T]) for i in range(2)]
                    Ap = pst(es, "A", [128, 2, T])
                    Op = pst(es, "O", [128, T])
                    S.dma('cst', m32[:, :, :], mask_d[:, :, :], writes=['m32'])
                    for h in range(2):
                        CP('pool', M[:, :, h, :], m32[:, :, :], ['m32'], ['M'])
                    I32 = mybir.dt.int32
                    flagbuf = sbt(es, "flagbuf", [128, 512], I32)
                    mx = sbt(es, "mx", [128, 4], F32)
                    THRESH = 150.0
                    CH = 32
                    nchk = (NKB + CH - 1) // CH
                    itn = [0]
                    fidx = [0]

                    def load_pair(p):
                        for c in reversed(range(nchk)):
                            k0, k1 = c * CH, min(NKB, (c + 1) * CH)
                            S.dma(f'ktp{c}', KTp[:, k0 * 128:k1 * 128], kt_d[p, :, k0 * 128:k1 * 128], reads=['kt_d'], writes=[f'KTp{c}'])
                            S.dma(f'vp{c}', Vp[:, k0:k1, :], v_d[p, :, k0:k1, :], reads=['v_d'], writes=[f'Vp{c}'])

                    def st1(it):
                        W, kb, p, qi, c0, n = it['W'], it['kb'], it['p'], it['qi'], it['c0'], it['n']
                        for h in range(2):
                            MM(Zp[n % 2][:, h, 0:W], KTp[64 * h:64 * h + 64, kb * 128:(kb + 1) * 128],
                               QT[64 * h:64 * h + 64, p, qi * T + c0:qi * T + c0 + W], True, True,
                               [f'KTp{kb // CH}', f'QT{p}_{qi}'], [f'Z{n % 2}'])

                    def st2(it):
                        W, n = it['W'], it['n']
                        en = f'e{n % NE}'
                        ACTV(eb[:, n % NE, :, 0:W], Zp[n % 2][:, :, 0:W], AF.Exp, [f'Z{n % 2}'], [en], scale=0.125)
                        if it['d'] is not None:
                            TT('dve', eb[:, n % NE, :, 0:W], eb[:, n % NE, :, 0:W], M[:, it['d'], :, 0:W], ALU.mult, [en, 'M'], [en])

                    def st3(it):
                        W, n = it['W'], it['n']
                        ACTV(spb[:, n % NSP, :, 0:W], eb[:, n % NE, :, 0:W], AF.Ln, [f'e{n % NE}'], [f'sp{n % NSP}'], bias=cst[:, 1:2])

                    def st4(it):
                        W, n = it['W'], it['n']
                        for h in range(2):
                            MM(Ap[:, h, 0:W], negu[:, :], spb[:, n % NSP, h, 0:W], it['first'], False,
                               [f'sp{n % NSP}', 'const1'], ['A'])

                    def st5(it):
                        W, n = it['W'], it['n']
                        ACTV(xwb[:, n % NXW, :, 0:W], Ap[:, :, 0:W], AF.Exp, ['A'], [f'xw{n % NXW}'])

                    def st6(it):
                        W, n = it['W'], it['n']
                        if it['last']:
                            return
                        for h in range(2):
                            MM(Ap[:, h, 0:W], negl[:, :], spb[:, n % NSP, h, 0:W], False, False,
                               [f'sp{n % NSP}', 'const2'], ['A'])

                    def st7(it):
                        W, n = it['W'], it['n']
                        TT('dve', wbuf[:, n % NW, :, 0:W], eb[:, n % NE, :, 0:W], xwb[:, n % NXW, :, 0:W], ALU.mult,
                           [f'e{n % NE}', f'xw{n % NXW}'], [f'w{n % NW}'])

                    def st8(it):
                        W, n, kb = it['W'], it['n'], it['kb']
                        for h in range(2):
                            MM(Op[64 * h:64 * h + 64, 0:W], Vp[:, kb, 64 * h:64 * h + 64], wbuf[:, n % NW, h, 0:W],
                               it['first'], it['last'], [f'w{n % NW}', f'Vp{kb // CH}'], ['O'])

                    stages = [(st1, 0), (st2, 1), (st3, 2), (st6, 4), (st4, 3), (st5, 3), (st7, 4), (st8, 5)]

                    def emit_segment(items):
                        nit = len(items)
                        for s_ in range(nit + 7):
                            for fn, dly in stages:
                                i_ = s_ - dly
                                if 0 <= i_ < nit:
                                    fn(items[i_])

                    def emit_flag(W):
                        fi = fidx[0]
                        fidx[0] += 1
                        for h in range(2):
                            S.op('dve', 'tensor_reduce', (), dict(out=mx[0:1, h:h + 1], in_=Ap[0:1, h, 0:W], axis=mybir.AxisListType.X, op=ALU.max),
                                 ['A'], [f'mx{h}'])
                        TT('dve', mx[0:1, 2:3], mx[0:1, 0:1], mx[0:1, 1:2], ALU.max, ['mx0', 'mx1'], ['mx2'])
                        S.flag_op('tensor_scalar', (), dict(out=flagbuf[0:1, fi:fi + 1], in0=mx[0:1, 2:3], scalar1=-THRESH, scalar2=None, op0=ALU.is_gt),
                                  ['mx2'], [f'flag{fi}'])
                        return fi

                    for p in range(4):
                        load_pair(p)
                        for qi in range(NQ):
                            g = T0 + qi
                            if qi == 0:
                                W, c0, q0, ndiag = 128, 384, g * T + 384, 1
                            else:
                                W, c0, q0, ndiag = T, 0, g * T, 4
                            kb_hi = (q0 + W) // 128 - 1
                            nb = kb_hi + 1
                            segs = [list(range(0, min(nb, ndiag + 3)))]
                            sz = 2
                            while segs[-1][-1] + 1 < nb:
                                st_ = segs[-1][-1] + 1
                                segs.append(list(range(st_, min(nb, st_ + sz))))
                                if len(segs) > 2:
                                    sz *= 2
                            nopen = 0
                            for si, seg in enumerate(segs):
                                items = []
                                for b in seg:
                                    kb = kb_hi - b
                                    d = kb - q0 // 128
                                    items.append(dict(p=p, qi=qi, W=W, c0=c0, kb=kb, d=(d if d >= 0 else None),
                                                      first=(b == 0), last=(b == nb - 1), n=itn[0]))
                                    itn[0] += 1
                                emit_segment(items)
                                if si < len(segs) - 1:
                                    fi = emit_flag(W)
                                    S.begin_cond(flagbuf[0:1, fi:fi + 1], ['pe', 'act', 'dve'])
                                    nopen += 1
                            for _ in range(nopen):
                                S.end_cond()
                            CP('dve', OT[:, p, qi * T + c0:qi * T + c0 + W], Op[:, 0:W], ['O', 'OTz'], [f'OT{p}_{qi}'])
                    S.barrier()
                    run_block()

            sc34 = contextlib.ExitStack()
            with sc34:
                OAT = sbt(sc34, "OAT", [128, 4, NQ * T], BF16)
                with contextlib.ExitStack() as es:
                    wA = sbt(es, "wA", [128, KC, 1536], BF16)
                    with contextlib.ExitStack() as es2:
                        stg = sbt(es2, "stg3", [128, 2, 1536], F32)
                        for kc in range(KC):
                            load_weight(stg, wA[:, kc, :], win_d[kc * 128:(kc + 1) * 128, 0:1536], 1536, 'wA')
                        S.barrier()
                    TB = sbt(es, "TB", [128, 8, 640], F32)
                    gq = sbt(es, "gq", [128, 1], F32)
                    gk = sbt(es, "gk", [128, 1], F32)
                    Bf = new_B(es)
                    hnT = sbt(es, "hnT", [128, KC, T], BF16)
                    QAT = sbt(es, "QAT", [128, 4, T], BF16)
                    KAT = sbt(es, "KAT", [128, 4, 2, T], BF16)
                    VA = sbt(es, "VA", [128, 2, 4, T], BF16)
                    VLD = sbt(es, "VLD", [128, 2, 4, 64], BF16)
                    sqb = sbt(es, "sqb", [128, 2, T], BF16)
                    qgb = sbt(es, "qgb", [128, 2, T], F32)
                    lnb = sbt(es, "lnb", [128, 2, T], F32)
                    sbb = sbt(es, "sbb", [128, 2, 2, T], F32)
                    pTb = sbt(es, "pTb", [128, 2, 2, T], BF16)
                    rdb = sbt(es, "rdb", [128, T], F32)
                    tp3 = pst(es, "tp3", [128, KC, 128], BF16)
                    acc3t = pst(es, "acc3", [128, 2, T])
                    acc3 = [acc3t[:, i, :] for i in range(2)]
                    SSp = pst(es, "SSp", [128, T])
                    SPSt = pst(es, "SPS", [128, 2, T])
                    sbufs = [(SPSt, 'SPS'), (acc3t, 'acc3_')]
                    OAp = pst(es, "OAp", [128, T])
                    DENp = pst(es, "DENp", [128, T])
                    S.dma('cst', TB[:, :, :], tb_d[:, :, :], writes=['TB'])
                    S.dma('cst', gq[:, :], gq_d[:, :], writes=['gq'])
                    S.dma('cst', gk[:, :], gk_d[:, :], writes=['gk'])
                    S.barrier()
                    blkc = 0
                    acc_i = 0
                    nrm = [0]
                    jj = [0]

                    def qknorm(a, gcol, gname, dst_ap, dst_name):
                        k = nrm[0] % 2
                        nrm[0] += 1
                        ACTV(sqb[:, k, :], acc3[a][:, :], AF.Square, [f'acc3_{a}'], [f'sqb{k}'])
                        S.op('act', 'mul', (), dict(out=qgb[:, k, :], in_=acc3[a][:, :], mul=gcol[:, 0:1]), [f'acc3_{a}', gname], [f'qgb{k}'])
                        MM(SSp[:, :], bdm[:, :], sqb[:, k, :], True, True, [f'sqb{k}', 'const3'], ['SSp'])
                        ACTV(lnb[:, k, :], SSp[:, :], AF.Ln, ['SSp'], [f'lnb{k}'], scale=1.0 / 64, bias=cst[:, 0:1])
                        ACTV(lnb[:, k, :], lnb[:, k, :], AF.Exp, [f'lnb{k}'], [f'lnb{k}'], scale=-0.5)
                        TT('dve', dst_ap, qgb[:, k, :], lnb[:, k, :], ALU.mult, [f'qgb{k}', f'lnb{k}'], [dst_name])

                    for t in range(T0 - 1, NT):
                        qi = t - T0
                        sl = t % 2
                        sl0 = blkc
                        norm_a(Bf, x_d[t * 512:t * 512 + 128, :], [], sl0 % 2, g1b)
                        for bi in range(4):
                            if bi + 1 < 4:
                                norm_a(Bf, x_d[(t * 4 + bi + 1) * 128:(t * 4 + bi + 2) * 128, :], [], (sl0 + bi + 1) % 2, g1b)
                            norm_b(Bf, (sl0 + bi) % 2, hnT[:, :, bi * 128:(bi + 1) * 128], f'hnT_{bi}', tp3, 'tp3')
                            blkc += 1
                        hread = [f'hnT_{bi}' for bi in range(4)]
                        for p in range(4):
                            a = acc_i % 2
                            acc_i += 1
                            for kc in range(KC):
                                MM(acc3[a][:, :], wA[:, kc, 512 + p * 128:512 + (p + 1) * 128], hnT[:, kc, :],
                                   kc == 0, kc == KC - 1, ['wA'] + hread, [f'acc3_{a}'])
                            qknorm(a, gk, 'gk', KAT[:, p, sl, :], f'KAT{sl}_{p}')
                        for bi in range(4):
                            a = acc_i % 2
                            acc_i += 1
                            for kc in range(KC):
                                MM(acc3[a][:, :], hnT[:, kc, bi * 128:(bi + 1) * 128], wA[:, kc, 1024:1536],
                                   kc == 0, kc == KC - 1, ['wA', f'hnT_{bi}'], [f'acc3_{a}'])
                            CP('dve', VA[:, sl, bi, :], acc3[a][:, :], [f'acc3_{a}'], [f'VA{sl}_{bi}'])
                            CP('pool', VLD[:, sl, bi, :], vt[:, t * 4 + bi:t * 4 + bi + 1].to_broadcast([128, 64]),
                               ['vt'], [f'VLD{sl}_{bi}'])
                        if qi < 0:
                            continue
                        for p in range(4):
                            a = acc_i % 2
                            acc_i += 1
                            for kc in range(KC):
                                MM(acc3[a][:, :], wA[:, kc, p * 128:(p + 1) * 128], hnT[:, kc, :],
                                   kc == 0, kc == KC - 1, ['wA'] + hread, [f'acc3_{a}'])
                            qknorm(a, gq, 'gq', QAT[:, p, :], f'QAT{p}')
                        for p in range(4):
                            MM(OAp[:, :], zero[:, :], hnT[:, 0, :], True, False, ['zero'] + hread, ['OAp'])
                            MM(DENp[:, :], zero[:, :], hnT[:, 0, :], True, False, ['zero'] + hread, ['DENp'])
                            def jgeom(j):
                                i_lo, i_hi = max(0, j - 4), min(3, j)
                                N = (i_hi - i_lo + 1) * 128
                                ksl = (1 - sl) if j < 4 else sl
                                return i_lo, N, ksl, j % 4, (4 - j + i_lo) * 128

                            def emitS(j, k):
                                i_lo, N, ksl, cj, tb0 = jgeom(j)
                                spt, spn = sbufs[k]
                                for h in range(2):
                                    MM(spt[:, h, 0:N], KAT[64 * h:64 * h + 64, p, ksl, cj * 128:(cj + 1) * 128],
                                       QAT[64 * h:64 * h + 64, p, i_lo * 128:i_lo * 128 + N], True, True,
                                       [f'KAT{ksl}_{p}', f'QAT{p}'], [f'{spn}{h}'])

                            kbase = jj[0]
                            jj[0] += 8
                            emitS(0, kbase % 2)
                            for j in range(8):
                                i_lo, N, ksl, cj, tb0 = jgeom(j)
                                k = (kbase + j) % 2
                                spt, spn = sbufs[k]
                                if j + 1 < 8:
                                    emitS(j + 1, (kbase + j + 1) % 2)
                                STT('dve', sbb[:, k, :, 0:N], spt[:, :, 0:N], 0.125, TB[:, 2 * p:2 * p + 2, tb0:tb0 + N], ALU.mult, ALU.add,
                                    [f'{spn}0', f'{spn}1', 'TB'], [f'sbb{k}'])
                                ACTV(pTb[:, k, :, 0:N], sbb[:, k, :, 0:N], AF.Exp, [f'sbb{k}'], [f'pTb{k}'])
                                for h in range(2):
                                    hh = 2 * p + h
                                    MM(OAp[64 * h:64 * h + 64, i_lo * 128:i_lo * 128 + N], VA[:, ksl, cj, hh * 64:(hh + 1) * 64],
                                       pTb[:, k, h, 0:N], False, False, [f'pTb{k}', f'VA{ksl}_{cj}'], ['OAp'])
                                    MM(DENp[64 * h:64 * h + 64, i_lo * 128:i_lo * 128 + N], VLD[:, ksl, cj, :],
                                       pTb[:, k, h, 0:N], False, False, [f'pTb{k}', f'VLD{ksl}_{cj}'], ['DENp'])
                            TS('dve', rdb[:, :], DENp[:, :], 1e-30, None, ALU.max, None, ['DENp'], ['rdb'])
                            ACTV(rdb[:, :], rdb[:, :], AF.Ln, ['rdb'], ['rdb'])
                            ACTV(rdb[:, :], rdb[:, :], AF.Exp, ['rdb'], ['rdb'], scale=-1.0)
                            TT('dve', OAT[:, p, qi * T:(qi + 1) * T], OAp[:, :], rdb[:, :], ALU.mult, ['OAp', 'rdb'], [f'OAT{p}_{qi}'])
                    S.barrier()
                    run_block()

                with contextlib.ExitStack() as es:
                    wG = sbt(es, "wG", [128, KC, 2048], BF16)
                    wbra = sbt(es, "wbra", [128, 4, D], BF16)
                    wbrb = sbt(es, "wbrb", [128, 4, D], BF16)
                    wout = sbt(es, "wout", [128, KC, D], BF16)
                    stg = sbt(es, "stg4", [128, 2, D], F32)
                    Bf = new_B(es)
                    xr = sbt(es, "xr", [128, 2, D], F32)
                    hnT = sbt(es, "hnT", [128, KC, T], BF16)
                    sg = sbt(es, "sg", [128, 2, T], F32)
                    mm = sbt(es, "mm", [128, 2, T], F32)
                    MT = sbt(es, "MT", [128, KC, T], BF16)
                    tp4 = pst(es, "tp4", [128, KC, 128], BF16)
                    Gp = [pst(es, f"G{i}", [128, T]) for i in range(2)]
                    Yp = [pst(es, f"Y{i}", [128, T]) for i in range(2)]
                    Xp = [pst(es, f"X{i}", [128, T]) for i in range(2)]
                    for kc in range(KC):
                        for hf in range(2):
                            load_weight(stg, wG[:, kc, hf * D:(hf + 1) * D], win_d[kc * 128:(kc + 1) * 128, 3072 + hf * D:3072 + (hf + 1) * D], D, 'wG')
                    for p in range(4):
                        load_weight(stg, wbra[:, p, :], wa_d[p * 128:(p + 1) * 128, :], D, 'wbra')
                        load_weight(stg, wbrb[:, p, :], wb_d[p * 128:(p + 1) * 128, :], D, 'wbrb')
                    for kc in range(KC):
                        load_weight(stg, wout[:, kc, :], wo_d[kc * 128:(kc + 1) * 128, :], D, 'wout')
                    blkc = 0
                    ob = 0
                    for qi in range(NQ):
                        t = T0 + qi
                        bis = [3] if qi == 0 else [0, 1, 2, 3]
                        cs, cn = (384, 128) if qi == 0 else (0, T)
                        sl0 = blkc
                        norm_a(Bf, x_d[(t * 4 + bis[0]) * 128:(t * 4 + bis[0] + 1) * 128, :], [], sl0 % 2, g1b)
                        for ii, bi in enumerate(bis):
                            if ii + 1 < len(bis):
                                nb_ = bis[ii + 1]
                                norm_a(Bf, x_d[(t * 4 + nb_) * 128:(t * 4 + nb_ + 1) * 128, :], [], (sl0 + ii + 1) % 2, g1b)
                            norm_b(Bf, (sl0 + ii) % 2, hnT[:, :, bi * 128:(bi + 1) * 128], f'hnT_{bi}', tp4, 'tp4')
                            blkc += 1
                        hread = [f'hnT_{bi}' for bi in bis]
                        for oc in range(KC):
                            for br in range(2):
                                for kc in range(KC):
                                    MM(Gp[br][:, cs:cs + cn], wG[:, kc, br * D + oc * 128:br * D + (oc + 1) * 128], hnT[:, kc, cs:cs + cn],
                                       kc == 0, kc == KC - 1, ['wG'] + hread, [f'G{br}'])
                                ACTV(sg[:, br, cs:cs + cn], Gp[br][:, cs:cs + cn], AF.Sigmoid, [f'G{br}'], [f'sg{br}'])
                                wsrc = wbra if br == 0 else wbrb
                                osrc = OAT if br == 0 else OT
                                for p in range(4):
                                    MM(Yp[br][:, cs:cs + cn], wsrc[:, p, oc * 128:(oc + 1) * 128], osrc[:, p, qi * T + cs:qi * T + cs + cn],
                                       p == 0, p == 3,
                                       ['wbra' if br == 0 else 'wbrb', (f'OAT{p}_{qi}' if br == 0 else f'OT{p}_{qi}'), 'OTz'], [f'Y{br}'])
                                TT('dve', mm[:, br, cs:cs + cn], Yp[br][:, cs:cs + cn], sg[:, br, cs:cs + cn], ALU.mult, [f'Y{br}', f'sg{br}'], [f'mm{br}'])
                            TT('pool', MT[:, oc, cs:cs + cn], mm[:, 0, cs:cs + cn], mm[:, 1, cs:cs + cn], ALU.add, ['mm0', 'mm1'], [f'MT{oc}'])
                        mread = [f'MT{oc}' for oc in range(KC)]
                        for bi in bis:
                            blk_i = t * 4 + bi
                            o = ob % 2
                            ob += 1
                            S.dma(f'xr{o}', xr[:, o, :], x_d[blk_i * 128:(blk_i + 1) * 128, :], writes=[f'xr{o}'])
                            for half in range(2):
                                for oc in range(KC):
                                    MM(Xp[half][:, :], MT[:, oc, bi * 128:(bi + 1) * 128], wout[:, oc, half * T:(half + 1) * T],
                                       oc == 0, oc == KC - 1, ['wout'] + mread, [f'X{half}'])
                                TT('dve', xr[:, o, half * T:(half + 1) * T], Xp[half][:, :], xr[:, o, half * T:(half + 1) * T], ALU.add,
                                   [f'X{half}', f'xr{o}'], [f'xr{o}'])
                            row = qi * T + bi * 128
                            S.dma(f'x1w{o}', x1_d[row:row + 128, :], xr[:, o, :], reads=[f'xr{o}'], writes=['x1_d'], q='pool')
                    S.barrier()
                    run_block()

        with contextlib.ExitStack() as es:
            wup = sbt(es, "wup", [128, KC, 2 * DFF], BF16)
            wdn = sbt(es, "wdn", [128, NFC, D], BF16)
            g2b = sbt(es, "g2b", [128, D], F32)
            cw = sbt(es, "cw", [128, 44, 3], F32)
            cb = sbt(es, "cb", [128, 44], F32)
            HALO = sbt(es, "HALO", [128, 44, 2], F32)
            with contextlib.ExitStack() as es2:
                stg = sbt(es2, "stg5", [128, 2, 2816], F32)
                for kc in range(KC):
                    for hf in range(2):
                        load_weight(stg, wup[:, kc, hf * DFF:(hf + 1) * DFF], wup_d[kc * 128:(kc + 1) * 128, hf * DFF:(hf + 1) * DFF], DFF, 'wup')
                for fc in range(NFC):
                    load_weight(stg, wdn[:, fc, :], wdn_d[fc * 128:(fc + 1) * 128, :], D, 'wdn')
                S.dma('cst', g2b[:, :], g2_d[:, :], writes=['g2b'])
                S.dma('cst', cw[:, :, :], cw_d[:, :, :], writes=['cw'])
                S.dma('cst', cb[:, :], cb_d[:, :], writes=['cb'])
                MS('pool', HALO[:, :, :], 0.0, ['HALO'])
                S.barrier()
            Bf = dict(xs=sbt(es, "xs", [128, 2, D], F32), junk=sbt(es, "junk", [128, D], BF16),
                      ss=sbt(es, "ss", [128, 2, 4], F32), hn=sbt(es, "hn", [128, 2, D], BF16))
            hnT = sbt(es, "hnT", [128, KC, T], BF16)
            hb = sbt(es, "hb", [128, 2, 2, T + 2], F32)
            cv = sbt(es, "cv", [128, 2, T], F32)
            sgl = sbt(es, "sgl", [128, T], F32)
            ACTT = sbt(es, "ACTT", [128, NFC, T], BF16)
            yo = sbt(es, "yo", [128, D], F32)
            tp5 = pst(es, "tp5", [128, KC, 128], BF16)
            Hp = [[pst(es, f"H{b}_{u}", [128, T]) for u in range(2)] for b in range(2)]
            Yd = [pst(es, f"Yd{i}", [128, T]) for i in range(2)]
            blkc = 0
            fcc = 0
            for qi in range(NQ):
                bis = [3] if qi == 0 else [0, 1, 2, 3]
                sl0 = blkc
                norm_a(Bf, x1_d[qi * T + bis[0] * 128:qi * T + bis[0] * 128 + 128, :], ['x1_d'], sl0 % 2, g2b)
                for ii, bi in enumerate(bis):
                    if ii + 1 < len(bis):
                        r2 = qi * T + bis[ii + 1] * 128
                        norm_a(Bf, x1_d[r2:r2 + 128, :], ['x1_d'], (sl0 + ii + 1) % 2, g2b)
                    norm_b(Bf, (sl0 + ii) % 2, hnT[:, :, bi * 128:(bi + 1) * 128], f'hnT_{bi}', tp5, 'tp5')
                    blkc += 1
                hread = [f'hnT_{bi}' for bi in bis]
                for fc in range(NFC):
                    b = fcc % 2
                    fcc += 1
                    for u in range(2):
                        ch = fc + u * NFC
                        hbn = f'hb{b}_{u}'
                        if qi == 0:
                            for kc in range(KC):
                                MM(Hp[b][u][:, 384:T], wup[:, kc, ch * 128:(ch + 1) * 128], hnT[:, kc, 384:T],
                                   kc == 0, kc == KC - 1, ['wup'] + hread, [f'H{b}_{u}'])
                            CP('act', HALO[:, ch, :], Hp[b][u][:, T - 2:T], [f'H{b}_{u}'], [f'HALO{ch}'])
                            continue
                        for kc in range(KC):
                            MM(Hp[b][u][:, :], wup[:, kc, ch * 128:(ch + 1) * 128], hnT[:, kc, :],
                               kc == 0, kc == KC - 1, ['wup'] + hread, [f'H{b}_{u}'])
                        CP('pool', hb[:, b, u, 0:2], HALO[:, ch, :], [f'HALO{ch}'], [hbn])
                        CP('act', hb[:, b, u, 2:T + 2], Hp[b][u][:, :], [f'H{b}_{u}'], [hbn])
                        CP('pool', HALO[:, ch, :], hb[:, b, u, T:T + 2], [hbn], [f'HALO{ch}'])
                        ce = 'dve'
                        cvn = f'cv{u}'
                        ACTV(cv[:, u, :], Hp[b][u][:, :], AF.Identity, [f'H{b}_{u}', 'cw', 'cb'], [cvn], scale=cw[:, ch, 2:3], bias=cb[:, ch:ch + 1])
                        STT(ce, cv[:, u, :], hb[:, b, u, 1:T + 1], cw[:, ch, 1:2], cv[:, u, :], ALU.mult, ALU.add, [hbn, cvn, 'cw'], [cvn])
                        STT(ce, cv[:, u, :], hb[:, b, u, 0:T], cw[:, ch, 0:1], cv[:, u, :], ALU.mult, ALU.add, [hbn, cvn, 'cw'], [cvn])
                    if qi == 0:
                        continue
                    ACTV(sgl[:, :], cv[:, 0, :], AF.Silu, ['cv0'], ['sgl'])
                    TT('dve', ACTT[:, fc, :], sgl[:, :], cv[:, 1, :], ALU.mult, ['sgl', 'cv1'], [f'ACTT{fc}'])
                if qi == 0:
                    continue
                aread = [f'ACTT{fc}' for fc in range(NFC)]
                for bi in range(4):
                    row = qi * T + bi * 128
                    S.dma('yo_in', yo[:, :], x1_d[row:row + 128, :], reads=['x1_d', 'y_d'], writes=['yo'])
                    for half in range(2):
                        for fc in range(NFC):
                            MM(Yd[half][:, :], ACTT[:, fc, bi * 128:(bi + 1) * 128], wdn[:, fc, half * T:(half + 1) * T],
                               fc == 0, fc == NFC - 1, ['wdn'] + aread, [f'Yd{half}'])
                        TT('dve', yo[:, half * T:(half + 1) * T], Yd[half][:, :], yo[:, half * T:(half + 1) * T], ALU.add,
                           [f'Yd{half}', 'yo'], ['yo'])
                    orow = (qi - 1) * T + bi * 128
                    S.dma('yw', y_d[orow:orow + 128, :], yo[:, :], reads=['yo'], writes=['y_d'], q='pool')
            S.barrier()
            run_block()
        print("megakernel ops", S.nops, "waits", S.nwaits, {k: v for k, v in S.cnt.items() if k in engnames})
    return nc


_CACHE = {}


def _consts():
    ident = np.eye(128, dtype=np.float32)
    kk = np.arange(128)
    negu = -(kk[:, None] >= kk[None, :]).astype(np.float32)
    negl = -(kk[:, None] < kk[None, :]).astype(np.float32)
    bd = np.zeros((128, 128), np.float32)
    bd[:64, :64] = 1.0
    bd[64:, 64:] = 1.0
    qq = np.arange(T)
    mask = np.zeros((128, 4, T), np.float32)
    for d in range(4):
        mask[:, d, :] = ((128 * d + kk[:, None]) < qq[None, :]).astype(np.float32)
    return ident, negu, negl, bd, mask


def _tb_table(rel_bias):
    kk = np.arange(128)[:, None]
    qq = np.arange(128)[None, :]
    tb = np.empty((128, 8, 640), np.float32)
    for rp in range(5):
        idx = np.clip(qq - kk + rp * 128, -128, 128) + 128
        blk = rel_bias[:, idx]
        vis = np.ones((128, 128), bool)
        if rp == 0:
            vis = (kk < 64) | (qq >= 64)
        if rp == 4:
            vis = (kk >= 64) | (qq < 64)
        blk = np.where(vis[None], blk, np.float32(NEG))
        tb[:, :, rp * 128:(rp + 1) * 128] = blk.transpose(1, 0, 2)
    return tb


def kernel(x, norm1_g, w_in, q_norm_g, k_norm_g, rel_bias, w_branch_a, w_branch_b, w_out, norm2_g,
           w_ffn_up, ffn_conv_w, ffn_conv_b, w_ffn_down):
    x = np.asarray(x, np.float32)
    Bn, Sq, Dm = x.shape
    assert Bn == 2 and Dm == D and Sq % 2048 == 0
    NO = Sq // 2048
    NT = Sq // T
    key = (NT, NO)
    if key not in _CACHE:
        _CACHE[key] = build(NT, NO)
    nc = _CACHE[key]
    f = lambda a: np.ascontiguousarray(np.asarray(a, np.float32))
    ident, negu, negl, bd, mask = _consts()
    own = NO * T
    shared = {
        "g1": f(np.broadcast_to(np.asarray(norm1_g, np.float32)[0][None, :], (128, D))),
        "g2": f(np.broadcast_to(np.asarray(norm2_g, np.float32)[0][None, :], (128, D))),
        "w_in": f(w_in[0]),
        "gq": f(np.tile(np.asarray(q_norm_g, np.float32)[0], 2)[:, None]),
        "gk": f(np.tile(np.asarray(k_norm_g, np.float32)[0], 2)[:, None]),
        "tb": f(_tb_table(np.asarray(rel_bias, np.float32)[0])),
        "w_a": f(w_branch_a[0]), "w_b": f(w_branch_b[0]), "w_o": f(w_out[0]),
        "w_up": f(w_ffn_up[0]),
        "cw": f(np.asarray(ffn_conv_w, np.float32)[0].reshape(3, 44, 128).transpose(2, 1, 0)),
        "cb": f(np.asarray(ffn_conv_b, np.float32)[0].reshape(44, 128).T),
        "w_dn": f(w_ffn_down[0]),
        "ident": ident, "negu": negu, "negl": negl, "bd": bd, "mask": mask,
    }
    in_maps = []
    for c in range(8):
        b, j = c // 4, c % 4
        real = (j + 1) * own
        pad = Sq - real
        xl = np.zeros((Sq, D), np.float32)
        xl[pad:] = x[b, :real]
        valid = np.zeros((Sq,), np.float32)
        valid[pad:] = 1.0
        m = dict(shared)
        m["x"] = xl
        m["valid"] = f(valid.reshape(Sq // 128, 128).T)
        in_maps.append(m)
    res = run_bass_kernel_spmd(nc, in_maps, core_ids=list(range(8)))
    out = np.empty((Bn, Sq, D), np.float32)
    for c in range(8):
        b, j = c // 4, c % 4
        out[b, j * own:(j + 1) * own] = res.results[c]["y"]
    return out
```

```python
import contextlib
import numpy as np
import concourse.bass as bass
import concourse.mybir as mybir
from concourse.bass_utils import run_bass_kernel_spmd

F32 = mybir.dt.float32
BF16 = mybir.dt.bfloat16
AF = mybir.ActivationFunctionType
ALU = mybir.AluOpType

D = 1024
KC = 8
T = 512
DFF = 2816
NFC = 22
EPS = 1e-6
NEG = -30000.0


class Sched:
    def __init__(self, engnames, sems):
        self.engnames = engnames
        self.prog = {k: [] for k in engnames}
        self.stack = {k: [self.prog[k]] for k in engnames}
        self.cond = None
        self.sems = sems
        self.free_sems = [k for k in sems if k.startswith('c')]
        self.alias = {}
        self.cnt = {}
        self.mult = {}
        for k in engnames:
            self.cnt[k] = 0
            self.mult[k] = 1
        self.cnt['flag'] = 0
        self.mult['flag'] = 1
        self.lastw = {}
        self.readers = {}
        self.waited = {}
        self.nops = 0
        self.nwaits = 0

    def chan(self, name):
        if name not in self.alias:
            s = self.free_sems.pop(0)
            self.alias[name] = s
            self.cnt[name] = 0
            self.mult[name] = 16
        return name

    def sem(self, p):
        return self.sems[self.alias.get(p, p)]

    def _deps(self, reads, writes):
        deps = {}

        def add(ps):
            p, s = ps
            if deps.get(p, 0) < s:
                deps[p] = s
        for r in reads:
            if r in self.lastw:
                add(self.lastw[r])
        for w in writes:
            if w in self.lastw:
                add(self.lastw[w])
            for rd in self.readers.get(w, ()):
                add(rd)
        return deps

    def _emit_waits(self, eng, deps, skip_self):
        wd = self.waited.setdefault(eng, {})
        for p, s in deps.items():
            if p == eng and skip_self:
                continue
            if wd.get(p, 0) >= s:
                continue
            self.stack[eng][-1].append(('w', self.sem(p), s * self.mult[p]))
            wd[p] = s
            self.nwaits += 1

    def _record(self, prod, seq, reads, writes):
        for r in reads:
            self.readers.setdefault(r, []).append((prod, seq))
        for w in writes:
            self.lastw[w] = (prod, seq)
            self.readers[w] = []

    def op(self, eng, meth, args, kw, reads=(), writes=()):
        deps = self._deps(reads, writes)
        self._emit_waits(eng, deps, skip_self=(eng == 'pe'))
        self.stack[eng][-1].append(('o', (meth, args, kw), self.sems[eng], 1))
        self.cnt[eng] += 1
        self._record(eng, self.cnt[eng], reads, writes)
        self.nops += 1

    def dma(self, ch, out, in_, reads=(), writes=(), q='sp'):
        self.chan(ch)
        deps = self._deps(reads, writes)
        self._emit_waits(q, deps, skip_self=False)
        self.stack[q][-1].append(('o', ('dma_start', (), dict(out=out, in_=in_)), self.sem(ch), 16))
        self.cnt[ch] += 1
        self._record(ch, self.cnt[ch], reads, writes)
        self.nops += 1

    def flag_op(self, meth, args, kw, reads=(), writes=()):
        deps = self._deps(reads, writes)
        self._emit_waits('dve', deps, skip_self=False)
        self.stack['dve'][-1].append(('o', (meth, args, kw), self.sems['flag'], 1))
        self.cnt['flag'] += 1
        self._record('flag', self.cnt['flag'], reads, writes)

    def begin_cond(self, flag_ap, engines):
        import copy
        if self.cond is None:
            self.cond = []
        self.cond.append(dict(flag_ap=flag_ap, engines=engines, seq=self.cnt['flag'],
                              start={e: self.cnt[e] for e in engines}, fstart=self.cnt['flag'],
                              snap=copy.deepcopy(self.waited)))
        for e in engines:
            self.stack[e].append([])

    def end_cond(self):
        c = self.cond.pop()
        nf = self.cnt['flag'] - c['fstart']
        for e in c['engines']:
            body = self.stack[e].pop()
            n = self.cnt[e] - c['start'][e]
            self.stack[e][-1].append(('if', c['flag_ap'], c['seq'], body, c['start'][e], n,
                                      nf if e == 'dve' else 0))
        self.waited = c['snap']

    def barrier(self):
        for eng in self.engnames:
            wd = self.waited.setdefault(eng, {})
            for p, c in self.cnt.items():
                if c > 0 and p != eng and wd.get(p, 0) < c:
                    self.stack[eng][-1].append(('w', self.sem(p), c * self.mult[p]))
                    wd[p] = c
        for eng in self.engnames:
            if eng != 'sp' and self.cnt[eng] > 0:
                self.stack[eng][-1].append(('w', self.sems[eng], self.cnt[eng]))

    def _replay_list(self, eng, e, lst):
        for it in lst:
            if it[0] == 'w':
                e.wait_ge(it[1], it[2])
            elif it[0] == 'o':
                meth, args, kw = it[1]
                getattr(e, meth)(*args, **kw).then_inc(it[2], it[3])
            else:
                _, flag_ap, seq, body, start, n, dve_else = it
                e.wait_ge(self.sems['flag'], seq)
                reg = self.regs[eng]
                e.reg_load(reg, flag_ap)
                with e.If_ne(reg, 0):
                    self._replay_list(eng, e, body)
                with e.Else():
                    if start > 0:
                        e.wait_ge(self.sems[eng], start)
                    if n > 0:
                        e.sem_inc(self.sems[eng], n)
                    if dve_else:
                        e.sem_inc(self.sems['flag'], dve_else)

    def replay(self, eng, e):
        assert len(self.stack[eng]) == 1
        self._replay_list(eng, e, self.prog[eng])
        self.prog[eng] = []
        self.stack[eng] = [self.prog[eng]]


def build(NT, NO):
    SL = NT * T
    NKB = SL // 128
    NQ = NO + 1
    T0 = NT - NO - 1
    nc = bass.Bass("TRN2", target_bir_lowering=False)

    def din(name, shape):
        return nc.dram_tensor(name, shape, F32, kind="ExternalInput").ap()
    x_d = din("x", [SL, D])
    valid_d = din("valid", [128, NKB])
    g1_d = din("g1", [128, D])
    g2_d = din("g2", [128, D])
    win_d = din("w_in", [D, 5120])
    gq_d = din("gq", [128, 1])
    gk_d = din("gk", [128, 1])
    tb_d = din("tb", [128, 8, 640])
    wa_d = din("w_a", [512, D])
    wb_d = din("w_b", [512, D])
    wo_d = din("w_o", [D, D])
    wup_d = din("w_up", [D, 2 * DFF])
    cw_d = din("cw", [128, 44, 3])
    cb_d = din("cb", [128, 44])
    wdn_d = din("w_dn", [DFF, D])
    ident_d = din("ident", [128, 128])
    negu_d = din("negu", [128, 128])
    negl_d = din("negl", [128, 128])
    bd_d = din("bd", [128, 128])
    mask_d = din("mask", [128, 4, T])
    y_d = nc.dram_tensor("y", [NO * T, D], F32, kind="ExternalOutput").ap()
    kt_d = nc.dram_tensor("kt_scr", [4, 128, SL], BF16).ap()
    v_d = nc.dram_tensor("v_scr", [128, NKB, 4 * 128], BF16).ap()
    x1_d = nc.dram_tensor("x1_scr", [NQ * T, D], F32).ap()

    top = contextlib.ExitStack()
    with top:
        engnames = ['pe', 'act', 'dve', 'pool', 'sp']
        sems = {}
        for n in ['pe', 'act', 'dve', 'pool', 'flag']:
            sems[n] = top.enter_context(nc.semaphore(n))
        for i in range(28):
            sems[f'c{i}'] = top.enter_context(nc.semaphore(f'c{i}'))
        S = Sched(engnames, sems)
        S.regs = {'pe': nc.alloc_register(mybir.EngineType.PE, 'flag_pe'),
                  'act': nc.alloc_register(mybir.EngineType.Activation, 'flag_act'),
                  'dve': nc.alloc_register(mybir.EngineType.DVE, 'flag_dve')}

        def run_block():
            blk = nc.Block()
            with blk:
                blk.tensor(lambda e: S.replay('pe', e))
                blk.scalar(lambda e: S.replay('act', e))
                blk.vector(lambda e: S.replay('dve', e))
                blk.gpsimd(lambda e: S.replay('pool', e))
                blk.sync(lambda e: S.replay('sp', e))

        uid = [0]

        def sbt(es, name, shape, dt):
            uid[0] += 1
            return es.enter_context(nc.sbuf_tensor(f"s{uid[0]}_{name}", shape, dt))

        def pst(es, name, shape, dt=F32):
            uid[0] += 1
            return es.enter_context(nc.psum_tensor(f"p{uid[0]}_{name}", shape, dt))

        def MM(out, lhsT, rhs, start, stop, reads, writes):
            S.op('pe', 'matmul', (out,), dict(lhsT=lhsT, rhs=rhs, start=start, stop=stop), reads, writes)

        def ACTV(out, in_, func, reads, writes, **kw):
            S.op('act', 'activation', (), dict(out=out, in_=in_, func=func, **kw), reads, writes)

        def CP(eng, out, in_, reads, writes):
            S.op(eng, 'copy' if eng == 'act' else 'tensor_copy', (), dict(out=out, in_=in_), reads, writes)

        def TT(eng, out, in0, in1, op, reads, writes):
            S.op(eng, 'tensor_tensor', (), dict(out=out, in0=in0, in1=in1, op=op), reads, writes)

        def STT(eng, out, in0, scalar, in1, op0, op1, reads, writes):
            S.op(eng, 'scalar_tensor_tensor', (), dict(out=out, in0=in0, scalar=scalar, in1=in1, op0=op0, op1=op1), reads, writes)

        def TS(eng, out, in0, s1, s2, op0, op1, reads, writes):
            kw = dict(out=out, in0=in0, scalar1=s1, scalar2=s2, op0=op0)
            if op1 is not None:
                kw['op1'] = op1
            S.op(eng, 'tensor_scalar', (), kw, reads, writes)

        def MS(eng, ap, val, writes):
            S.op(eng, 'memset', (ap, val), {}, (), writes)

        ident = sbt(top, "ident", [128, 128], BF16)
        negu = sbt(top, "negu", [128, 128], BF16)
        negl = sbt(top, "negl", [128, 128], BF16)
        bdm = sbt(top, "bdm", [128, 128], BF16)
        zero = sbt(top, "zero", [128, 128], BF16)
        cst = sbt(top, "cst", [128, 4], F32)

        stg_state = {'i': 0}

        def load_weight(stg, dst_ap, src_ap, n, wname):
            sl = stg_state['i'] % 2
            stg_state['i'] += 1
            S.dma(f'stg{sl}', stg[:, sl, 0:n], src_ap, writes=[f'stg{sl}'])
            ce = ('pool', 'dve', 'act')[stg_state['i'] % 3]
            CP(ce, dst_ap, stg[:, sl, 0:n], [f'stg{sl}'], [wname])

        def norm_a(Bf, src_ap, src_reads, slot, gb):
            xs, junk, ss, hn = Bf['xs'], Bf['junk'], Bf['ss'], Bf['hn']
            xn, hnn, ssn = f'xs{slot}', f'hn{slot}', f'ss{slot}'
            S.dma(xn, xs[:, slot, :], src_ap, reads=src_reads, writes=[xn])
            MS('dve', ss[:, slot, 0:1], 0.0, [ssn])
            ACTV(junk[:, :], xs[:, slot, :], AF.Square, [xn], ['junk', ssn], accum_out=ss[:, slot, 0:1])
            ACTV(ss[:, slot, 1:2], ss[:, slot, 0:1], AF.Ln, [ssn], [ssn], scale=1.0 / D, bias=cst[:, 0:1])
            ACTV(ss[:, slot, 2:3], ss[:, slot, 1:2], AF.Exp, [ssn], [ssn], scale=-0.5)
            STT('dve', hn[:, slot, :], xs[:, slot, :], ss[:, slot, 2:3], gb[:, :], ALU.mult, ALU.mult, [xn, ssn], [hnn])

        def norm_b(Bf, slot, hnT_ap, hnT_name, tp, tp_name, ev='act'):
            hn = Bf['hn']
            for kc in range(KC):
                S.op('pe', 'transpose', (tp[:, kc, :], hn[:, slot, kc * 128:(kc + 1) * 128], ident[:, :]), {}, [f'hn{slot}'], [tp_name])
            CP(ev, hnT_ap, tp[:, :, :], [tp_name], [hnT_name])

        def norm_block(Bf, src_ap, src_reads, slot, gb, hnT_ap, hnT_name, tp, tp_name):
            norm_a(Bf, src_ap, src_reads, slot, gb)
            norm_b(Bf, slot, hnT_ap, hnT_name, tp, tp_name)

        def new_B(es, ns=2):
            return dict(xs=sbt(es, "xs", [128, ns, D], F32), junk=sbt(es, "junk", [128, D], BF16),
                        ss=sbt(es, "ss", [128, ns, 4], F32), hn=sbt(es, "hn", [128, ns, D], BF16))

        sc04 = contextlib.ExitStack()
        with sc04:
            g1b = sbt(sc04, "g1b", [128, D], F32)
            vt = sbt(sc04, "vt", [128, NKB], F32)
            OT = sbt(sc04, "OT", [128, 4, NQ * T], BF16)

            with contextlib.ExitStack() as es:
                i32 = sbt(es, "i32", [128, 4, 128], F32)
                for i, srcd in enumerate([ident_d, negu_d, negl_d, bd_d]):
                    S.dma('cst', i32[:, i, :], srcd[:, :], writes=[f'i32_{i}'])
                S.barrier()
                for i, dst in enumerate([ident, negu, negl, bdm]):
                    CP('pool', dst[:, :], i32[:, i, :], [f'i32_{i}'], [f'const{i}'])
                MS('pool', zero[:, :], 0.0, ['zero'])
                MS('pool', cst[:, 0:1], EPS, ['cst'])
                MS('pool', cst[:, 1:2], 1.0, ['cst'])
                MS('pool', OT[:, :, 0:T], 0.0, ['OTz'])
                S.dma('cst', g1b[:, :], g1_d[:, :], writes=['g1b'])
                S.dma('cst', vt[:, :], valid_d[:, :], writes=['vt'])
                S.barrier()
                run_block()

            sc12 = contextlib.ExitStack()
            with sc12:
                QT = sbt(sc12, "QT", [128, 4, NQ * T], BF16)
                with contextlib.ExitStack() as es:
                    wB = sbt(es, "wB", [128, KC, 1536], BF16)
                    stg = sbt(es, "stg1", [128, 2, 1536], F32)
                    Bf = new_B(es, 4)
                    hnT = sbt(es, "hnT", [128, 2, KC, T], BF16)
                    KTs = sbt(es, "KTs", [128, 2, 4, T], BF16)
                    Vs = sbt(es, "Vs", [128, 2, 4, T], BF16)
                    tps = [pst(es, f"tp{i}", [128, KC, 128], BF16) for i in range(2)]
                    accs = [pst(es, f"acc{i}", [128, T], F32) for i in range(4)]
                    for kc in range(KC):
                        load_weight(stg, wB[:, kc, :], win_d[kc * 128:(kc + 1) * 128, 1536:3072], 1536, 'wB')
                    acc_i = 0
                    ev_i = 0

                    def nA(t, bi):
                        blk_i = t * 4 + bi
                        norm_a(Bf, x_d[blk_i * 128:(blk_i + 1) * 128, :], [], blk_i % 4, g1b)

                    def nB(t, bi):
                        blk_i = t * 4 + bi
                        hs_ = t % 2
                        norm_b(Bf, blk_i % 4, hnT[:, hs_, :, bi * 128:(bi + 1) * 128], f'hnT{hs_}_{bi}', tps[blk_i % 2], f'tp{blk_i % 2}',
                               ev=('act' if bi % 2 == 0 else 'dve'))

                    def grpK(t, p):
                        nonlocal acc_i, ev_i
                        hs = t % 2
                        hread = [f'hnT{hs}_{bi}' for bi in range(4)]
                        a = acc_i % 4
                        acc_i += 1
                        for kc in range(KC):
                            MM(accs[a][:, :], wB[:, kc, 512 + p * 128:512 + (p + 1) * 128], hnT[:, hs, kc, :],
                               kc == 0, kc == KC - 1, ['wB'] + hread, [f'acc{a}'])
                        CP('dve' if ev_i % 2 == 0 else 'act', KTs[:, hs, p, :], accs[a][:, :], [f'acc{a}'], [f'KTs{hs}'])
                        ev_i += 1
                        if p == 3:
                            S.dma(f'kts{hs}', kt_d[:, :, t * T:(t + 1) * T].rearrange("q p c -> p q c"), KTs[:, hs, :, :],
                                  reads=[f'KTs{hs}'], writes=['kt_d'], q='pool')

                    def grpV(t, bi):
                        nonlocal acc_i, ev_i
                        hs = t % 2
                        a = acc_i % 4
                        acc_i += 1
                        for kc in range(KC):
                            MM(accs[a][:, :], hnT[:, hs, kc, bi * 128:(bi + 1) * 128], wB[:, kc, 1024:1536],
                               kc == 0, kc == KC - 1, ['wB', f'hnT{hs}_{bi}'], [f'acc{a}'])
                        CP('dve' if ev_i % 2 == 0 else 'act', Vs[:, hs, bi, :], accs[a][:, :], [f'acc{a}'], [f'Vs{hs}'])
                        ev_i += 1
                        if bi == 3:
                            S.dma(f'vs{hs}', v_d[:, t * 4:(t + 1) * 4, :], Vs[:, hs, :, :],
                                  reads=[f'Vs{hs}'], writes=['v_d'], q='pool')

                    def grpQ(t, p):
                        nonlocal acc_i, ev_i
                        hs = t % 2
                        qi = t - T0
                        hread = [f'hnT{hs}_{bi}' for bi in range(4)]
                        a = acc_i % 4
                        acc_i += 1
                        for kc in range(KC):
                            MM(accs[a][:, :], wB[:, kc, p * 128:(p + 1) * 128], hnT[:, hs, kc, :],
                               kc == 0, kc == KC - 1, ['wB'] + hread, [f'acc{a}'])
                        CP('dve' if ev_i % 2 == 0 else 'act', QT[:, p, qi * T:(qi + 1) * T], accs[a][:, :], [f'acc{a}'], [f'QT{p}_{qi}'])
                        ev_i += 1

                    for bi in range(4):
                        nA(0, bi)
                        nB(0, bi)
                    for t in range(NT):
                        nx = t + 1 < NT
                        if nx:
                            nA(t + 1, 0)
                            nA(t + 1, 1)
                        grpK(t, 0)
                        if nx:
                            nB(t + 1, 0)
                        grpK(t, 1)
                        if nx:
                            nA(t + 1, 2)
                        grpK(t, 2)
                        if nx:
                            nB(t + 1, 1)
                        grpK(t, 3)
                        if nx:
                            nA(t + 1, 3)
                        grpV(t, 0)
                        if nx:
                            nB(t + 1, 2)
                        grpV(t, 1)
                        grpV(t, 2)
                        if nx:
                            nB(t + 1, 3)
                        grpV(t, 3)
                        if t >= T0:
                            for p in range(4):
                                grpQ(t, p)
                    S.barrier()
                    run_block()

                with contextlib.ExitStack() as es:
                    KTp = sbt(es, "KTp", [128, SL], BF16)
                    Vp = sbt(es, "Vp", [128, NKB, 128], BF16)
                    M = sbt(es, "M", [128, 4, 2, T], BF16)
                    NE, NSP, NXW, NW = 4, 3, 2, 2
                    eb = sbt(es, "eb", [128, NE, 2, T], F32)
                    spb = sbt(es, "spb", [128, NSP, 2, T], BF16)
                    xwb = sbt(es, "xwb", [128, NXW, 2, T], F32)
                    wbuf = sbt(es, "wbuf", [128, NW, 2, T], BF16)
                    m32 = sbt(es, "m32", [128, 4, T], F32)
                    Zp = [pst(es, f"Z{i}", [128, 2, T]) for i in range(2)]
                    Ap = pst(es, "A", [128, 2, T])
                    Op = pst(es, "O", [128, T])
                    S.dma('cst', m32[:, :, :], mask_d[:, :, :], writes=['m32'])
                    for h in range(2):
                        CP('pool', M[:, :, h, :], m32[:, :, :], ['m32'], ['M'])
                    I32 = mybir.dt.int32
                    flagbuf = sbt(es, "flagbuf", [128, 512], I32)
                    mx = sbt(es, "mx", [128, 4], F32)
                    THRESH = 150.0
                    CH = 32
                    nchk = (NKB + CH - 1) // CH
                    itn = [0]
                    fidx = [0]

                    def load_pair(p):
                        for c in reversed(range(nchk)):
                            k0, k1 = c * CH, min(NKB, (c + 1) * CH)
                            S.dma(f'ktp{c}', KTp[:, k0 * 128:k1 * 128], kt_d[p, :, k0 * 128:k1 * 128], reads=['kt_d'], writes=[f'KTp{c}'])
                            S.dma(f'vp{c}', Vp[:, k0:k1, :], v_d[:, k0:k1, p * 128:(p + 1) * 128], reads=['v_d'], writes=[f'Vp{c}'])

                    def st1(it):
                        W, kb, p, qi, c0, n = it['W'], it['kb'], it['p'], it['qi'], it['c0'], it['n']
                        for h in range(2):
                            MM(Zp[n % 2][:, h, 0:W], KTp[64 * h:64 * h + 64, kb * 128:(kb + 1) * 128],
                               QT[64 * h:64 * h + 64, p, qi * T + c0:qi * T + c0 + W], True, True,
                               [f'KTp{kb // CH}', f'QT{p}_{qi}'], [f'Z{n % 2}'])

                    def st2(it):
                        W, n = it['W'], it['n']
                        en = f'e{n % NE}'
                        ACTV(eb[:, n % NE, :, 0:W], Zp[n % 2][:, :, 0:W], AF.Exp, [f'Z{n % 2}'], [en], scale=0.125)
                        if it['d'] is not None:
                            TT('dve', eb[:, n % NE, :, 0:W], eb[:, n % NE, :, 0:W], M[:, it['d'], :, 0:W], ALU.mult, [en, 'M'], [en])

                    def st3(it):
                        W, n = it['W'], it['n']
                        ACTV(spb[:, n % NSP, :, 0:W], eb[:, n % NE, :, 0:W], AF.Ln, [f'e{n % NE}'], [f'sp{n % NSP}'], bias=cst[:, 1:2])

                    def st4(it):
                        W, n = it['W'], it['n']
                        for h in range(2):
                            MM(Ap[:, h, 0:W], negu[:, :], spb[:, n % NSP, h, 0:W], it['first'], False,
                               [f'sp{n % NSP}', 'const1'], ['A'])

                    def st5(it):
                        W, n = it['W'], it['n']
                        ACTV(xwb[:, n % NXW, :, 0:W], Ap[:, :, 0:W], AF.Exp, ['A'], [f'xw{n % NXW}'])

                    def st6(it):
                        W, n = it['W'], it['n']
                        if it['last']:
                            return
                        for h in range(2):
                            MM(Ap[:, h, 0:W], negl[:, :], spb[:, n % NSP, h, 0:W], False, False,
                               [f'sp{n % NSP}', 'const2'], ['A'])

                    def st7(it):
                        W, n = it['W'], it['n']
                        TT('dve', wbuf[:, n % NW, :, 0:W], eb[:, n % NE, :, 0:W], xwb[:, n % NXW, :, 0:W], ALU.mult,
                           [f'e{n % NE}', f'xw{n % NXW}'], [f'w{n % NW}'])

                    def st8(it):
                        W, n, kb = it['W'], it['n'], it['kb']
                        for h in range(2):
                            MM(Op[64 * h:64 * h + 64, 0:W], Vp[:, kb, 64 * h:64 * h + 64], wbuf[:, n % NW, h, 0:W],
                               it['first'], it['last'], [f'w{n % NW}', f'Vp{kb // CH}'], ['O'])

                    stages = [(st1, 0), (st2, 1), (st3, 2), (st6, 4), (st4, 3), (st5, 3), (st7, 4), (st8, 5)]

                    def emit_segment(items):
                        nit = len(items)
                        for s_ in range(nit + 7):
                            for fn, dly in stages:
                                i_ = s_ - dly
                                if 0 <= i_ < nit:
                                    fn(items[i_])

                    def emit_flag(W):
                        fi = fidx[0]
                        fidx[0] += 1
                        for h in range(2):
                            S.op('dve', 'tensor_reduce', (), dict(out=mx[0:1, h:h + 1], in_=Ap[0:1, h, 0:W], axis=mybir.AxisListType.X, op=ALU.max),
                                 ['A'], [f'mx{h}'])
                        TT('dve', mx[0:1, 2:3], mx[0:1, 0:1], mx[0:1, 1:2], ALU.max, ['mx0', 'mx1'], ['mx2'])
                        S.flag_op('tensor_scalar', (), dict(out=flagbuf[0:1, fi:fi + 1], in0=mx[0:1, 2:3], scalar1=-THRESH, scalar2=None, op0=ALU.is_gt),
                                  ['mx2'], [f'flag{fi}'])
                        return fi

                    for p in range(4):
                        load_pair(p)
                        for qi in range(NQ):
                            g = T0 + qi
                            if qi == 0:
                                W, c0, q0, ndiag = 128, 384, g * T + 384, 1
                            else:
                                W, c0, q0, ndiag = T, 0, g * T, 4
                            kb_hi = (q0 + W) // 128 - 1
                            nb = kb_hi + 1
                            segs = [list(range(0, min(nb, ndiag + 3)))]
                            sz = 2
                            while segs[-1][-1] + 1 < nb:
                                st_ = segs[-1][-1] + 1
                                segs.append(list(range(st_, min(nb, st_ + sz))))
                                if len(segs) > 2:
                                    sz *= 2
                            nopen = 0
                            for si, seg in enumerate(segs):
                                items = []
                                for b in seg:
                                    kb = kb_hi - b
                                    d = kb - q0 // 128
                                    items.append(dict(p=p, qi=qi, W=W, c0=c0, kb=kb, d=(d if d >= 0 else None),
                                                      first=(b == 0), last=(b == nb - 1), n=itn[0]))
                                    itn[0] += 1
                                emit_segment(items)
                                if si < len(segs) - 1:
                                    fi = emit_flag(W)
                                    S.begin_cond(flagbuf[0:1, fi:fi + 1], ['pe', 'act', 'dve'])
                                    nopen += 1
                            for _ in range(nopen):
                                S.end_cond()
                            CP('dve', OT[:, p, qi * T + c0:qi * T + c0 + W], Op[:, 0:W], ['O', 'OTz'], [f'OT{p}_{qi}'])
                    S.barrier()
                    run_block()

            sc34 = contextlib.ExitStack()
            with sc34:
                OAT = sbt(sc34, "OAT", [128, 4, NQ * T], BF16)
                with contextlib.ExitStack() as es:
                    wA = sbt(es, "wA", [128, KC, 1536], BF16)
                    with contextlib.ExitStack() as es2:
                        stg = sbt(es2, "stg3", [128, 2, 1536], F32)
                        for kc in range(KC):
                            load_weight(stg, wA[:, kc, :], win_d[kc * 128:(kc + 1) * 128, 0:1536], 1536, 'wA')
                        S.barrier()
                    TB = sbt(es, "TB", [128, 8, 640], F32)
                    gq = sbt(es, "gq", [128, 1], F32)
                    gk = sbt(es, "gk", [128, 1], F32)
                    Bf = new_B(es)
                    hnT = sbt(es, "hnT", [128, KC, T], BF16)
                    QAT = sbt(es, "QAT", [128, 4, T], BF16)
                    KAT = sbt(es, "KAT", [128, 4, 2, T], BF16)
                    VA = sbt(es, "VA", [128, 2, 4, T], BF16)
                    VLD = sbt(es, "VLD", [128, 2, 4, 64], BF16)
                    sqb = sbt(es, "sqb", [128, 2, T], BF16)
                    qgb = sbt(es, "qgb", [128, 2, T], F32)
                    lnb = sbt(es, "lnb", [128, 2, T], F32)
                    sbb = sbt(es, "sbb", [128, 2, 2, T], F32)
                    pTb = sbt(es, "pTb", [128, 2, 2, T], BF16)
                    rdb = sbt(es, "rdb", [128, T], F32)
                    tp3 = pst(es, "tp3", [128, KC, 128], BF16)
                    acc3t = pst(es, "acc3", [128, 2, T])
                    acc3 = [acc3t[:, i, :] for i in range(2)]
                    SSp = pst(es, "SSp", [128, T])
                    SPSt = pst(es, "SPS", [128, 2, T])
                    sbufs = [(SPSt, 'SPS'), (acc3t, 'acc3_')]
                    OAp = pst(es, "OAp", [128, T])
                    DENp = pst(es, "DENp", [128, T])
                    S.dma('cst', TB[:, :, :], tb_d[:, :, :], writes=['TB'])
                    S.dma('cst', gq[:, :], gq_d[:, :], writes=['gq'])
                    S.dma('cst', gk[:, :], gk_d[:, :], writes=['gk'])
                    S.barrier()
                    blkc = 0
                    acc_i = 0
                    nrm = [0]
                    jj = [0]

                    def qknorm(a, gcol, gname, dst_ap, dst_name):
                        k = nrm[0] % 2
                        nrm[0] += 1
                        ACTV(sqb[:, k, :], acc3[a][:, :], AF.Square, [f'acc3_{a}'], [f'sqb{k}'])
                        S.op('act', 'mul', (), dict(out=qgb[:, k, :], in_=acc3[a][:, :], mul=gcol[:, 0:1]), [f'acc3_{a}', gname], [f'qgb{k}'])
                        MM(SSp[:, :], bdm[:, :], sqb[:, k, :], True, True, [f'sqb{k}', 'const3'], ['SSp'])
                        ACTV(lnb[:, k, :], SSp[:, :], AF.Ln, ['SSp'], [f'lnb{k}'], scale=1.0 / 64, bias=cst[:, 0:1])
                        ACTV(lnb[:, k, :], lnb[:, k, :], AF.Exp, [f'lnb{k}'], [f'lnb{k}'], scale=-0.5)
                        TT('dve', dst_ap, qgb[:, k, :], lnb[:, k, :], ALU.mult, [f'qgb{k}', f'lnb{k}'], [dst_name])

                    for t in range(T0 - 1, NT):
                        qi = t - T0
                        sl = t % 2
                        sl0 = blkc
                        norm_a(Bf, x_d[t * 512:t * 512 + 128, :], [], sl0 % 2, g1b)
                        for bi in range(4):
                            if bi + 1 < 4:
                                norm_a(Bf, x_d[(t * 4 + bi + 1) * 128:(t * 4 + bi + 2) * 128, :], [], (sl0 + bi + 1) % 2, g1b)
                            norm_b(Bf, (sl0 + bi) % 2, hnT[:, :, bi * 128:(bi + 1) * 128], f'hnT_{bi}', tp3, 'tp3')
                            blkc += 1
                        hread = [f'hnT_{bi}' for bi in range(4)]
                        for p in range(4):
                            a = acc_i % 2
                            acc_i += 1
                            for kc in range(KC):
                                MM(acc3[a][:, :], wA[:, kc, 512 + p * 128:512 + (p + 1) * 128], hnT[:, kc, :],
                                   kc == 0, kc == KC - 1, ['wA'] + hread, [f'acc3_{a}'])
                            qknorm(a, gk, 'gk', KAT[:, p, sl, :], f'KAT{sl}_{p}')
                        for bi in range(4):
                            a = acc_i % 2
                            acc_i += 1
                            for kc in range(KC):
                                MM(acc3[a][:, :], hnT[:, kc, bi * 128:(bi + 1) * 128], wA[:, kc, 1024:1536],
                                   kc == 0, kc == KC - 1, ['wA', f'hnT_{bi}'], [f'acc3_{a}'])
                            CP('dve', VA[:, sl, bi, :], acc3[a][:, :], [f'acc3_{a}'], [f'VA{sl}_{bi}'])
                            CP('pool', VLD[:, sl, bi, :], vt[:, t * 4 + bi:t * 4 + bi + 1].to_broadcast([128, 64]),
                               ['vt'], [f'VLD{sl}_{bi}'])
                        if qi < 0:
                            continue
                        for p in range(4):
                            a = acc_i % 2
                            acc_i += 1
                            for kc in range(KC):
                                MM(acc3[a][:, :], wA[:, kc, p * 128:(p + 1) * 128], hnT[:, kc, :],
                                   kc == 0, kc == KC - 1, ['wA'] + hread, [f'acc3_{a}'])
                            qknorm(a, gq, 'gq', QAT[:, p, :], f'QAT{p}')
                        for p in range(4):
                            MM(OAp[:, :], zero[:, :], hnT[:, 0, :], True, False, ['zero'] + hread, ['OAp'])
                            MM(DENp[:, :], zero[:, :], hnT[:, 0, :], True, False, ['zero'] + hread, ['DENp'])
                            def jgeom(j):
                                i_lo, i_hi = max(0, j - 4), min(3, j)
                                N = (i_hi - i_lo + 1) * 128
                                ksl = (1 - sl) if j < 4 else sl
                                return i_lo, N, ksl, j % 4, (4 - j + i_lo) * 128

                            def emitS(j, k):
                                i_lo, N, ksl, cj, tb0 = jgeom(j)
                                spt, spn = sbufs[k]
                                for h in range(2):
                                    MM(spt[:, h, 0:N], KAT[64 * h:64 * h + 64, p, ksl, cj * 128:(cj + 1) * 128],
                                       QAT[64 * h:64 * h + 64, p, i_lo * 128:i_lo * 128 + N], True, True,
                                       [f'KAT{ksl}_{p}', f'QAT{p}'], [f'{spn}{h}'])

                            kbase = jj[0]
                            jj[0] += 8
                            emitS(0, kbase % 2)
                            for j in range(8):
                                i_lo, N, ksl, cj, tb0 = jgeom(j)
                                k = (kbase + j) % 2
                                spt, spn = sbufs[k]
                                if j + 1 < 8:
                                    emitS(j + 1, (kbase + j + 1) % 2)
                                STT('dve', sbb[:, k, :, 0:N], spt[:, :, 0:N], 0.125, TB[:, 2 * p:2 * p + 2, tb0:tb0 + N], ALU.mult, ALU.add,
                                    [f'{spn}0', f'{spn}1', 'TB'], [f'sbb{k}'])
                                ACTV(pTb[:, k, :, 0:N], sbb[:, k, :, 0:N], AF.Exp, [f'sbb{k}'], [f'pTb{k}'])
                                for h in range(2):
                                    hh = 2 * p + h
                                    MM(OAp[64 * h:64 * h + 64, i_lo * 128:i_lo * 128 + N], VA[:, ksl, cj, hh * 64:(hh + 1) * 64],
                                       pTb[:, k, h, 0:N], False, False, [f'pTb{k}', f'VA{ksl}_{cj}'], ['OAp'])
                                    MM(DENp[64 * h:64 * h + 64, i_lo * 128:i_lo * 128 + N], VLD[:, ksl, cj, :],
                                       pTb[:, k, h, 0:N], False, False, [f'pTb{k}', f'VLD{ksl}_{cj}'], ['DENp'])
                            TS('dve', rdb[:, :], DENp[:, :], 1e-30, None, ALU.max, None, ['DENp'], ['rdb'])
                            ACTV(rdb[:, :], rdb[:, :], AF.Ln, ['rdb'], ['rdb'])
                            ACTV(rdb[:, :], rdb[:, :], AF.Exp, ['rdb'], ['rdb'], scale=-1.0)
                            TT('dve', OAT[:, p, qi * T:(qi + 1) * T], OAp[:, :], rdb[:, :], ALU.mult, ['OAp', 'rdb'], [f'OAT{p}_{qi}'])
                    S.barrier()
                    run_block()

                with contextlib.ExitStack() as es:
                    wG = sbt(es, "wG", [128, KC, 2048], BF16)
                    wbra = sbt(es, "wbra", [128, 4, D], BF16)
                    wbrb = sbt(es, "wbrb", [128, 4, D], BF16)
                    wout = sbt(es, "wout", [128, KC, D], BF16)
                    stg = sbt(es, "stg4", [128, 2, D], F32)
                    Bf = new_B(es)
                    xr = sbt(es, "xr", [128, 2, D], F32)
                    hnT = sbt(es, "hnT", [128, KC, T], BF16)
                    sg = sbt(es, "sg", [128, 2, T], F32)
                    mm = sbt(es, "mm", [128, 2, T], F32)
                    MT = sbt(es, "MT", [128, KC, T], BF16)
                    tp4 = pst(es, "tp4", [128, KC, 128], BF16)
                    Gp = [pst(es, f"G{i}", [128, T]) for i in range(2)]
                    Yp = [pst(es, f"Y{i}", [128, T]) for i in range(2)]
                    Xp = [pst(es, f"X{i}", [128, T]) for i in range(2)]
                    for kc in range(KC):
                        for hf in range(2):
                            load_weight(stg, wG[:, kc, hf * D:(hf + 1) * D], win_d[kc * 128:(kc + 1) * 128, 3072 + hf * D:3072 + (hf + 1) * D], D, 'wG')
                    for p in range(4):
                        load_weight(stg, wbra[:, p, :], wa_d[p * 128:(p + 1) * 128, :], D, 'wbra')
                        load_weight(stg, wbrb[:, p, :], wb_d[p * 128:(p + 1) * 128, :], D, 'wbrb')
                    for kc in range(KC):
                        load_weight(stg, wout[:, kc, :], wo_d[kc * 128:(kc + 1) * 128, :], D, 'wout')
                    blkc = 0
                    ob = 0
                    for qi in range(NQ):
                        t = T0 + qi
                        bis = [3] if qi == 0 else [0, 1, 2, 3]
                        cs, cn = (384, 128) if qi == 0 else (0, T)
                        sl0 = blkc
                        norm_a(Bf, x_d[(t * 4 + bis[0]) * 128:(t * 4 + bis[0] + 1) * 128, :], [], sl0 % 2, g1b)
                        for ii, bi in enumerate(bis):
                            if ii + 1 < len(bis):
                                nb_ = bis[ii + 1]
                                norm_a(Bf, x_d[(t * 4 + nb_) * 128:(t * 4 + nb_ + 1) * 128, :], [], (sl0 + ii + 1) % 2, g1b)
                            norm_b(Bf, (sl0 + ii) % 2, hnT[:, :, bi * 128:(bi + 1) * 128], f'hnT_{bi}', tp4, 'tp4')
                            blkc += 1
                        hread = [f'hnT_{bi}' for bi in bis]
                        for oc in range(KC):
                            for br in range(2):
                                for kc in range(KC):
                                    MM(Gp[br][:, cs:cs + cn], wG[:, kc, br * D + oc * 128:br * D + (oc + 1) * 128], hnT[:, kc, cs:cs + cn],
                                       kc == 0, kc == KC - 1, ['wG'] + hread, [f'G{br}'])
                                ACTV(sg[:, br, cs:cs + cn], Gp[br][:, cs:cs + cn], AF.Sigmoid, [f'G{br}'], [f'sg{br}'])
                                wsrc = wbra if br == 0 else wbrb
                                osrc = OAT if br == 0 else OT
                                for p in range(4):
                                    MM(Yp[br][:, cs:cs + cn], wsrc[:, p, oc * 128:(oc + 1) * 128], osrc[:, p, qi * T + cs:qi * T + cs + cn],
                                       p == 0, p == 3,
                                       ['wbra' if br == 0 else 'wbrb', (f'OAT{p}_{qi}' if br == 0 else f'OT{p}_{qi}'), 'OTz'], [f'Y{br}'])
                                TT('dve', mm[:, br, cs:cs + cn], Yp[br][:, cs:cs + cn], sg[:, br, cs:cs + cn], ALU.mult, [f'Y{br}', f'sg{br}'], [f'mm{br}'])
                            TT('pool', MT[:, oc, cs:cs + cn], mm[:, 0, cs:cs + cn], mm[:, 1, cs:cs + cn], ALU.add, ['mm0', 'mm1'], [f'MT{oc}'])
                        mread = [f'MT{oc}' for oc in range(KC)]
                        for bi in bis:
                            blk_i = t * 4 + bi
                            o = ob % 2
                            ob += 1
                            S.dma(f'xr{o}', xr[:, o, :], x_d[blk_i * 128:(blk_i + 1) * 128, :], writes=[f'xr{o}'])
                            for half in range(2):
                                for oc in range(KC):
                                    MM(Xp[half][:, :], MT[:, oc, bi * 128:(bi + 1) * 128], wout[:, oc, half * T:(half + 1) * T],
                                       oc == 0, oc == KC - 1, ['wout'] + mread, [f'X{half}'])
                                TT('dve', xr[:, o, half * T:(half + 1) * T], Xp[half][:, :], xr[:, o, half * T:(half + 1) * T], ALU.add,
                                   [f'X{half}', f'xr{o}'], [f'xr{o}'])
                            row = qi * T + bi * 128
                            S.dma(f'x1w{o}', x1_d[row:row + 128, :], xr[:, o, :], reads=[f'xr{o}'], writes=['x1_d'], q='pool')
                    S.barrier()
                    run_block()

        with contextlib.ExitStack() as es:
            wup = sbt(es, "wup", [128, KC, 2 * DFF], BF16)
            wdn = sbt(es, "wdn", [128, NFC, D], BF16)
            g2b = sbt(es, "g2b", [128, D], F32)
            cw = sbt(es, "cw", [128, 44, 3], F32)
            cb = sbt(es, "cb", [128, 44], F32)
            HALO = sbt(es, "HALO", [128, 44, 2], F32)
            with contextlib.ExitStack() as es2:
                stg = sbt(es2, "stg5", [128, 2, 2816], F32)
                for kc in range(KC):
                    for hf in range(2):
                        load_weight(stg, wup[:, kc, hf * DFF:(hf + 1) * DFF], wup_d[kc * 128:(kc + 1) * 128, hf * DFF:(hf + 1) * DFF], DFF, 'wup')
                for fc in range(NFC):
                    load_weight(stg, wdn[:, fc, :], wdn_d[fc * 128:(fc + 1) * 128, :], D, 'wdn')
                S.dma('cst', g2b[:, :], g2_d[:, :], writes=['g2b'])
                S.dma('cst', cw[:, :, :], cw_d[:, :, :], writes=['cw'])
                S.dma('cst', cb[:, :], cb_d[:, :], writes=['cb'])
                MS('pool', HALO[:, :, :], 0.0, ['HALO'])
                S.barrier()
            Bf = dict(xs=sbt(es, "xs", [128, 2, D], F32), junk=sbt(es, "junk", [128, D], BF16),
                      ss=sbt(es, "ss", [128, 2, 4], F32), hn=sbt(es, "hn", [128, 2, D], BF16))
            hnT = sbt(es, "hnT", [128, KC, T], BF16)
            hb = sbt(es, "hb", [128, 2, 2, T + 2], F32)
            cv = sbt(es, "cv", [128, 2, T], F32)
            sgl = sbt(es, "sgl", [128, T], F32)
            ACTT = sbt(es, "ACTT", [128, NFC, T], BF16)
            yo = sbt(es, "yo", [128, D], F32)
            tp5 = pst(es, "tp5", [128, KC, 128], BF16)
            Hp = [[pst(es, f"H{b}_{u}", [128, T]) for u in range(2)] for b in range(2)]
            Yd = [pst(es, f"Yd{i}", [128, T]) for i in range(2)]
            blkc = 0
            fcc = 0
            for qi in range(NQ):
                bis = [3] if qi == 0 else [0, 1, 2, 3]
                sl0 = blkc
                norm_a(Bf, x1_d[qi * T + bis[0] * 128:qi * T + bis[0] * 128 + 128, :], ['x1_d'], sl0 % 2, g2b)
                for ii, bi in enumerate(bis):
                    if ii + 1 < len(bis):
                        r2 = qi * T + bis[ii + 1] * 128
                        norm_a(Bf, x1_d[r2:r2 + 128, :], ['x1_d'], (sl0 + ii + 1) % 2, g2b)
                    norm_b(Bf, (sl0 + ii) % 2, hnT[:, :, bi * 128:(bi + 1) * 128], f'hnT_{bi}', tp5, 'tp5')
                    blkc += 1
                hread = [f'hnT_{bi}' for bi in bis]
                for fc in range(NFC):
                    b = fcc % 2
                    fcc += 1
                    for u in range(2):
                        ch = fc + u * NFC
                        hbn = f'hb{b}_{u}'
                        if qi == 0:
                            for kc in range(KC):
                                MM(Hp[b][u][:, 384:T], wup[:, kc, ch * 128:(ch + 1) * 128], hnT[:, kc, 384:T],
                                   kc == 0, kc == KC - 1, ['wup'] + hread, [f'H{b}_{u}'])
                            CP('act', HALO[:, ch, :], Hp[b][u][:, T - 2:T], [f'H{b}_{u}'], [f'HALO{ch}'])
                            continue
                        for kc in range(KC):
                            MM(Hp[b][u][:, :], wup[:, kc, ch * 128:(ch + 1) * 128], hnT[:, kc, :],
                               kc == 0, kc == KC - 1, ['wup'] + hread, [f'H{b}_{u}'])
                        CP('pool', hb[:, b, u, 0:2], HALO[:, ch, :], [f'HALO{ch}'], [hbn])
                        CP('act', hb[:, b, u, 2:T + 2], Hp[b][u][:, :], [f'H{b}_{u}'], [hbn])
                        CP('pool', HALO[:, ch, :], hb[:, b, u, T:T + 2], [hbn], [f'HALO{ch}'])
                        ce = 'dve'
                        cvn = f'cv{u}'
                        ACTV(cv[:, u, :], Hp[b][u][:, :], AF.Identity, [f'H{b}_{u}', 'cw', 'cb'], [cvn], scale=cw[:, ch, 2:3], bias=cb[:, ch:ch + 1])
                        STT(ce, cv[:, u, :], hb[:, b, u, 1:T + 1], cw[:, ch, 1:2], cv[:, u, :], ALU.mult, ALU.add, [hbn, cvn, 'cw'], [cvn])
                        STT(ce, cv[:, u, :], hb[:, b, u, 0:T], cw[:, ch, 0:1], cv[:, u, :], ALU.mult, ALU.add, [hbn, cvn, 'cw'], [cvn])
                    if qi == 0:
                        continue
                    ACTV(sgl[:, :], cv[:, 0, :], AF.Silu, ['cv0'], ['sgl'])
                    TT('dve', ACTT[:, fc, :], sgl[:, :], cv[:, 1, :], ALU.mult, ['sgl', 'cv1'], [f'ACTT{fc}'])
                if qi == 0:
                    continue
                aread = [f'ACTT{fc}' for fc in range(NFC)]
                for bi in range(4):
                    row = qi * T + bi * 128
                    S.dma('yo_in', yo[:, :], x1_d[row:row + 128, :], reads=['x1_d', 'y_d'], writes=['yo'])
                    for half in range(2):
                        for fc in range(NFC):
                            MM(Yd[half][:, :], ACTT[:, fc, bi * 128:(bi + 1) * 128], wdn[:, fc, half * T:(half + 1) * T],
                               fc == 0, fc == NFC - 1, ['wdn'] + aread, [f'Yd{half}'])
                        TT('dve', yo[:, half * T:(half + 1) * T], Yd[half][:, :], yo[:, half * T:(half + 1) * T], ALU.add,
                           [f'Yd{half}', 'yo'], ['yo'])
                    orow = (qi - 1) * T + bi * 128
                    S.dma('yw', y_d[orow:orow + 128, :], yo[:, :], reads=['yo'], writes=['y_d'], q='pool')
            S.barrier()
            run_block()
        print("megakernel ops", S.nops, "waits", S.nwaits, {k: v for k, v in S.cnt.items() if k in engnames})
    return nc


_CACHE = {}


def _consts():
    ident = np.eye(128, dtype=np.float32)
    kk = np.arange(128)
    negu = -(kk[:, None] >= kk[None, :]).astype(np.float32)
    negl = -(kk[:, None] < kk[None, :]).astype(np.float32)
    bd = np.zeros((128, 128), np.float32)
    bd[:64, :64] = 1.0
    bd[64:, 64:] = 1.0
    qq = np.arange(T)
    mask = np.zeros((128, 4, T), np.float32)
    for d in range(4):
        mask[:, d, :] = ((128 * d + kk[:, None]) < qq[None, :]).astype(np.float32)
    return ident, negu, negl, bd, mask


def _tb_table(rel_bias):
    kk = np.arange(128)[:, None]
    qq = np.arange(128)[None, :]
    tb = np.empty((128, 8, 640), np.float32)
    for rp in range(5):
        idx = np.clip(qq - kk + rp * 128, -128, 128) + 128
        blk = rel_bias[:, idx]
        vis = np.ones((128, 128), bool)
        if rp == 0:
            vis = (kk < 64) | (qq >= 64)
        if rp == 4:
            vis = (kk >= 64) | (qq < 64)
        blk = np.where(vis[None], blk, np.float32(NEG))
        tb[:, :, rp * 128:(rp + 1) * 128] = blk.transpose(1, 0, 2)
    return tb


def kernel(x, norm1_g, w_in, q_norm_g, k_norm_g, rel_bias, w_branch_a, w_branch_b, w_out, norm2_g,
           w_ffn_up, ffn_conv_w, ffn_conv_b, w_ffn_down):
    x = np.asarray(x, np.float32)
    Bn, Sq, Dm = x.shape
    assert Bn == 2 and Dm == D and Sq % 2048 == 0
    NO = Sq // 2048
    NT = Sq // T
    key = (NT, NO)
    if key not in _CACHE:
        _CACHE[key] = build(NT, NO)
    nc = _CACHE[key]
    f = lambda a: np.ascontiguousarray(np.asarray(a, np.float32))
    ident, negu, negl, bd, mask = _consts()
    own = NO * T
    shared = {
        "g1": f(np.broadcast_to(np.asarray(norm1_g, np.float32)[0][None, :], (128, D))),
        "g2": f(np.broadcast_to(np.asarray(norm2_g, np.float32)[0][None, :], (128, D))),
        "w_in": f(w_in[0]),
        "gq": f(np.tile(np.asarray(q_norm_g, np.float32)[0], 2)[:, None]),
        "gk": f(np.tile(np.asarray(k_norm_g, np.float32)[0], 2)[:, None]),
        "tb": f(_tb_table(np.asarray(rel_bias, np.float32)[0])),
        "w_a": f(w_branch_a[0]), "w_b": f(w_branch_b[0]), "w_o": f(w_out[0]),
        "w_up": f(w_ffn_up[0]),
        "cw": f(np.asarray(ffn_conv_w, np.float32)[0].reshape(3, 44, 128).transpose(2, 1, 0)),
        "cb": f(np.asarray(ffn_conv_b, np.float32)[0].reshape(44, 128).T),
        "w_dn": f(w_ffn_down[0]),
        "ident": ident, "negu": negu, "negl": negl, "bd": bd, "mask": mask,
    }
    in_maps = []
    for c in range(8):
        b, j = c // 4, c % 4
        real = (j + 1) * own
        pad = Sq - real
        xl = np.zeros((Sq, D), np.float32)
        xl[pad:] = x[b, :real]
        valid = np.zeros((Sq,), np.float32)
        valid[pad:] = 1.0
        m = dict(shared)
        m["x"] = xl
        m["valid"] = f(valid.reshape(Sq // 128, 128).T)
        in_maps.append(m)
    res = run_bass_kernel_spmd(nc, in_maps, core_ids=list(range(8)))
    out = np.empty((Bn, Sq, D), np.float32)
    for c in range(8):
        b, j = c // 4, c % 4
        out[b, j * own:(j + 1) * own] = res.results[c]["y"]
    return out
```

```python
import contextlib
import numpy as np
import concourse.bass as bass
import concourse.mybir as mybir
from concourse.bass_utils import run_bass_kernel_spmd

F32 = mybir.dt.float32
BF16 = mybir.dt.bfloat16
AF = mybir.ActivationFunctionType
ALU = mybir.AluOpType

D = 1024
KC = 8
T = 512
DFF = 2816
NFC = 22
EPS = 1e-6
NEG = -30000.0


class Sched:
    def __init__(self, engnames, sems):
        self.engnames = engnames
        self.prog = {k: [] for k in engnames}
        self.stack = {k: [self.prog[k]] for k in engnames}
        self.cond = None
        self.sems = sems
        self.free_sems = [k for k in sems if k.startswith('c')]
        self.alias = {}
        self.cnt = {}
        self.mult = {}
        for k in engnames:
            self.cnt[k] = 0
            self.mult[k] = 1
        self.cnt['flag'] = 0
        self.mult['flag'] = 1
        self.lastw = {}
        self.readers = {}
        self.waited = {}
        self.nops = 0
        self.nwaits = 0

    def chan(self, name):
        if name not in self.alias:
            s = self.free_sems.pop(0)
            self.alias[name] = s
            self.cnt[name] = 0
            self.mult[name] = 16
        return name

    def sem(self, p):
        return self.sems[self.alias.get(p, p)]

    def _deps(self, reads, writes):
        deps = {}

        def add(ps):
            p, s = ps
            if deps.get(p, 0) < s:
                deps[p] = s
        for r in reads:
            if r in self.lastw:
                add(self.lastw[r])
        for w in writes:
            if w in self.lastw:
                add(self.lastw[w])
            for rd in self.readers.get(w, ()):
                add(rd)
        return deps

    def _emit_waits(self, eng, deps, skip_self):
        wd = self.waited.setdefault(eng, {})
        for p, s in deps.items():
            if p == eng and skip_self:
                continue
            if wd.get(p, 0) >= s:
                continue
            self.stack[eng][-1].append(('w', self.sem(p), s * self.mult[p]))
            wd[p] = s
            self.nwaits += 1

    def _record(self, prod, seq, reads, writes):
        for r in reads:
            self.readers.setdefault(r, []).append((prod, seq))
        for w in writes:
            self.lastw[w] = (prod, seq)
            self.readers[w] = []

    def op(self, eng, meth, args, kw, reads=(), writes=()):
        deps = self._deps(reads, writes)
        self._emit_waits(eng, deps, skip_self=(eng == 'pe'))
        self.stack[eng][-1].append(('o', (meth, args, kw), self.sems[eng], 1))
        self.cnt[eng] += 1
        self._record(eng, self.cnt[eng], reads, writes)
        self.nops += 1

    def dma(self, ch, out, in_, reads=(), writes=(), q='sp'):
        self.chan(ch)
        deps = self._deps(reads, writes)
        self._emit_waits(q, deps, skip_self=False)
        self.stack[q][-1].append(('o', ('dma_start', (), dict(out=out, in_=in_)), self.sem(ch), 16))
        self.cnt[ch] += 1
        self._record(ch, self.cnt[ch], reads, writes)
        self.nops += 1

    def flag_op(self, meth, args, kw, reads=(), writes=()):
        deps = self._deps(reads, writes)
        self._emit_waits('dve', deps, skip_self=False)
        self.stack['dve'][-1].append(('o', (meth, args, kw), self.sems['flag'], 1))
        self.cnt['flag'] += 1
        self._record('flag', self.cnt['flag'], reads, writes)

    def begin_cond(self, flag_ap, engines):
        import copy
        if self.cond is None:
            self.cond = []
        self.cond.append(dict(flag_ap=flag_ap, engines=engines, seq=self.cnt['flag'],
                              start={e: self.cnt[e] for e in engines}, fstart=self.cnt['flag'],
                              snap=copy.deepcopy(self.waited)))
        for e in engines:
            self.stack[e].append([])

    def end_cond(self):
        c = self.cond.pop()
        nf = self.cnt['flag'] - c['fstart']
        for e in c['engines']:
            body = self.stack[e].pop()
            n = self.cnt[e] - c['start'][e]
            self.stack[e][-1].append(('if', c['flag_ap'], c['seq'], body, c['start'][e], n,
                                      nf if e == 'dve' else 0))
        self.waited = c['snap']

    def barrier(self):
        for eng in self.engnames:
            wd = self.waited.setdefault(eng, {})
            for p, c in self.cnt.items():
                if c > 0 and p != eng and wd.get(p, 0) < c:
                    self.stack[eng][-1].append(('w', self.sem(p), c * self.mult[p]))
                    wd[p] = c
        for eng in self.engnames:
            if eng != 'sp' and self.cnt[eng] > 0:
                self.stack[eng][-1].append(('w', self.sems[eng], self.cnt[eng]))

    def _replay_list(self, eng, e, lst):
        for it in lst:
            if it[0] == 'w':
                e.wait_ge(it[1], it[2])
            elif it[0] == 'o':
                meth, args, kw = it[1]
                getattr(e, meth)(*args, **kw).then_inc(it[2], it[3])
            else:
                _, flag_ap, seq, body, start, n, dve_else = it
                e.wait_ge(self.sems['flag'], seq)
                reg = self.regs[eng]
                e.reg_load(reg, flag_ap)
                with e.If_ne(reg, 0):
                    self._replay_list(eng, e, body)
                with e.Else():
                    if start > 0:
                        e.wait_ge(self.sems[eng], start)
                    if n > 0:
                        e.sem_inc(self.sems[eng], n)
                    if dve_else:
                        e.sem_inc(self.sems['flag'], dve_else)

    def replay(self, eng, e):
        assert len(self.stack[eng]) == 1
        self._replay_list(eng, e, self.prog[eng])
        self.prog[eng] = []
        self.stack[eng] = [self.prog[eng]]


def build(NT, NO):
    SL = NT * T
    NKB = SL // 128
    NQ = NO + 1
    T0 = NT - NO - 1
    nc = bass.Bass("TRN2", target_bir_lowering=False)

    def din(name, shape):
        return nc.dram_tensor(name, shape, F32, kind="ExternalInput").ap()
    x_d = din("x", [SL, D])
    valid_d = din("valid", [128, NKB])
    g1_d = din("g1", [128, D])
    g2_d = din("g2", [128, D])
    win_d = din("w_in", [D, 5120])
    gq_d = din("gq", [128, 1])
    gk_d = din("gk", [128, 1])
    tb_d = din("tb", [128, 8, 640])
    wa_d = din("w_a", [512, D])
    wb_d = din("w_b", [512, D])
    wo_d = din("w_o", [D, D])
    wup_d = din("w_up", [D, 2 * DFF])
    cw_d = din("cw", [128, 44, 3])
    cb_d = din("cb", [128, 44])
    wdn_d = din("w_dn", [DFF, D])
    ident_d = din("ident", [128, 128])
    negu_d = din("negu", [128, 128])
    negl_d = din("negl", [128, 128])
    bd_d = din("bd", [128, 128])
    mask_d = din("mask", [128, 4, T])
    y_d = nc.dram_tensor("y", [NO * T, D], F32, kind="ExternalOutput").ap()
    kt_d = nc.dram_tensor("kt_scr", [4, 128, SL], BF16).ap()
    v_d = nc.dram_tensor("v_scr", [128, NKB, 4 * 128], BF16).ap()
    x1_d = nc.dram_tensor("x1_scr", [NQ * T, D], F32).ap()

    top = contextlib.ExitStack()
    with top:
        engnames = ['pe', 'act', 'dve', 'pool', 'sp']
        sems = {}
        for n in ['pe', 'act', 'dve', 'pool', 'flag']:
            sems[n] = top.enter_context(nc.semaphore(n))
        for i in range(28):
            sems[f'c{i}'] = top.enter_context(nc.semaphore(f'c{i}'))
        S = Sched(engnames, sems)
        S.regs = {'pe': nc.alloc_register(mybir.EngineType.PE, 'flag_pe'),
                  'act': nc.alloc_register(mybir.EngineType.Activation, 'flag_act'),
                  'dve': nc.alloc_register(mybir.EngineType.DVE, 'flag_dve')}

        def run_block():
            blk = nc.Block()
            with blk:
                blk.tensor(lambda e: S.replay('pe', e))
                blk.scalar(lambda e: S.replay('act', e))
                blk.vector(lambda e: S.replay('dve', e))
                blk.gpsimd(lambda e: S.replay('pool', e))
                blk.sync(lambda e: S.replay('sp', e))

        uid = [0]

        def sbt(es, name, shape, dt):
            uid[0] += 1
            return es.enter_context(nc.sbuf_tensor(f"s{uid[0]}_{name}", shape, dt))

        def pst(es, name, shape, dt=F32):
            uid[0] += 1
            return es.enter_context(nc.psum_tensor(f"p{uid[0]}_{name}", shape, dt))

        def MM(out, lhsT, rhs, start, stop, reads, writes):
            S.op('pe', 'matmul', (out,), dict(lhsT=lhsT, rhs=rhs, start=start, stop=stop), reads, writes)

        def ACTV(out, in_, func, reads, writes, **kw):
            S.op('act', 'activation', (), dict(out=out, in_=in_, func=func, **kw), reads, writes)

        def CP(eng, out, in_, reads, writes):
            S.op(eng, 'copy' if eng == 'act' else 'tensor_copy', (), dict(out=out, in_=in_), reads, writes)

        def TT(eng, out, in0, in1, op, reads, writes):
            S.op(eng, 'tensor_tensor', (), dict(out=out, in0=in0, in1=in1, op=op), reads, writes)

        def STT(eng, out, in0, scalar, in1, op0, op1, reads, writes):
            S.op(eng, 'scalar_tensor_tensor', (), dict(out=out, in0=in0, scalar=scalar, in1=in1, op0=op0, op1=op1), reads, writes)

        def TS(eng, out, in0, s1, s2, op0, op1, reads, writes):
            kw = dict(out=out, in0=in0, scalar1=s1, scalar2=s2, op0=op0)
            if op1 is not None:
                kw['op1'] = op1
            S.op(eng, 'tensor_scalar', (), kw, reads, writes)

        def MS(eng, ap, val, writes):
            S.op(eng, 'memset', (ap, val), {}, (), writes)

        ident = sbt(top, "ident", [128, 128], BF16)
        negu = sbt(top, "negu", [128, 128], BF16)
        negl = sbt(top, "negl", [128, 128], BF16)
        bdm = sbt(top, "bdm", [128, 128], BF16)
        zero = sbt(top, "zero", [128, 128], BF16)
        cst = sbt(top, "cst", [128, 4], F32)

        stg_state = {'i': 0, 'ns': 2}

        def load_weight(stg, dst_ap, src_ap, n, wname):
            sl = stg_state['i'] % stg_state['ns']
            stg_state['i'] += 1
            S.dma(f'stg{sl}', stg[:, sl, 0:n], src_ap, writes=[f'stg{sl}'])
            ce = ('pool', 'dve', 'act')[stg_state['i'] % 3]
            CP(ce, dst_ap, stg[:, sl, 0:n], [f'stg{sl}'], [wname])

        def norm_a(Bf, src_ap, src_reads, slot, gb):
            xs, junk, ss, hn = Bf['xs'], Bf['junk'], Bf['ss'], Bf['hn']
            xn, hnn, ssn = f'xs{slot}', f'hn{slot}', f'ss{slot}'
            S.dma(xn, xs[:, slot, :], src_ap, reads=src_reads, writes=[xn])
            MS('dve', ss[:, slot, 0:1], 0.0, [ssn])
            ACTV(junk[:, :], xs[:, slot, :], AF.Square, [xn], ['junk', ssn], accum_out=ss[:, slot, 0:1])
            ACTV(ss[:, slot, 1:2], ss[:, slot, 0:1], AF.Ln, [ssn], [ssn], scale=1.0 / D, bias=cst[:, 0:1])
            ACTV(ss[:, slot, 2:3], ss[:, slot, 1:2], AF.Exp, [ssn], [ssn], scale=-0.5)
            STT('dve', hn[:, slot, :], xs[:, slot, :], ss[:, slot, 2:3], gb[:, :], ALU.mult, ALU.mult, [xn, ssn], [hnn])

        def norm_b(Bf, slot, hnT_ap, hnT_name, tp, tp_name, ev='act'):
            hn = Bf['hn']
            for kc in range(KC):
                S.op('pe', 'transpose', (tp[:, kc, :], hn[:, slot, kc * 128:(kc + 1) * 128], ident[:, :]), {}, [f'hn{slot}'], [tp_name])
            CP(ev, hnT_ap, tp[:, :, :], [tp_name], [hnT_name])

        def norm_block(Bf, src_ap, src_reads, slot, gb, hnT_ap, hnT_name, tp, tp_name):
            norm_a(Bf, src_ap, src_reads, slot, gb)
            norm_b(Bf, slot, hnT_ap, hnT_name, tp, tp_name)

        def new_B(es, ns=2):
            return dict(xs=sbt(es, "xs", [128, ns, D], F32), junk=sbt(es, "junk", [128, D], BF16),
                        ss=sbt(es, "ss", [128, ns, 4], F32), hn=sbt(es, "hn", [128, ns, D], BF16))

        sc04 = contextlib.ExitStack()
        with sc04:
            g1b = sbt(sc04, "g1b", [128, D], F32)
            vt = sbt(sc04, "vt", [128, NKB], F32)
            OT = sbt(sc04, "OT", [128, 4, NQ * T], BF16)

            with contextlib.ExitStack() as es:
                i32 = sbt(es, "i32", [128, 4, 128], F32)
                for i, srcd in enumerate([ident_d, negu_d, negl_d, bd_d]):
                    S.dma('cst', i32[:, i, :], srcd[:, :], writes=[f'i32_{i}'])
                S.barrier()
                for i, dst in enumerate([ident, negu, negl, bdm]):
                    CP('pool', dst[:, :], i32[:, i, :], [f'i32_{i}'], [f'const{i}'])
                MS('pool', zero[:, :], 0.0, ['zero'])
                MS('pool', cst[:, 0:1], EPS, ['cst'])
                MS('pool', cst[:, 1:2], 1.0, ['cst'])
                MS('pool', OT[:, :, 0:T], 0.0, ['OTz'])
                S.dma('cst', g1b[:, :], g1_d[:, :], writes=['g1b'])
                S.dma('cst', vt[:, :], valid_d[:, :], writes=['vt'])
                S.barrier()
                run_block()

            sc12 = contextlib.ExitStack()
            with sc12:
                QT = sbt(sc12, "QT", [128, 4, NQ * T], BF16)
                with contextlib.ExitStack() as es:
                    wB = sbt(es, "wB", [128, KC, 1536], BF16)
                    stg = sbt(es, "stg1", [128, 2, 1536], F32)
                    stg_state['ns'] = 2
                    Bf = new_B(es, 4)
                    hnT = sbt(es, "hnT", [128, 2, KC, T], BF16)
                    KTs = sbt(es, "KTs", [128, 2, 4, T], BF16)
                    Vs = sbt(es, "Vs", [128, 2, 4, T], BF16)
                    tps = [pst(es, f"tp{i}", [128, KC, 128], BF16) for i in range(2)]
                    accs = [pst(es, f"acc{i}", [128, T], F32) for i in range(4)]
                    for kc in range(KC):
                        load_weight(stg, wB[:, kc, :], win_d[kc * 128:(kc + 1) * 128, 1536:3072], 1536, 'wB')
                    acc_i = 0
                    ev_i = 0

                    def nA(t, bi):
                        blk_i = t * 4 + bi
                        norm_a(Bf, x_d[blk_i * 128:(blk_i + 1) * 128, :], [], blk_i % 4, g1b)

                    def nB(t, bi):
                        blk_i = t * 4 + bi
                        hs_ = t % 2
                        norm_b(Bf, blk_i % 4, hnT[:, hs_, :, bi * 128:(bi + 1) * 128], f'hnT{hs_}_{bi}', tps[blk_i % 2], f'tp{blk_i % 2}',
                               ev=('act' if bi % 2 == 0 else 'dve'))

                    def grpK(t, p):
                        nonlocal acc_i, ev_i
                        hs = t % 2
                        hread = [f'hnT{hs}_{bi}' for bi in range(4)]
                        a = acc_i % 4
                        acc_i += 1
                        for kc in range(KC):
                            MM(accs[a][:, :], wB[:, kc, 512 + p * 128:512 + (p + 1) * 128], hnT[:, hs, kc, :],
                               kc == 0, kc == KC - 1, ['wB'] + hread, [f'acc{a}'])
                        CP('dve' if ev_i % 2 == 0 else 'act', KTs[:, hs, p, :], accs[a][:, :], [f'acc{a}'], [f'KTs{hs}'])
                        ev_i += 1
                        if p == 3:
                            S.dma(f'kts{hs}', kt_d[:, :, t * T:(t + 1) * T].rearrange("q p c -> p q c"), KTs[:, hs, :, :],
                                  reads=[f'KTs{hs}'], writes=['kt_d'], q='pool')

                    def grpV(t, bi):
                        nonlocal acc_i, ev_i
                        hs = t % 2
                        a = acc_i % 4
                        acc_i += 1
                        for kc in range(KC):
                            MM(accs[a][:, :], hnT[:, hs, kc, bi * 128:(bi + 1) * 128], wB[:, kc, 1024:1536],
                               kc == 0, kc == KC - 1, ['wB', f'hnT{hs}_{bi}'], [f'acc{a}'])
                        CP('dve' if ev_i % 2 == 0 else 'act', Vs[:, hs, bi, :], accs[a][:, :], [f'acc{a}'], [f'Vs{hs}'])
                        ev_i += 1
                        if bi == 3:
                            S.dma(f'vs{hs}', v_d[:, t * 4:(t + 1) * 4, :], Vs[:, hs, :, :],
                                  reads=[f'Vs{hs}'], writes=['v_d'], q='pool')

                    def grpQ(t, p):
                        nonlocal acc_i, ev_i
                        hs = t % 2
                        qi = t - T0
                        hread = [f'hnT{hs}_{bi}' for bi in range(4)]
                        a = acc_i % 4
                        acc_i += 1
                        for kc in range(KC):
                            MM(accs[a][:, :], wB[:, kc, p * 128:(p + 1) * 128], hnT[:, hs, kc, :],
                               kc == 0, kc == KC - 1, ['wB'] + hread, [f'acc{a}'])
                        CP('dve' if ev_i % 2 == 0 else 'act', QT[:, p, qi * T:(qi + 1) * T], accs[a][:, :], [f'acc{a}'], [f'QT{p}_{qi}'])
                        ev_i += 1

                    for bi in range(4):
                        nA(0, bi)
                        nB(0, bi)
                    for t in range(NT):
                        nx = t + 1 < NT
                        if nx:
                            nA(t + 1, 0)
                            nA(t + 1, 1)
                        grpK(t, 0)
                        if nx:
                            nB(t + 1, 0)
                        grpK(t, 1)
                        if nx:
                            nA(t + 1, 2)
                        grpK(t, 2)
                        if nx:
                            nB(t + 1, 1)
                        grpK(t, 3)
                        if nx:
                            nA(t + 1, 3)
                        grpV(t, 0)
                        if nx:
                            nB(t + 1, 2)
                        grpV(t, 1)
                        grpV(t, 2)
                        if nx:
                            nB(t + 1, 3)
                        grpV(t, 3)
                        if t >= T0:
                            for p in range(4):
                                grpQ(t, p)
                    S.barrier()
                    run_block()

                with contextlib.ExitStack() as es:
                    KTp = sbt(es, "KTp", [128, SL], BF16)
                    Vp = sbt(es, "Vp", [128, NKB, 128], BF16)
                    M = sbt(es, "M", [128, 4, 2, T], BF16)
                    NE, NSP, NXW, NW = 4, 3, 2, 2
                    eb = sbt(es, "eb", [128, NE, 2, T], F32)
                    spb = sbt(es, "spb", [128, NSP, 2, T], BF16)
                    xwb = sbt(es, "xwb", [128, NXW, 2, T], F32)
                    wbuf = sbt(es, "wbuf", [128, NW, 2, T], BF16)
                    m32 = sbt(es, "m32", [128, 4, T], F32)
                    Zp = [pst(es, f"Z{i}", [128, 2, T]) for i in range(2)]
                    Ap = pst(es, "A", [128, 2, T])
                    Op = pst(es, "O", [128, T])
                    S.dma('cst', m32[:, :, :], mask_d[:, :, :], writes=['m32'])
                    for h in range(2):
                        CP('pool', M[:, :, h, :], m32[:, :, :], ['m32'], ['M'])
                    I32 = mybir.dt.int32
                    flagbuf = sbt(es, "flagbuf", [128, 512], I32)
                    mx = sbt(es, "mx", [128, 4], F32)
                    THRESH = 150.0
                    CH = 32
                    nchk = (NKB + CH - 1) // CH
                    itn = [0]
                    fidx = [0]

                    def load_pair(p):
                        for c in reversed(range(nchk)):
                            k0, k1 = c * CH, min(NKB, (c + 1) * CH)
                            S.dma(f'ktp{c}', KTp[:, k0 * 128:k1 * 128], kt_d[p, :, k0 * 128:k1 * 128], reads=['kt_d'], writes=[f'KTp{c}'])
                            S.dma(f'vp{c}', Vp[:, k0:k1, :], v_d[:, k0:k1, p * 128:(p + 1) * 128], reads=['v_d'], writes=[f'Vp{c}'])

                    def st1(it):
                        W, kb, p, qi, c0, n = it['W'], it['kb'], it['p'], it['qi'], it['c0'], it['n']
                        for h in range(2):
                            MM(Zp[n % 2][:, h, 0:W], KTp[64 * h:64 * h + 64, kb * 128:(kb + 1) * 128],
                               QT[64 * h:64 * h + 64, p, qi * T + c0:qi * T + c0 + W], True, True,
                               [f'KTp{kb // CH}', f'QT{p}_{qi}'], [f'Z{n % 2}'])

                    def st2(it):
                        W, n = it['W'], it['n']
                        en = f'e{n % NE}'
                        ACTV(eb[:, n % NE, :, 0:W], Zp[n % 2][:, :, 0:W], AF.Exp, [f'Z{n % 2}'], [en], scale=0.125)
                        if it['d'] is not None:
                            TT('dve', eb[:, n % NE, :, 0:W], eb[:, n % NE, :, 0:W], M[:, it['d'], :, 0:W], ALU.mult, [en, 'M'], [en])

                    def st3(it):
                        W, n = it['W'], it['n']
                        ACTV(spb[:, n % NSP, :, 0:W], eb[:, n % NE, :, 0:W], AF.Ln, [f'e{n % NE}'], [f'sp{n % NSP}'], bias=cst[:, 1:2])

                    def st4(it):
                        W, n = it['W'], it['n']
                        for h in range(2):
                            MM(Ap[:, h, 0:W], negu[:, :], spb[:, n % NSP, h, 0:W], it['first'], False,
                               [f'sp{n % NSP}', 'const1'], ['A'])

                    def st5(it):
                        W, n = it['W'], it['n']
                        ACTV(xwb[:, n % NXW, :, 0:W], Ap[:, :, 0:W], AF.Exp, ['A'], [f'xw{n % NXW}'])

                    def st6(it):
                        W, n = it['W'], it['n']
                        if it['last']:
                            return
                        for h in range(2):
                            MM(Ap[:, h, 0:W], negl[:, :], spb[:, n % NSP, h, 0:W], False, False,
                               [f'sp{n % NSP}', 'const2'], ['A'])

                    def st7(it):
                        W, n = it['W'], it['n']
                        TT('dve', wbuf[:, n % NW, :, 0:W], eb[:, n % NE, :, 0:W], xwb[:, n % NXW, :, 0:W], ALU.mult,
                           [f'e{n % NE}', f'xw{n % NXW}'], [f'w{n % NW}'])

                    def st8(it):
                        W, n, kb = it['W'], it['n'], it['kb']
                        for h in range(2):
                            MM(Op[64 * h:64 * h + 64, 0:W], Vp[:, kb, 64 * h:64 * h + 64], wbuf[:, n % NW, h, 0:W],
                               it['first'], it['last'], [f'w{n % NW}', f'Vp{kb // CH}'], ['O'])

                    stages = [(st1, 0), (st2, 1), (st3, 2), (st6, 4), (st4, 3), (st5, 3), (st7, 4), (st8, 5)]

                    def emit_segment(items):
                        nit = len(items)
                        for s_ in range(nit + 7):
                            for fn, dly in stages:
                                i_ = s_ - dly
                                if 0 <= i_ < nit:
                                    fn(items[i_])

                    def emit_flag(W):
                        fi = fidx[0]
                        fidx[0] += 1
                        for h in range(2):
                            S.op('dve', 'tensor_reduce', (), dict(out=mx[0:1, h:h + 1], in_=Ap[0:1, h, 0:W], axis=mybir.AxisListType.X, op=ALU.max),
                                 ['A'], [f'mx{h}'])
                        TT('dve', mx[0:1, 2:3], mx[0:1, 0:1], mx[0:1, 1:2], ALU.max, ['mx0', 'mx1'], ['mx2'])
                        S.flag_op('tensor_scalar', (), dict(out=flagbuf[0:1, fi:fi + 1], in0=mx[0:1, 2:3], scalar1=-THRESH, scalar2=None, op0=ALU.is_gt),
                                  ['mx2'], [f'flag{fi}'])
                        return fi

                    for p in range(4):
                        load_pair(p)
                        for qi in range(NQ):
                            g = T0 + qi
                            if qi == 0:
                                W, c0, q0, ndiag = 128, 384, g * T + 384, 1
                            else:
                                W, c0, q0, ndiag = T, 0, g * T, 4
                            kb_hi = (q0 + W) // 128 - 1
                            nb = kb_hi + 1
                            segs = [list(range(0, min(nb, ndiag + 2)))]
                            sz = 2
                            while segs[-1][-1] + 1 < nb:
                                st_ = segs[-1][-1] + 1
                                segs.append(list(range(st_, min(nb, st_ + sz))))
                                if len(segs) > 2:
                                    sz *= 2
                            nopen = 0
                            for si, seg in enumerate(segs):
                                items = []
                                for b in seg:
                                    kb = kb_hi - b
                                    d = kb - q0 // 128
                                    items.append(dict(p=p, qi=qi, W=W, c0=c0, kb=kb, d=(d if d >= 0 else None),
                                                      first=(b == 0), last=(b == nb - 1), n=itn[0]))
                                    itn[0] += 1
                                emit_segment(items)
                                if si < len(segs) - 1:
                                    fi = emit_flag(W)
                                    S.begin_cond(flagbuf[0:1, fi:fi + 1], ['pe', 'act', 'dve'])
                                    nopen += 1
                            for _ in range(nopen):
                                S.end_cond()
                            CP('dve', OT[:, p, qi * T + c0:qi * T + c0 + W], Op[:, 0:W], ['O', 'OTz'], [f'OT{p}_{qi}'])
                    S.barrier()
                    run_block()

            sc34 = contextlib.ExitStack()
            with sc34:
                OAT = sbt(sc34, "OAT", [128, 4, NQ * T], BF16)
                with contextlib.ExitStack() as es:
                    wA = sbt(es, "wA", [128, KC, 1536], BF16)
                    with contextlib.ExitStack() as es2:
                        stg = sbt(es2, "stg3", [128, 4, 1536], F32)
                        stg_state['ns'] = 4
                        for kc in range(KC):
                            load_weight(stg, wA[:, kc, :], win_d[kc * 128:(kc + 1) * 128, 0:1536], 1536, 'wA')
                        S.barrier()
                    TB = sbt(es, "TB", [128, 8, 640], F32)
                    gq = sbt(es, "gq", [128, 1], F32)
                    gk = sbt(es, "gk", [128, 1], F32)
                    Bf = new_B(es)
                    hnT = sbt(es, "hnT", [128, KC, T], BF16)
                    QAT = sbt(es, "QAT", [128, 4, T], BF16)
                    KAT = sbt(es, "KAT", [128, 4, 2, T], BF16)
                    VA = sbt(es, "VA", [128, 2, 4, T], BF16)
                    VLD = sbt(es, "VLD", [128, 2, 4, 64], BF16)
                    sqb = sbt(es, "sqb", [128, 2, T], BF16)
                    qgb = sbt(es, "qgb", [128, 2, T], F32)
                    lnb = sbt(es, "lnb", [128, 2, T], F32)
                    sbb = sbt(es, "sbb", [128, 2, 2, T], F32)
                    pTb = sbt(es, "pTb", [128, 2, 2, T], BF16)
                    rdb = sbt(es, "rdb", [128, T], F32)
                    tp3 = pst(es, "tp3", [128, KC, 128], BF16)
                    acc3t = pst(es, "acc3", [128, 2, T])
                    acc3 = [acc3t[:, i, :] for i in range(2)]
                    SSp = pst(es, "SSp", [128, T])
                    SPSt = pst(es, "SPS", [128, 2, T])
                    sbufs = [(SPSt, 'SPS'), (acc3t, 'acc3_')]
                    OAp = pst(es, "OAp", [128, T])
                    DENp = pst(es, "DENp", [128, T])
                    S.dma('cst', TB[:, :, :], tb_d[:, :, :], writes=['TB'])
                    S.dma('cst', gq[:, :], gq_d[:, :], writes=['gq'])
                    S.dma('cst', gk[:, :], gk_d[:, :], writes=['gk'])
                    S.barrier()
                    blkc = 0
                    acc_i = 0
                    nrm = [0]
                    jj = [0]

                    def qknorm(a, gcol, gname, dst_ap, dst_name):
                        k = nrm[0] % 2
                        nrm[0] += 1
                        ACTV(sqb[:, k, :], acc3[a][:, :], AF.Square, [f'acc3_{a}'], [f'sqb{k}'])
                        S.op('act', 'mul', (), dict(out=qgb[:, k, :], in_=acc3[a][:, :], mul=gcol[:, 0:1]), [f'acc3_{a}', gname], [f'qgb{k}'])
                        MM(SSp[:, :], bdm[:, :], sqb[:, k, :], True, True, [f'sqb{k}', 'const3'], ['SSp'])
                        ACTV(lnb[:, k, :], SSp[:, :], AF.Ln, ['SSp'], [f'lnb{k}'], scale=1.0 / 64, bias=cst[:, 0:1])
                        ACTV(lnb[:, k, :], lnb[:, k, :], AF.Exp, [f'lnb{k}'], [f'lnb{k}'], scale=-0.5)
                        TT('dve', dst_ap, qgb[:, k, :], lnb[:, k, :], ALU.mult, [f'qgb{k}', f'lnb{k}'], [dst_name])

                    for t in range(T0 - 1, NT):
                        qi = t - T0
                        sl = t % 2
                        sl0 = blkc
                        norm_a(Bf, x_d[t * 512:t * 512 + 128, :], [], sl0 % 2, g1b)
                        for bi in range(4):
                            if bi + 1 < 4:
                                norm_a(Bf, x_d[(t * 4 + bi + 1) * 128:(t * 4 + bi + 2) * 128, :], [], (sl0 + bi + 1) % 2, g1b)
                            norm_b(Bf, (sl0 + bi) % 2, hnT[:, :, bi * 128:(bi + 1) * 128], f'hnT_{bi}', tp3, 'tp3')
                            blkc += 1
                        hread = [f'hnT_{bi}' for bi in range(4)]
                        for p in range(4):
                            a = acc_i % 2
                            acc_i += 1
                            for kc in range(KC):
                                MM(acc3[a][:, :], wA[:, kc, 512 + p * 128:512 + (p + 1) * 128], hnT[:, kc, :],
                                   kc == 0, kc == KC - 1, ['wA'] + hread, [f'acc3_{a}'])
                            qknorm(a, gk, 'gk', KAT[:, p, sl, :], f'KAT{sl}_{p}')
                        for bi in range(4):
                            a = acc_i % 2
                            acc_i += 1
                            for kc in range(KC):
                                MM(acc3[a][:, :], hnT[:, kc, bi * 128:(bi + 1) * 128], wA[:, kc, 1024:1536],
                                   kc == 0, kc == KC - 1, ['wA', f'hnT_{bi}'], [f'acc3_{a}'])
                            CP('dve', VA[:, sl, bi, :], acc3[a][:, :], [f'acc3_{a}'], [f'VA{sl}_{bi}'])
                            CP('pool', VLD[:, sl, bi, :], vt[:, t * 4 + bi:t * 4 + bi + 1].to_broadcast([128, 64]),
                               ['vt'], [f'VLD{sl}_{bi}'])
                        if qi < 0:
                            continue
                        for p in range(4):
                            a = acc_i % 2
                            acc_i += 1
                            for kc in range(KC):
                                MM(acc3[a][:, :], wA[:, kc, p * 128:(p + 1) * 128], hnT[:, kc, :],
                                   kc == 0, kc == KC - 1, ['wA'] + hread, [f'acc3_{a}'])
                            qknorm(a, gq, 'gq', QAT[:, p, :], f'QAT{p}')
                        for p in range(4):
                            MM(OAp[:, :], zero[:, :], hnT[:, 0, :], True, False, ['zero'] + hread, ['OAp'])
                            MM(DENp[:, :], zero[:, :], hnT[:, 0, :], True, False, ['zero'] + hread, ['DENp'])
                            def jgeom(j):
                                i_lo, i_hi = max(0, j - 4), min(3, j)
                                N = (i_hi - i_lo + 1) * 128
                                ksl = (1 - sl) if j < 4 else sl
                                return i_lo, N, ksl, j % 4, (4 - j + i_lo) * 128

                            def emitS(j, k):
                                i_lo, N, ksl, cj, tb0 = jgeom(j)
                                spt, spn = sbufs[k]
                                for h in range(2):
                                    MM(spt[:, h, 0:N], KAT[64 * h:64 * h + 64, p, ksl, cj * 128:(cj + 1) * 128],
                                       QAT[64 * h:64 * h + 64, p, i_lo * 128:i_lo * 128 + N], True, True,
                                       [f'KAT{ksl}_{p}', f'QAT{p}'], [f'{spn}{h}'])

                            kbase = jj[0]
                            jj[0] += 8
                            emitS(0, kbase % 2)
                            for j in range(8):
                                i_lo, N, ksl, cj, tb0 = jgeom(j)
                                k = (kbase + j) % 2
                                spt, spn = sbufs[k]
                                if j + 1 < 8:
                                    emitS(j + 1, (kbase + j + 1) % 2)
                                STT('dve', sbb[:, k, :, 0:N], spt[:, :, 0:N], 0.125, TB[:, 2 * p:2 * p + 2, tb0:tb0 + N], ALU.mult, ALU.add,
                                    [f'{spn}0', f'{spn}1', 'TB'], [f'sbb{k}'])
                                ACTV(pTb[:, k, :, 0:N], sbb[:, k, :, 0:N], AF.Exp, [f'sbb{k}'], [f'pTb{k}'])
                                for h in range(2):
                                    hh = 2 * p + h
                                    MM(OAp[64 * h:64 * h + 64, i_lo * 128:i_lo * 128 + N], VA[:, ksl, cj, hh * 64:(hh + 1) * 64],
                                       pTb[:, k, h, 0:N], False, False, [f'pTb{k}', f'VA{ksl}_{cj}'], ['OAp'])
                                    MM(DENp[64 * h:64 * h + 64, i_lo * 128:i_lo * 128 + N], VLD[:, ksl, cj, :],
                                       pTb[:, k, h, 0:N], False, False, [f'pTb{k}', f'VLD{ksl}_{cj}'], ['DENp'])
                            TS('dve', rdb[:, :], DENp[:, :], 1e-30, None, ALU.max, None, ['DENp'], ['rdb'])
                            ACTV(rdb[:, :], rdb[:, :], AF.Ln, ['rdb'], ['rdb'])
                            ACTV(rdb[:, :], rdb[:, :], AF.Exp, ['rdb'], ['rdb'], scale=-1.0)
                            TT('dve', OAT[:, p, qi * T:(qi + 1) * T], OAp[:, :], rdb[:, :], ALU.mult, ['OAp', 'rdb'], [f'OAT{p}_{qi}'])
                    S.barrier()
                    run_block()

                with contextlib.ExitStack() as es:
                    wG = sbt(es, "wG", [128, KC, 2048], BF16)
                    wbra = sbt(es, "wbra", [128, 4, D], BF16)
                    wbrb = sbt(es, "wbrb", [128, 4, D], BF16)
                    wout = sbt(es, "wout", [128, KC, D], BF16)
                    stg = sbt(es, "stg4", [128, 2, D], F32)
                    stg_state['ns'] = 2
                    Bf = new_B(es)
                    xr = sbt(es, "xr", [128, 2, D], F32)
                    hnT = sbt(es, "hnT", [128, KC, T], BF16)
                    sg = sbt(es, "sg", [128, 2, T], F32)
                    mm = sbt(es, "mm", [128, 2, T], F32)
                    MT = sbt(es, "MT", [128, KC, T], BF16)
                    tp4 = pst(es, "tp4", [128, KC, 128], BF16)
                    Gp = [pst(es, f"G{i}", [128, T]) for i in range(2)]
                    Yp = [pst(es, f"Y{i}", [128, T]) for i in range(2)]
                    Xp = [pst(es, f"X{i}", [128, T]) for i in range(2)]
                    for kc in range(KC):
                        for hf in range(2):
                            load_weight(stg, wG[:, kc, hf * D:(hf + 1) * D], win_d[kc * 128:(kc + 1) * 128, 3072 + hf * D:3072 + (hf + 1) * D], D, 'wG')
                    for p in range(4):
                        load_weight(stg, wbra[:, p, :], wa_d[p * 128:(p + 1) * 128, :], D, 'wbra')
                        load_weight(stg, wbrb[:, p, :], wb_d[p * 128:(p + 1) * 128, :], D, 'wbrb')
                    for kc in range(KC):
                        load_weight(stg, wout[:, kc, :], wo_d[kc * 128:(kc + 1) * 128, :], D, 'wout')
                    blkc = 0
                    ob = 0
                    for qi in range(NQ):
                        t = T0 + qi
                        bis = [3] if qi == 0 else [0, 1, 2, 3]
                        cs, cn = (384, 128) if qi == 0 else (0, T)
                        sl0 = blkc
                        norm_a(Bf, x_d[(t * 4 + bis[0]) * 128:(t * 4 + bis[0] + 1) * 128, :], [], sl0 % 2, g1b)
                        for ii, bi in enumerate(bis):
                            if ii + 1 < len(bis):
                                nb_ = bis[ii + 1]
                                norm_a(Bf, x_d[(t * 4 + nb_) * 128:(t * 4 + nb_ + 1) * 128, :], [], (sl0 + ii + 1) % 2, g1b)
                            norm_b(Bf, (sl0 + ii) % 2, hnT[:, :, bi * 128:(bi + 1) * 128], f'hnT_{bi}', tp4, 'tp4')
                            blkc += 1
                        hread = [f'hnT_{bi}' for bi in bis]
                        for oc in range(KC):
                            for br in range(2):
                                for kc in range(KC):
                                    MM(Gp[br][:, cs:cs + cn], wG[:, kc, br * D + oc * 128:br * D + (oc + 1) * 128], hnT[:, kc, cs:cs + cn],
                                       kc == 0, kc == KC - 1, ['wG'] + hread, [f'G{br}'])
                                ACTV(sg[:, br, cs:cs + cn], Gp[br][:, cs:cs + cn], AF.Sigmoid, [f'G{br}'], [f'sg{br}'])
                                wsrc = wbra if br == 0 else wbrb
                                osrc = OAT if br == 0 else OT
                                for p in range(4):
                                    MM(Yp[br][:, cs:cs + cn], wsrc[:, p, oc * 128:(oc + 1) * 128], osrc[:, p, qi * T + cs:qi * T + cs + cn],
                                       p == 0, p == 3,
                                       ['wbra' if br == 0 else 'wbrb', (f'OAT{p}_{qi}' if br == 0 else f'OT{p}_{qi}'), 'OTz'], [f'Y{br}'])
                                TT('dve', mm[:, br, cs:cs + cn], Yp[br][:, cs:cs + cn], sg[:, br, cs:cs + cn], ALU.mult, [f'Y{br}', f'sg{br}'], [f'mm{br}'])
                            TT('pool', MT[:, oc, cs:cs + cn], mm[:, 0, cs:cs + cn], mm[:, 1, cs:cs + cn], ALU.add, ['mm0', 'mm1'], [f'MT{oc}'])
                        mread = [f'MT{oc}' for oc in range(KC)]
                        for bi in bis:
                            blk_i = t * 4 + bi
                            o = ob % 2
                            ob += 1
                            S.dma(f'xr{o}', xr[:, o, :], x_d[blk_i * 128:(blk_i + 1) * 128, :], writes=[f'xr{o}'])
                            for half in range(2):
                                for oc in range(KC):
                                    MM(Xp[half][:, :], MT[:, oc, bi * 128:(bi + 1) * 128], wout[:, oc, half * T:(half + 1) * T],
                                       oc == 0, oc == KC - 1, ['wout'] + mread, [f'X{half}'])
                                TT('dve', xr[:, o, half * T:(half + 1) * T], Xp[half][:, :], xr[:, o, half * T:(half + 1) * T], ALU.add,
                                   [f'X{half}', f'xr{o}'], [f'xr{o}'])
                            row = qi * T + bi * 128
                            S.dma(f'x1w{o}', x1_d[row:row + 128, :], xr[:, o, :], reads=[f'xr{o}'], writes=['x1_d'], q='pool')
                    S.barrier()
                    run_block()

        with contextlib.ExitStack() as es:
            wup = sbt(es, "wup", [128, KC, 2 * DFF], BF16)
            wdn = sbt(es, "wdn", [128, NFC, D], BF16)
            g2b = sbt(es, "g2b", [128, D], F32)
            cw = sbt(es, "cw", [128, 44, 3], F32)
            cb = sbt(es, "cb", [128, 44], F32)
            HALO = sbt(es, "HALO", [128, 44, 2], F32)
            with contextlib.ExitStack() as es2:
                stg = sbt(es2, "stg5", [128, 4, 2816], F32)
                stg_state['ns'] = 4
                for kc in range(KC):
                    for hf in range(2):
                        load_weight(stg, wup[:, kc, hf * DFF:(hf + 1) * DFF], wup_d[kc * 128:(kc + 1) * 128, hf * DFF:(hf + 1) * DFF], DFF, 'wup')
                for fc in range(NFC):
                    load_weight(stg, wdn[:, fc, :], wdn_d[fc * 128:(fc + 1) * 128, :], D, 'wdn')
                S.dma('cst', g2b[:, :], g2_d[:, :], writes=['g2b'])
                S.dma('cst', cw[:, :, :], cw_d[:, :, :], writes=['cw'])
                S.dma('cst', cb[:, :], cb_d[:, :], writes=['cb'])
                MS('pool', HALO[:, :, :], 0.0, ['HALO'])
                S.barrier()
            Bf = dict(xs=sbt(es, "xs", [128, 2, D], F32), junk=sbt(es, "junk", [128, D], BF16),
                      ss=sbt(es, "ss", [128, 2, 4], F32), hn=sbt(es, "hn", [128, 2, D], BF16))
            hnT = sbt(es, "hnT", [128, KC, T], BF16)
            hb = sbt(es, "hb", [128, 2, 2, T + 2], F32)
            cv = sbt(es, "cv", [128, 2, T], F32)
            sgl = sbt(es, "sgl", [128, T], F32)
            ACTT = sbt(es, "ACTT", [128, NFC, T], BF16)
            yo = sbt(es, "yo", [128, D], F32)
            tp5 = pst(es, "tp5", [128, KC, 128], BF16)
            Hp = [[pst(es, f"H{b}_{u}", [128, T]) for u in range(2)] for b in range(2)]
            Yd = [pst(es, f"Yd{i}", [128, T]) for i in range(2)]
            blkc = 0
            fcc = 0
            for qi in range(NQ):
                bis = [3] if qi == 0 else [0, 1, 2, 3]
                sl0 = blkc
                norm_a(Bf, x1_d[qi * T + bis[0] * 128:qi * T + bis[0] * 128 + 128, :], ['x1_d'], sl0 % 2, g2b)
                for ii, bi in enumerate(bis):
                    if ii + 1 < len(bis):
                        r2 = qi * T + bis[ii + 1] * 128
                        norm_a(Bf, x1_d[r2:r2 + 128, :], ['x1_d'], (sl0 + ii + 1) % 2, g2b)
                    norm_b(Bf, (sl0 + ii) % 2, hnT[:, :, bi * 128:(bi + 1) * 128], f'hnT_{bi}', tp5, 'tp5')
                    blkc += 1
                hread = [f'hnT_{bi}' for bi in bis]
                for fc in range(NFC):
                    b = fcc % 2
                    fcc += 1
                    for u in range(2):
                        ch = fc + u * NFC
                        hbn = f'hb{b}_{u}'
                        if qi == 0:
                            for kc in range(KC):
                                MM(Hp[b][u][:, 384:T], wup[:, kc, ch * 128:(ch + 1) * 128], hnT[:, kc, 384:T],
                                   kc == 0, kc == KC - 1, ['wup'] + hread, [f'H{b}_{u}'])
                            CP('act', HALO[:, ch, :], Hp[b][u][:, T - 2:T], [f'H{b}_{u}'], [f'HALO{ch}'])
                            continue
                        for kc in range(KC):
                            MM(Hp[b][u][:, :], wup[:, kc, ch * 128:(ch + 1) * 128], hnT[:, kc, :],
                               kc == 0, kc == KC - 1, ['wup'] + hread, [f'H{b}_{u}'])
                        CP('pool', hb[:, b, u, 0:2], HALO[:, ch, :], [f'HALO{ch}'], [hbn])
                        CP('act', hb[:, b, u, 2:T + 2], Hp[b][u][:, :], [f'H{b}_{u}'], [hbn])
                        CP('pool', HALO[:, ch, :], hb[:, b, u, T:T + 2], [hbn], [f'HALO{ch}'])
                        ce = 'dve'
                        cvn = f'cv{u}'
                        ACTV(cv[:, u, :], Hp[b][u][:, :], AF.Identity, [f'H{b}_{u}', 'cw', 'cb'], [cvn], scale=cw[:, ch, 2:3], bias=cb[:, ch:ch + 1])
                        STT(ce, cv[:, u, :], hb[:, b, u, 1:T + 1], cw[:, ch, 1:2], cv[:, u, :], ALU.mult, ALU.add, [hbn, cvn, 'cw'], [cvn])
                        STT(ce, cv[:, u, :], hb[:, b, u, 0:T], cw[:, ch, 0:1], cv[:, u, :], ALU.mult, ALU.add, [hbn, cvn, 'cw'], [cvn])
                    if qi == 0:
                        continue
                    ACTV(sgl[:, :], cv[:, 0, :], AF.Silu, ['cv0'], ['sgl'])
                    TT('dve', ACTT[:, fc, :], sgl[:, :], cv[:, 1, :], ALU.mult, ['sgl', 'cv1'], [f'ACTT{fc}'])
                if qi == 0:
                    continue
                aread = [f'ACTT{fc}' for fc in range(NFC)]
                for bi in range(4):
                    row = qi * T + bi * 128
                    S.dma('yo_in', yo[:, :], x1_d[row:row + 128, :], reads=['x1_d', 'y_d'], writes=['yo'])
                    for half in range(2):
                        for fc in range(NFC):
                            MM(Yd[half][:, :], ACTT[:, fc, bi * 128:(bi + 1) * 128], wdn[:, fc, half * T:(half + 1) * T],
                               fc == 0, fc == NFC - 1, ['wdn'] + aread, [f'Yd{half}'])
                        TT('dve', yo[:, half * T:(half + 1) * T], Yd[half][:, :], yo[:, half * T:(half + 1) * T], ALU.add,
                           [f'Yd{half}', 'yo'], ['yo'])
                    orow = (qi - 1) * T + bi * 128
                    S.dma('yw', y_d[orow:orow + 128, :], yo[:, :], reads=['yo'], writes=['y_d'], q='pool')
            S.barrier()
            run_block()
        print("megakernel ops", S.nops, "waits", S.nwaits, {k: v for k, v in S.cnt.items() if k in engnames})
    return nc


_CACHE = {}


def _consts():
    ident = np.eye(128, dtype=np.float32)
    kk = np.arange(128)
    negu = -(kk[:, None] >= kk[None, :]).astype(np.float32)
    negl = -(kk[:, None] < kk[None, :]).astype(np.float32)
    bd = np.zeros((128, 128), np.float32)
    bd[:64, :64] = 1.0
    bd[64:, 64:] = 1.0
    qq = np.arange(T)
    mask = np.zeros((128, 4, T), np.float32)
    for d in range(4):
        mask[:, d, :] = ((128 * d + kk[:, None]) < qq[None, :]).astype(np.float32)
    return ident, negu, negl, bd, mask


def _tb_table(rel_bias):
    kk = np.arange(128)[:, None]
    qq = np.arange(128)[None, :]
    tb = np.empty((128, 8, 640), np.float32)
    for rp in range(5):
        idx = np.clip(qq - kk + rp * 128, -128, 128) + 128
        blk = rel_bias[:, idx]
        vis = np.ones((128, 128), bool)
        if rp == 0:
            vis = (kk < 64) | (qq >= 64)
        if rp == 4:
            vis = (kk >= 64) | (qq < 64)
        blk = np.where(vis[None], blk, np.float32(NEG))
        tb[:, :, rp * 128:(rp + 1) * 128] = blk.transpose(1, 0, 2)
    return tb


def kernel(x, norm1_g, w_in, q_norm_g, k_norm_g, rel_bias, w_branch_a, w_branch_b, w_out, norm2_g,
           w_ffn_up, ffn_conv_w, ffn_conv_b, w_ffn_down):
    x = np.asarray(x, np.float32)
    Bn, Sq, Dm = x.shape
    assert Bn == 2 and Dm == D and Sq % 2048 == 0
    NO = Sq // 2048
    NT = Sq // T
    key = (NT, NO)
    if key not in _CACHE:
        _CACHE[key] = build(NT, NO)
    nc = _CACHE[key]
    f = lambda a: np.ascontiguousarray(np.asarray(a, np.float32))
    ident, negu, negl, bd, mask = _consts()
    own = NO * T
    shared = {
        "g1": f(np.broadcast_to(np.asarray(norm1_g, np.float32)[0][None, :], (128, D))),
        "g2": f(np.broadcast_to(np.asarray(norm2_g, np.float32)[0][None, :], (128, D))),
        "w_in": f(w_in[0]),
        "gq": f(np.tile(np.asarray(q_norm_g, np.float32)[0], 2)[:, None]),
        "gk": f(np.tile(np.asarray(k_norm_g, np.float32)[0], 2)[:, None]),
        "tb": f(_tb_table(np.asarray(rel_bias, np.float32)[0])),
        "w_a": f(w_branch_a[0]), "w_b": f(w_branch_b[0]), "w_o": f(w_out[0]),
        "w_up": f(w_ffn_up[0]),
        "cw": f(np.asarray(ffn_conv_w, np.float32)[0].reshape(3, 44, 128).transpose(2, 1, 0)),
        "cb": f(np.asarray(ffn_conv_b, np.float32)[0].reshape(44, 128).T),
        "w_dn": f(w_ffn_down[0]),
        "ident": ident, "negu": negu, "negl": negl, "bd": bd, "mask": mask,
    }
    in_maps = []
    for c in range(8):
        b, j = c // 4, c % 4
        real = (j + 1) * own
        pad = Sq - real
        xl = np.zeros((Sq, D), np.float32)
        xl[pad:] = x[b, :real]
        valid = np.zeros((Sq,), np.float32)
        valid[pad:] = 1.0
        m = dict(shared)
        m["x"] = xl
        m["valid"] = f(valid.reshape(Sq // 128, 128).T)
        in_maps.append(m)
    res = run_bass_kernel_spmd(nc, in_maps, core_ids=list(range(8)))
    out = np.empty((Bn, Sq, D), np.float32)
    for c in range(8):
        b, j = c // 4, c % 4
        out[b, j * own:(j + 1) * own] = res.results[c]["y"]
    return out
```

```python
import contextlib
import numpy as np
import concourse.bass as bass
import concourse.mybir as mybir
from concourse.bass_utils import run_bass_kernel_spmd

F32 = mybir.dt.float32
BF16 = mybir.dt.bfloat16
AF = mybir.ActivationFunctionType
ALU = mybir.AluOpType

D = 1024
KC = 8
T = 512
DFF = 2816
NFC = 22
EPS = 1e-6
NEG = -30000.0


class Sched:
    def __init__(self, engnames, sems):
        self.engnames = engnames
        self.prog = {k: [] for k in engnames}
        self.stack = {k: [self.prog[k]] for k in engnames}
        self.cond = None
        self.sems = sems
        self.free_sems = [k for k in sems if k.startswith('c')]
        self.alias = {}
        self.cnt = {}
        self.mult = {}
        for k in engnames:
            self.cnt[k] = 0
            self.mult[k] = 1
        self.cnt['flag'] = 0
        self.mult['flag'] = 1
        self.lastw = {}
        self.readers = {}
        self.waited = {}
        self.nops = 0
        self.nwaits = 0

    def chan(self, name):
        if name not in self.alias:
            s = self.free_sems.pop(0)
            self.alias[name] = s
            self.cnt[name] = 0
            self.mult[name] = 16
        return name

    def sem(self, p):
        return self.sems[self.alias.get(p, p)]

    def _deps(self, reads, writes):
        deps = {}

        def add(ps):
            p, s = ps
            if deps.get(p, 0) < s:
                deps[p] = s
        for r in reads:
            if r in self.lastw:
                add(self.lastw[r])
        for w in writes:
            if w in self.lastw:
                add(self.lastw[w])
            for rd in self.readers.get(w, ()):
                add(rd)
        return deps

    def _emit_waits(self, eng, deps, skip_self):
        wd = self.waited.setdefault(eng, {})
        for p, s in deps.items():
            if p == eng and skip_self:
                continue
            if wd.get(p, 0) >= s:
                continue
            self.stack[eng][-1].append(('w', self.sem(p), s * self.mult[p]))
            wd[p] = s
            self.nwaits += 1

    def _record(self, prod, seq, reads, writes):
        for r in reads:
            self.readers.setdefault(r, []).append((prod, seq))
        for w in writes:
            self.lastw[w] = (prod, seq)
            self.readers[w] = []

    def op(self, eng, meth, args, kw, reads=(), writes=()):
        deps = self._deps(reads, writes)
        self._emit_waits(eng, deps, skip_self=(eng == 'pe'))
        self.stack[eng][-1].append(('o', (meth, args, kw), self.sems[eng], 1))
        self.cnt[eng] += 1
        self._record(eng, self.cnt[eng], reads, writes)
        self.nops += 1

    def dma(self, ch, out, in_, reads=(), writes=(), q='sp'):
        self.chan(ch)
        deps = self._deps(reads, writes)
        self._emit_waits(q, deps, skip_self=False)
        self.stack[q][-1].append(('o', ('dma_start', (), dict(out=out, in_=in_)), self.sem(ch), 16))
        self.cnt[ch] += 1
        self._record(ch, self.cnt[ch], reads, writes)
        self.nops += 1

    def flag_op(self, meth, args, kw, reads=(), writes=()):
        deps = self._deps(reads, writes)
        self._emit_waits('dve', deps, skip_self=False)
        self.stack['dve'][-1].append(('o', (meth, args, kw), self.sems['flag'], 1))
        self.cnt['flag'] += 1
        self._record('flag', self.cnt['flag'], reads, writes)

    def begin_cond(self, flag_ap, engines):
        import copy
        if self.cond is None:
            self.cond = []
        self.cond.append(dict(flag_ap=flag_ap, engines=engines, seq=self.cnt['flag'],
                              start={e: self.cnt[e] for e in engines}, fstart=self.cnt['flag'],
                              snap=copy.deepcopy(self.waited)))
        for e in engines:
            self.stack[e].append([])

    def end_cond(self):
        c = self.cond.pop()
        nf = self.cnt['flag'] - c['fstart']
        for e in c['engines']:
            body = self.stack[e].pop()
            n = self.cnt[e] - c['start'][e]
            self.stack[e][-1].append(('if', c['flag_ap'], c['seq'], body, c['start'][e], n,
                                      nf if e == 'dve' else 0))
        self.waited = c['snap']

    def barrier(self):
        for eng in self.engnames:
            wd = self.waited.setdefault(eng, {})
            for p, c in self.cnt.items():
                if c > 0 and p != eng and wd.get(p, 0) < c:
                    self.stack[eng][-1].append(('w', self.sem(p), c * self.mult[p]))
                    wd[p] = c
        for eng in self.engnames:
            if eng != 'sp' and self.cnt[eng] > 0:
                self.stack[eng][-1].append(('w', self.sems[eng], self.cnt[eng]))

    def _replay_list(self, eng, e, lst):
        for it in lst:
            if it[0] == 'w':
                e.wait_ge(it[1], it[2])
            elif it[0] == 'o':
                meth, args, kw = it[1]
                getattr(e, meth)(*args, **kw).then_inc(it[2], it[3])
            else:
                _, flag_ap, seq, body, start, n, dve_else = it
                e.wait_ge(self.sems['flag'], seq)
                reg = self.regs[eng]
                e.reg_load(reg, flag_ap)
                with e.If_ne(reg, 0):
                    self._replay_list(eng, e, body)
                with e.Else():
                    if start > 0:
                        e.wait_ge(self.sems[eng], start)
                    if n > 0:
                        e.sem_inc(self.sems[eng], n)
                    if dve_else:
                        e.sem_inc(self.sems['flag'], dve_else)

    def replay(self, eng, e):
        assert len(self.stack[eng]) == 1
        self._replay_list(eng, e, self.prog[eng])
        self.prog[eng] = []
        self.stack[eng] = [self.prog[eng]]


def build(NT, NO):
    SL = NT * T
    NKB = SL // 128
    NQ = NO + 1
    T0 = NT - NO - 1
    nc = bass.Bass("TRN2", target_bir_lowering=False)

    def din(name, shape):
        return nc.dram_tensor(name, shape, F32, kind="ExternalInput").ap()
    x_d = din("x", [SL, D])
    valid_d = din("valid", [128, NKB])
    g1_d = din("g1", [128, D])
    g2_d = din("g2", [128, D])
    win_d = din("w_in", [D, 5120])
    gq_d = din("gq", [128, 1])
    gk_d = din("gk", [128, 1])
    tb_d = din("tb", [128, 8, 640])
    wa_d = din("w_a", [512, D])
    wb_d = din("w_b", [512, D])
    wo_d = din("w_o", [D, D])
    wup_d = din("w_up", [D, 2 * DFF])
    cw_d = din("cw", [128, 44, 3])
    cb_d = din("cb", [128, 44])
    wdn_d = din("w_dn", [DFF, D])
    ident_d = din("ident", [128, 128])
    negu_d = din("negu", [128, 128])
    negl_d = din("negl", [128, 128])
    bd_d = din("bd", [128, 128])
    mask_d = din("mask", [128, 4, T])
    y_d = nc.dram_tensor("y", [NO * T, D], F32, kind="ExternalOutput").ap()
    kt_d = nc.dram_tensor("kt_scr", [4, 128, SL], BF16).ap()
    v_d = nc.dram_tensor("v_scr", [128, NKB, 4 * 128], BF16).ap()
    x1_d = nc.dram_tensor("x1_scr", [NQ * T, D], F32).ap()

    top = contextlib.ExitStack()
    with top:
        engnames = ['pe', 'act', 'dve', 'pool', 'sp']
        sems = {}
        for n in ['pe', 'act', 'dve', 'pool', 'flag']:
            sems[n] = top.enter_context(nc.semaphore(n))
        for i in range(44):
            sems[f'c{i}'] = top.enter_context(nc.semaphore(f'c{i}'))
        S = Sched(engnames, sems)
        S.regs = {'pe': nc.alloc_register(mybir.EngineType.PE, 'flag_pe'),
                  'act': nc.alloc_register(mybir.EngineType.Activation, 'flag_act'),
                  'dve': nc.alloc_register(mybir.EngineType.DVE, 'flag_dve')}

        def run_block():
            blk = nc.Block()
            with blk:
                blk.tensor(lambda e: S.replay('pe', e))
                blk.scalar(lambda e: S.replay('act', e))
                blk.vector(lambda e: S.replay('dve', e))
                blk.gpsimd(lambda e: S.replay('pool', e))
                blk.sync(lambda e: S.replay('sp', e))

        uid = [0]

        def sbt(es, name, shape, dt):
            uid[0] += 1
            return es.enter_context(nc.sbuf_tensor(f"s{uid[0]}_{name}", shape, dt))

        def pst(es, name, shape, dt=F32):
            uid[0] += 1
            return es.enter_context(nc.psum_tensor(f"p{uid[0]}_{name}", shape, dt))

        def MM(out, lhsT, rhs, start, stop, reads, writes):
            S.op('pe', 'matmul', (out,), dict(lhsT=lhsT, rhs=rhs, start=start, stop=stop), reads, writes)

        def ACTV(out, in_, func, reads, writes, **kw):
            S.op('act', 'activation', (), dict(out=out, in_=in_, func=func, **kw), reads, writes)

        def CP(eng, out, in_, reads, writes):
            S.op(eng, 'copy' if eng == 'act' else 'tensor_copy', (), dict(out=out, in_=in_), reads, writes)

        def TT(eng, out, in0, in1, op, reads, writes):
            S.op(eng, 'tensor_tensor', (), dict(out=out, in0=in0, in1=in1, op=op), reads, writes)

        def STT(eng, out, in0, scalar, in1, op0, op1, reads, writes):
            S.op(eng, 'scalar_tensor_tensor', (), dict(out=out, in0=in0, scalar=scalar, in1=in1, op0=op0, op1=op1), reads, writes)

        def TS(eng, out, in0, s1, s2, op0, op1, reads, writes):
            kw = dict(out=out, in0=in0, scalar1=s1, scalar2=s2, op0=op0)
            if op1 is not None:
                kw['op1'] = op1
            S.op(eng, 'tensor_scalar', (), kw, reads, writes)

        def MS(eng, ap, val, writes):
            S.op(eng, 'memset', (ap, val), {}, (), writes)

        ident = sbt(top, "ident", [128, 128], BF16)
        negu = sbt(top, "negu", [128, 128], BF16)
        negl = sbt(top, "negl", [128, 128], BF16)
        bdm = sbt(top, "bdm", [128, 128], BF16)
        zero = sbt(top, "zero", [128, 128], BF16)
        cst = sbt(top, "cst", [128, 4], F32)

        stg_state = {'i': 0, 'ns': 2}

        def load_weight(stg, dst_ap, src_ap, n, wname):
            sl = stg_state['i'] % stg_state['ns']
            stg_state['i'] += 1
            S.dma(f'stg{sl}', stg[:, sl, 0:n], src_ap, writes=[f'stg{sl}'])
            ce = ('pool', 'dve', 'act')[stg_state['i'] % 3]
            CP(ce, dst_ap, stg[:, sl, 0:n], [f'stg{sl}'], [wname])

        def norm_a(Bf, src_ap, src_reads, slot, gb):
            xs, junk, ss, hn = Bf['xs'], Bf['junk'], Bf['ss'], Bf['hn']
            xn, hnn, ssn = f'xs{slot}', f'hn{slot}', f'ss{slot}'
            S.dma(xn, xs[:, slot, :], src_ap, reads=src_reads, writes=[xn])
            MS('dve', ss[:, slot, 0:1], 0.0, [ssn])
            ACTV(junk[:, :], xs[:, slot, :], AF.Square, [xn], ['junk', ssn], accum_out=ss[:, slot, 0:1])
            ACTV(ss[:, slot, 1:2], ss[:, slot, 0:1], AF.Ln, [ssn], [ssn], scale=1.0 / D, bias=cst[:, 0:1])
            ACTV(ss[:, slot, 2:3], ss[:, slot, 1:2], AF.Exp, [ssn], [ssn], scale=-0.5)
            STT('dve', hn[:, slot, :], xs[:, slot, :], ss[:, slot, 2:3], gb[:, :], ALU.mult, ALU.mult, [xn, ssn], [hnn])

        def norm_b(Bf, slot, hnT_ap, hnT_name, tp, tp_name, ev='act'):
            hn = Bf['hn']
            for kc in range(KC):
                S.op('pe', 'transpose', (tp[:, kc, :], hn[:, slot, kc * 128:(kc + 1) * 128], ident[:, :]), {}, [f'hn{slot}'], [tp_name])
            CP(ev, hnT_ap, tp[:, :, :], [tp_name], [hnT_name])

        def norm_block(Bf, src_ap, src_reads, slot, gb, hnT_ap, hnT_name, tp, tp_name):
            norm_a(Bf, src_ap, src_reads, slot, gb)
            norm_b(Bf, slot, hnT_ap, hnT_name, tp, tp_name)

        def new_B(es, ns=2):
            return dict(xs=sbt(es, "xs", [128, ns, D], F32), junk=sbt(es, "junk", [128, D], BF16),
                        ss=sbt(es, "ss", [128, ns, 4], F32), hn=sbt(es, "hn", [128, ns, D], BF16))

        sc04 = contextlib.ExitStack()
        with sc04:
            g1b = sbt(sc04, "g1b", [128, D], F32)
            vt = sbt(sc04, "vt", [128, NKB], F32)
            OT = sbt(sc04, "OT", [128, 4, NQ * T], BF16)

            with contextlib.ExitStack() as es:
                i32 = sbt(es, "i32", [128, 4, 128], F32)
                for i, srcd in enumerate([ident_d, negu_d, negl_d, bd_d]):
                    S.dma('cst', i32[:, i, :], srcd[:, :], writes=[f'i32_{i}'])
                S.barrier()
                for i, dst in enumerate([ident, negu, negl, bdm]):
                    CP('pool', dst[:, :], i32[:, i, :], [f'i32_{i}'], [f'const{i}'])
                MS('pool', zero[:, :], 0.0, ['zero'])
                MS('pool', cst[:, 0:1], EPS, ['cst'])
                MS('pool', cst[:, 1:2], 1.0, ['cst'])
                MS('pool', OT[:, :, 0:T], 0.0, ['OTz'])
                S.dma('cst', g1b[:, :], g1_d[:, :], writes=['g1b'])
                S.dma('cst', vt[:, :], valid_d[:, :], writes=['vt'])
                S.barrier()
                run_block()

            sc12 = contextlib.ExitStack()
            with sc12:
                QT = sbt(sc12, "QT", [128, 4, NQ * T], BF16)
                with contextlib.ExitStack() as es:
                    wB = sbt(es, "wB", [128, KC, 1536], BF16)
                    stg = sbt(es, "stg1", [128, 2, 1536], F32)
                    stg_state['ns'] = 2
                    Bf = new_B(es, 4)
                    hnT = sbt(es, "hnT", [128, 2, KC, T], BF16)
                    KTs = sbt(es, "KTs", [128, 2, 4, T], BF16)
                    Vs = sbt(es, "Vs", [128, 2, 4, T], BF16)
                    tps = [pst(es, f"tp{i}", [128, KC, 128], BF16) for i in range(2)]
                    accs = [pst(es, f"acc{i}", [128, T], F32) for i in range(4)]
                    for kc in range(KC):
                        load_weight(stg, wB[:, kc, :], win_d[kc * 128:(kc + 1) * 128, 1536:3072], 1536, 'wB')
                    acc_i = 0
                    ev_i = 0

                    def nA(t, bi):
                        blk_i = t * 4 + bi
                        norm_a(Bf, x_d[blk_i * 128:(blk_i + 1) * 128, :], [], blk_i % 4, g1b)

                    def nB(t, bi):
                        blk_i = t * 4 + bi
                        hs_ = t % 2
                        norm_b(Bf, blk_i % 4, hnT[:, hs_, :, bi * 128:(bi + 1) * 128], f'hnT{hs_}_{bi}', tps[blk_i % 2], f'tp{blk_i % 2}',
                               ev=('act' if bi % 2 == 0 else 'dve'))

                    def grpK(t, p):
                        nonlocal acc_i, ev_i
                        hs = t % 2
                        hread = [f'hnT{hs}_{bi}' for bi in range(4)]
                        a = acc_i % 4
                        acc_i += 1
                        for kc in range(KC):
                            MM(accs[a][:, :], wB[:, kc, 512 + p * 128:512 + (p + 1) * 128], hnT[:, hs, kc, :],
                               kc == 0, kc == KC - 1, ['wB'] + hread, [f'acc{a}'])
                        CP('dve' if ev_i % 2 == 0 else 'act', KTs[:, hs, p, :], accs[a][:, :], [f'acc{a}'], [f'KTs{hs}'])
                        ev_i += 1
                        if p == 3:
                            S.dma(f'kts{hs}', kt_d[:, :, t * T:(t + 1) * T].rearrange("q p c -> p q c"), KTs[:, hs, :, :],
                                  reads=[f'KTs{hs}'], writes=['kt_d'], q='pool')

                    def grpV(t, bi):
                        nonlocal acc_i, ev_i
                        hs = t % 2
                        a = acc_i % 4
                        acc_i += 1
                        for kc in range(KC):
                            MM(accs[a][:, :], hnT[:, hs, kc, bi * 128:(bi + 1) * 128], wB[:, kc, 1024:1536],
                               kc == 0, kc == KC - 1, ['wB', f'hnT{hs}_{bi}'], [f'acc{a}'])
                        CP('dve' if ev_i % 2 == 0 else 'act', Vs[:, hs, bi, :], accs[a][:, :], [f'acc{a}'], [f'Vs{hs}'])
                        ev_i += 1
                        if bi == 3:
                            S.dma(f'vs{hs}', v_d[:, t * 4:(t + 1) * 4, :], Vs[:, hs, :, :],
                                  reads=[f'Vs{hs}'], writes=['v_d'], q='pool')

                    def grpQ(t, p):
                        nonlocal acc_i, ev_i
                        hs = t % 2
                        qi = t - T0
                        hread = [f'hnT{hs}_{bi}' for bi in range(4)]
                        a = acc_i % 4
                        acc_i += 1
                        for kc in range(KC):
                            MM(accs[a][:, :], wB[:, kc, p * 128:(p + 1) * 128], hnT[:, hs, kc, :],
                               kc == 0, kc == KC - 1, ['wB'] + hread, [f'acc{a}'])
                        CP('dve' if ev_i % 2 == 0 else 'act', QT[:, p, qi * T:(qi + 1) * T], accs[a][:, :], [f'acc{a}'], [f'QT{p}_{qi}'])
                        ev_i += 1

                    for bi in range(4):
                        nA(0, bi)
                        nB(0, bi)
                    for t in range(NT):
                        nx = t + 1 < NT
                        if nx:
                            nA(t + 1, 0)
                            nA(t + 1, 1)
                        grpK(t, 0)
                        if nx:
                            nB(t + 1, 0)
                        grpK(t, 1)
                        if nx:
                            nA(t + 1, 2)
                        grpK(t, 2)
                        if nx:
                            nB(t + 1, 1)
                        grpK(t, 3)
                        if nx:
                            nA(t + 1, 3)
                        grpV(t, 0)
                        if nx:
                            nB(t + 1, 2)
                        grpV(t, 1)
                        grpV(t, 2)
                        if nx:
                            nB(t + 1, 3)
                        grpV(t, 3)
                        if t >= T0:
                            for p in range(4):
                                grpQ(t, p)
                    S.barrier()
                    run_block()

                with contextlib.ExitStack() as es:
                    KTp = sbt(es, "KTp", [128, SL], BF16)
                    Vp = sbt(es, "Vp", [128, NKB, 128], BF16)
                    M = sbt(es, "M", [128, 4, 2, T], BF16)
                    NE, NSP, NXW, NW = 4, 3, 2, 2
                    eb = sbt(es, "eb", [128, NE, 2, T], F32)
                    spb = sbt(es, "spb", [128, NSP, 2, T], BF16)
                    xwb = sbt(es, "xwb", [128, NXW, 2, T], F32)
                    wbuf = sbt(es, "wbuf", [128, NW, 2, T], BF16)
                    m32 = sbt(es, "m32", [128, 4, T], F32)
                    Zp = [pst(es, f"Z{i}", [128, 2, T]) for i in range(2)]
                    Ap = pst(es, "A", [128, 2, T])
                    Op = pst(es, "O", [128, T])
                    S.dma('cst', m32[:, :, :], mask_d[:, :, :], writes=['m32'])
                    for h in range(2):
                        CP('pool', M[:, :, h, :], m32[:, :, :], ['m32'], ['M'])
                    I32 = mybir.dt.int32
                    flagbuf = sbt(es, "flagbuf", [128, 512], I32)
                    mx = sbt(es, "mx", [128, 4], F32)
                    THRESH = 150.0
                    CH = 32
                    nchk = (NKB + CH - 1) // CH
                    itn = [0]
                    fidx = [0]

                    def load_pair(p):
                        for c in reversed(range(nchk)):
                            k0, k1 = c * CH, min(NKB, (c + 1) * CH)
                            S.dma(f'ktp{c}', KTp[:, k0 * 128:k1 * 128], kt_d[p, :, k0 * 128:k1 * 128], reads=['kt_d'], writes=[f'KTp{c}'])
                            S.dma(f'vp{c}', Vp[:, k0:k1, :], v_d[:, k0:k1, p * 128:(p + 1) * 128], reads=['v_d'], writes=[f'Vp{c}'])

                    def st1(it):
                        W, kb, p, qi, c0, n = it['W'], it['kb'], it['p'], it['qi'], it['c0'], it['n']
                        for h in range(2):
                            MM(Zp[n % 2][:, h, 0:W], KTp[64 * h:64 * h + 64, kb * 128:(kb + 1) * 128],
                               QT[64 * h:64 * h + 64, p, qi * T + c0:qi * T + c0 + W], True, True,
                               [f'KTp{kb // CH}', f'QT{p}_{qi}'], [f'Z{n % 2}'])

                    def st2(it):
                        W, n = it['W'], it['n']
                        en = f'e{n % NE}'
                        ACTV(eb[:, n % NE, :, 0:W], Zp[n % 2][:, :, 0:W], AF.Exp, [f'Z{n % 2}'], [en], scale=0.125)
                        if it['d'] is not None:
                            TT('dve', eb[:, n % NE, :, 0:W], eb[:, n % NE, :, 0:W], M[:, it['d'], :, 0:W], ALU.mult, [en, 'M'], [en])

                    def st3(it):
                        W, n = it['W'], it['n']
                        ACTV(spb[:, n % NSP, :, 0:W], eb[:, n % NE, :, 0:W], AF.Ln, [f'e{n % NE}'], [f'sp{n % NSP}'], bias=cst[:, 1:2])

                    def st4(it):
                        W, n = it['W'], it['n']
                        for h in range(2):
                            MM(Ap[:, h, 0:W], negu[:, :], spb[:, n % NSP, h, 0:W], it['first'], False,
                               [f'sp{n % NSP}', 'const1'], ['A'])

                    def st5(it):
                        W, n = it['W'], it['n']
                        ACTV(xwb[:, n % NXW, :, 0:W], Ap[:, :, 0:W], AF.Exp, ['A'], [f'xw{n % NXW}'])

                    def st6(it):
                        W, n = it['W'], it['n']
                        if it['last']:
                            return
                        for h in range(2):
                            MM(Ap[:, h, 0:W], negl[:, :], spb[:, n % NSP, h, 0:W], False, False,
                               [f'sp{n % NSP}', 'const2'], ['A'])

                    def st7(it):
                        W, n = it['W'], it['n']
                        TT('dve', wbuf[:, n % NW, :, 0:W], eb[:, n % NE, :, 0:W], xwb[:, n % NXW, :, 0:W], ALU.mult,
                           [f'e{n % NE}', f'xw{n % NXW}'], [f'w{n % NW}'])

                    def st8(it):
                        W, n, kb = it['W'], it['n'], it['kb']
                        for h in range(2):
                            MM(Op[64 * h:64 * h + 64, 0:W], Vp[:, kb, 64 * h:64 * h + 64], wbuf[:, n % NW, h, 0:W],
                               it['first'], it['last'], [f'w{n % NW}', f'Vp{kb // CH}'], ['O'])

                    stages = [(st1, 0), (st2, 1), (st3, 2), (st6, 4), (st4, 3), (st5, 3), (st7, 4), (st8, 5)]

                    def emit_segment(items):
                        nit = len(items)
                        for s_ in range(nit + 7):
                            for fn, dly in stages:
                                i_ = s_ - dly
                                if 0 <= i_ < nit:
                                    fn(items[i_])

                    def emit_flag(W):
                        fi = fidx[0]
                        fidx[0] += 1
                        for h in range(2):
                            S.op('dve', 'tensor_reduce', (), dict(out=mx[0:1, h:h + 1], in_=Ap[0:1, h, 0:W], axis=mybir.AxisListType.X, op=ALU.max),
                                 ['A'], [f'mx{h}'])
                        TT('dve', mx[0:1, 2:3], mx[0:1, 0:1], mx[0:1, 1:2], ALU.max, ['mx0', 'mx1'], ['mx2'])
                        S.flag_op('tensor_scalar', (), dict(out=flagbuf[0:1, fi:fi + 1], in0=mx[0:1, 2:3], scalar1=-THRESH, scalar2=None, op0=ALU.is_gt),
                                  ['mx2'], [f'flag{fi}'])
                        return fi

                    for p in range(4):
                        load_pair(p)
                        for qi in range(NQ):
                            g = T0 + qi
                            if qi == 0:
                                W, c0, q0, ndiag = 128, 384, g * T + 384, 1
                            else:
                                W, c0, q0, ndiag = T, 0, g * T, 4
                            kb_hi = (q0 + W) // 128 - 1
                            nb = kb_hi + 1
                            segs = [list(range(0, min(nb, ndiag + 2)))]
                            sz = 2
                            while segs[-1][-1] + 1 < nb:
                                st_ = segs[-1][-1] + 1
                                segs.append(list(range(st_, min(nb, st_ + sz))))
                                if len(segs) > 2:
                                    sz *= 2
                            nopen = 0
                            for si, seg in enumerate(segs):
                                items = []
                                for b in seg:
                                    kb = kb_hi - b
                                    d = kb - q0 // 128
                                    items.append(dict(p=p, qi=qi, W=W, c0=c0, kb=kb, d=(d if d >= 0 else None),
                                                      first=(b == 0), last=(b == nb - 1), n=itn[0]))
                                    itn[0] += 1
                                emit_segment(items)
                                if si < len(segs) - 1:
                                    fi = emit_flag(W)
                                    S.begin_cond(flagbuf[0:1, fi:fi + 1], ['pe', 'act', 'dve'])
                                    nopen += 1
                            for _ in range(nopen):
                                S.end_cond()
                            CP('dve', OT[:, p, qi * T + c0:qi * T + c0 + W], Op[:, 0:W], ['O', 'OTz'], [f'OT{p}_{qi}'])
                    S.barrier()
                    run_block()

            sc34 = contextlib.ExitStack()
            with sc34:
                OAT = sbt(sc34, "OAT", [128, 4, NQ * T], BF16)
                with contextlib.ExitStack() as es:
                    wA = sbt(es, "wA", [128, KC, 1536], BF16)
                    with contextlib.ExitStack() as es2:
                        stg = sbt(es2, "stg3", [128, 4, 1536], F32)
                        stg_state['ns'] = 4
                        for kc in range(KC):
                            load_weight(stg, wA[:, kc, :], win_d[kc * 128:(kc + 1) * 128, 0:1536], 1536, 'wA')
                        S.barrier()
                    TB = sbt(es, "TB", [128, 8, 640], F32)
                    gq = sbt(es, "gq", [128, 1], F32)
                    gk = sbt(es, "gk", [128, 1], F32)
                    Bf = new_B(es)
                    hnT = sbt(es, "hnT", [128, KC, T], BF16)
                    QAT = sbt(es, "QAT", [128, 4, T], BF16)
                    KAT = sbt(es, "KAT", [128, 4, 2, T], BF16)
                    VA = sbt(es, "VA", [128, 2, 4, T], BF16)
                    VLD = sbt(es, "VLD", [128, 2, 4, 64], BF16)
                    sqb = sbt(es, "sqb", [128, 2, T], BF16)
                    qgb = sbt(es, "qgb", [128, 2, T], F32)
                    lnb = sbt(es, "lnb", [128, 2, T], F32)
                    sbb = sbt(es, "sbb", [128, 2, 2, T], F32)
                    pTb = sbt(es, "pTb", [128, 2, 2, T], BF16)
                    rdb = sbt(es, "rdb", [128, T], F32)
                    tp3 = pst(es, "tp3", [128, KC, 128], BF16)
                    acc3t = pst(es, "acc3", [128, 2, T])
                    acc3 = [acc3t[:, i, :] for i in range(2)]
                    SSp = pst(es, "SSp", [128, T])
                    SPSt = pst(es, "SPS", [128, 2, T])
                    sbufs = [(SPSt, 'SPS'), (acc3t, 'acc3_')]
                    OAp = pst(es, "OAp", [128, T])
                    DENp = pst(es, "DENp", [128, T])
                    S.dma('cst', TB[:, :, :], tb_d[:, :, :], writes=['TB'])
                    S.dma('cst', gq[:, :], gq_d[:, :], writes=['gq'])
                    S.dma('cst', gk[:, :], gk_d[:, :], writes=['gk'])
                    S.barrier()
                    blkc = 0
                    acc_i = 0
                    nrm = [0]
                    jj = [0]

                    def qknorm(a, gcol, gname, dst_ap, dst_name):
                        k = nrm[0] % 2
                        nrm[0] += 1
                        ACTV(sqb[:, k, :], acc3[a][:, :], AF.Square, [f'acc3_{a}'], [f'sqb{k}'])
                        S.op('act', 'mul', (), dict(out=qgb[:, k, :], in_=acc3[a][:, :], mul=gcol[:, 0:1]), [f'acc3_{a}', gname], [f'qgb{k}'])
                        MM(SSp[:, :], bdm[:, :], sqb[:, k, :], True, True, [f'sqb{k}', 'const3'], ['SSp'])
                        ACTV(lnb[:, k, :], SSp[:, :], AF.Ln, ['SSp'], [f'lnb{k}'], scale=1.0 / 64, bias=cst[:, 0:1])
                        ACTV(lnb[:, k, :], lnb[:, k, :], AF.Exp, [f'lnb{k}'], [f'lnb{k}'], scale=-0.5)
                        TT('dve', dst_ap, qgb[:, k, :], lnb[:, k, :], ALU.mult, [f'qgb{k}', f'lnb{k}'], [dst_name])

                    for t in range(T0 - 1, NT):
                        qi = t - T0
                        sl = t % 2
                        sl0 = blkc
                        norm_a(Bf, x_d[t * 512:t * 512 + 128, :], [], sl0 % 2, g1b)
                        for bi in range(4):
                            if bi + 1 < 4:
                                norm_a(Bf, x_d[(t * 4 + bi + 1) * 128:(t * 4 + bi + 2) * 128, :], [], (sl0 + bi + 1) % 2, g1b)
                            norm_b(Bf, (sl0 + bi) % 2, hnT[:, :, bi * 128:(bi + 1) * 128], f'hnT_{bi}', tp3, 'tp3')
                            blkc += 1
                        hread = [f'hnT_{bi}' for bi in range(4)]
                        for p in range(4):
                            a = acc_i % 2
                            acc_i += 1
                            for kc in range(KC):
                                MM(acc3[a][:, :], wA[:, kc, 512 + p * 128:512 + (p + 1) * 128], hnT[:, kc, :],
                                   kc == 0, kc == KC - 1, ['wA'] + hread, [f'acc3_{a}'])
                            qknorm(a, gk, 'gk', KAT[:, p, sl, :], f'KAT{sl}_{p}')
                        for bi in range(4):
                            a = acc_i % 2
                            acc_i += 1
                            for kc in range(KC):
                                MM(acc3[a][:, :], hnT[:, kc, bi * 128:(bi + 1) * 128], wA[:, kc, 1024:1536],
                                   kc == 0, kc == KC - 1, ['wA', f'hnT_{bi}'], [f'acc3_{a}'])
                            CP('dve', VA[:, sl, bi, :], acc3[a][:, :], [f'acc3_{a}'], [f'VA{sl}_{bi}'])
                            CP('pool', VLD[:, sl, bi, :], vt[:, t * 4 + bi:t * 4 + bi + 1].to_broadcast([128, 64]),
                               ['vt'], [f'VLD{sl}_{bi}'])
                        if qi < 0:
                            continue
                        for p in range(4):
                            a = acc_i % 2
                            acc_i += 1
                            for kc in range(KC):
                                MM(acc3[a][:, :], wA[:, kc, p * 128:(p + 1) * 128], hnT[:, kc, :],
                                   kc == 0, kc == KC - 1, ['wA'] + hread, [f'acc3_{a}'])
                            qknorm(a, gq, 'gq', QAT[:, p, :], f'QAT{p}')
                        for p in range(4):
                            MM(OAp[:, :], zero[:, :], hnT[:, 0, :], True, False, ['zero'] + hread, ['OAp'])
                            MM(DENp[:, :], zero[:, :], hnT[:, 0, :], True, False, ['zero'] + hread, ['DENp'])
                            def jgeom(j):
                                i_lo, i_hi = max(0, j - 4), min(3, j)
                                N = (i_hi - i_lo + 1) * 128
                                ksl = (1 - sl) if j < 4 else sl
                                return i_lo, N, ksl, j % 4, (4 - j + i_lo) * 128

                            def emitS(j, k):
                                i_lo, N, ksl, cj, tb0 = jgeom(j)
                                spt, spn = sbufs[k]
                                for h in range(2):
                                    MM(spt[:, h, 0:N], KAT[64 * h:64 * h + 64, p, ksl, cj * 128:(cj + 1) * 128],
                                       QAT[64 * h:64 * h + 64, p, i_lo * 128:i_lo * 128 + N], True, True,
                                       [f'KAT{ksl}_{p}', f'QAT{p}'], [f'{spn}{h}'])

                            kbase = jj[0]
                            jj[0] += 8
                            emitS(0, kbase % 2)
                            for j in range(8):
                                i_lo, N, ksl, cj, tb0 = jgeom(j)
                                k = (kbase + j) % 2
                                spt, spn = sbufs[k]
                                if j + 1 < 8:
                                    emitS(j + 1, (kbase + j + 1) % 2)
                                STT('dve', sbb[:, k, :, 0:N], spt[:, :, 0:N], 0.125, TB[:, 2 * p:2 * p + 2, tb0:tb0 + N], ALU.mult, ALU.add,
                                    [f'{spn}0', f'{spn}1', 'TB'], [f'sbb{k}'])
                                ACTV(pTb[:, k, :, 0:N], sbb[:, k, :, 0:N], AF.Exp, [f'sbb{k}'], [f'pTb{k}'])
                                for h in range(2):
                                    hh = 2 * p + h
                                    MM(OAp[64 * h:64 * h + 64, i_lo * 128:i_lo * 128 + N], VA[:, ksl, cj, hh * 64:(hh + 1) * 64],
                                       pTb[:, k, h, 0:N], False, False, [f'pTb{k}', f'VA{ksl}_{cj}'], ['OAp'])
                                    MM(DENp[64 * h:64 * h + 64, i_lo * 128:i_lo * 128 + N], VLD[:, ksl, cj, :],
                                       pTb[:, k, h, 0:N], False, False, [f'pTb{k}', f'VLD{ksl}_{cj}'], ['DENp'])
                            TS('dve', rdb[:, :], DENp[:, :], 1e-30, None, ALU.max, None, ['DENp'], ['rdb'])
                            ACTV(rdb[:, :], rdb[:, :], AF.Ln, ['rdb'], ['rdb'])
                            ACTV(rdb[:, :], rdb[:, :], AF.Exp, ['rdb'], ['rdb'], scale=-1.0)
                            TT('dve', OAT[:, p, qi * T:(qi + 1) * T], OAp[:, :], rdb[:, :], ALU.mult, ['OAp', 'rdb'], [f'OAT{p}_{qi}'])
                    S.barrier()
                    run_block()

                with contextlib.ExitStack() as es:
                    wG = sbt(es, "wG", [128, KC, 2048], BF16)
                    wbra = sbt(es, "wbra", [128, 4, D], BF16)
                    wbrb = sbt(es, "wbrb", [128, 4, D], BF16)
                    wout = sbt(es, "wout", [128, KC, D], BF16)
                    stg = sbt(es, "stg4", [128, 2, D], F32)
                    stg_state['ns'] = 2
                    Bf = new_B(es)
                    xr = sbt(es, "xr", [128, 4, D], F32)
                    hnT = sbt(es, "hnT", [128, KC, T], BF16)
                    sg = sbt(es, "sg", [128, 2, T], F32)
                    mm = sbt(es, "mm", [128, 2, T], F32)
                    MT = sbt(es, "MT", [128, KC, T], BF16)
                    tp4 = pst(es, "tp4", [128, KC, 128], BF16)
                    Gp = [pst(es, f"G{i}", [128, T]) for i in range(2)]
                    Yp = [pst(es, f"Y{i}", [128, T]) for i in range(2)]
                    Xp = [pst(es, f"X{i}", [128, T]) for i in range(2)]
                    for kc in range(KC):
                        for hf in range(2):
                            load_weight(stg, wG[:, kc, hf * D:(hf + 1) * D], win_d[kc * 128:(kc + 1) * 128, 3072 + hf * D:3072 + (hf + 1) * D], D, 'wG')
                    for p in range(4):
                        load_weight(stg, wbra[:, p, :], wa_d[p * 128:(p + 1) * 128, :], D, 'wbra')
                        load_weight(stg, wbrb[:, p, :], wb_d[p * 128:(p + 1) * 128, :], D, 'wbrb')
                    for kc in range(KC):
                        load_weight(stg, wout[:, kc, :], wo_d[kc * 128:(kc + 1) * 128, :], D, 'wout')
                    blkc = 0
                    ob = 0
                    for qi in range(NQ):
                        t = T0 + qi
                        bis = [3] if qi == 0 else [0, 1, 2, 3]
                        cs, cn = (384, 128) if qi == 0 else (0, T)
                        sl0 = blkc
                        norm_a(Bf, x_d[(t * 4 + bis[0]) * 128:(t * 4 + bis[0] + 1) * 128, :], [], sl0 % 2, g1b)
                        for ii, bi in enumerate(bis):
                            if ii + 1 < len(bis):
                                nb_ = bis[ii + 1]
                                norm_a(Bf, x_d[(t * 4 + nb_) * 128:(t * 4 + nb_ + 1) * 128, :], [], (sl0 + ii + 1) % 2, g1b)
                            norm_b(Bf, (sl0 + ii) % 2, hnT[:, :, bi * 128:(bi + 1) * 128], f'hnT_{bi}', tp4, 'tp4')
                            blkc += 1
                        hread = [f'hnT_{bi}' for bi in bis]
                        for bi in bis:
                            blk_i = t * 4 + bi
                            S.dma(f'xr{bi}', xr[:, bi, :], x_d[blk_i * 128:(blk_i + 1) * 128, :], writes=[f'xr{bi}'])
                        for oc in range(KC):
                            for br in range(2):
                                for kc in range(KC):
                                    MM(Gp[br][:, cs:cs + cn], wG[:, kc, br * D + oc * 128:br * D + (oc + 1) * 128], hnT[:, kc, cs:cs + cn],
                                       kc == 0, kc == KC - 1, ['wG'] + hread, [f'G{br}'])
                                ACTV(sg[:, br, cs:cs + cn], Gp[br][:, cs:cs + cn], AF.Sigmoid, [f'G{br}'], [f'sg{br}'])
                                wsrc = wbra if br == 0 else wbrb
                                osrc = OAT if br == 0 else OT
                                for p in range(4):
                                    MM(Yp[br][:, cs:cs + cn], wsrc[:, p, oc * 128:(oc + 1) * 128], osrc[:, p, qi * T + cs:qi * T + cs + cn],
                                       p == 0, p == 3,
                                       ['wbra' if br == 0 else 'wbrb', (f'OAT{p}_{qi}' if br == 0 else f'OT{p}_{qi}'), 'OTz'], [f'Y{br}'])
                                TT('dve', mm[:, br, cs:cs + cn], Yp[br][:, cs:cs + cn], sg[:, br, cs:cs + cn], ALU.mult, [f'Y{br}', f'sg{br}'], [f'mm{br}'])
                            TT('pool', MT[:, oc, cs:cs + cn], mm[:, 0, cs:cs + cn], mm[:, 1, cs:cs + cn], ALU.add, ['mm0', 'mm1'], [f'MT{oc}'])
                        mread = [f'MT{oc}' for oc in range(KC)]
                        for bi in bis:
                            blk_i = t * 4 + bi
                            o = bi
                            for half in range(2):
                                for oc in range(KC):
                                    MM(Xp[half][:, :], MT[:, oc, bi * 128:(bi + 1) * 128], wout[:, oc, half * T:(half + 1) * T],
                                       oc == 0, oc == KC - 1, ['wout'] + mread, [f'X{half}'])
                                TT('dve', xr[:, o, half * T:(half + 1) * T], Xp[half][:, :], xr[:, o, half * T:(half + 1) * T], ALU.add,
                                   [f'X{half}', f'xr{o}'], [f'xr{o}'])
                            row = qi * T + bi * 128
                            S.dma(f'x1w{o}', x1_d[row:row + 128, :], xr[:, o, :], reads=[f'xr{o}'], writes=['x1_d'], q='pool')
                    S.barrier()
                    run_block()

        with contextlib.ExitStack() as es:
            wup = sbt(es, "wup", [128, KC, 2 * DFF], BF16)
            wdn = sbt(es, "wdn", [128, NFC, D], BF16)
            g2b = sbt(es, "g2b", [128, D], F32)
            cw = sbt(es, "cw", [128, 44, 3], F32)
            cb = sbt(es, "cb", [128, 44], F32)
            HALO = sbt(es, "HALO", [128, 44, 2], F32)
            with contextlib.ExitStack() as es2:
                stg = sbt(es2, "stg5", [128, 4, 2816], F32)
                stg_state['ns'] = 4
                for kc in range(KC):
                    for hf in range(2):
                        load_weight(stg, wup[:, kc, hf * DFF:(hf + 1) * DFF], wup_d[kc * 128:(kc + 1) * 128, hf * DFF:(hf + 1) * DFF], DFF, 'wup')
                for fc in range(NFC):
                    load_weight(stg, wdn[:, fc, :], wdn_d[fc * 128:(fc + 1) * 128, :], D, 'wdn')
                S.dma('cst', g2b[:, :], g2_d[:, :], writes=['g2b'])
                S.dma('cst', cw[:, :, :], cw_d[:, :, :], writes=['cw'])
                S.dma('cst', cb[:, :], cb_d[:, :], writes=['cb'])
                MS('pool', HALO[:, :, :], 0.0, ['HALO'])
                S.barrier()
            Bf = dict(xs=sbt(es, "xs", [128, 2, D], F32), junk=sbt(es, "junk", [128, D], BF16),
                      ss=sbt(es, "ss", [128, 2, 4], F32), hn=sbt(es, "hn", [128, 2, D], BF16))
            hnT = sbt(es, "hnT", [128, KC, T], BF16)
            hb = sbt(es, "hb", [128, 2, 2, T + 2], F32)
            cv = sbt(es, "cv", [128, 2, T], F32)
            sgl = sbt(es, "sgl", [128, T], F32)
            ACTT = sbt(es, "ACTT", [128, NFC, T], BF16)
            yo = sbt(es, "yo", [128, D], F32)
            tp5 = pst(es, "tp5", [128, KC, 128], BF16)
            Hp = [[pst(es, f"H{b}_{u}", [128, T]) for u in range(2)] for b in range(2)]
            Yd = [pst(es, f"Yd{i}", [128, T]) for i in range(2)]
            blkc = 0
            fcc = 0
            for qi in range(NQ):
                bis = [3] if qi == 0 else [0, 1, 2, 3]
                sl0 = blkc
                norm_a(Bf, x1_d[qi * T + bis[0] * 128:qi * T + bis[0] * 128 + 128, :], ['x1_d'], sl0 % 2, g2b)
                for ii, bi in enumerate(bis):
                    if ii + 1 < len(bis):
                        r2 = qi * T + bis[ii + 1] * 128
                        norm_a(Bf, x1_d[r2:r2 + 128, :], ['x1_d'], (sl0 + ii + 1) % 2, g2b)
                    norm_b(Bf, (sl0 + ii) % 2, hnT[:, :, bi * 128:(bi + 1) * 128], f'hnT_{bi}', tp5, 'tp5')
                    blkc += 1
                hread = [f'hnT_{bi}' for bi in bis]
                for fc in range(NFC):
                    b = fcc % 2
                    fcc += 1
                    for u in range(2):
                        ch = fc + u * NFC
                        hbn = f'hb{b}_{u}'
                        if qi == 0:
                            for kc in range(KC):
                                MM(Hp[b][u][:, 384:T], wup[:, kc, ch * 128:(ch + 1) * 128], hnT[:, kc, 384:T],
                                   kc == 0, kc == KC - 1, ['wup'] + hread, [f'H{b}_{u}'])
                            CP('act', HALO[:, ch, :], Hp[b][u][:, T - 2:T], [f'H{b}_{u}'], [f'HALO{ch}'])
                            continue
                        for kc in range(KC):
                            MM(Hp[b][u][:, :], wup[:, kc, ch * 128:(ch + 1) * 128], hnT[:, kc, :],
                               kc == 0, kc == KC - 1, ['wup'] + hread, [f'H{b}_{u}'])
                        CP('pool', hb[:, b, u, 0:2], HALO[:, ch, :], [f'HALO{ch}'], [hbn])
                        CP('act', hb[:, b, u, 2:T + 2], Hp[b][u][:, :], [f'H{b}_{u}'], [hbn])
                        CP('pool', HALO[:, ch, :], hb[:, b, u, T:T + 2], [hbn], [f'HALO{ch}'])
                        ce = 'dve'
                        cvn = f'cv{u}'
                        ACTV(cv[:, u, :], Hp[b][u][:, :], AF.Identity, [f'H{b}_{u}', 'cw', 'cb'], [cvn], scale=cw[:, ch, 2:3], bias=cb[:, ch:ch + 1])
                        STT(ce, cv[:, u, :], hb[:, b, u, 1:T + 1], cw[:, ch, 1:2], cv[:, u, :], ALU.mult, ALU.add, [hbn, cvn, 'cw'], [cvn])
                        STT(ce, cv[:, u, :], hb[:, b, u, 0:T], cw[:, ch, 0:1], cv[:, u, :], ALU.mult, ALU.add, [hbn, cvn, 'cw'], [cvn])
                    if qi == 0:
                        continue
                    ACTV(sgl[:, :], cv[:, 0, :], AF.Silu, ['cv0'], ['sgl'])
                    TT('dve', ACTT[:, fc, :], sgl[:, :], cv[:, 1, :], ALU.mult, ['sgl', 'cv1'], [f'ACTT{fc}'])
                if qi == 0:
                    continue
                aread = [f'ACTT{fc}' for fc in range(NFC)]
                for bi in range(4):
                    row = qi * T + bi * 128
                    S.dma('yo_in', yo[:, :], x1_d[row:row + 128, :], reads=['x1_d', 'y_d'], writes=['yo'])
                    for half in range(2):
                        for fc in range(NFC):
                            MM(Yd[half][:, :], ACTT[:, fc, bi * 128:(bi + 1) * 128], wdn[:, fc, half * T:(half + 1) * T],
                               fc == 0, fc == NFC - 1, ['wdn'] + aread, [f'Yd{half}'])
                        TT('dve', yo[:, half * T:(half + 1) * T], Yd[half][:, :], yo[:, half * T:(half + 1) * T], ALU.add,
                           [f'Yd{half}', 'yo'], ['yo'])
                    orow = (qi - 1) * T + bi * 128
                    S.dma('yw', y_d[orow:orow + 128, :], yo[:, :], reads=['yo'], writes=['y_d'], q='pool')
            S.barrier()
            run_block()
        print("megakernel ops", S.nops, "waits", S.nwaits, {k: v for k, v in S.cnt.items() if k in engnames})
    return nc


_CACHE = {}


def _consts():
    ident = np.eye(128, dtype=np.float32)
    kk = np.arange(128)
    negu = -(kk[:, None] >= kk[None, :]).astype(np.float32)
    negl = -(kk[:, None] < kk[None, :]).astype(np.float32)
    bd = np.zeros((128, 128), np.float32)
    bd[:64, :64] = 1.0
    bd[64:, 64:] = 1.0
    qq = np.arange(T)
    mask = np.zeros((128, 4, T), np.float32)
    for d in range(4):
        mask[:, d, :] = ((128 * d + kk[:, None]) < qq[None, :]).astype(np.float32)
    return ident, negu, negl, bd, mask


def _tb_table(rel_bias):
    kk = np.arange(128)[:, None]
    qq = np.arange(128)[None, :]
    tb = np.empty((128, 8, 640), np.float32)
    for rp in range(5):
        idx = np.clip(qq - kk + rp * 128, -128, 128) + 128
        blk = rel_bias[:, idx]
        vis = np.ones((128, 128), bool)
        if rp == 0:
            vis = (kk < 64) | (qq >= 64)
        if rp == 4:
            vis = (kk >= 64) | (qq < 64)
        blk = np.where(vis[None], blk, np.float32(NEG))
        tb[:, :, rp * 128:(rp + 1) * 128] = blk.transpose(1, 0, 2)
    return tb


def kernel(x, norm1_g, w_in, q_norm_g, k_norm_g, rel_bias, w_branch_a, w_branch_b, w_out, norm2_g,
           w_ffn_up, ffn_conv_w, ffn_conv_b, w_ffn_down):
    x = np.asarray(x, np.float32)
    Bn, Sq, Dm = x.shape
    assert Bn == 2 and Dm == D and Sq % 2048 == 0
    NO = Sq // 2048
    NT = Sq // T
    key = (NT, NO)
    if key not in _CACHE:
        _CACHE[key] = build(NT, NO)
    nc = _CACHE[key]
    f = lambda a: np.ascontiguousarray(np.asarray(a, np.float32))
    ident, negu, negl, bd, mask = _consts()
    own = NO * T
    shared = {
        "g1": f(np.broadcast_to(np.asarray(norm1_g, np.float32)[0][None, :], (128, D))),
        "g2": f(np.broadcast_to(np.asarray(norm2_g, np.float32)[0][None, :], (128, D))),
        "w_in": f(w_in[0]),
        "gq": f(np.tile(np.asarray(q_norm_g, np.float32)[0], 2)[:, None]),
        "gk": f(np.tile(np.asarray(k_norm_g, np.float32)[0], 2)[:, None]),
        "tb": f(_tb_table(np.asarray(rel_bias, np.float32)[0])),
        "w_a": f(w_branch_a[0]), "w_b": f(w_branch_b[0]), "w_o": f(w_out[0]),
        "w_up": f(w_ffn_up[0]),
        "cw": f(np.asarray(ffn_conv_w, np.float32)[0].reshape(3, 44, 128).transpose(2, 1, 0)),
        "cb": f(np.asarray(ffn_conv_b, np.float32)[0].reshape(44, 128).T),
        "w_dn": f(w_ffn_down[0]),
        "ident": ident, "negu": negu, "negl": negl, "bd": bd, "mask": mask,
    }
    in_maps = []
    for c in range(8):
        b, j = c // 4, c % 4
        real = (j + 1) * own
        pad = Sq - real
        xl = np.zeros((Sq, D), np.float32)
        xl[pad:] = x[b, :real]
        valid = np.zeros((Sq,), np.float32)
        valid[pad:] = 1.0
        m = dict(shared)
        m["x"] = xl
        m["valid"] = f(valid.reshape(Sq // 128, 128).T)
        in_maps.append(m)
    res = run_bass_kernel_spmd(nc, in_maps, core_ids=list(range(8)))
    out = np.empty((Bn, Sq, D), np.float32)
    for c in range(8):
        b, j = c // 4, c % 4
        out[b, j * own:(j + 1) * own] = res.results[c]["y"]
    return out
```

```python
import contextlib
import numpy as np
import concourse.bass as bass
import concourse.mybir as mybir
from concourse.bass_utils import run_bass_kernel_spmd

F32 = mybir.dt.float32
BF16 = mybir.dt.bfloat16
AF = mybir.ActivationFunctionType
ALU = mybir.AluOpType

D = 1024
KC = 8
T = 512
DFF = 2816
NFC = 22
EPS = 1e-6
NEG = -30000.0


class Sched:
    def __init__(self, engnames, sems):
        self.engnames = engnames
        self.prog = {k: [] for k in engnames}
        self.stack = {k: [self.prog[k]] for k in engnames}
        self.cond = None
        self.sems = sems
        self.free_sems = [k for k in sems if k.startswith('c')]
        self.alias = {}
        self.cnt = {}
        self.mult = {}
        for k in engnames:
            self.cnt[k] = 0
            self.mult[k] = 1
        self.cnt['flag'] = 0
        self.mult['flag'] = 1
        self.lastw = {}
        self.readers = {}
        self.waited = {}
        self.nops = 0
        self.nwaits = 0

    def chan(self, name):
        if name not in self.alias:
            s = self.free_sems.pop(0)
            self.alias[name] = s
            self.cnt[name] = 0
            self.mult[name] = 16
        return name

    def sem(self, p):
        return self.sems[self.alias.get(p, p)]

    def _deps(self, reads, writes):
        deps = {}

        def add(ps):
            p, s = ps
            if deps.get(p, 0) < s:
                deps[p] = s
        for r in reads:
            if r in self.lastw:
                add(self.lastw[r])
        for w in writes:
            if w in self.lastw:
                add(self.lastw[w])
            for rd in self.readers.get(w, ()):
                add(rd)
        return deps

    def _emit_waits(self, eng, deps, skip_self):
        wd = self.waited.setdefault(eng, {})
        for p, s in deps.items():
            if p == eng and skip_self:
                continue
            if wd.get(p, 0) >= s:
                continue
            self.stack[eng][-1].append(('w', self.sem(p), s * self.mult[p]))
            wd[p] = s
            self.nwaits += 1

    def _record(self, prod, seq, reads, writes):
        for r in reads:
            self.readers.setdefault(r, []).append((prod, seq))
        for w in writes:
            self.lastw[w] = (prod, seq)
            self.readers[w] = []

    def op(self, eng, meth, args, kw, reads=(), writes=()):
        deps = self._deps(reads, writes)
        self._emit_waits(eng, deps, skip_self=(eng == 'pe'))
        self.stack[eng][-1].append(('o', (meth, args, kw), self.sems[eng], 1))
        self.cnt[eng] += 1
        self._record(eng, self.cnt[eng], reads, writes)
        self.nops += 1

    def dma(self, ch, out, in_, reads=(), writes=(), q='sp'):
        self.chan(ch)
        deps = self._deps(reads, writes)
        self._emit_waits(q, deps, skip_self=False)
        self.stack[q][-1].append(('o', ('dma_start', (), dict(out=out, in_=in_)), self.sem(ch), 16))
        self.cnt[ch] += 1
        self._record(ch, self.cnt[ch], reads, writes)
        self.nops += 1

    def flag_op(self, meth, args, kw, reads=(), writes=()):
        deps = self._deps(reads, writes)
        self._emit_waits('dve', deps, skip_self=False)
        self.stack['dve'][-1].append(('o', (meth, args, kw), self.sems['flag'], 1))
        self.cnt['flag'] += 1
        self._record('flag', self.cnt['flag'], reads, writes)

    def begin_cond(self, flag_ap, engines):
        import copy
        if self.cond is None:
            self.cond = []
        self.cond.append(dict(flag_ap=flag_ap, engines=engines, seq=self.cnt['flag'],
                              start={e: self.cnt[e] for e in engines}, fstart=self.cnt['flag'],
                              snap=copy.deepcopy(self.waited)))
        for e in engines:
            self.stack[e].append([])

    def end_cond(self):
        c = self.cond.pop()
        nf = self.cnt['flag'] - c['fstart']
        for e in c['engines']:
            body = self.stack[e].pop()
            n = self.cnt[e] - c['start'][e]
            self.stack[e][-1].append(('if', c['flag_ap'], c['seq'], body, c['start'][e], n,
                                      nf if e == 'dve' else 0))
        self.waited = c['snap']

    def barrier(self):
        for eng in self.engnames:
            wd = self.waited.setdefault(eng, {})
            for p, c in self.cnt.items():
                if c > 0 and p != eng and wd.get(p, 0) < c:
                    self.stack[eng][-1].append(('w', self.sem(p), c * self.mult[p]))
                    wd[p] = c
        for eng in self.engnames:
            if eng != 'sp' and self.cnt[eng] > 0:
                self.stack[eng][-1].append(('w', self.sems[eng], self.cnt[eng]))

    def _replay_list(self, eng, e, lst):
        for it in lst:
            if it[0] == 'w':
                e.wait_ge(it[1], it[2])
            elif it[0] == 'o':
                meth, args, kw = it[1]
                getattr(e, meth)(*args, **kw).then_inc(it[2], it[3])
            else:
                _, flag_ap, seq, body, start, n, dve_else = it
                e.wait_ge(self.sems['flag'], seq)
                reg = self.regs[eng]
                e.reg_load(reg, flag_ap)
                with e.If_ne(reg, 0):
                    self._replay_list(eng, e, body)
                with e.Else():
                    if start > 0:
                        e.wait_ge(self.sems[eng], start)
                    if n > 0:
                        e.sem_inc(self.sems[eng], n)
                    if dve_else:
                        e.sem_inc(self.sems['flag'], dve_else)

    def replay(self, eng, e):
        assert len(self.stack[eng]) == 1
        self._replay_list(eng, e, self.prog[eng])
        self.prog[eng] = []
        self.stack[eng] = [self.prog[eng]]


def build(NT, NO):
    SL = NT * T
    NKB = SL // 128
    NQ = NO + 1
    T0 = NT - NO - 1
    nc = bass.Bass("TRN2", target_bir_lowering=False)

    def din(name, shape):
        return nc.dram_tensor(name, shape, F32, kind="ExternalInput").ap()
    x_d = din("x", [SL, D])
    valid_d = din("valid", [128, NKB])
    g1_d = din("g1", [128, D])
    g2_d = din("g2", [128, D])
    win_d = din("w_in", [D, 5120])
    gq_d = din("gq", [128, 1])
    gk_d = din("gk", [128, 1])
    tb_d = din("tb", [128, 8, 640])
    wa_d = din("w_a", [512, D])
    wb_d = din("w_b", [512, D])
    wo_d = din("w_o", [D, D])
    wup_d = din("w_up", [D, 2 * DFF])
    cw_d = din("cw", [128, 44, 3])
    cb_d = din("cb", [128, 44])
    wdn_d = din("w_dn", [DFF, D])
    ident_d = din("ident", [128, 128])
    negu_d = din("negu", [128, 128])
    negl_d = din("negl", [128, 128])
    bd_d = din("bd", [128, 128])
    mask_d = din("mask", [128, 4, T])
    y_d = nc.dram_tensor("y", [NO * T, D], F32, kind="ExternalOutput").ap()
    kt_d = nc.dram_tensor("kt_scr", [4, 128, SL], BF16).ap()
    v_d = nc.dram_tensor("v_scr", [128, NKB, 4 * 128], BF16).ap()
    x1_d = nc.dram_tensor("x1_scr", [NQ * T, D], F32).ap()

    top = contextlib.ExitStack()
    with top:
        engnames = ['pe', 'act', 'dve', 'pool', 'sp']
        sems = {}
        for n in ['pe', 'act', 'dve', 'pool', 'flag']:
            sems[n] = top.enter_context(nc.semaphore(n))
        for i in range(44):
            sems[f'c{i}'] = top.enter_context(nc.semaphore(f'c{i}'))
        S = Sched(engnames, sems)
        S.regs = {'pe': nc.alloc_register(mybir.EngineType.PE, 'flag_pe'),
                  'act': nc.alloc_register(mybir.EngineType.Activation, 'flag_act'),
                  'dve': nc.alloc_register(mybir.EngineType.DVE, 'flag_dve')}

        def run_block():
            blk = nc.Block()
            with blk:
                blk.tensor(lambda e: S.replay('pe', e))
                blk.scalar(lambda e: S.replay('act', e))
                blk.vector(lambda e: S.replay('dve', e))
                blk.gpsimd(lambda e: S.replay('pool', e))
                blk.sync(lambda e: S.replay('sp', e))

        uid = [0]

        def sbt(es, name, shape, dt):
            uid[0] += 1
            return es.enter_context(nc.sbuf_tensor(f"s{uid[0]}_{name}", shape, dt))

        def pst(es, name, shape, dt=F32):
            uid[0] += 1
            return es.enter_context(nc.psum_tensor(f"p{uid[0]}_{name}", shape, dt))

        def MM(out, lhsT, rhs, start, stop, reads, writes):
            S.op('pe', 'matmul', (out,), dict(lhsT=lhsT, rhs=rhs, start=start, stop=stop), reads, writes)

        def ACTV(out, in_, func, reads, writes, **kw):
            S.op('act', 'activation', (), dict(out=out, in_=in_, func=func, **kw), reads, writes)

        def CP(eng, out, in_, reads, writes):
            S.op(eng, 'copy' if eng == 'act' else 'tensor_copy', (), dict(out=out, in_=in_), reads, writes)

        def TT(eng, out, in0, in1, op, reads, writes):
            S.op(eng, 'tensor_tensor', (), dict(out=out, in0=in0, in1=in1, op=op), reads, writes)

        def STT(eng, out, in0, scalar, in1, op0, op1, reads, writes):
            S.op(eng, 'scalar_tensor_tensor', (), dict(out=out, in0=in0, scalar=scalar, in1=in1, op0=op0, op1=op1), reads, writes)

        def TS(eng, out, in0, s1, s2, op0, op1, reads, writes):
            kw = dict(out=out, in0=in0, scalar1=s1, scalar2=s2, op0=op0)
            if op1 is not None:
                kw['op1'] = op1
            S.op(eng, 'tensor_scalar', (), kw, reads, writes)

        def MS(eng, ap, val, writes):
            S.op(eng, 'memset', (ap, val), {}, (), writes)

        ident = sbt(top, "ident", [128, 128], BF16)
        negu = sbt(top, "negu", [128, 128], BF16)
        negl = sbt(top, "negl", [128, 128], BF16)
        bdm = sbt(top, "bdm", [128, 128], BF16)
        zero = sbt(top, "zero", [128, 128], BF16)
        cst = sbt(top, "cst", [128, 4], F32)

        stg_state = {'i': 0, 'ns': 2}

        def load_weight(stg, dst_ap, src_ap, n, wname):
            sl = stg_state['i'] % stg_state['ns']
            stg_state['i'] += 1
            S.dma(f'stg{sl}', stg[:, sl, 0:n], src_ap, writes=[f'stg{sl}'])
            ce = ('pool', 'dve', 'act')[stg_state['i'] % 3]
            CP(ce, dst_ap, stg[:, sl, 0:n], [f'stg{sl}'], [wname])

        def norm_a(Bf, src_ap, src_reads, slot, gb):
            xs, junk, ss, hn = Bf['xs'], Bf['junk'], Bf['ss'], Bf['hn']
            xn, hnn, ssn = f'xs{slot}', f'hn{slot}', f'ss{slot}'
            S.dma(xn, xs[:, slot, :], src_ap, reads=src_reads, writes=[xn])
            MS('dve', ss[:, slot, 0:1], 0.0, [ssn])
            ACTV(junk[:, :], xs[:, slot, :], AF.Square, [xn], ['junk', ssn], accum_out=ss[:, slot, 0:1])
            ACTV(ss[:, slot, 1:2], ss[:, slot, 0:1], AF.Ln, [ssn], [ssn], scale=1.0 / D, bias=cst[:, 0:1])
            ACTV(ss[:, slot, 2:3], ss[:, slot, 1:2], AF.Exp, [ssn], [ssn], scale=-0.5)
            STT('dve', hn[:, slot, :], xs[:, slot, :], ss[:, slot, 2:3], gb[:, :], ALU.mult, ALU.mult, [xn, ssn], [hnn])

        def norm_b(Bf, slot, hnT_ap, hnT_name, tp, tp_name, ev='act'):
            hn = Bf['hn']
            for kc in range(KC):
                S.op('pe', 'transpose', (tp[:, kc, :], hn[:, slot, kc * 128:(kc + 1) * 128], ident[:, :]), {}, [f'hn{slot}'], [tp_name])
            CP(ev, hnT_ap, tp[:, :, :], [tp_name], [hnT_name])

        def norm_block(Bf, src_ap, src_reads, slot, gb, hnT_ap, hnT_name, tp, tp_name):
            norm_a(Bf, src_ap, src_reads, slot, gb)
            norm_b(Bf, slot, hnT_ap, hnT_name, tp, tp_name)

        def new_B(es, ns=2):
            return dict(xs=sbt(es, "xs", [128, ns, D], F32), junk=sbt(es, "junk", [128, D], BF16),
                        ss=sbt(es, "ss", [128, ns, 4], F32), hn=sbt(es, "hn", [128, ns, D], BF16))

        sc04 = contextlib.ExitStack()
        with sc04:
            g1b = sbt(sc04, "g1b", [128, D], F32)
            vt = sbt(sc04, "vt", [128, NKB], F32)
            OT = sbt(sc04, "OT", [128, 4, NQ * T], BF16)

            with contextlib.ExitStack() as es:
                i32 = sbt(es, "i32", [128, 4, 128], F32)
                for i, srcd in enumerate([ident_d, negu_d, negl_d, bd_d]):
                    S.dma('cst', i32[:, i, :], srcd[:, :], writes=[f'i32_{i}'])
                S.barrier()
                for i, dst in enumerate([ident, negu, negl, bdm]):
                    CP('pool', dst[:, :], i32[:, i, :], [f'i32_{i}'], [f'const{i}'])
                MS('pool', zero[:, :], 0.0, ['zero'])
                MS('pool', cst[:, 0:1], EPS, ['cst'])
                MS('pool', cst[:, 1:2], 1.0, ['cst'])
                MS('pool', OT[:, :, 0:T], 0.0, ['OTz'])
                S.dma('cst', g1b[:, :], g1_d[:, :], writes=['g1b'])
                S.dma('cst', vt[:, :], valid_d[:, :], writes=['vt'])
                S.barrier()
                run_block()

            sc12 = contextlib.ExitStack()
            with sc12:
                QT = sbt(sc12, "QT", [128, 4, NQ * T], BF16)
                with contextlib.ExitStack() as es:
                    wB = sbt(es, "wB", [128, KC, 1536], BF16)
                    stg = sbt(es, "stg1", [128, 2, 1536], F32)
                    stg_state['ns'] = 2
                    Bf = new_B(es, 4)
                    hnT = sbt(es, "hnT", [128, 2, KC, T], BF16)
                    KTs = sbt(es, "KTs", [128, 2, 4, T], BF16)
                    Vs = sbt(es, "Vs", [128, 2, 4, T], BF16)
                    tps = [pst(es, f"tp{i}", [128, KC, 128], BF16) for i in range(2)]
                    accs = [pst(es, f"acc{i}", [128, T], F32) for i in range(4)]
                    for kc in range(KC):
                        load_weight(stg, wB[:, kc, :], win_d[kc * 128:(kc + 1) * 128, 1536:3072], 1536, 'wB')
                    acc_i = 0
                    ev_i = 0

                    def nA(t, bi):
                        blk_i = t * 4 + bi
                        norm_a(Bf, x_d[blk_i * 128:(blk_i + 1) * 128, :], [], blk_i % 4, g1b)

                    def nB(t, bi):
                        blk_i = t * 4 + bi
                        hs_ = t % 2
                        norm_b(Bf, blk_i % 4, hnT[:, hs_, :, bi * 128:(bi + 1) * 128], f'hnT{hs_}_{bi}', tps[blk_i % 2], f'tp{blk_i % 2}',
                               ev=('act' if bi % 2 == 0 else 'dve'))

                    def grpK(t, p):
                        nonlocal acc_i, ev_i
                        hs = t % 2
                        hread = [f'hnT{hs}_{bi}' for bi in range(4)]
                        a = acc_i % 4
                        acc_i += 1
                        for kc in range(KC):
                            MM(accs[a][:, :], wB[:, kc, 512 + p * 128:512 + (p + 1) * 128], hnT[:, hs, kc, :],
                               kc == 0, kc == KC - 1, ['wB'] + hread, [f'acc{a}'])
                        CP('dve' if ev_i % 2 == 0 else 'act', KTs[:, hs, p, :], accs[a][:, :], [f'acc{a}'], [f'KTs{hs}'])
                        ev_i += 1
                        if p == 3:
                            S.dma(f'kts{hs}', kt_d[:, :, t * T:(t + 1) * T].rearrange("q p c -> p q c"), KTs[:, hs, :, :],
                                  reads=[f'KTs{hs}'], writes=['kt_d'], q='pool')

                    def grpV(t, bi):
                        nonlocal acc_i, ev_i
                        hs = t % 2
                        a = acc_i % 4
                        acc_i += 1
                        for kc in range(KC):
                            MM(accs[a][:, :], hnT[:, hs, kc, bi * 128:(bi + 1) * 128], wB[:, kc, 1024:1536],
                               kc == 0, kc == KC - 1, ['wB', f'hnT{hs}_{bi}'], [f'acc{a}'])
                        CP('dve' if ev_i % 2 == 0 else 'act', Vs[:, hs, bi, :], accs[a][:, :], [f'acc{a}'], [f'Vs{hs}'])
                        ev_i += 1
                        if bi == 3:
                            S.dma(f'vs{hs}', v_d[:, t * 4:(t + 1) * 4, :], Vs[:, hs, :, :],
                                  reads=[f'Vs{hs}'], writes=['v_d'], q='pool')

                    def grpQ(t, p):
                        nonlocal acc_i, ev_i
                        hs = t % 2
                        qi = t - T0
                        hread = [f'hnT{hs}_{bi}' for bi in range(4)]
                        a = acc_i % 4
                        acc_i += 1
                        for kc in range(KC):
                            MM(accs[a][:, :], wB[:, kc, p * 128:(p + 1) * 128], hnT[:, hs, kc, :],
                               kc == 0, kc == KC - 1, ['wB'] + hread, [f'acc{a}'])
                        CP('dve' if ev_i % 2 == 0 else 'act', QT[:, p, qi * T:(qi + 1) * T], accs[a][:, :], [f'acc{a}'], [f'QT{p}_{qi}'])
                        ev_i += 1

                    for bi in range(4):
                        nA(0, bi)
                        nB(0, bi)
                    for t in range(NT):
                        nx = t + 1 < NT
                        if nx:
                            nA(t + 1, 0)
                            nA(t + 1, 1)
                        grpK(t, 0)
                        if nx:
                            nB(t + 1, 0)
                        grpK(t, 1)
                        if nx:
                            nA(t + 1, 2)
                        grpK(t, 2)
                        if nx:
                            nB(t + 1, 1)
                        grpK(t, 3)
                        if nx:
                            nA(t + 1, 3)
                        grpV(t, 0)
                        if nx:
                            nB(t + 1, 2)
                        grpV(t, 1)
                        grpV(t, 2)
                        if nx:
                            nB(t + 1, 3)
                        grpV(t, 3)
                        if t >= T0:
                            for p in range(4):
                                grpQ(t, p)
                    S.barrier()
                    run_block()

                with contextlib.ExitStack() as es:
                    KTp = sbt(es, "KTp", [128, SL], BF16)
                    Vp = sbt(es, "Vp", [128, NKB, 128], BF16)
                    M = sbt(es, "M", [128, 4, 2, T], BF16)
                    NE, NSP, NXW, NW = 4, 3, 2, 2
                    eb = sbt(es, "eb", [128, NE, 2, T], F32)
                    spb = sbt(es, "spb", [128, NSP, 2, T], BF16)
                    xwb = sbt(es, "xwb", [128, NXW, 2, T], F32)
                    wbuf = sbt(es, "wbuf", [128, NW, 2, T], BF16)
                    m32 = sbt(es, "m32", [128, 4, T], F32)
                    Zp = [pst(es, f"Z{i}", [128, 2, T]) for i in range(2)]
                    Ap = pst(es, "A", [128, 2, T])
                    Op = pst(es, "O", [128, T])
                    S.dma('cst', m32[:, :, :], mask_d[:, :, :], writes=['m32'])
                    for h in range(2):
                        CP('pool', M[:, :, h, :], m32[:, :, :], ['m32'], ['M'])
                    I32 = mybir.dt.int32
                    flagbuf = sbt(es, "flagbuf", [128, 512], I32)
                    mx = sbt(es, "mx", [128, 4], F32)
                    THRESH = 150.0
                    CH = 32
                    nchk = (NKB + CH - 1) // CH
                    itn = [0]
                    fidx = [0]

                    def load_pair(p):
                        for c in reversed(range(nchk)):
                            k0, k1 = c * CH, min(NKB, (c + 1) * CH)
                            S.dma(f'ktp{c}', KTp[:, k0 * 128:k1 * 128], kt_d[p, :, k0 * 128:k1 * 128], reads=['kt_d'], writes=[f'KTp{c}'])
                            S.dma(f'vp{c}', Vp[:, k0:k1, :], v_d[:, k0:k1, p * 128:(p + 1) * 128], reads=['v_d'], writes=[f'Vp{c}'])

                    def st1(it):
                        W, kb, p, qi, c0, n = it['W'], it['kb'], it['p'], it['qi'], it['c0'], it['n']
                        for h in range(2):
                            MM(Zp[n % 2][:, h, 0:W], KTp[64 * h:64 * h + 64, kb * 128:(kb + 1) * 128],
                               QT[64 * h:64 * h + 64, p, qi * T + c0:qi * T + c0 + W], True, True,
                               [f'KTp{kb // CH}', f'QT{p}_{qi}'], [f'Z{n % 2}'])

                    def st2(it):
                        W, n = it['W'], it['n']
                        en = f'e{n % NE}'
                        ACTV(eb[:, n % NE, :, 0:W], Zp[n % 2][:, :, 0:W], AF.Exp, [f'Z{n % 2}'], [en], scale=0.125)
                        if it['d'] is not None:
                            TT('dve', eb[:, n % NE, :, 0:W], eb[:, n % NE, :, 0:W], M[:, it['d'], :, 0:W], ALU.mult, [en, 'M'], [en])

                    def st3(it):
                        W, n = it['W'], it['n']
                        ACTV(spb[:, n % NSP, :, 0:W], eb[:, n % NE, :, 0:W], AF.Ln, [f'e{n % NE}'], [f'sp{n % NSP}'], bias=cst[:, 1:2])

                    def st4(it):
                        W, n = it['W'], it['n']
                        for h in range(2):
                            MM(Ap[:, h, 0:W], negu[:, :], spb[:, n % NSP, h, 0:W], it['first'], False,
                               [f'sp{n % NSP}', 'const1'], ['A'])

                    def st5(it):
                        W, n = it['W'], it['n']
                        ACTV(xwb[:, n % NXW, :, 0:W], Ap[:, :, 0:W], AF.Exp, ['A'], [f'xw{n % NXW}'])

                    def st6(it):
                        W, n = it['W'], it['n']
                        if it['last']:
                            return
                        for h in range(2):
                            MM(Ap[:, h, 0:W], negl[:, :], spb[:, n % NSP, h, 0:W], False, False,
                               [f'sp{n % NSP}', 'const2'], ['A'])

                    def st7(it):
                        W, n = it['W'], it['n']
                        TT('dve', wbuf[:, n % NW, :, 0:W], eb[:, n % NE, :, 0:W], xwb[:, n % NXW, :, 0:W], ALU.mult,
                           [f'e{n % NE}', f'xw{n % NXW}'], [f'w{n % NW}'])

                    def st8(it):
                        W, n, kb = it['W'], it['n'], it['kb']
                        for h in range(2):
                            MM(Op[64 * h:64 * h + 64, 0:W], Vp[:, kb, 64 * h:64 * h + 64], wbuf[:, n % NW, h, 0:W],
                               it['first'], it['last'], [f'w{n % NW}', f'Vp{kb // CH}'], ['O'])

                    stages = [(st1, 0), (st2, 1), (st3, 2), (st6, 4), (st4, 3), (st5, 3), (st7, 4), (st8, 5)]

                    def emit_segment(items):
                        nit = len(items)
                        for s_ in range(nit + 7):
                            for fn, dly in stages:
                                i_ = s_ - dly
                                if 0 <= i_ < nit:
                                    fn(items[i_])

                    def emit_flag(W):
                        fi = fidx[0]
                        fidx[0] += 1
                        for h in range(2):
                            S.op('dve', 'tensor_reduce', (), dict(out=mx[0:1, h:h + 1], in_=Ap[0:1, h, 0:W], axis=mybir.AxisListType.X, op=ALU.max),
                                 ['A'], [f'mx{h}'])
                        TT('dve', mx[0:1, 2:3], mx[0:1, 0:1], mx[0:1, 1:2], ALU.max, ['mx0', 'mx1'], ['mx2'])
                        S.flag_op('tensor_scalar', (), dict(out=flagbuf[0:1, fi:fi + 1], in0=mx[0:1, 2:3], scalar1=-THRESH, scalar2=None, op0=ALU.is_gt),
                                  ['mx2'], [f'flag{fi}'])
                        return fi

                    for p in range(4):
                        load_pair(p)
                        for qi in range(NQ):
                            g = T0 + qi
                            if qi == 0:
                                W, c0, q0, ndiag = 128, 384, g * T + 384, 1
                            else:
                                W, c0, q0, ndiag = T, 0, g * T, 4
                            kb_hi = (q0 + W) // 128 - 1
                            nb = kb_hi + 1
                            segs = [list(range(0, min(nb, ndiag + 2)))]
                            sz = 2
                            while segs[-1][-1] + 1 < nb:
                                st_ = segs[-1][-1] + 1
                                segs.append(list(range(st_, min(nb, st_ + sz))))
                                if len(segs) > 2:
                                    sz *= 2
                            nopen = 0
                            for si, seg in enumerate(segs):
                                items = []
                                for b in seg:
                                    kb = kb_hi - b
                                    d = kb - q0 // 128
                                    items.append(dict(p=p, qi=qi, W=W, c0=c0, kb=kb, d=(d if d >= 0 else None),
                                                      first=(b == 0), last=(b == nb - 1), n=itn[0]))
                                    itn[0] += 1
                                emit_segment(items)
                                if si < len(segs) - 1:
                                    fi = emit_flag(W)
                                    S.begin_cond(flagbuf[0:1, fi:fi + 1], ['pe', 'act', 'dve'])
                                    nopen += 1
                            for _ in range(nopen):
                                S.end_cond()
                            CP('dve', OT[:, p, qi * T + c0:qi * T + c0 + W], Op[:, 0:W], ['O', 'OTz'], [f'OT{p}_{qi}'])
                    S.barrier()
                    run_block()

            sc34 = contextlib.ExitStack()
            with sc34:
                OAT = sbt(sc34, "OAT", [128, 4, NQ * T], BF16)
                with contextlib.ExitStack() as es:
                    wA = sbt(es, "wA", [128, KC, 1536], BF16)
                    with contextlib.ExitStack() as es2:
                        stg = sbt(es2, "stg3", [128, 4, 1536], F32)
                        stg_state['ns'] = 4
                        for kc in range(KC):
                            load_weight(stg, wA[:, kc, :], win_d[kc * 128:(kc + 1) * 128, 0:1536], 1536, 'wA')
                        S.barrier()
                    TB = sbt(es, "TB", [128, 8, 640], F32)
                    gq = sbt(es, "gq", [128, 1], F32)
                    gk = sbt(es, "gk", [128, 1], F32)
                    Bf = new_B(es)
                    hnT = sbt(es, "hnT", [128, KC, T], BF16)
                    QAT = sbt(es, "QAT", [128, 4, T], BF16)
                    KAT = sbt(es, "KAT", [128, 4, 2, T], BF16)
                    VA = sbt(es, "VA", [128, 2, 4, T], BF16)
                    VLD = sbt(es, "VLD", [128, 2, 4, 64], BF16)
                    sqb = sbt(es, "sqb", [128, 2, T], BF16)
                    qgb = sbt(es, "qgb", [128, 2, T], F32)
                    lnb = sbt(es, "lnb", [128, 2, T], F32)
                    sbb = sbt(es, "sbb", [128, 2, 2, T], F32)
                    pTb = sbt(es, "pTb", [128, 2, 2, T], BF16)
                    rdb = sbt(es, "rdb", [128, T], F32)
                    tp3 = pst(es, "tp3", [128, KC, 128], BF16)
                    acc3t = pst(es, "acc3", [128, 2, T])
                    acc3 = [acc3t[:, i, :] for i in range(2)]
                    SSp = pst(es, "SSp", [128, T])
                    SPSt = pst(es, "SPS", [128, 2, T])
                    sbufs = [(SPSt, 'SPS'), (acc3t, 'acc3_')]
                    OAp = pst(es, "OAp", [128, T])
                    DENp = pst(es, "DENp", [128, T])
                    S.dma('cst', TB[:, :, :], tb_d[:, :, :], writes=['TB'])
                    S.dma('cst', gq[:, :], gq_d[:, :], writes=['gq'])
                    S.dma('cst', gk[:, :], gk_d[:, :], writes=['gk'])
                    S.barrier()
                    blkc = 0
                    acc_i = 0
                    nrm = [0]
                    jj = [0]

                    def qknorm(a, gcol, gname, dst_ap, dst_name):
                        k = nrm[0] % 2
                        nrm[0] += 1
                        ACTV(sqb[:, k, :], acc3[a][:, :], AF.Square, [f'acc3_{a}'], [f'sqb{k}'])
                        S.op('act', 'mul', (), dict(out=qgb[:, k, :], in_=acc3[a][:, :], mul=gcol[:, 0:1]), [f'acc3_{a}', gname], [f'qgb{k}'])
                        MM(SSp[:, :], bdm[:, :], sqb[:, k, :], True, True, [f'sqb{k}', 'const3'], ['SSp'])
                        ACTV(lnb[:, k, :], SSp[:, :], AF.Ln, ['SSp'], [f'lnb{k}'], scale=1.0 / 64, bias=cst[:, 0:1])
                        ACTV(lnb[:, k, :], lnb[:, k, :], AF.Exp, [f'lnb{k}'], [f'lnb{k}'], scale=-0.5)
                        TT('dve', dst_ap, qgb[:, k, :], lnb[:, k, :], ALU.mult, [f'qgb{k}', f'lnb{k}'], [dst_name])

                    for t in range(T0 - 1, NT):
                        qi = t - T0
                        sl = t % 2
                        sl0 = blkc
                        norm_a(Bf, x_d[t * 512:t * 512 + 128, :], [], sl0 % 2, g1b)
                        for bi in range(4):
                            if bi + 1 < 4:
                                norm_a(Bf, x_d[(t * 4 + bi + 1) * 128:(t * 4 + bi + 2) * 128, :], [], (sl0 + bi + 1) % 2, g1b)
                            norm_b(Bf, (sl0 + bi) % 2, hnT[:, :, bi * 128:(bi + 1) * 128], f'hnT_{bi}', tp3, 'tp3')
                            blkc += 1
                        hread = [f'hnT_{bi}' for bi in range(4)]
                        for p in range(4):
                            a = acc_i % 2
                            acc_i += 1
                            for kc in range(KC):
                                MM(acc3[a][:, :], wA[:, kc, 512 + p * 128:512 + (p + 1) * 128], hnT[:, kc, :],
                                   kc == 0, kc == KC - 1, ['wA'] + hread, [f'acc3_{a}'])
                            qknorm(a, gk, 'gk', KAT[:, p, sl, :], f'KAT{sl}_{p}')
                        for bi in range(4):
                            a = acc_i % 2
                            acc_i += 1
                            for kc in range(KC):
                                MM(acc3[a][:, :], hnT[:, kc, bi * 128:(bi + 1) * 128], wA[:, kc, 1024:1536],
                                   kc == 0, kc == KC - 1, ['wA', f'hnT_{bi}'], [f'acc3_{a}'])
                            CP('dve', VA[:, sl, bi, :], acc3[a][:, :], [f'acc3_{a}'], [f'VA{sl}_{bi}'])
                            CP('pool', VLD[:, sl, bi, :], vt[:, t * 4 + bi:t * 4 + bi + 1].to_broadcast([128, 64]),
                               ['vt'], [f'VLD{sl}_{bi}'])
                        if qi < 0:
                            continue
                        for p in range(4):
                            a = acc_i % 2
                            acc_i += 1
                            for kc in range(KC):
                                MM(acc3[a][:, :], wA[:, kc, p * 128:(p + 1) * 128], hnT[:, kc, :],
                                   kc == 0, kc == KC - 1, ['wA'] + hread, [f'acc3_{a}'])
                            qknorm(a, gq, 'gq', QAT[:, p, :], f'QAT{p}')
                        for p in range(4):
                            MM(OAp[:, :], zero[:, :], hnT[:, 0, :], True, False, ['zero'] + hread, ['OAp'])
                            MM(DENp[:, :], zero[:, :], hnT[:, 0, :], True, False, ['zero'] + hread, ['DENp'])
                            def jgeom(j):
                                i_lo, i_hi = max(0, j - 4), min(3, j)
                                N = (i_hi - i_lo + 1) * 128
                                ksl = (1 - sl) if j < 4 else sl
                                return i_lo, N, ksl, j % 4, (4 - j + i_lo) * 128

                            def emitS(j, k):
                                i_lo, N, ksl, cj, tb0 = jgeom(j)
                                spt, spn = sbufs[k]
                                for h in range(2):
                                    MM(spt[:, h, 0:N], KAT[64 * h:64 * h + 64, p, ksl, cj * 128:(cj + 1) * 128],
                                       QAT[64 * h:64 * h + 64, p, i_lo * 128:i_lo * 128 + N], True, True,
                                       [f'KAT{ksl}_{p}', f'QAT{p}'], [f'{spn}{h}'])

                            kbase = jj[0]
                            jj[0] += 8
                            emitS(0, kbase % 2)
                            emitS(1, (kbase + 1) % 2)
                            for j in range(8):
                                i_lo, N, ksl, cj, tb0 = jgeom(j)
                                k = (kbase + j) % 2
                                spt, spn = sbufs[k]
                                STT('dve', sbb[:, k, :, 0:N], spt[:, :, 0:N], 0.125, TB[:, 2 * p:2 * p + 2, tb0:tb0 + N], ALU.mult, ALU.add,
                                    [f'{spn}0', f'{spn}1', 'TB'], [f'sbb{k}'])
                                if j + 2 < 8:
                                    emitS(j + 2, k)
                                ACTV(pTb[:, k, :, 0:N], sbb[:, k, :, 0:N], AF.Exp, [f'sbb{k}'], [f'pTb{k}'])
                                for h in range(2):
                                    hh = 2 * p + h
                                    MM(OAp[64 * h:64 * h + 64, i_lo * 128:i_lo * 128 + N], VA[:, ksl, cj, hh * 64:(hh + 1) * 64],
                                       pTb[:, k, h, 0:N], False, False, [f'pTb{k}', f'VA{ksl}_{cj}'], ['OAp'])
                                    MM(DENp[64 * h:64 * h + 64, i_lo * 128:i_lo * 128 + N], VLD[:, ksl, cj, :],
                                       pTb[:, k, h, 0:N], False, False, [f'pTb{k}', f'VLD{ksl}_{cj}'], ['DENp'])
                            TS('dve', rdb[:, :], DENp[:, :], 1e-30, None, ALU.max, None, ['DENp'], ['rdb'])
                            ACTV(rdb[:, :], rdb[:, :], AF.Ln, ['rdb'], ['rdb'])
                            ACTV(rdb[:, :], rdb[:, :], AF.Exp, ['rdb'], ['rdb'], scale=-1.0)
                            TT('dve', OAT[:, p, qi * T:(qi + 1) * T], OAp[:, :], rdb[:, :], ALU.mult, ['OAp', 'rdb'], [f'OAT{p}_{qi}'])
                    S.barrier()
                    run_block()

                with contextlib.ExitStack() as es:
                    wG = sbt(es, "wG", [128, KC, 2048], BF16)
                    wbra = sbt(es, "wbra", [128, 4, D], BF16)
                    wbrb = sbt(es, "wbrb", [128, 4, D], BF16)
                    wout = sbt(es, "wout", [128, KC, D], BF16)
                    stg = sbt(es, "stg4", [128, 2, D], F32)
                    stg_state['ns'] = 2
                    Bf = new_B(es)
                    xr = sbt(es, "xr", [128, 4, D], F32)
                    hnT = sbt(es, "hnT", [128, KC, T], BF16)
                    sg = sbt(es, "sg", [128, 2, T], F32)
                    mm = sbt(es, "mm", [128, 2, T], F32)
                    MT = sbt(es, "MT", [128, KC, T], BF16)
                    tp4 = pst(es, "tp4", [128, KC, 128], BF16)
                    Gp = [pst(es, f"G{i}", [128, T]) for i in range(2)]
                    Yp = [pst(es, f"Y{i}", [128, T]) for i in range(2)]
                    Xp = [pst(es, f"X{i}", [128, T]) for i in range(2)]
                    for kc in range(KC):
                        for hf in range(2):
                            load_weight(stg, wG[:, kc, hf * D:(hf + 1) * D], win_d[kc * 128:(kc + 1) * 128, 3072 + hf * D:3072 + (hf + 1) * D], D, 'wG')
                    for p in range(4):
                        load_weight(stg, wbra[:, p, :], wa_d[p * 128:(p + 1) * 128, :], D, 'wbra')
                        load_weight(stg, wbrb[:, p, :], wb_d[p * 128:(p + 1) * 128, :], D, 'wbrb')
                    for kc in range(KC):
                        load_weight(stg, wout[:, kc, :], wo_d[kc * 128:(kc + 1) * 128, :], D, 'wout')
                    blkc = 0
                    ob = 0
                    for qi in range(NQ):
                        t = T0 + qi
                        bis = [3] if qi == 0 else [0, 1, 2, 3]
                        cs, cn = (384, 128) if qi == 0 else (0, T)
                        sl0 = blkc
                        norm_a(Bf, x_d[(t * 4 + bis[0]) * 128:(t * 4 + bis[0] + 1) * 128, :], [], sl0 % 2, g1b)
                        for ii, bi in enumerate(bis):
                            if ii + 1 < len(bis):
                                nb_ = bis[ii + 1]
                                norm_a(Bf, x_d[(t * 4 + nb_) * 128:(t * 4 + nb_ + 1) * 128, :], [], (sl0 + ii + 1) % 2, g1b)
                            norm_b(Bf, (sl0 + ii) % 2, hnT[:, :, bi * 128:(bi + 1) * 128], f'hnT_{bi}', tp4, 'tp4')
                            blkc += 1
                        hread = [f'hnT_{bi}' for bi in bis]
                        for bi in bis:
                            blk_i = t * 4 + bi
                            S.dma(f'xr{bi}', xr[:, bi, :], x_d[blk_i * 128:(blk_i + 1) * 128, :], writes=[f'xr{bi}'])
                        for oc in range(KC):
                            for br in range(2):
                                for kc in range(KC):
                                    MM(Gp[br][:, cs:cs + cn], wG[:, kc, br * D + oc * 128:br * D + (oc + 1) * 128], hnT[:, kc, cs:cs + cn],
                                       kc == 0, kc == KC - 1, ['wG'] + hread, [f'G{br}'])
                                ACTV(sg[:, br, cs:cs + cn], Gp[br][:, cs:cs + cn], AF.Sigmoid, [f'G{br}'], [f'sg{br}'])
                                wsrc = wbra if br == 0 else wbrb
                                osrc = OAT if br == 0 else OT
                                for p in range(4):
                                    MM(Yp[br][:, cs:cs + cn], wsrc[:, p, oc * 128:(oc + 1) * 128], osrc[:, p, qi * T + cs:qi * T + cs + cn],
                                       p == 0, p == 3,
                                       ['wbra' if br == 0 else 'wbrb', (f'OAT{p}_{qi}' if br == 0 else f'OT{p}_{qi}'), 'OTz'], [f'Y{br}'])
                                TT('dve', mm[:, br, cs:cs + cn], Yp[br][:, cs:cs + cn], sg[:, br, cs:cs + cn], ALU.mult, [f'Y{br}', f'sg{br}'], [f'mm{br}'])
                            TT('pool', MT[:, oc, cs:cs + cn], mm[:, 0, cs:cs + cn], mm[:, 1, cs:cs + cn], ALU.add, ['mm0', 'mm1'], [f'MT{oc}'])
                        mread = [f'MT{oc}' for oc in range(KC)]
                        for bi in bis:
                            blk_i = t * 4 + bi
                            o = bi
                            for half in range(2):
                                for oc in range(KC):
                                    MM(Xp[half][:, :], MT[:, oc, bi * 128:(bi + 1) * 128], wout[:, oc, half * T:(half + 1) * T],
                                       oc == 0, oc == KC - 1, ['wout'] + mread, [f'X{half}'])
                                TT('dve', xr[:, o, half * T:(half + 1) * T], Xp[half][:, :], xr[:, o, half * T:(half + 1) * T], ALU.add,
                                   [f'X{half}', f'xr{o}'], [f'xr{o}'])
                            row = qi * T + bi * 128
                            S.dma(f'x1w{o}', x1_d[row:row + 128, :], xr[:, o, :], reads=[f'xr{o}'], writes=['x1_d'], q='pool')
                    S.barrier()
                    run_block()

        with contextlib.ExitStack() as es:
            wup = sbt(es, "wup", [128, KC, 2 * DFF], BF16)
            wdn = sbt(es, "wdn", [128, NFC, D], BF16)
            g2b = sbt(es, "g2b", [128, D], F32)
            cw = sbt(es, "cw", [128, 44, 3], F32)
            cb = sbt(es, "cb", [128, 44], F32)
            HALO = sbt(es, "HALO", [128, 44, 2], F32)
            with contextlib.ExitStack() as es2:
                stg = sbt(es2, "stg5", [128, 4, 2816], F32)
                stg_state['ns'] = 4
                for kc in range(KC):
                    for hf in range(2):
                        load_weight(stg, wup[:, kc, hf * DFF:(hf + 1) * DFF], wup_d[kc * 128:(kc + 1) * 128, hf * DFF:(hf + 1) * DFF], DFF, 'wup')
                for fc in range(NFC):
                    load_weight(stg, wdn[:, fc, :], wdn_d[fc * 128:(fc + 1) * 128, :], D, 'wdn')
                S.dma('cst', g2b[:, :], g2_d[:, :], writes=['g2b'])
                S.dma('cst', cw[:, :, :], cw_d[:, :, :], writes=['cw'])
                S.dma('cst', cb[:, :], cb_d[:, :], writes=['cb'])
                MS('pool', HALO[:, :, :], 0.0, ['HALO'])
                S.barrier()
            Bf = dict(xs=sbt(es, "xs", [128, 2, D], F32), junk=sbt(es, "junk", [128, D], BF16),
                      ss=sbt(es, "ss", [128, 2, 4], F32), hn=sbt(es, "hn", [128, 2, D], BF16))
            hnT = sbt(es, "hnT", [128, KC, T], BF16)
            hb = sbt(es, "hb", [128, 2, 2, T + 2], F32)
            cv = sbt(es, "cv", [128, 2, T], F32)
            sgl = sbt(es, "sgl", [128, T], F32)
            ACTT = sbt(es, "ACTT", [128, NFC, T], BF16)
            yo = sbt(es, "yo", [128, D], F32)
            tp5 = pst(es, "tp5", [128, KC, 128], BF16)
            Hp = [[pst(es, f"H{b}_{u}", [128, T]) for u in range(2)] for b in range(2)]
            Yd = [pst(es, f"Yd{i}", [128, T]) for i in range(2)]
            blkc = 0
            fcc = 0
            for qi in range(NQ):
                bis = [3] if qi == 0 else [0, 1, 2, 3]
                sl0 = blkc
                norm_a(Bf, x1_d[qi * T + bis[0] * 128:qi * T + bis[0] * 128 + 128, :], ['x1_d'], sl0 % 2, g2b)
                for ii, bi in enumerate(bis):
                    if ii + 1 < len(bis):
                        r2 = qi * T + bis[ii + 1] * 128
                        norm_a(Bf, x1_d[r2:r2 + 128, :], ['x1_d'], (sl0 + ii + 1) % 2, g2b)
                    norm_b(Bf, (sl0 + ii) % 2, hnT[:, :, bi * 128:(bi + 1) * 128], f'hnT_{bi}', tp5, 'tp5')
                    blkc += 1
                hread = [f'hnT_{bi}' for bi in bis]
                for fc in range(NFC):
                    b = fcc % 2
                    fcc += 1
                    for u in range(2):
                        ch = fc + u * NFC
                        hbn = f'hb{b}_{u}'
                        if qi == 0:
                            for kc in range(KC):
                                MM(Hp[b][u][:, 384:T], wup[:, kc, ch * 128:(ch + 1) * 128], hnT[:, kc, 384:T],
                                   kc == 0, kc == KC - 1, ['wup'] + hread, [f'H{b}_{u}'])
                            CP('act', HALO[:, ch, :], Hp[b][u][:, T - 2:T], [f'H{b}_{u}'], [f'HALO{ch}'])
                            continue
                        for kc in range(KC):
                            MM(Hp[b][u][:, :], wup[:, kc, ch * 128:(ch + 1) * 128], hnT[:, kc, :],
                               kc == 0, kc == KC - 1, ['wup'] + hread, [f'H{b}_{u}'])
                        CP('pool', hb[:, b, u, 0:2], HALO[:, ch, :], [f'HALO{ch}'], [hbn])
                        CP('act', hb[:, b, u, 2:T + 2], Hp[b][u][:, :], [f'H{b}_{u}'], [hbn])
                        CP('pool', HALO[:, ch, :], hb[:, b, u, T:T + 2], [hbn], [f'HALO{ch}'])
                        ce = 'dve'
                        cvn = f'cv{u}'
                        ACTV(cv[:, u, :], Hp[b][u][:, :], AF.Identity, [f'H{b}_{u}', 'cw', 'cb'], [cvn], scale=cw[:, ch, 2:3], bias=cb[:, ch:ch + 1])
                        STT(ce, cv[:, u, :], hb[:, b, u, 1:T + 1], cw[:, ch, 1:2], cv[:, u, :], ALU.mult, ALU.add, [hbn, cvn, 'cw'], [cvn])
                        STT(ce, cv[:, u, :], hb[:, b, u, 0:T], cw[:, ch, 0:1], cv[:, u, :], ALU.mult, ALU.add, [hbn, cvn, 'cw'], [cvn])
                    if qi == 0:
                        continue
                    ACTV(sgl[:, :], cv[:, 0, :], AF.Silu, ['cv0'], ['sgl'])
                    TT('dve', ACTT[:, fc, :], sgl[:, :], cv[:, 1, :], ALU.mult, ['sgl', 'cv1'], [f'ACTT{fc}'])
                if qi == 0:
                    continue
                aread = [f'ACTT{fc}' for fc in range(NFC)]
                for bi in range(4):
                    row = qi * T + bi * 128
                    S.dma('yo_in', yo[:, :], x1_d[row:row + 128, :], reads=['x1_d', 'y_d'], writes=['yo'])
                    for half in range(2):
                        for fc in range(NFC):
                            MM(Yd[half][:, :], ACTT[:, fc, bi * 128:(bi + 1) * 128], wdn[:, fc, half * T:(half + 1) * T],
                               fc == 0, fc == NFC - 1, ['wdn'] + aread, [f'Yd{half}'])
                        TT('dve', yo[:, half * T:(half + 1) * T], Yd[half][:, :], yo[:, half * T:(half + 1) * T], ALU.add,
                           [f'Yd{half}', 'yo'], ['yo'])
                    orow = (qi - 1) * T + bi * 128
                    S.dma('yw', y_d[orow:orow + 128, :], yo[:, :], reads=['yo'], writes=['y_d'], q='pool')
            S.barrier()
            run_block()
        print("megakernel ops", S.nops, "waits", S.nwaits, {k: v for k, v in S.cnt.items() if k in engnames})
    return nc


_CACHE = {}


def _consts():
    ident = np.eye(128, dtype=np.float32)
    kk = np.arange(128)
    negu = -(kk[:, None] >= kk[None, :]).astype(np.float32)
    negl = -(kk[:, None] < kk[None, :]).astype(np.float32)
    bd = np.zeros((128, 128), np.float32)
    bd[:64, :64] = 1.0
    bd[64:, 64:] = 1.0
    qq = np.arange(T)
    mask = np.zeros((128, 4, T), np.float32)
    for d in range(4):
        mask[:, d, :] = ((128 * d + kk[:, None]) < qq[None, :]).astype(np.float32)
    return ident, negu, negl, bd, mask


def _tb_table(rel_bias):
    kk = np.arange(128)[:, None]
    qq = np.arange(128)[None, :]
    tb = np.empty((128, 8, 640), np.float32)
    for rp in range(5):
        idx = np.clip(qq - kk + rp * 128, -128, 128) + 128
        blk = rel_bias[:, idx]
        vis = np.ones((128, 128), bool)
        if rp == 0:
            vis = (kk < 64) | (qq >= 64)
        if rp == 4:
            vis = (kk >= 64) | (qq < 64)
        blk = np.where(vis[None], blk, np.float32(NEG))
        tb[:, :, rp * 128:(rp + 1) * 128] = blk.transpose(1, 0, 2)
    return tb


def kernel(x, norm1_g, w_in, q_norm_g, k_norm_g, rel_bias, w_branch_a, w_branch_b, w_out, norm2_g,
           w_ffn_up, ffn_conv_w, ffn_conv_b, w_ffn_down):
    x = np.asarray(x, np.float32)
    Bn, Sq, Dm = x.shape
    assert Bn == 2 and Dm == D and Sq % 2048 == 0
    NO = Sq // 2048
    NT = Sq // T
    key = (NT, NO)
    if key not in _CACHE:
        _CACHE[key] = build(NT, NO)
    nc = _CACHE[key]
    f = lambda a: np.ascontiguousarray(np.asarray(a, np.float32))
    ident, negu, negl, bd, mask = _consts()
    own = NO * T
    shared = {
        "g1": f(np.broadcast_to(np.asarray(norm1_g, np.float32)[0][None, :], (128, D))),
        "g2": f(np.broadcast_to(np.asarray(norm2_g, np.float32)[0][None, :], (128, D))),
        "w_in": f(w_in[0]),
        "gq": f(np.tile(np.asarray(q_norm_g, np.float32)[0], 2)[:, None]),
        "gk": f(np.tile(np.asarray(k_norm_g, np.float32)[0], 2)[:, None]),
        "tb": f(_tb_table(np.asarray(rel_bias, np.float32)[0])),
        "w_a": f(w_branch_a[0]), "w_b": f(w_branch_b[0]), "w_o": f(w_out[0]),
        "w_up": f(w_ffn_up[0]),
        "cw": f(np.asarray(ffn_conv_w, np.float32)[0].reshape(3, 44, 128).transpose(2, 1, 0)),
        "cb": f(np.asarray(ffn_conv_b, np.float32)[0].reshape(44, 128).T),
        "w_dn": f(w_ffn_down[0]),
        "ident": ident, "negu": negu, "negl": negl, "bd": bd, "mask": mask,
    }
    in_maps = []
    for c in range(8):
        b, j = c // 4, c % 4
        real = (j + 1) * own
        pad = Sq - real
        xl = np.zeros((Sq, D), np.float32)
        xl[pad:] = x[b, :real]
        valid = np.zeros((Sq,), np.float32)
        valid[pad:] = 1.0
        m = dict(shared)
        m["x"] = xl
        m["valid"] = f(valid.reshape(Sq // 128, 128).T)
        in_maps.append(m)
    res = run_bass_kernel_spmd(nc, in_maps, core_ids=list(range(8)))
    out = np.empty((Bn, Sq, D), np.float32)
    for c in range(8):
        b, j = c // 4, c % 4
        out[b, j * own:(j + 1) * own] = res.results[c]["y"]
    return out
```
